# Optimizing a Trainium2 kernel written in Bass

```python
import math
import jax, jax.numpy as jnp
from jax import lax
import numpy as np

D_MODEL = 1024
BATCH = 32
SEQ = 2048
DEPTH = 2

CHUNK = 64
N_META = 16
PAD = 2 * CHUNK - N_META
LEAD = PAD + N_META
Q_BLOCK = 128
N_A_LAYERS = DEPTH // 2
N_B_LAYERS = DEPTH - N_A_LAYERS
EPS = 1e-6
NEG = -1e30

DN_HEADS = D_MODEL // 128
DN_DK = 128
DN_DV = 128
DN_KW = DN_HEADS * DN_DK
DN_VW = DN_HEADS * DN_DV
CONV_K = 4

MLA_HEADS = D_MODEL // 128
QK_NOPE = 128
QK_ROPE = 64
QK_DIM = QK_NOPE + QK_ROPE
V_HEAD = 128
MLA_VW = MLA_HEADS * V_HEAD
KV_RANK = D_MODEL // 4
Q_RANK = 3 * D_MODEL // 8
ROPE_THETA = 10000.0

kernel_name = "yoco_gdn_mla_hybrid"


def rms_norm(x, g):
    xf = x.astype(jnp.float32)
    y = xf * lax.rsqrt(jnp.mean(xf * xf, -1, keepdims=True) + EPS)
    return (y * g.astype(jnp.float32)).astype(x.dtype)


def l2_norm(x):
    xf = x.astype(jnp.float32)
    return (xf * lax.rsqrt(jnp.sum(xf * xf, -1, keepdims=True) + EPS)).astype(x.dtype)


def causal_dwconv(x, w):
    c = x.shape[-1]
    return lax.conv_general_dilated(
        x, w[:, None, :].astype(x.dtype), window_strides=(1,),
        padding=[(w.shape[0] - 1, 0)], dimension_numbers=('NWC', 'WIO', 'NWC'),
        feature_group_count=c)


def rope(x, pos):
    half = x.shape[-1] // 2
    inv = ROPE_THETA ** (-jnp.arange(half, dtype=jnp.float32) / half)
    ang = pos.astype(jnp.float32)[:, None] * inv[None, :]
    cos = jnp.cos(ang)[None, :, None, :]
    sin = jnp.sin(ang)[None, :, None, :]
    x1 = x[..., :half].astype(jnp.float32)
    x2 = x[..., half:].astype(jnp.float32)
    return jnp.concatenate([x1 * cos - x2 * sin, x2 * cos + x1 * sin], -1).astype(x.dtype)


def gated_delta_rule(q, k, v, g, beta):
    out_dtype = v.dtype
    q, k, v = q.astype(jnp.float32), k.astype(jnp.float32), v.astype(jnp.float32)
    bsz, nh, _, c, dk = q.shape
    dv = v.shape[-1]
    gc = jnp.cumsum(g, -1)
    tril = jnp.tril(jnp.ones((c, c), bool))
    tril_strict = jnp.tril(jnp.ones((c, c), bool), -1)
    decay = jnp.exp(jnp.where(tril, gc[..., :, None] - gc[..., None, :], -jnp.inf))
    kb = k * beta[..., None]
    lower = jnp.where(tril_strict, jnp.einsum('bhnid,bhnjd->bhnij', kb, k) * decay, 0.0)
    system = lower + jnp.eye(c, dtype=jnp.float32)
    rhs = jnp.concatenate([v * beta[..., None], kb * jnp.exp(gc)[..., None]], -1)
    sol = lax.linalg.triangular_solve(system, rhs, left_side=True, lower=True)
    u_base, w_dec = sol[..., :dv], sol[..., dv:]
    attn_intra = jnp.where(tril, jnp.einsum('bhnid,bhnjd->bhnij', q, k) * decay, 0.0)
    q_dec = q * jnp.exp(gc)[..., None]
    k_dec = k * jnp.exp(gc[..., -1:] - gc)[..., None]
    g_last = jnp.exp(gc[..., -1])

    def step(state, xs):
        u0, wd, qd, att, kd, gl = xs
        u = u0 - jnp.einsum('bhck,bhkv->bhcv', wd, state)
        o = jnp.einsum('bhck,bhkv->bhcv', qd, state) + jnp.einsum('bhij,bhjv->bhiv', att, u)
        state = state * gl[..., None, None] + jnp.einsum('bhck,bhcv->bhkv', kd, u)
        return state, o

    xs = tuple(jnp.moveaxis(a, 2, 0) for a in (u_base, w_dec, q_dec, attn_intra, k_dec, g_last))
    s0 = jnp.zeros((bsz, nh, dk, dv), jnp.float32)
    _, o = lax.scan(step, s0, xs)
    return jnp.moveaxis(o, 0, 2).astype(out_dtype)


def deltanet_layer(x, valid, g_norm, w_in, conv_w, a_log, dt_bias, o_gain, w_out):
    bsz, lp, _ = x.shape
    n = lp // CHUNK
    h = jnp.where(valid[None, :, None], rms_norm(x, g_norm), 0)
    z = h @ w_in
    qkv = jax.nn.silu(causal_dwconv(z[..., :2 * DN_KW + DN_VW], conv_w))
    gate = z[..., 2 * DN_KW + DN_VW:2 * DN_KW + 2 * DN_VW]
    b_raw = z[..., 2 * DN_KW + 2 * DN_VW:2 * DN_KW + 2 * DN_VW + DN_HEADS].astype(jnp.float32)
    a_raw = z[..., 2 * DN_KW + 2 * DN_VW + DN_HEADS:].astype(jnp.float32)
    q = qkv[..., :DN_KW].reshape(bsz, lp, DN_HEADS, DN_DK)
    k = qkv[..., DN_KW:2 * DN_KW].reshape(bsz, lp, DN_HEADS, DN_DK)
    v = qkv[..., 2 * DN_KW:].reshape(bsz, lp, DN_HEADS, DN_DV)
    q = l2_norm(q) * (DN_DK ** -0.5)
    k = l2_norm(k)
    beta = jax.nn.sigmoid(b_raw)
    g = -jnp.exp(a_log.astype(jnp.float32)) * jax.nn.softplus(a_raw + dt_bias.astype(jnp.float32))

    def to_chunks(a):
        return a.reshape(bsz, n, CHUNK, DN_HEADS, -1).transpose(0, 3, 1, 2, 4)

    def gate_chunks(a):
        return a.reshape(bsz, n, CHUNK, DN_HEADS).transpose(0, 3, 1, 2)

    o = gated_delta_rule(to_chunks(q), to_chunks(k), to_chunks(v), gate_chunks(g), gate_chunks(beta))
    o = o.transpose(0, 2, 3, 1, 4).reshape(bsz, lp, DN_HEADS, DN_DV)
    o = rms_norm(o, o_gain) * jax.nn.silu(gate).reshape(bsz, lp, DN_HEADS, DN_DV)
    return x + o.reshape(bsz, lp, DN_VW) @ w_out


def shared_latent_kv(x, pos, g_norm, w_down, g_latent, w_uk, w_uv, k_gain):
    bsz, lp, _ = x.shape
    h = rms_norm(x, g_norm)
    c = h @ w_down
    c_kv = rms_norm(c[..., :KV_RANK], g_latent)
    k_pe = jnp.broadcast_to(c[..., None, KV_RANK:], (bsz, lp, MLA_HEADS, QK_ROPE))
    k_nope = (c_kv @ w_uk).reshape(bsz, lp, MLA_HEADS, QK_NOPE)
    v = (c_kv @ w_uv).reshape(bsz, lp, MLA_HEADS, V_HEAD)
    k = rms_norm(jnp.concatenate([k_nope, k_pe], -1), k_gain)
    k = jnp.concatenate([k[..., :QK_NOPE], rope(k[..., QK_NOPE:], pos)], -1)
    return k, v


def chunk_causal_attention(q, k, v):
    bsz, lp, nh, d = q.shape
    nb = lp // Q_BLOCK
    scale = d ** -0.5
    key_pos = jnp.arange(lp)
    key_chunk = key_pos // CHUNK
    key_ok = key_pos >= PAD
    kf = k.astype(jnp.float32)
    vf = v.astype(jnp.float32)
    qb = q.reshape(bsz, nb, Q_BLOCK, nh, d).transpose(1, 0, 2, 3, 4)

    def one_block(args):
        q_blk, i = args
        q_chunk = (i * Q_BLOCK + jnp.arange(Q_BLOCK)) // CHUNK
        mask = (key_chunk[None, :] <= q_chunk[:, None]) & key_ok[None, :]
        s = jnp.einsum('bqhd,bkhd->bhqk', q_blk.astype(jnp.float32), kf) * scale
        p = jax.nn.softmax(jnp.where(mask[None, None], s, NEG), -1)
        return jnp.einsum('bhqk,bkhd->bqhd', p, vf).astype(v.dtype)

    o = lax.map(one_block, (qb, jnp.arange(nb)))
    return o.transpose(1, 0, 2, 3, 4).reshape(bsz, lp, nh, v.shape[-1])


def mla_layer(x, pos, k, v, g_norm, w_in, g_q_latent, w_uq, q_gain, w_out):
    bsz, lp, _ = x.shape
    h = rms_norm(x, g_norm)
    z = h @ w_in
    c_q = rms_norm(z[..., :Q_RANK], g_q_latent)
    gate = z[..., Q_RANK:]
    q = rms_norm((c_q @ w_uq).reshape(bsz, lp, MLA_HEADS, QK_DIM), q_gain)
    q = jnp.concatenate([q[..., :QK_NOPE], rope(q[..., QK_NOPE:], pos)], -1)
    o = chunk_causal_attention(q, k, v).reshape(bsz, lp, MLA_VW) * jax.nn.silu(gate)
    return x + o @ w_out


def setup_inputs(seed: int = 0) -> dict:
    key = jax.random.key(seed)
    ks = jax.random.split(key, 24)
    f32 = jnp.float32

    def nrm(k, shape, fan_in):
        return jax.random.normal(k, shape, f32) * (fan_in ** -0.5)

    def gain(k, shape):
        return 1.0 + 0.01 * jax.random.normal(k, shape, f32)

    a_in_w = 2 * DN_KW + 2 * DN_VW + 2 * DN_HEADS
    dt = jnp.exp(jax.random.uniform(ks[5], (N_A_LAYERS, DN_HEADS), f32) * (math.log(0.1) - math.log(0.001)) + math.log(0.001))
    return {
        "x": jax.random.normal(ks[0], (BATCH, SEQ, D_MODEL), f32),
        "meta_tokens": jax.random.normal(ks[1], (N_META, D_MODEL), f32),
        "a_norm": gain(ks[2], (N_A_LAYERS, D_MODEL)),
        "a_w_in": nrm(ks[3], (N_A_LAYERS, D_MODEL, a_in_w), D_MODEL),
        "a_conv": nrm(ks[4], (N_A_LAYERS, CONV_K, 2 * DN_KW + DN_VW), CONV_K),
        "a_log": jnp.log(jax.random.uniform(ks[6], (N_A_LAYERS, DN_HEADS), f32, 1.0, 16.0)),
        "a_dt_bias": dt + jnp.log(-jnp.expm1(-dt)),
        "a_o_gain": gain(ks[7], (N_A_LAYERS, DN_DV)),
        "a_w_out": nrm(ks[8], (N_A_LAYERS, DN_VW, D_MODEL), DN_VW),
        "kv_norm": gain(ks[9], (D_MODEL,)),
        "kv_w_down": nrm(ks[10], (D_MODEL, KV_RANK + QK_ROPE), D_MODEL),
        "kv_latent_norm": gain(ks[11], (KV_RANK,)),
        "kv_w_uk": nrm(ks[12], (KV_RANK, MLA_HEADS * QK_NOPE), KV_RANK),
        "kv_w_uv": nrm(ks[13], (KV_RANK, MLA_VW), KV_RANK),
        "k_gain": gain(ks[14], (QK_DIM,)),
        "b_norm": gain(ks[15], (N_B_LAYERS, D_MODEL)),
        "b_w_in": nrm(ks[16], (N_B_LAYERS, D_MODEL, Q_RANK + MLA_VW), D_MODEL),
        "b_q_latent_norm": gain(ks[17], (N_B_LAYERS, Q_RANK)),
        "b_w_uq": nrm(ks[18], (N_B_LAYERS, Q_RANK, MLA_HEADS * QK_DIM), Q_RANK),
        "b_q_gain": gain(ks[19], (N_B_LAYERS, QK_DIM)),
        "b_w_out": nrm(ks[20], (N_B_LAYERS, MLA_VW, D_MODEL), MLA_VW),
    }


def reference(x, meta_tokens, a_norm, a_w_in, a_conv, a_log, a_dt_bias, a_o_gain, a_w_out,
              kv_norm, kv_w_down, kv_latent_norm, kv_w_uk, kv_w_uv, k_gain,
              b_norm, b_w_in, b_q_latent_norm, b_w_uq, b_q_gain, b_w_out):
    bsz = x.shape[0]
    pad = jnp.zeros((bsz, PAD, D_MODEL), x.dtype)
    meta = jnp.broadcast_to(meta_tokens.astype(x.dtype)[None], (bsz, N_META, D_MODEL))
    h = jnp.concatenate([pad, meta, x], 1)
    lp = h.shape[1]
    p = jnp.arange(lp)
    valid = p >= PAD
    pos = jnp.maximum(p - PAD, 0)
    k_sh = None
    v_sh = None
    for i in range(DEPTH):
        if i < N_A_LAYERS:
            h = deltanet_layer(h, valid, a_norm[i], a_w_in[i], a_conv[i], a_log[i],
                               a_dt_bias[i], a_o_gain[i], a_w_out[i])
            if i == N_A_LAYERS - 1:
                k_sh, v_sh = shared_latent_kv(h, pos, kv_norm, kv_w_down, kv_latent_norm,
                                              kv_w_uk, kv_w_uv, k_gain)
        else:
            j = i - N_A_LAYERS
            h = mla_layer(h, pos, k_sh, v_sh, b_norm[j], b_w_in[j], b_q_latent_norm[j],
                          b_w_uq[j], b_q_gain[j], b_w_out[j])
    return h[:, LEAD:]
```

```python
import numpy as np
from contextlib import ExitStack
import concourse.bass as bass
import concourse.mybir as mybir
from concourse.bass_utils import run_bass_kernel_spmd

F32 = mybir.dt.float32
BF16 = mybir.dt.bfloat16
AF = mybir.ActivationFunctionType
ALU = mybir.AluOpType
AX = mybir.AxisListType
ENGS = ("sp", "act", "pool", "dve", "pe")

D = 1024
H = 8
BIG = 30000.0
EPS = 1e-6
N_CORES = 8
CHAIN_DT = F32


class Buf:
    __slots__ = ("name", "w", "r", "excl")

    def __init__(self, name):
        self.name = name
        self.w = None
        self.r = {}
        self.excl = False


class Op:
    __slots__ = ("fn", "waits", "tok", "dma")

    def __init__(self, fn, waits, tok, dma):
        self.fn = fn
        self.waits = waits
        self.tok = tok
        self.dma = dma


class Sched:
    def __init__(self):
        self.ops = {e: [] for e in ENGS}
        self.count = {}
        self.known = {e: {} for e in ENGS}
        self.needed = set()
        self.nb = 0
        self.marks = []
        self.info = {}
        self.dbg = []
        self.names = {}

    def buf(self, name=None):
        self.nb += 1
        return Buf(name or f"b{self.nb}")

    def bufs(self, n, name="b"):
        return [self.buf(f"{name}{i}") for i in range(n)]

    def op(self, eng, fn, reads=(), writes=(), dma_dom=None):
        deps = {}

        def need(tok):
            if tok is not None and deps.get(tok[0], 0) < tok[1]:
                deps[tok[0]] = tok[1]

        own = dma_dom if dma_dom is not None else eng
        for b in reads:
            need(b.w)
            if b.excl:
                for d, i in b.r.items():
                    if d != own:
                        need((d, i))
        for b in writes:
            need(b.w)
            for d, i in b.r.items():
                need((d, i))
        is_dma = dma_dom is not None
        dom = dma_dom if is_dma else eng
        waits = {}
        kn = self.known[eng]
        for d, i in deps.items():
            if d == "pe" and eng == "pe" and not is_dma:
                continue
            if kn.get(d, 0) >= i:
                continue
            waits[d] = i
            kn[d] = i
            self.needed.add((d, i))
        c = self.count.get(dom, 0) + 1
        self.count[dom] = c
        tok = (dom, c)
        for b in reads:
            if b.r.get(dom, 0) < c:
                b.r[dom] = c
        for b in writes:
            b.w = tok
            b.r = {}
        o_ = Op(fn, waits, tok, is_dma)
        self.ops[eng].append(o_)
        self.dbg.append((o_, [b.name for b in reads], [b.name for b in writes]))
        return tok

    def barrier(self, label=None):
        self.marks.append((label, dict(self.count)))
        for e in ENGS:
            waits = {}
            kn = self.known[e]
            for d, c in self.count.items():
                if kn.get(d, 0) < c:
                    waits[d] = c
                    kn[d] = c
                    self.needed.add((d, c))
            if waits:
                self.ops[e].append(Op(None, waits, None, False))

    def emit(self, nc, stack):
        doms = sorted(self.count.keys())
        dma_doms = set()
        for e in ENGS:
            for o in self.ops[e]:
                if o.dma:
                    dma_doms.add(o.tok[0])
        sems = {d: stack.enter_context(nc.semaphore(f"s_{d}")) for d in doms}
        rank = {}
        for d in doms:
            if d in dma_doms:
                rank[d] = None
            else:
                idxs = sorted(i for (dd, i) in self.needed if dd == d)
                rank[d] = {i: k + 1 for k, i in enumerate(idxs)}
        block = stack.enter_context(nc.Block())
        ops = self.ops
        self.info = dict(sems={d: sems[d].num for d in doms},
                         marks=[(lab, {d: (rank[d][c] if rank[d] is not None else 16 * c) for d, c in cnt.items()
                                       if d in ("pe", "act", "dve", "pool")}) for lab, cnt in self.marks])

        def run(engh, lst):
            for o in lst:
                for d, i in o.waits.items():
                    engh.wait_ge(sems[d], 16 * i if rank[d] is None else rank[d][i])
                if o.fn is None:
                    continue
                ins = o.fn(engh)
                try:
                    self.names[ins.ins.name] = o
                except Exception:
                    pass
                d, i = o.tok
                if o.dma:
                    ins.then_inc(sems[d], 16)
                elif i in rank[d]:
                    ins.then_inc(sems[d], 1)

        @block.sync
        def _(e):
            run(e, ops["sp"])

        @block.scalar
        def _(e):
            run(e, ops["act"])

        @block.gpsimd
        def _(e):
            run(e, ops["pool"])

        @block.vector
        def _(e):
            run(e, ops["dve"])

        @block.tensor
        def _(e):
            run(e, ops["pe"])


def host_consts(T):
    p = np.arange(128)
    ident = np.eye(128, dtype=np.float32)
    tri = (p[:, None] <= p[None, :]).astype(np.float32)
    mui = np.where(p[None, :] < p[:, None], -BIG, 0.0).astype(np.float32)
    mls = np.where(p[None, :] >= p[:, None], BIG, 0.0).astype(np.float32)
    mus = np.where(p[None, :] <= p[:, None], -BIG, 0.0).astype(np.float32)
    rot = np.zeros((64, 64), np.float32)
    for m in range(32):
        rot[m + 32, m] = -1.0
        rot[m, m + 32] = 1.0
    pos = np.maximum(np.arange(T) - 112, 0).astype(np.float32)
    inv = (np.float32(10000.0) ** (-np.arange(32, dtype=np.float32) / np.float32(32))).astype(np.float32)
    ang = (pos[:, None] * inv[None, :]).astype(np.float32)
    cos = np.cos(ang).astype(np.float32).T
    sin = np.sin(ang).astype(np.float32).T
    cos2 = np.concatenate([cos, cos], 0).astype(np.float32)
    sin2 = np.concatenate([sin, sin], 0).astype(np.float32)
    padb = np.where(p < 112, -BIG, 0.0).astype(np.float32)[:, None]
    urow = np.where(p >= 64, 1.0, 0.0).astype(np.float32)[None, :]
    wrow = np.where(p < 64, -BIG, 0.0).astype(np.float32)[None, :]
    return dict(c_ident=ident, c_tri=tri, c_mui=mui, c_mls=mls, c_mus=mus, c_rot=rot, c_cos=cos2, c_sin=sin2,
                c_padb=padb, c_urow=urow, c_wrow=wrow)


class _Stop(Exception):
    pass


def build(NSEQ, SEQ, STOP=None):
    T = SEQ + 128
    NT = T // 128
    CB = [(c0, min(512, T - c0)) for c0 in range(0, T, 512)]
    nc = bass.Bass("TRN2", target_bir_lowering=False)

    def din(name, shape, dt=F32):
        return nc.dram_tensor(name, list(shape), dt, kind="ExternalInput").ap()

    x_d = din("x", [NSEQ * SEQ, D])
    meta_d = din("meta_tokens", [16, D])
    a_norm_d = din("a_norm", [D]); a_w_in_d = din("a_w_in", [D, 4112]); a_conv_d = din("a_conv", [4, 3072])
    a_log_d = din("a_log", [8]); a_dtb_d = din("a_dt_bias", [8]); a_og_d = din("a_o_gain", [128])
    a_w_out_d = din("a_w_out", [D, D]); kv_norm_d = din("kv_norm", [D]); kv_wd_d = din("kv_w_down", [D, 320])
    kv_ln_d = din("kv_latent_norm", [256]); kv_uk_d = din("kv_w_uk", [256, D]); kv_uv_d = din("kv_w_uv", [256, D])
    k_gain_d = din("k_gain", [192]); b_norm_d = din("b_norm", [D]); b_w_in_d = din("b_w_in", [D, 1408])
    b_qln_d = din("b_q_latent_norm", [384]); b_uq_d = din("b_w_uq", [384, 1536]); b_qg_d = din("b_q_gain", [192])
    b_w_out_d = din("b_w_out", [D, D])
    c_ident_d = din("c_ident", [128, 128]); c_tri_d = din("c_tri", [128, 128]); c_mui_d = din("c_mui", [128, 128])
    c_mls_d = din("c_mls", [128, 128]); c_mus_d = din("c_mus", [128, 128]); c_rot_d = din("c_rot", [64, 64]); c_cos_d = din("c_cos", [64, T])
    c_sin_d = din("c_sin", [64, T]); c_padb_d = din("c_padb", [128, 1]); c_urow_d = din("c_urow", [1, 128])
    c_wrow_d = din("c_wrow", [1, 128])
    out_d = nc.dram_tensor("out", [NSEQ * SEQ, D], F32, kind="ExternalOutput").ap()

    def dscr(name, shape, dt):
        return nc.dram_tensor(name, list(shape), dt).ap()

    w_in_b = dscr("w_in_b", [D, 4112], BF16); w_outa_b = dscr("w_outa_b", [D, D], BF16)
    w_down_b = dscr("w_down_b", [D, 320], BF16); w_uk_b = dscr("w_uk_b", [256, D], BF16)
    w_uv_b = dscr("w_uv_b", [256, D], BF16); w_inb_b = dscr("w_inb_b", [D, 1408], BF16)
    w_uq_b = dscr("w_uq_b", [384, 1536], BF16); w_outb_b = dscr("w_outb_b", [D, D], BF16)
    h1_d = dscr("h1_d", [T, D], F32)

    S = Sched()
    with ExitStack() as st:
        st.enter_context(nc.allow_non_contiguous_dma("small strided parameter loads"))

        def sb(name, shape, dt):
            return st.enter_context(nc.sbuf_tensor(name, list(shape), dt))

        def V(eng, fname, reads, writes, *a, **k):
            S.op(eng, lambda e: getattr(e, fname)(*a, **k), reads, writes)

        def ACT(reads, writes, out, in_, func, **k):
            S.op("act", lambda e: e.activation(out=out, in_=in_, func=func, **k), reads, writes)

        def MM(out, lhsT, rhs, start, stop, reads, writes):
            S.op("pe", lambda e: e.matmul(out, lhsT=lhsT, rhs=rhs, start=start, stop=stop), reads, writes)

        def TR(out, in_, ident, reads, writes):
            S.op("pe", lambda e: e.transpose(out=out, in_=in_, identity=ident), reads, writes)

        def DMA(q, out, in_, reads, writes, dom):
            S.op(q, lambda e: e.dma_start(out=out, in_=in_), reads, writes, dma_dom=dom)

        PB = [st.enter_context(nc.psum_tensor(f"pb{i}", [128, 512], F32)) for i in range(8)]
        PBb = S.bufs(8, "pb")
        for _b in PBb:
            _b.excl = True
        bctr = [0]
        bpool = [list(range(8))]

        live = set()

        def bank():
            pool_ = bpool[0]
            for k in range(len(pool_)):
                i = pool_[(bctr[0] + k) % len(pool_)]
                if i not in live:
                    bctr[0] += k + 1
                    live.add(i)
                    return PB[i], PBb[i]
            raise RuntimeError("no free PSUM bank")

        def rel(*bbs):
            for bb in bbs:
                live.discard(PBb.index(bb))

        def fbank(i):
            return PB[i], PBb[i]

        try:
            identF = sb("identF", [128, 128], F32); identB = sb("identB", [128, 128], BF16)
            triF = sb("triF", [128, 128], F32); muiF = sb("muiF", [128, 128], F32); mlsF = sb("mlsF", [128, 128], F32)
            musF = sb("musF", [128, 128], F32)
            onesB = sb("onesB", [128, 128], BF16); rotF = sb("rotF", [64, 64], F32); rotB = sb("rotB", [64, 64], BF16)
            cos2 = sb("cos2", [64, T], BF16); sin2 = sb("sin2", [64, T], BF16)
            padb = sb("padb", [128, 1], F32); zcol = sb("zcol", [128, 1], F32)
            urowF = sb("urowF", [1, 128], F32); wrowF = sb("wrowF", [1, 128], F32)
            urowB = sb("urowB", [1, 128], BF16); wrowB = sb("wrowB", [1, 128], BF16)
            g_anorm = sb("g_anorm", [128, 8], F32); g_kvnorm = sb("g_kvnorm", [128, 8], F32)
            g_bnorm = sb("g_bnorm", [128, 8], F32); g_kvln = sb("g_kvln", [128, 2], F32)
            g_qln = sb("g_qln", [128, 3], F32); g_og = sb("g_og", [128, 1], F32)
            kgn = sb("kgn", [128, 1], F32); kgr = sb("kgr", [64, 1], F32)
            qgn = sb("qgn", [128, 1], F32); qgr = sb("qgr", [64, 1], F32)
            convw = sb("convw", [128, 4, 24], F32)
            alog = sb("alog", [128, 8], F32); negA = sb("negA", [128, 8], F32); dtb = sb("dtb", [128, 8], F32)
            CONST = S.buf("const")

            def cload(dst, src):
                DMA("sp", dst, src, [], [CONST], "cst")

            cload(identF[:], c_ident_d); cload(triF[:], c_tri_d); cload(muiF[:], c_mui_d); cload(mlsF[:], c_mls_d); cload(musF[:], c_mus_d)
            cload(rotF[:], c_rot_d); cload(padb[:], c_padb_d)
            cload(urowF[:], c_urow_d); cload(wrowF[:], c_wrow_d)
            cload(g_anorm[:], a_norm_d.rearrange("(k p) -> p k", p=128))
            cload(g_kvnorm[:], kv_norm_d.rearrange("(k p) -> p k", p=128))
            cload(g_bnorm[:], b_norm_d.rearrange("(k p) -> p k", p=128))
            cload(g_kvln[:], kv_ln_d.rearrange("(k p) -> p k", p=128))
            cload(g_qln[:], b_qln_d.rearrange("(k p) -> p k", p=128))
            cload(g_og[:], a_og_d.rearrange("(p o) -> p o", o=1))
            cload(kgn[:], k_gain_d[0:128].rearrange("(p o) -> p o", o=1))
            cload(kgr[:], k_gain_d[128:192].rearrange("(p o) -> p o", o=1))
            cload(qgn[:], b_qg_d[0:128].rearrange("(p o) -> p o", o=1))
            cload(qgr[:], b_qg_d[128:192].rearrange("(p o) -> p o", o=1))
            for jj in range(4):
                cload(convw[:, jj, :], a_conv_d[jj, :].rearrange("(c p) -> p c", p=128))
            cload(alog[:], a_log_d.partition_broadcast(128))
            cload(dtb[:], a_dtb_d.partition_broadcast(128))
            CONST.w = ("cst", S.count["cst"])
            V("dve", "tensor_copy", [CONST], [CONST], out=identB[:], in_=identF[:])
            V("dve", "tensor_copy", [CONST], [CONST], out=rotB[:], in_=rotF[:])
            V("dve", "tensor_copy", [CONST], [CONST], out=urowB[:], in_=urowF[:])
            V("dve", "tensor_copy", [CONST], [CONST], out=wrowB[:], in_=wrowF[:])
            V("dve", "memset", [], [CONST], onesB[:], 1.0)
            V("dve", "memset", [], [CONST], zcol[:], 0.0)
            ACT([CONST], [CONST], negA[:], alog[:], AF.Exp)
            V("dve", "tensor_scalar_mul", [CONST], [CONST], out=negA[:], in0=negA[:], scalar1=-1.0)

            hT = sb("hT", [128, 8, T], BF16); hT_b = S.bufs(NT, "hT")
            OV = sb("OV", [128, NT, 1024], BF16); OV_b = S.bufs(NT, "OV")
            RR = sb("RR", [64, T], BF16); RR_b = S.buf("RR")
            xin = [sb(f"xin{i}", [128, 1024], F32) for i in range(2)]; xin_b = S.bufs(2, "xin")
            xn = [sb(f"xn{i}", [128, 1024], BF16) for i in range(2)]; xn_b = S.bufs(2, "xn")
            junk = sb("junk", [128, 1024], BF16); junk_b = S.buf("junk")
            ssq = [sb(f"ssq{i}", [128, 1], F32) for i in range(2)]; ssq_b = S.bufs(2, "ssq")
            ARENA_F32 = 26624
            arena = sb("arena", [128, ARENA_F32], F32)
            apos = [0]

            def areset(off=0):
                apos[0] = off

            def carve(shape, dt):
                P = shape[0]
                n = int(np.prod(shape[1:]))
                nbytes = n * (4 if dt == F32 else 2)
                n32 = (nbytes + 3) // 4
                o = apos[0]
                apos[0] += (n32 + 7) // 8 * 8
                assert apos[0] <= ARENA_F32, ("arena overflow", apos[0])
                v = arena[0:P, o:o + n32]
                if dt != F32:
                    v = v.bitcast(dt)[:, 0:n]
                if len(shape) == 3:
                    v = v.rearrange("p (a b) -> p a b", a=shape[1], b=shape[2])
                return v

            areset(0)
            wl = [carve([128, 1024], F32) for i in range(2)]; wl_b = S.bufs(2, "wl")
            ws = [carve([128, 1024], BF16) for i in range(2)]; ws_b = S.bufs(2, "ws")
            for tab_d, tab in ((c_cos_d, cos2), (c_sin_d, sin2)):
                for c0 in range(0, T, 1024):
                    cw = min(1024, T - c0)
                    sl = wctr_ = 0
                    DMA("sp", wl[0][0:64, :cw], tab_d[:, c0:c0 + cw], [], [wl_b[0]], "wl0")
                    V("dve", "tensor_copy", [wl_b[0]], [CONST], out=tab[:, c0:c0 + cw], in_=wl[0][0:64, :cw])
            wctr = [0]

            def prep_weight(src, K, N, dst, gcol):
                for kc in range(K // 128):
                    for c0 in range(0, N, 1024):
                        cw = min(1024, N - c0)
                        sl = wctr[0] % 2
                        wctr[0] += 1
                        DMA("sp", wl[sl][:, :cw], src[kc * 128:(kc + 1) * 128, c0:c0 + cw], [], [wl_b[sl]], f"wl{sl}")
                        if gcol is not None:
                            V("dve", "tensor_scalar_mul", [wl_b[sl], CONST], [ws_b[sl]], out=ws[sl][:, :cw],
                              in0=wl[sl][:, :cw], scalar1=gcol(kc))
                        else:
                            V("dve", "tensor_copy", [wl_b[sl]], [ws_b[sl]], out=ws[sl][:, :cw], in_=wl[sl][:, :cw])
                        DMA("act", dst[kc * 128:(kc + 1) * 128, c0:c0 + cw], ws[sl][:, :cw], [ws_b[sl]], [], f"ws{sl}")

            prep_weight(a_w_in_d, D, 4112, w_in_b, lambda kc: g_anorm[:, kc:kc + 1])
            prep_weight(a_w_out_d, D, D, w_outa_b, lambda kc: g_og[:, 0:1])
            prep_weight(kv_wd_d, D, 320, w_down_b, lambda kc: g_kvnorm[:, kc:kc + 1])
            prep_weight(kv_uk_d, 256, D, w_uk_b, lambda kc: g_kvln[:, kc:kc + 1])
            prep_weight(kv_uv_d, 256, D, w_uv_b, lambda kc: g_kvln[:, kc:kc + 1])
            prep_weight(b_w_in_d, D, 1408, w_inb_b, lambda kc: g_bnorm[:, kc:kc + 1])
            prep_weight(b_uq_d, 384, 1536, w_uq_b, lambda kc: g_qln[:, kc:kc + 1])
            prep_weight(b_w_out_d, D, D, w_outb_b, None)
            S.barrier()
            if STOP == "W":
                raise _Stop()

            def wview(wb, c0, cw):
                return wb[:, c0:c0 + cw].rearrange("(k p) n -> p k n", p=128)

            def load_x_tile(s, t, sl):
                if t == 0:
                    V("pool", "memset", [], [xin_b[sl]], xin[sl][:], 0.0)
                    DMA("sp", xin[sl][112:128, :], meta_d, [], [xin_b[sl]], f"xl{sl}")
                else:
                    r0 = s * SEQ + (t - 1) * 128
                    DMA("sp", xin[sl][:], x_d[r0:r0 + 128, :], [], [xin_b[sl]], f"xl{sl}")

            def norm_to_hT(src, src_b, t, sl):
                ACT([src_b], [junk_b, ssq_b[sl]], junk[:], src, AF.Square, accum_out=ssq[sl][:])
                ACT([ssq_b[sl]], [ssq_b[sl]], ssq[sl][:], ssq[sl][:], AF.Ln, scale=1.0 / D, bias=EPS)
                ACT([ssq_b[sl]], [ssq_b[sl]], ssq[sl][:], ssq[sl][:], AF.Exp, scale=-0.5)
                V("dve", "tensor_scalar_mul", [src_b, ssq_b[sl]], [xn_b[sl]], out=xn[sl][:], in0=src, scalar1=ssq[sl][:, 0:1])
                pb, pbb = bank()
                pbv = pb[:].bitcast(BF16)
                for kc in range(8):
                    TR(pbv[:, kc * 128:(kc + 1) * 128], xn[sl][:, kc * 128:(kc + 1) * 128], identB[:], [xn_b[sl], CONST], [pbb])
                V("dve", "tensor_copy", [pbb], [hT_b[t]], out=hT[:, :, t * 128:(t + 1) * 128],
                  in_=pbv.rearrange("p (k c) -> p k c", k=8))
                rel(pbb)

            def hT_reads(c0, cw):
                return [hT_b[t] for t in range(c0 // 128, (c0 + cw + 127) // 128)]

            def silu_from(src_ap, src_reads, tmp, tmp_b, shape_sl):
                ACT(src_reads, [tmp_b], tmp, src_ap, AF.Exp, scale=-1.0)
                ACT([tmp_b], [tmp_b], tmp, tmp, AF.Ln, bias=1.0)
                ACT([tmp_b], [tmp_b], tmp, tmp, AF.Exp, scale=-1.0)

            def rsqrt_inplace(ap, b, scale, reads_extra=()):
                ACT([b] + list(reads_extra), [b], ap, ap, AF.Ln, scale=scale, bias=EPS)
                ACT([b], [b], ap, ap, AF.Exp, scale=-0.5)

            for s in range(NSEQ):
                bpool[0] = list(range(8))
                for t in range(NT):
                    sl = t % 2
                    load_x_tile(s, t, sl)
                    norm_to_hT(xin[sl][:], xin_b[sl], t, sl)
                S.barrier()
                if STOP == "A0":
                    raise _Stop()
                areset(0)
                wab = carve([128, 8, 16], BF16); wab_b = S.buf()
                ab = carve([128, NT, 16], F32); ab_b = S.buf()
                tm1 = carve([128, NT, 8], F32); tm1_b = S.buf()
                gcol = carve([128, NT, 8], F32); g_b = S.buf()
                lnb = carve([128, NT, 8], F32); lnb_b = S.buf()
                gc = carve([128, NT, 8], F32); gc_b = S.buf()
                ngc = carve([128, NT, 8], F32); egc = carve([128, NT, 8], F32); bls = carve([128, NT, 8], F32)
                beta = carve([128, NT, 8], F32); begc = carve([128, NT, 8], F32)
                der_b = S.buf()
                DMA("sp", wab, wview(w_in_b, 4096, 16), [], [wab_b], "wab")
                pb, pbb = bank()
                for t in range(NT):
                    for kc in range(8):
                        MM(pb[:, t * 16:(t + 1) * 16], hT[:, kc, t * 128:(t + 1) * 128], wab[:, kc, :], kc == 0, kc == 7,
                           [hT_b[t], wab_b], [pbb])
                V("dve", "tensor_copy", [pbb], [ab_b], out=ab, in_=pb[:, 0:NT * 16].rearrange("p (t c) -> p t c", c=16))
                rel(pbb)
                ACT([ab_b], [tm1_b], tm1, ab[:, :, 0:8], AF.Exp, scale=-1.0)
                ACT([tm1_b], [tm1_b], tm1, tm1, AF.Ln, bias=1.0)
                V("dve", "tensor_scalar_mul", [tm1_b], [lnb_b], out=lnb, in0=tm1, scalar1=-1.0)
                V("dve", "tensor_tensor", [ab_b, CONST], [tm1_b], out=tm1, in0=ab[:, :, 8:16],
                  in1=dtb[:].unsqueeze(1).to_broadcast([128, NT, 8]), op=ALU.add)
                ACT([tm1_b], [tm1_b], tm1, tm1, AF.Exp)
                ACT([tm1_b], [tm1_b], tm1, tm1, AF.Ln, bias=1.0)
                V("dve", "tensor_tensor", [tm1_b, CONST], [g_b], out=gcol, in0=tm1,
                  in1=negA[:].unsqueeze(1).to_broadcast([128, NT, 8]), op=ALU.mult)
                pb, pbb = bank()
                for t in range(NT):
                    MM(pb[:, t * 8:(t + 1) * 8], triF[:], gcol[:, t, :], True, True, [CONST, g_b], [pbb])
                V("dve", "tensor_copy", [pbb], [gc_b], out=gc, in_=pb[:, 0:NT * 8].rearrange("p (t c) -> p t c", c=8))
                rel(pbb)
                V("dve", "tensor_scalar_mul", [gc_b], [der_b], out=ngc, in0=gc, scalar1=-1.0)
                ACT([gc_b], [der_b], egc, gc, AF.Exp)
                V("dve", "tensor_tensor", [gc_b, lnb_b], [der_b], out=bls, in0=gc, in1=lnb, op=ALU.add)
                ACT([lnb_b], [der_b], beta, lnb, AF.Exp)
                ACT([der_b], [der_b], begc, bls, AF.Exp)

                if STOP == "Aab":
                    raise _Stop()
                a12_base = apos[0]
                for h in range(1):
                    areset(a12_base)
                    wst = [carve([128, 8, 128], BF16) for _ in range(2)]; wst_b = S.bufs(2)
                    zc = [carve([128, 515], F32) for _ in range(2)]; zc_b = S.bufs(2)
                    ta = carve([128, 512], F32); ta_b = S.buf()
                    tb = carve([128, 512], F32); tb_b = S.buf()
                    sq = carve([128, 512], BF16); sq_b = S.buf()
                    sT = [carve([128, T], BF16) for _ in range(3)]; sT_b = S.bufs(3)
                    Ktok = carve([128, NT, 128], BF16); Vtok = carve([128, NT, 128], BF16); KV_b = S.bufs(2)
                    Sst = carve([128, 128], F32); Sbf = carve([128, 128], BF16); S_b = S.buf(); Sb_b = S.buf()
                    NSL = 8
                    cm = []
                    for i in range(NSL):
                        cm.append(dict(
                            Eui=carve([128, 128], F32), Els=carve([128, 128], F32), Eus=carve([128, 128], F32),
                            Z=[carve([128, 256], CHAIN_DT) for _ in range(2)], P=[carve([128, 128], CHAIN_DT) for _ in range(2)],
                            attT=carve([128, 128], BF16), TmT=carve([128, 128], BF16), nWdT=carve([128, 128], BF16),
                            Bk=carve([128, 128], BF16), bV=carve([128, 128], BF16), Kd=carve([128, 128], BF16),
                            Ub=carve([128, 128], BF16), glc=carve([128, 1], F32), o1=carve([128, 128], F32),
                            b={k: S.buf() for k in ("Eui", "Els", "Eus", "Z0", "Z1", "P0", "P1", "attT", "TmT", "nWdT", "Bk", "bV",
                                                    "Kd", "Ub", "glc", "o1")}))
                for h in range(H):
                    for j in range(3):
                        fc = j * 8 + h
                        wsl = j % 2
                        DMA("sp", wst[wsl], wview(w_in_b, fc * 128, 128), [], [wst_b[wsl]], f"wst{wsl}")
                        for bi, (c0, cw) in enumerate(CB):
                            zs = bi % 2
                            pb, pbb = bank()
                            for kc in range(8):
                                MM(pb[:, :cw], wst[wsl][:, kc, :], hT[:, kc, c0:c0 + cw], kc == 0, kc == 7,
                                   [wst_b[wsl]] + hT_reads(c0, cw), [pbb])
                            if bi == 0:
                                V("pool", "memset", [], [zc_b[zs]], zc[zs][:, 0:3], 0.0)
                            else:
                                V("pool", "tensor_copy", [zc_b[1 - zs]], [zc_b[zs]], out=zc[zs][:, 0:3], in_=zc[1 - zs][:, 512:515])
                            S.op("act", (lambda o, i: (lambda e: e.copy(out=o, in_=i)))(zc[zs][:, 3:3 + cw], pb[:, :cw]),
                                 [pbb], [zc_b[zs]])
                            rel(pbb)
                            V("dve", "tensor_scalar_mul", [zc_b[zs], CONST], [ta_b], out=ta[:, :cw], in0=zc[zs][:, 3:3 + cw],
                              scalar1=convw[:, 3, fc:fc + 1])
                            for jj in (2, 1, 0):
                                V("dve", "scalar_tensor_tensor", [zc_b[zs], CONST, ta_b], [ta_b], out=ta[:, :cw],
                                  in0=zc[zs][:, jj:jj + cw], scalar=convw[:, jj, fc:fc + 1], in1=ta[:, :cw],
                                  op0=ALU.mult, op1=ALU.add)
                            silu_from(ta[:, :cw], [ta_b], tb[:, :cw], tb_b, None)
                            V("dve", "tensor_tensor", [ta_b, tb_b], [sT_b[j]], out=sT[j][:, c0:c0 + cw], in0=ta[:, :cw],
                              in1=tb[:, :cw], op=ALU.mult)
                            if j < 2:
                                ACT([sT_b[j]], [sq_b], sq[:, :cw], sT[j][:, c0:c0 + cw], AF.Square)
                                pb2, pbb2 = bank()
                                MM(pb2[:, :cw], onesB[:], sq[:, :cw], True, True, [CONST, sq_b], [pbb2])
                                ACT([pbb2], [tb_b], tb[:, :cw], pb2[:, :cw], AF.Ln, bias=EPS)
                                rel(pbb2)
                                ACT([tb_b], [tb_b], tb[:, :cw], tb[:, :cw], AF.Exp, scale=-0.5)
                                V("dve", "scalar_tensor_tensor", [sT_b[j], tb_b], [sT_b[j]], out=sT[j][:, c0:c0 + cw],
                                  in0=sT[j][:, c0:c0 + cw], scalar=(128.0 ** -0.5 if j == 0 else 1.0), in1=tb[:, :cw],
                                  op0=ALU.mult, op1=ALU.mult)
                    qT, kT, vT = sT
                    if STOP == "A12a":
                        raise _Stop()
                    for j, dst in ((1, Ktok), (2, Vtok)):
                        for t0 in range(0, NT, 8):
                            tn = min(8, NT - t0)
                            pb, pbb = bank()
                            pbv = pb[:].bitcast(BF16)
                            for i in range(tn):
                                TR(pbv[:, i * 128:(i + 1) * 128], sT[j][:, (t0 + i) * 128:(t0 + i + 1) * 128], identB[:],
                                   [sT_b[j], CONST], [pbb])
                            V("dve", "tensor_copy", [pbb], [KV_b[j - 1]], out=dst[:, t0:t0 + tn, :],
                              in_=pbv[:, 0:tn * 128].rearrange("p (t c) -> p t c", c=128))
                            rel(pbb)
                    V("pool", "memset", [], [S_b], Sst[:], 0.0)
                    V("pool", "memset", [], [Sb_b], Sbf[:], 0.0)
                    if STOP == "A12b":
                        raise _Stop()
                    def st_G(c):
                        m = cm[c % NSL]; mb = m["b"]
                        cs = slice(c * 128, (c + 1) * 128)
                        gb_l = gcol[:, c, h:h + 1].to_broadcast([128, 128])
                        lb_l = lnb[:, c, h:h + 1].to_broadcast([128, 128])
                        pg, pgb = bank()
                        pt, ptb = bank()
                        m["pg"], m["pgb"], m["pt"], m["ptb"] = pg, pgb, pt, ptb
                        MM(pg[:, 0:128], gb_l, triF[:], True, False, [g_b, CONST], [pgb])
                        MM(pg[:, 0:128], identF[:], muiF[:], False, True, [CONST], [pgb])
                        MM(pg[:, 128:256], gb_l, triF[:], True, False, [g_b, CONST], [pgb])
                        MM(pg[:, 128:256], identF[:], mlsF[:], False, True, [CONST], [pgb])
                        MM(pg[:, 256:384], kT[:, cs], kT[:, cs], True, True, [sT_b[1]], [pgb])
                        MM(pg[:, 384:512], kT[:, cs], qT[:, cs], True, True, [sT_b[1], sT_b[0]], [pgb])
                        MM(pt[:, 0:128], gb_l, triF[:], True, False, [g_b, CONST], [ptb])
                        MM(pt[:, 0:128], lb_l, identF[:], False, False, [lnb_b, CONST], [ptb])
                        MM(pt[:, 0:128], identF[:], musF[:], False, True, [CONST], [ptb])

                    def st_E(c):
                        m = cm[c % NSL]; mb = m["b"]
                        pg, pgb, pt, ptb = m["pg"], m["pgb"], m["pt"], m["ptb"]
                        ACT([pgb, der_b], [mb["Eui"]], m["Eui"], pg[:, 0:128], AF.Exp, bias=ngc[:, c, h:h + 1], scale=1.0)
                        ACT([pgb, der_b], [mb["Els"]], m["Els"], pg[:, 128:256], AF.Exp, bias=bls[:, c, h:h + 1], scale=-1.0)
                        ACT([pgb], [mb["glc"]], m["glc"], pg[:, 127:128], AF.Exp)
                        ACT([ptb, der_b], [mb["Eus"]], m["Eus"], pt[:, 0:128], AF.Exp, bias=ngc[:, c, h:h + 1], scale=1.0)
                        V("dve", "scalar_tensor_tensor", [pgb, mb["Els"]], [mb["Z0"]], out=m["Z"][0][:, 0:128],
                          in0=pg[:, 256:384], scalar=-1.0, in1=m["Els"], op0=ALU.mult, op1=ALU.mult)
                        V("dve", "scalar_tensor_tensor", [pgb, mb["Eus"]], [mb["Z0"]], out=m["Z"][0][:, 128:256],
                          in0=pg[:, 256:384], scalar=-1.0, in1=m["Eus"], op0=ALU.mult, op1=ALU.mult)
                        V("dve", "tensor_tensor", [pgb, mb["Eui"]], [mb["attT"]], out=m["attT"], in0=pg[:, 384:512],
                          in1=m["Eui"], op=ALU.mult)
                        V("dve", "tensor_tensor", [mb["Z0"], CONST], [mb["P0"]], out=m["P"][0], in0=m["Z"][0][:, 128:256],
                          in1=identF[:], op=ALU.add)
                        V("pool", "tensor_scalar_mul", [KV_b[0], der_b], [mb["Bk"]], out=m["Bk"], in0=Ktok[:, c, :],
                          scalar1=begc[:, c, h:h + 1])
                        V("pool", "tensor_scalar_mul", [KV_b[1], der_b], [mb["bV"]], out=m["bV"], in0=Vtok[:, c, :],
                          scalar1=beta[:, c, h:h + 1])
                        V("pool", "tensor_scalar_mul", [KV_b[0], mb["Eui"]], [mb["Kd"]], out=m["Kd"], in0=Ktok[:, c, :],
                          scalar1=m["Eui"][:, 127:128])
                        rel(pgb, ptb)

                    def st_Lmm(c, lev):
                        m = cm[c % NSL]; mb = m["b"]
                        zi = lev % 2
                        pi = (lev - 1) % 2
                        Zc, Zcb = m["Z"][zi], mb[f"Z{zi}"]
                        pk, pkb = bank()
                        m["pk"], m["pkb"] = pk, pkb
                        if lev >= 1:
                            MM(pk[:, 256:384], Zc[:, 0:128], m["P"][pi], True, True, [Zcb, mb[f"P{pi}"]], [pkb])
                        if lev < 6:
                            MM(pk[:, 0:128], Zc[:, 128:256], Zc[:, 0:128], True, True, [Zcb], [pkb])
                            MM(pk[:, 128:256], Zc[:, 0:128], Zc[:, 128:256], True, True, [Zcb], [pkb])

                    def st_Lev(c, lev):
                        m = cm[c % NSL]; mb = m["b"]
                        zi = lev % 2
                        pi = (lev - 1) % 2
                        Zn, Znb = m["Z"][1 - zi], mb[f"Z{1 - zi}"]
                        pk, pkb = m["pk"], m["pkb"]
                        if lev >= 1:
                            if lev < 6:
                                V("dve", "tensor_tensor", [pkb, mb[f"P{pi}"]], [mb[f"P{1 - pi}"]], out=m["P"][1 - pi],
                                  in0=pk[:, 256:384], in1=m["P"][pi], op=ALU.add)
                            else:
                                V("dve", "tensor_tensor", [pkb, mb[f"P{pi}"]], [mb["TmT"]], out=m["TmT"],
                                  in0=pk[:, 256:384], in1=m["P"][pi], op=ALU.add)
                        if lev < 6:
                            V("act", "copy", [pkb], [Znb], out=Zn[:, 0:256], in_=pk[:, 0:256])
                        rel(pkb)

                    def st_W(c):
                        m = cm[c % NSL]; mb = m["b"]
                        pw, pwb = bank()
                        MM(pw[:, 0:128], m["Bk"], m["TmT"], True, True, [mb["Bk"], mb["TmT"]], [pwb])
                        V("act", "mul", [pwb], [mb["nWdT"]], out=m["nWdT"], in_=pw[:, 0:128], mul=-1.0)
                        rel(pwb)

                    def st_R(c):
                        m = cm[c % NSL]; mb = m["b"]
                        cs = slice(c * 128, (c + 1) * 128)
                        pu, pub = bank()
                        MM(pu[:, 0:128], m["TmT"], m["bV"], True, c == 0, [mb["TmT"], mb["bV"]], [pub])
                        if c > 0:
                            MM(pu[:, 0:128], m["nWdT"], Sbf, False, True, [mb["nWdT"], Sb_b], [pub])
                        V("act", "copy", [pub], [mb["Ub"]], out=m["Ub"], in_=pu[:, 0:128])
                        rel(pub)
                        po, pob = bank()
                        po2, pob2 = bank()
                        MM(po[:, 0:128], m["Kd"], m["Ub"], True, True, [mb["Kd"], mb["Ub"]], [pob])
                        MM(po2[:, 0:128], qT[:, cs], Sbf, True, True, [sT_b[0], Sb_b], [pob2])
                        MM(po2[:, 128:256], m["attT"], m["Ub"], True, True, [mb["attT"], mb["Ub"]], [pob2])
                        V("dve", "scalar_tensor_tensor", [S_b, mb["glc"], pob], [Sb_b], out=Sbf[:], in0=Sst[:],
                          scalar=m["glc"][:, 0:1], in1=po[:, 0:128], op0=ALU.mult, op1=ALU.add)
                        V("dve", "scalar_tensor_tensor", [S_b, mb["glc"], pob], [S_b], out=Sst[:], in0=Sst[:],
                          scalar=m["glc"][:, 0:1], in1=po[:, 0:128], op0=ALU.mult, op1=ALU.add)
                        ACT([pob2, der_b], [mb["o1"]], m["o1"], po2[:, 0:128], AF.Copy, scale=egc[:, c, h:h + 1])
                        V("dve", "tensor_tensor", [pob2, mb["o1"]], [OV_b[c]], out=OV[:, c, h * 128:(h + 1) * 128],
                          in0=po2[:, 128:256], in1=m["o1"], op=ALU.add)
                        rel(pob, pob2)

                    GS = 4
                    groups = [list(range(i, min(i + GS, NT))) for i in range(0, NT, GS)]
                    pending = []
                    for grp in groups:
                        stages = [("GE", grp[0:2]), ("GE", grp[2:4])] + [("L", lev) for lev in range(7)] + [("W", None)]
                        for kind, lev in stages:
                            if kind == "GE":
                                for c in lev:
                                    st_G(c)
                                for c in lev:
                                    st_E(c)
                            elif kind == "L":
                                for c in grp:
                                    st_Lmm(c, lev)
                                for c in grp:
                                    st_Lev(c, lev)
                            else:
                                for c in grp:
                                    st_W(c)
                            if pending:
                                st_R(pending.pop(0))
                        while pending:
                            st_R(pending.pop(0))
                        pending = list(grp)
                    while pending:
                        st_R(pending.pop(0))
                S.barrier()
                if STOP == "A12":
                    raise _Stop()
                areset(0)
                gateW = carve([128, 8, 1024], BF16); outW = carve([128, 8, 1024], BF16); gw_b = S.buf(); ow_b = S.buf()
                sig = carve([128, 1024], F32); sig_b = S.buf()
                gs = carve([128, 1024], F32); gs_b = S.buf()
                osq = carve([128, 1024], F32); osq_b = S.buf()
                oss = carve([128, 8], F32); oss_b = S.buf()
                ogb = carve([128, 1024], BF16); ogb_b = S.buf()
                ogT = carve([128, 8, 128], BF16); ogT_b = S.buf()
                h1t = [carve([128, 1024], F32) for _ in range(2)]; h1t_b = S.bufs(2)
                DMA("sp", gateW, wview(w_in_b, 3072, 1024), [], [gw_b], "gw")
                DMA("sp", outW, wview(w_outa_b, 0, 1024), [], [ow_b], "ow")
                for t in range(NT):
                    sl = t % 2
                    load_x_tile(s, t, sl)
                    pgs = [bank(), bank()]
                    for hf in range(2):
                        pb, pbb = pgs[hf]
                        for kc in range(8):
                            MM(pb[:, :], hT[:, kc, t * 128:(t + 1) * 128], gateW[:, kc, hf * 512:(hf + 1) * 512], kc == 0, kc == 7,
                               [hT_b[t], gw_b], [pbb])
                        silu_from(pb[:, :], [pbb], sig[:, hf * 512:(hf + 1) * 512], sig_b, None)
                        V("dve", "tensor_tensor", [pbb, sig_b], [gs_b], out=gs[:, hf * 512:(hf + 1) * 512], in0=pb[:, :],
                          in1=sig[:, hf * 512:(hf + 1) * 512], op=ALU.mult)
                        rel(pbb)
                    ACT([OV_b[t]], [osq_b], osq, OV[:, t, :], AF.Square)
                    V("dve", "tensor_reduce", [osq_b], [oss_b], out=oss, in_=osq.rearrange("p (a b) -> p a b", a=8),
                      axis=AX.X, op=ALU.add)
                    ACT([oss_b], [oss_b], oss, oss, AF.Ln, scale=1.0 / 128, bias=EPS)
                    ACT([oss_b], [oss_b], oss, oss, AF.Exp, scale=-0.5)
                    V("dve", "tensor_tensor", [OV_b[t], oss_b], [osq_b], out=osq.rearrange("p (a b) -> p a b", a=8),
                      in0=OV[:, t, :].rearrange("p (a b) -> p a b", a=8), in1=oss.unsqueeze(2).to_broadcast([128, 8, 128]),
                      op=ALU.mult)
                    V("dve", "tensor_tensor", [osq_b, gs_b], [ogb_b], out=ogb, in0=osq, in1=gs, op=ALU.mult)
                    pb, pbb = bank()
                    pbv = pb[:].bitcast(BF16)
                    for kc in range(8):
                        TR(pbv[:, kc * 128:(kc + 1) * 128], ogb[:, kc * 128:(kc + 1) * 128], identB[:], [ogb_b, CONST], [pbb])
                    V("dve", "tensor_copy", [pbb], [ogT_b], out=ogT, in_=pbv.rearrange("p (k c) -> p k c", k=8))
                    rel(pbb)
                    for hf in range(2):
                        pb, pbb = bank()
                        for kc in range(8):
                            MM(pb[:, :], ogT[:, kc, :], outW[:, kc, hf * 512:(hf + 1) * 512], kc == 0, kc == 7, [ogT_b, ow_b], [pbb])
                        V("dve", "tensor_tensor", [pbb, xin_b[sl]], [h1t_b[sl]], out=h1t[sl][:, hf * 512:(hf + 1) * 512],
                          in0=pb[:, :], in1=xin[sl][:, hf * 512:(hf + 1) * 512], op=ALU.add)
                        rel(pbb)
                    DMA("act", h1_d[t * 128:(t + 1) * 128, :], h1t[sl], [h1t_b[sl]], [], f"h1s{sl}")
                    norm_to_hT(h1t[sl], h1t_b[sl], t, sl)
                S.barrier()
                if STOP == "A3":
                    raise _Stop()
                areset(0)
                KT = carve([128, 8, T], BF16); KT_b = S.buf()
                ksc = carve([128, NT, 8], F32); ksc_b = S.buf()
                kv_base = apos[0]
                bpool[0] = list(range(7))
                wdn = carve([128, 8, 320], BF16); wuk = carve([128, 2, 1024], BF16); wuv = carve([128, 2, 1024], BF16)
                wkv_b = S.buf()
                cTf = carve([128, 2, 512], F32); cT_b = S.buf()
                kpe = carve([64, 512], F32); kpe_b = S.buf()
                kpb = carve([64, 512], BF16); kpb_b = S.buf()
                csq = carve([128, 2, 512], BF16); csq_b = S.buf()
                rb = carve([128, 512], F32); rb_b = S.buf()
                ckv = carve([128, 2, 512], BF16); ckv_b = S.buf()
                sqK = [carve([128, 512], BF16) for _ in range(2)]; sqK_b = S.bufs(2)
                sqR = carve([64, 512], BF16); sqR_b = S.buf()
                t1 = carve([64, 512], F32); t1_b = S.buf()
                t2 = carve([64, 512], F32); t2_b = S.buf()
                DMA("sp", wdn, wview(w_down_b, 0, 320), [], [wkv_b], "wkv")
                DMA("sp", wuk, wview(w_uk_b, 0, 1024), [], [wkv_b], "wkv")
                DMA("sp", wuv, wview(w_uv_b, 0, 1024), [], [wkv_b], "wkv")
                pks, pksb = fbank(7)
                for (c0, cw) in CB:
                    rd = hT_reads(c0, cw)
                    pbs_ = [bank(), bank(), bank()]
                    for j, (lo, mw) in enumerate(((0, 128), (128, 128), (256, 64))):
                        pb, pbb = pbs_[j]
                        for kc in range(8):
                            MM(pb[0:mw, :cw], wdn[:, kc, lo:lo + mw], hT[:, kc, c0:c0 + cw], kc == 0, kc == 7, [wkv_b] + rd, [pbb])
                    for j in range(2):
                        pb, pbb = pbs_[j]
                        S.op("act", (lambda o, i: (lambda e: e.copy(out=o, in_=i)))(cTf[:, j, :cw], pb[:, :cw]), [pbb], [cT_b])
                        rel(pbb)
                    pb, pbb = pbs_[2]
                    V("dve", "tensor_scalar_mul", [pbb, CONST], [kpe_b], out=kpe[:, :cw], in0=pb[0:64, :cw], scalar1=kgr[:, 0:1])
                    ACT([pbb], [sqR_b], sqR[:, :cw], pb[0:64, :cw], AF.Square)
                    rel(pbb)
                    ACT([cT_b], [csq_b], csq[:, :, :cw], cTf[:, :, :cw], AF.Square)
                    pb, pbb = bank()
                    for j in range(2):
                        MM(pb[:, :cw], onesB[:], csq[:, j, :cw], j == 0, j == 1, [CONST, csq_b], [pbb])
                    ACT([pbb], [rb_b], rb[:, :cw], pb[:, :cw], AF.Ln, scale=1.0 / 256, bias=EPS)
                    rel(pbb)
                    ACT([rb_b], [rb_b], rb[:, :cw], rb[:, :cw], AF.Exp, scale=-0.5)
                    V("dve", "tensor_tensor", [cT_b, rb_b], [ckv_b], out=ckv[:, :, :cw], in0=cTf[:, :, :cw],
                      in1=rb[:, :cw].unsqueeze(1).to_broadcast([128, 2, cw]), op=ALU.mult)
                    V("pool", "tensor_copy", [kpe_b], [kpb_b], out=kpb[:, :cw], in_=kpe[:, :cw])
                    pb, pbb = bank()
                    MM(pb[0:64, :cw], rotB[:], kpb[:, :cw], True, True, [CONST, kpb_b], [pbb])
                    V("dve", "tensor_tensor", [pbb, CONST], [t1_b], out=t1[:, :cw], in0=pb[0:64, :cw], in1=sin2[:, c0:c0 + cw], op=ALU.mult)
                    rel(pbb)
                    V("dve", "tensor_tensor", [kpe_b, CONST], [t2_b], out=t2[:, :cw], in0=kpe[:, :cw], in1=cos2[:, c0:c0 + cw], op=ALU.mult)
                    V("dve", "tensor_tensor", [t1_b, t2_b], [RR_b], out=RR[:, c0:c0 + cw], in0=t1[:, :cw], in1=t2[:, :cw], op=ALU.add)
                    for h in range(H):
                        pb, pbb = bank()
                        for r in range(2):
                            MM(pb[:, :cw], wuk[:, r, h * 128:(h + 1) * 128], ckv[:, r, :cw], r == 0, r == 1, [wkv_b, ckv_b], [pbb])
                        V("dve", "tensor_scalar_mul", [pbb, CONST], [KT_b], out=KT[:, h, c0:c0 + cw], in0=pb[:, :cw], scalar1=kgn[:, 0:1])
                        q = h % 2
                        ACT([pbb], [sqK_b[q]], sqK[q][:, :cw], pb[:, :cw], AF.Square)
                        rel(pbb)
                        for ti in range(cw // 128):
                            t = c0 // 128 + ti
                            MM(pks[:, t * 8 + h:t * 8 + h + 1], sqK[q][:, ti * 128:(ti + 1) * 128], onesB[:, 0:1], True, False,
                               [sqK_b[q], CONST], [pksb])
                            MM(pks[:, t * 8 + h:t * 8 + h + 1], sqR[:, ti * 128:(ti + 1) * 128], onesB[0:64, 0:1], False, True,
                               [sqR_b, CONST], [pksb])
                    for ti in range(cw // 128):
                        t = c0 // 128 + ti
                        for hf in range(2):
                            pb, pbb = bank()
                            for r in range(2):
                                MM(pb[:, :], ckv[:, r, ti * 128:(ti + 1) * 128], wuv[:, r, hf * 512:(hf + 1) * 512], r == 0, r == 1,
                                   [ckv_b, wkv_b], [pbb])
                            S.op("act", (lambda o, i: (lambda e: e.copy(out=o, in_=i)))(OV[:, t, hf * 512:(hf + 1) * 512], pb[:, :]),
                                 [pbb], [OV_b[t]])
                            rel(pbb)
                ACT([pksb], [ksc_b], ksc, pks[:, 0:NT * 8].rearrange("p (t c) -> p t c", c=8), AF.Ln, scale=1.0 / 192, bias=EPS)
                ACT([ksc_b], [ksc_b], ksc, ksc, AF.Exp, scale=-0.5)
                V("dve", "tensor_scalar_mul", [ksc_b], [ksc_b], out=ksc, in0=ksc, scalar1=192.0 ** -0.5)
                S.barrier()
                if STOP == "KV":
                    raise _Stop()
                bpool[0] = list(range(4))
                b_base = kv_base
                for _once in range(1):
                    areset(b_base)
                    woutB = carve([128, 8, 1024], BF16); wo_b = S.buf()
                    wst = [carve([128, 8, 128], BF16) for _ in range(2)]; wst_b = S.bufs(2)
                    wuq = [carve([128, 3, 192], BF16) for _ in range(2)]; wuq_b = S.bufs(2)
                    cqT = carve([128, 3, 512], F32); cq_b = S.buf()
                    cqn = carve([128, 3, 512], BF16); cqn_b = S.buf()
                    cqs, cqs_b = cqn, cqn_b
                    rb = carve([128, 512], F32); rb_b = S.buf()
                    qn = carve([128, 512], F32); qn_b = S.buf()
                    qr = carve([64, 512], F32); qr_b = S.buf()
                    qrb = carve([64, 512], BF16); qrb_b = S.buf()
                    sqn = carve([128, 512], BF16); sqn_b = S.buf()
                    sqr = carve([64, 512], BF16); sqr_b = S.buf()
                    rq = carve([128, 512], F32); rq_b = S.buf()
                    QnT = carve([128, 512], BF16); QrT = carve([64, 512], BF16); Q_b = S.buf()
                    t1 = carve([64, 512], F32); t1_b = S.buf()
                    t2 = carve([64, 512], F32); t2_b = S.buf()
                    sig = carve([128, 512], F32); sig_b = S.buf()
                    gs = carve([128, 512], F32); gs_b = S.buf()
                    PT = [carve([128, 512], BF16) for _ in range(3)]; PT_b = S.bufs(3)
                    rd_ = carve([128, 512], F32); rd_b = S.buf()
                    ot = carve([128, 512], F32); ot_b = S.buf()
                    OGT = carve([128, 8, 512], BF16); OGT_b = S.buf()
                for q0 in range(1, NT, 4):
                    q1 = min(q0 + 3, NT - 1)
                    bw = (q1 - q0 + 1) * 128
                    bc0 = q0 * 128
                    hrd = hT_reads(bc0, bw)
                    DMA("sp", woutB, wview(w_outb_b, 0, 1024), [], [wo_b], "wo")
                    wc = [0]
                    for fc in range(3):
                        wsl = wc[0] % 2; wc[0] += 1
                        DMA("sp", wst[wsl], wview(w_inb_b, fc * 128, 128), [], [wst_b[wsl]], f"wst{wsl}")
                        pb, pbb = bank()
                        for kc in range(8):
                            MM(pb[:, :bw], wst[wsl][:, kc, :], hT[:, kc, bc0:bc0 + bw], kc == 0, kc == 7, [wst_b[wsl]] + hrd, [pbb])
                        S.op("act", (lambda o, i: (lambda e: e.copy(out=o, in_=i)))(cqT[:, fc, :bw], pb[:, :bw]), [pbb], [cq_b])
                        rel(pbb)
                    ACT([cq_b], [cqs_b], cqs[:, :, :bw], cqT[:, :, :bw], AF.Square)
                    pb, pbb = bank()
                    for j in range(3):
                        MM(pb[:, :bw], onesB[:], cqs[:, j, :bw], j == 0, j == 2, [CONST, cqs_b], [pbb])
                    ACT([pbb], [rb_b], rb[:, :bw], pb[:, :bw], AF.Ln, scale=1.0 / 384, bias=EPS)
                    rel(pbb)
                    ACT([rb_b], [rb_b], rb[:, :bw], rb[:, :bw], AF.Exp, scale=-0.5)
                    V("dve", "tensor_tensor", [cq_b, rb_b], [cqn_b], out=cqn[:, :, :bw], in0=cqT[:, :, :bw],
                      in1=rb[:, :bw].unsqueeze(1).to_broadcast([128, 3, bw]), op=ALU.mult)
                    pctr = [0]
                    for h in range(H):
                        us = h % 2
                        DMA("sp", wuq[us], wview(w_uq_b, h * 192, 192), [], [wuq_b[us]], f"wuq{us}")
                        pq, pqb = bank()
                        pr, prb = bank()
                        for j in range(3):
                            MM(pq[:, :bw], wuq[us][:, j, 0:128], cqn[:, j, :bw], j == 0, j == 2, [wuq_b[us], cqn_b], [pqb])
                        for j in range(3):
                            MM(pr[0:64, :bw], wuq[us][:, j, 128:192], cqn[:, j, :bw], j == 0, j == 2, [wuq_b[us], cqn_b], [prb])
                        V("dve", "tensor_scalar_mul", [pqb, CONST], [qn_b], out=qn[:, :bw], in0=pq[:, :bw], scalar1=qgn[:, 0:1])
                        V("dve", "tensor_scalar_mul", [prb, CONST], [qr_b], out=qr[:, :bw], in0=pr[0:64, :bw], scalar1=qgr[:, 0:1])
                        ACT([pqb], [sqn_b], sqn[:, :bw], pq[:, :bw], AF.Square)
                        ACT([prb], [sqr_b], sqr[:, :bw], pr[0:64, :bw], AF.Square)
                        rel(pqb, prb)
                        pb, pbb = bank()
                        MM(pb[:, :bw], onesB[:], sqn[:, :bw], True, False, [CONST, sqn_b], [pbb])
                        MM(pb[:, :bw], onesB[0:64, :], sqr[:, :bw], False, True, [CONST, sqr_b], [pbb])
                        ACT([pbb], [rq_b], rq[:, :bw], pb[:, :bw], AF.Ln, scale=1.0 / 192, bias=EPS)
                        rel(pbb)
                        ACT([rq_b], [rq_b], rq[:, :bw], rq[:, :bw], AF.Exp, scale=-0.5)
                        V("dve", "tensor_tensor", [qn_b, rq_b], [Q_b], out=QnT[:, :bw], in0=qn[:, :bw], in1=rq[:, :bw], op=ALU.mult)
                        V("pool", "tensor_copy", [qr_b], [qrb_b], out=qrb[:, :bw], in_=qr[:, :bw])
                        pb, pbb = bank()
                        MM(pb[0:64, :bw], rotB[:], qrb[:, :bw], True, True, [CONST, qrb_b], [pbb])
                        V("dve", "tensor_tensor", [pbb, CONST], [t1_b], out=t1[:, :bw], in0=pb[0:64, :bw], in1=sin2[:, bc0:bc0 + bw], op=ALU.mult)
                        rel(pbb)
                        V("dve", "tensor_tensor", [qr_b, CONST], [t2_b], out=t2[:, :bw], in0=qr[:, :bw], in1=cos2[:, bc0:bc0 + bw], op=ALU.mult)
                        V("dve", "tensor_tensor", [t1_b, t2_b], [t1_b], out=t1[:, :bw], in0=t1[:, :bw], in1=t2[:, :bw], op=ALU.add)
                        V("dve", "tensor_tensor", [t1_b, rq_b], [Q_b], out=QrT[:, :bw], in0=t1[:, :bw], in1=rq[0:64, :bw], op=ALU.mult)
                        wsl = wc[0] % 2; wc[0] += 1
                        DMA("sp", wst[wsl], wview(w_inb_b, 384 + h * 128, 128), [], [wst_b[wsl]], f"wst{wsl}")
                        pgt, pgtb = bank()
                        for kc in range(8):
                            MM(pgt[:, :bw], wst[wsl][:, kc, :], hT[:, kc, bc0:bc0 + bw], kc == 0, kc == 7, [wst_b[wsl]] + hrd, [pgtb])
                        silu_from(pgt[:, :bw], [pgtb], sig[:, :bw], sig_b, None)
                        V("dve", "tensor_tensor", [pgtb, sig_b], [gs_b], out=gs[:, :bw], in0=pgt[:, :bw], in1=sig[:, :bw], op=ALU.mult)
                        rel(pgtb)
                        pO, pOb = fbank(4 + 2 * (h % 2))
                        pD, pDb = fbank(5 + 2 * (h % 2))
                        for kt in range(0, q1 + 1):
                            o0 = (max(kt, q0) - q0) * 128
                            ks = slice(kt * 128, (kt + 1) * 128)
                            psc, pscb = bank()
                            diag = kt >= q0
                            MM(psc[:, o0:bw], KT[:, h, ks], QnT[:, o0:bw], True, False, [KT_b, Q_b], [pscb])
                            MM(psc[:, o0:bw], RR[:, ks], QrT[:, o0:bw], False, not diag, [RR_b, Q_b], [pscb])
                            if diag:
                                MM(psc[:, o0:o0 + 128], urowB[:], wrowB[:], False, True, [CONST], [pscb])
                            ps_ = pctr[0] % 3; pctr[0] += 1
                            ACT([pscb, ksc_b, CONST], [PT_b[ps_]], PT[ps_][:, o0:bw], psc[:, o0:bw], AF.Exp,
                                scale=ksc[:, kt, h:h + 1], bias=(padb[:, 0:1] if kt == 0 else zcol[:, 0:1]))
                            rel(pscb)
                            MM(pO[:, o0:bw], OV[:, kt, h * 128:(h + 1) * 128], PT[ps_][:, o0:bw], kt == 0, kt == q1, [OV_b[kt], PT_b[ps_]], [pOb])
                            MM(pD[:, o0:bw], onesB[:], PT[ps_][:, o0:bw], kt == 0, kt == q1, [CONST, PT_b[ps_]], [pDb])
                        V("dve", "reciprocal", [pDb], [rd_b], out=rd_[:, :bw], in_=pD[:, :bw])
                        V("dve", "tensor_tensor", [pOb, rd_b], [ot_b], out=ot[:, :bw], in0=pO[:, :bw], in1=rd_[:, :bw], op=ALU.mult)
                        V("dve", "tensor_tensor", [ot_b, gs_b], [OGT_b], out=OGT[:, h, :bw], in0=ot[:, :bw], in1=gs[:, :bw], op=ALU.mult)
                    for qt in range(q0, q1 + 1):
                        sl = qt % 2
                        DMA("sp", xin[sl][:], h1_d[qt * 128:(qt + 1) * 128, :], [], [xin_b[sl]], f"xl{sl}")
                        lc = (qt - q0) * 128
                        for hf in range(2):
                            pb, pbb = bank()
                            for kc in range(8):
                                MM(pb[:, :], OGT[:, kc, lc:lc + 128], woutB[:, kc, hf * 512:(hf + 1) * 512], kc == 0, kc == 7,
                                   [OGT_b, wo_b], [pbb])
                            V("dve", "tensor_tensor", [pbb, xin_b[sl]], [xin_b[sl]], out=xin[sl][:, hf * 512:(hf + 1) * 512],
                              in0=pb[:, :], in1=xin[sl][:, hf * 512:(hf + 1) * 512], op=ALU.add)
                            rel(pbb)
                        r0 = s * SEQ + (qt - 1) * 128
                        DMA("act", out_d[r0:r0 + 128, :], xin[sl][:], [xin_b[sl]], [], f"os{sl}")
                S.barrier()
        except _Stop:
            S.barrier()
        S.emit(nc, st)
    nc._sched_info = S.info
    nc._sched = S
    return nc


_NC_CACHE = {}


def kernel(**inputs):
    x = np.ascontiguousarray(inputs["x"], dtype=np.float32)
    B, SEQ, _ = x.shape
    NSEQ = B // N_CORES
    key = (NSEQ, SEQ)
    if key not in _NC_CACHE:
        _NC_CACHE[key] = build(NSEQ, SEQ)
    nc = _NC_CACHE[key]
    consts = host_consts(SEQ + 128)
    shared = {}
    for k, v in inputs.items():
        if k == "x":
            continue
        a = np.ascontiguousarray(np.asarray(v, dtype=np.float32))
        if a.ndim >= 2 and a.shape[0] == 1:
            a = a[0]
        shared[k] = np.ascontiguousarray(a)
    shared.update(consts)
    in_maps = []
    for c in range(N_CORES):
        m = dict(shared)
        m["x"] = np.ascontiguousarray(x[c * NSEQ:(c + 1) * NSEQ].reshape(NSEQ * SEQ, D))
        in_maps.append(m)
    res = run_bass_kernel_spmd(nc, in_maps, core_ids=list(range(N_CORES)))
    out = np.concatenate([np.asarray(r["out"]).reshape(NSEQ, SEQ, D) for r in res.results], axis=0)
    return out.astype(np.float32)
```

```python
import numpy as np
from contextlib import ExitStack
import concourse.bass as bass
import concourse.mybir as mybir
from concourse.bass_utils import run_bass_kernel_spmd

F32 = mybir.dt.float32
BF16 = mybir.dt.bfloat16
AF = mybir.ActivationFunctionType
ALU = mybir.AluOpType
AX = mybir.AxisListType
ENGS = ("sp", "act", "pool", "dve", "pe")

D = 1024
H = 8
BIG = 30000.0
EPS = 1e-6
N_CORES = 8
CHAIN_DT = F32


class Buf:
    __slots__ = ("name", "w", "r", "excl")

    def __init__(self, name):
        self.name = name
        self.w = None
        self.r = {}
        self.excl = False


class Op:
    __slots__ = ("fn", "waits", "tok", "dma")

    def __init__(self, fn, waits, tok, dma):
        self.fn = fn
        self.waits = waits
        self.tok = tok
        self.dma = dma


class Sched:
    def __init__(self):
        self.ops = {e: [] for e in ENGS}
        self.count = {}
        self.known = {e: {} for e in ENGS}
        self.needed = set()
        self.nb = 0
        self.marks = []
        self.info = {}
        self.dbg = []
        self.names = {}

    def buf(self, name=None):
        self.nb += 1
        return Buf(name or f"b{self.nb}")

    def bufs(self, n, name="b"):
        return [self.buf(f"{name}{i}") for i in range(n)]

    def op(self, eng, fn, reads=(), writes=(), dma_dom=None):
        deps = {}

        def need(tok):
            if tok is not None and deps.get(tok[0], 0) < tok[1]:
                deps[tok[0]] = tok[1]

        own = dma_dom if dma_dom is not None else eng
        for b in reads:
            need(b.w)
            if b.excl:
                for d, i in b.r.items():
                    if d != own:
                        need((d, i))
        for b in writes:
            need(b.w)
            for d, i in b.r.items():
                need((d, i))
        is_dma = dma_dom is not None
        dom = dma_dom if is_dma else eng
        waits = {}
        kn = self.known[eng]
        for d, i in deps.items():
            if d == "pe" and eng == "pe" and not is_dma:
                continue
            if kn.get(d, 0) >= i:
                continue
            waits[d] = i
            kn[d] = i
            self.needed.add((d, i))
        c = self.count.get(dom, 0) + 1
        self.count[dom] = c
        tok = (dom, c)
        for b in reads:
            if b.r.get(dom, 0) < c:
                b.r[dom] = c
        for b in writes:
            b.w = tok
            b.r = {}
        o_ = Op(fn, waits, tok, is_dma)
        self.ops[eng].append(o_)
        self.dbg.append((o_, [b.name for b in reads], [b.name for b in writes]))
        return tok

    def barrier(self, label=None):
        self.marks.append((label, dict(self.count)))
        for e in ENGS:
            waits = {}
            kn = self.known[e]
            for d, c in self.count.items():
                if kn.get(d, 0) < c:
                    waits[d] = c
                    kn[d] = c
                    self.needed.add((d, c))
            if waits:
                self.ops[e].append(Op(None, waits, None, False))

    def emit(self, nc, stack):
        doms = sorted(self.count.keys())
        dma_doms = set()
        for e in ENGS:
            for o in self.ops[e]:
                if o.dma:
                    dma_doms.add(o.tok[0])
        sems = {d: stack.enter_context(nc.semaphore(f"s_{d}")) for d in doms}
        rank = {}
        for d in doms:
            if d in dma_doms:
                rank[d] = None
            else:
                idxs = sorted(i for (dd, i) in self.needed if dd == d)
                rank[d] = {i: k + 1 for k, i in enumerate(idxs)}
        block = stack.enter_context(nc.Block())
        ops = self.ops
        self.info = dict(sems={d: sems[d].num for d in doms},
                         marks=[(lab, {d: (rank[d][c] if rank[d] is not None else 16 * c) for d, c in cnt.items()
                                       if d in ("pe", "act", "dve", "pool")}) for lab, cnt in self.marks])

        def run(engh, lst):
            for o in lst:
                for d, i in o.waits.items():
                    engh.wait_ge(sems[d], 16 * i if rank[d] is None else rank[d][i])
                if o.fn is None:
                    continue
                ins = o.fn(engh)
                try:
                    self.names[ins.ins.name] = o
                except Exception:
                    pass
                d, i = o.tok
                if o.dma:
                    ins.then_inc(sems[d], 16)
                elif i in rank[d]:
                    ins.then_inc(sems[d], 1)

        @block.sync
        def _(e):
            run(e, ops["sp"])

        @block.scalar
        def _(e):
            run(e, ops["act"])

        @block.gpsimd
        def _(e):
            run(e, ops["pool"])

        @block.vector
        def _(e):
            run(e, ops["dve"])

        @block.tensor
        def _(e):
            run(e, ops["pe"])


def host_consts(T):
    p = np.arange(128)
    ident = np.eye(128, dtype=np.float32)
    tri = (p[:, None] <= p[None, :]).astype(np.float32)
    mui = np.where(p[None, :] < p[:, None], -BIG, 0.0).astype(np.float32)
    mls = np.where(p[None, :] >= p[:, None], BIG, 0.0).astype(np.float32)
    mus = np.where(p[None, :] <= p[:, None], -BIG, 0.0).astype(np.float32)
    rot = np.zeros((64, 64), np.float32)
    for m in range(32):
        rot[m + 32, m] = -1.0
        rot[m, m + 32] = 1.0
    pos = np.maximum(np.arange(T) - 112, 0).astype(np.float32)
    inv = (np.float32(10000.0) ** (-np.arange(32, dtype=np.float32) / np.float32(32))).astype(np.float32)
    ang = (pos[:, None] * inv[None, :]).astype(np.float32)
    cos = np.cos(ang).astype(np.float32).T
    sin = np.sin(ang).astype(np.float32).T
    cos2 = np.concatenate([cos, cos], 0).astype(np.float32)
    sin2 = np.concatenate([sin, sin], 0).astype(np.float32)
    padb = np.where(p < 112, -BIG, 0.0).astype(np.float32)[:, None]
    urow = np.where(p >= 64, 1.0, 0.0).astype(np.float32)[None, :]
    wrow = np.where(p < 64, -BIG, 0.0).astype(np.float32)[None, :]
    return dict(c_ident=ident, c_tri=tri, c_mui=mui, c_mls=mls, c_mus=mus, c_rot=rot, c_cos=cos2, c_sin=sin2,
                c_padb=padb, c_urow=urow, c_wrow=wrow)


class _Stop(Exception):
    pass


def build(NSEQ, SEQ, STOP=None):
    T = SEQ + 128
    NT = T // 128
    CB = [(c0, min(512, T - c0)) for c0 in range(0, T, 512)]
    nc = bass.Bass("TRN2", target_bir_lowering=False)

    def din(name, shape, dt=F32):
        return nc.dram_tensor(name, list(shape), dt, kind="ExternalInput").ap()

    x_d = din("x", [NSEQ * SEQ, D])
    meta_d = din("meta_tokens", [16, D])
    a_norm_d = din("a_norm", [D]); a_w_in_d = din("a_w_in", [D, 4112]); a_conv_d = din("a_conv", [4, 3072])
    a_log_d = din("a_log", [8]); a_dtb_d = din("a_dt_bias", [8]); a_og_d = din("a_o_gain", [128])
    a_w_out_d = din("a_w_out", [D, D]); kv_norm_d = din("kv_norm", [D]); kv_wd_d = din("kv_w_down", [D, 320])
    kv_ln_d = din("kv_latent_norm", [256]); kv_uk_d = din("kv_w_uk", [256, D]); kv_uv_d = din("kv_w_uv", [256, D])
    k_gain_d = din("k_gain", [192]); b_norm_d = din("b_norm", [D]); b_w_in_d = din("b_w_in", [D, 1408])
    b_qln_d = din("b_q_latent_norm", [384]); b_uq_d = din("b_w_uq", [384, 1536]); b_qg_d = din("b_q_gain", [192])
    b_w_out_d = din("b_w_out", [D, D])
    c_ident_d = din("c_ident", [128, 128]); c_tri_d = din("c_tri", [128, 128]); c_mui_d = din("c_mui", [128, 128])
    c_mls_d = din("c_mls", [128, 128]); c_mus_d = din("c_mus", [128, 128]); c_rot_d = din("c_rot", [64, 64]); c_cos_d = din("c_cos", [64, T])
    c_sin_d = din("c_sin", [64, T]); c_padb_d = din("c_padb", [128, 1]); c_urow_d = din("c_urow", [1, 128])
    c_wrow_d = din("c_wrow", [1, 128])
    out_d = nc.dram_tensor("out", [NSEQ * SEQ, D], F32, kind="ExternalOutput").ap()

    def dscr(name, shape, dt):
        return nc.dram_tensor(name, list(shape), dt).ap()

    w_in_b = dscr("w_in_b", [D, 4112], BF16); w_outa_b = dscr("w_outa_b", [D, D], BF16)
    w_down_b = dscr("w_down_b", [D, 320], BF16); w_uk_b = dscr("w_uk_b", [256, D], BF16)
    w_uv_b = dscr("w_uv_b", [256, D], BF16); w_inb_b = dscr("w_inb_b", [D, 1408], BF16)
    w_uq_b = dscr("w_uq_b", [384, 1536], BF16); w_outb_b = dscr("w_outb_b", [D, D], BF16)
    h1_d = dscr("h1_d", [T, D], F32)

    S = Sched()
    with ExitStack() as st:
        st.enter_context(nc.allow_non_contiguous_dma("small strided parameter loads"))

        def sb(name, shape, dt):
            return st.enter_context(nc.sbuf_tensor(name, list(shape), dt))

        def V(eng, fname, reads, writes, *a, **k):
            S.op(eng, lambda e: getattr(e, fname)(*a, **k), reads, writes)

        def ACT(reads, writes, out, in_, func, **k):
            S.op("act", lambda e: e.activation(out=out, in_=in_, func=func, **k), reads, writes)

        def MM(out, lhsT, rhs, start, stop, reads, writes):
            S.op("pe", lambda e: e.matmul(out, lhsT=lhsT, rhs=rhs, start=start, stop=stop), reads, writes)

        def TR(out, in_, ident, reads, writes):
            S.op("pe", lambda e: e.transpose(out=out, in_=in_, identity=ident), reads, writes)

        def DMA(q, out, in_, reads, writes, dom):
            S.op(q, lambda e: e.dma_start(out=out, in_=in_), reads, writes, dma_dom=dom)

        PB = [st.enter_context(nc.psum_tensor(f"pb{i}", [128, 512], F32)) for i in range(8)]
        PBb = S.bufs(8, "pb")
        for _b in PBb:
            _b.excl = True
        bctr = [0]
        bpool = [list(range(8))]

        live = set()

        def bank():
            pool_ = bpool[0]
            for k in range(len(pool_)):
                i = pool_[(bctr[0] + k) % len(pool_)]
                if i not in live:
                    bctr[0] += k + 1
                    live.add(i)
                    return PB[i], PBb[i]
            raise RuntimeError("no free PSUM bank")

        def rel(*bbs):
            for bb in bbs:
                live.discard(PBb.index(bb))

        def fbank(i):
            return PB[i], PBb[i]

        try:
            identF = sb("identF", [128, 128], F32); identB = sb("identB", [128, 128], BF16)
            triF = sb("triF", [128, 128], F32); muiF = sb("muiF", [128, 128], F32); mlsF = sb("mlsF", [128, 128], F32)
            musF = sb("musF", [128, 128], F32)
            onesB = sb("onesB", [128, 128], BF16); rotF = sb("rotF", [64, 64], F32); rotB = sb("rotB", [64, 64], BF16)
            cos2 = sb("cos2", [64, T], BF16); sin2 = sb("sin2", [64, T], BF16)
            padb = sb("padb", [128, 1], F32); zcol = sb("zcol", [128, 1], F32)
            urowF = sb("urowF", [1, 128], F32); wrowF = sb("wrowF", [1, 128], F32)
            urowB = sb("urowB", [1, 128], BF16); wrowB = sb("wrowB", [1, 128], BF16)
            g_anorm = sb("g_anorm", [128, 8], F32); g_kvnorm = sb("g_kvnorm", [128, 8], F32)
            g_bnorm = sb("g_bnorm", [128, 8], F32); g_kvln = sb("g_kvln", [128, 2], F32)
            g_qln = sb("g_qln", [128, 3], F32); g_og = sb("g_og", [128, 1], F32)
            kgn = sb("kgn", [128, 1], F32); kgr = sb("kgr", [64, 1], F32)
            qgn = sb("qgn", [128, 1], F32); qgr = sb("qgr", [64, 1], F32)
            convw = sb("convw", [128, 4, 24], F32)
            alog = sb("alog", [128, 8], F32); negA = sb("negA", [128, 8], F32); dtb = sb("dtb", [128, 8], F32)
            CONST = S.buf("const")

            def cload(dst, src):
                DMA("sp", dst, src, [], [CONST], "cst")

            cload(identF[:], c_ident_d); cload(triF[:], c_tri_d); cload(muiF[:], c_mui_d); cload(mlsF[:], c_mls_d); cload(musF[:], c_mus_d)
            cload(rotF[:], c_rot_d); cload(padb[:], c_padb_d)
            cload(urowF[:], c_urow_d); cload(wrowF[:], c_wrow_d)
            cload(g_anorm[:], a_norm_d.rearrange("(k p) -> p k", p=128))
            cload(g_kvnorm[:], kv_norm_d.rearrange("(k p) -> p k", p=128))
            cload(g_bnorm[:], b_norm_d.rearrange("(k p) -> p k", p=128))
            cload(g_kvln[:], kv_ln_d.rearrange("(k p) -> p k", p=128))
            cload(g_qln[:], b_qln_d.rearrange("(k p) -> p k", p=128))
            cload(g_og[:], a_og_d.rearrange("(p o) -> p o", o=1))
            cload(kgn[:], k_gain_d[0:128].rearrange("(p o) -> p o", o=1))
            cload(kgr[:], k_gain_d[128:192].rearrange("(p o) -> p o", o=1))
            cload(qgn[:], b_qg_d[0:128].rearrange("(p o) -> p o", o=1))
            cload(qgr[:], b_qg_d[128:192].rearrange("(p o) -> p o", o=1))
            for jj in range(4):
                cload(convw[:, jj, :], a_conv_d[jj, :].rearrange("(c p) -> p c", p=128))
            cload(alog[:], a_log_d.partition_broadcast(128))
            cload(dtb[:], a_dtb_d.partition_broadcast(128))
            CONST.w = ("cst", S.count["cst"])
            V("dve", "tensor_copy", [CONST], [CONST], out=identB[:], in_=identF[:])
            V("dve", "tensor_copy", [CONST], [CONST], out=rotB[:], in_=rotF[:])
            V("dve", "tensor_copy", [CONST], [CONST], out=urowB[:], in_=urowF[:])
            V("dve", "tensor_copy", [CONST], [CONST], out=wrowB[:], in_=wrowF[:])
            V("dve", "memset", [], [CONST], onesB[:], 1.0)
            V("dve", "memset", [], [CONST], zcol[:], 0.0)
            ACT([CONST], [CONST], negA[:], alog[:], AF.Exp)
            V("dve", "tensor_scalar_mul", [CONST], [CONST], out=negA[:], in0=negA[:], scalar1=-1.0)

            hT = sb("hT", [128, 8, T], BF16); hT_b = S.bufs(NT, "hT")
            OV = sb("OV", [128, NT, 1024], BF16); OV_b = S.bufs(NT, "OV")
            RR = sb("RR", [64, T], BF16); RR_b = S.buf("RR")
            xin = [sb(f"xin{i}", [128, 1024], F32) for i in range(2)]; xin_b = S.bufs(2, "xin")
            xn = [sb(f"xn{i}", [128, 1024], BF16) for i in range(2)]; xn_b = S.bufs(2, "xn")
            junk = sb("junk", [128, 1024], BF16); junk_b = S.buf("junk")
            ssq = [sb(f"ssq{i}", [128, 1], F32) for i in range(2)]; ssq_b = S.bufs(2, "ssq")
            ARENA_F32 = 27136
            arena = sb("arena", [128, ARENA_F32], F32)
            apos = [0]

            def areset(off=0):
                apos[0] = off

            def carve(shape, dt):
                P = shape[0]
                n = int(np.prod(shape[1:]))
                nbytes = n * (4 if dt == F32 else 2)
                n32 = (nbytes + 3) // 4
                o = apos[0]
                apos[0] += (n32 + 7) // 8 * 8
                assert apos[0] <= ARENA_F32, ("arena overflow", apos[0])
                v = arena[0:P, o:o + n32]
                if dt != F32:
                    v = v.bitcast(dt)[:, 0:n]
                if len(shape) == 3:
                    v = v.rearrange("p (a b) -> p a b", a=shape[1], b=shape[2])
                return v

            areset(0)
            wl = [carve([128, 1024], F32) for i in range(2)]; wl_b = S.bufs(2, "wl")
            ws = [carve([128, 1024], BF16) for i in range(2)]; ws_b = S.bufs(2, "ws")
            for tab_d, tab in ((c_cos_d, cos2), (c_sin_d, sin2)):
                for c0 in range(0, T, 1024):
                    cw = min(1024, T - c0)
                    sl = wctr_ = 0
                    DMA("sp", wl[0][0:64, :cw], tab_d[:, c0:c0 + cw], [], [wl_b[0]], "wl0")
                    V("dve", "tensor_copy", [wl_b[0]], [CONST], out=tab[:, c0:c0 + cw], in_=wl[0][0:64, :cw])
            wctr = [0]

            def prep_weight(src, K, N, dst, gcol):
                for kc in range(K // 128):
                    for c0 in range(0, N, 1024):
                        cw = min(1024, N - c0)
                        sl = wctr[0] % 2
                        wctr[0] += 1
                        DMA("sp", wl[sl][:, :cw], src[kc * 128:(kc + 1) * 128, c0:c0 + cw], [], [wl_b[sl]], f"wl{sl}")
                        if gcol is not None:
                            V("dve", "tensor_scalar_mul", [wl_b[sl], CONST], [ws_b[sl]], out=ws[sl][:, :cw],
                              in0=wl[sl][:, :cw], scalar1=gcol(kc))
                        else:
                            V("dve", "tensor_copy", [wl_b[sl]], [ws_b[sl]], out=ws[sl][:, :cw], in_=wl[sl][:, :cw])
                        DMA("act", dst[kc * 128:(kc + 1) * 128, c0:c0 + cw], ws[sl][:, :cw], [ws_b[sl]], [], f"ws{sl}")

            prep_weight(a_w_in_d, D, 4112, w_in_b, lambda kc: g_anorm[:, kc:kc + 1])
            prep_weight(a_w_out_d, D, D, w_outa_b, lambda kc: g_og[:, 0:1])
            prep_weight(kv_wd_d, D, 320, w_down_b, lambda kc: g_kvnorm[:, kc:kc + 1])
            prep_weight(kv_uk_d, 256, D, w_uk_b, lambda kc: g_kvln[:, kc:kc + 1])
            prep_weight(kv_uv_d, 256, D, w_uv_b, lambda kc: g_kvln[:, kc:kc + 1])
            prep_weight(b_w_in_d, D, 1408, w_inb_b, lambda kc: g_bnorm[:, kc:kc + 1])
            prep_weight(b_uq_d, 384, 1536, w_uq_b, lambda kc: g_qln[:, kc:kc + 1])
            prep_weight(b_w_out_d, D, D, w_outb_b, None)
            S.barrier()
            if STOP == "W":
                raise _Stop()

            def wview(wb, c0, cw):
                return wb[:, c0:c0 + cw].rearrange("(k p) n -> p k n", p=128)

            def load_x_tile(s, t, sl):
                if t == 0:
                    V("pool", "memset", [], [xin_b[sl]], xin[sl][:], 0.0)
                    DMA("sp", xin[sl][112:128, :], meta_d, [], [xin_b[sl]], f"xl{sl}")
                else:
                    r0 = s * SEQ + (t - 1) * 128
                    DMA("sp", xin[sl][:], x_d[r0:r0 + 128, :], [], [xin_b[sl]], f"xl{sl}")

            def norm_to_hT(src, src_b, t, sl):
                ACT([src_b], [junk_b, ssq_b[sl]], junk[:], src, AF.Square, accum_out=ssq[sl][:])
                ACT([ssq_b[sl]], [ssq_b[sl]], ssq[sl][:], ssq[sl][:], AF.Ln, scale=1.0 / D, bias=EPS)
                ACT([ssq_b[sl]], [ssq_b[sl]], ssq[sl][:], ssq[sl][:], AF.Exp, scale=-0.5)
                V("dve", "tensor_scalar_mul", [src_b, ssq_b[sl]], [xn_b[sl]], out=xn[sl][:], in0=src, scalar1=ssq[sl][:, 0:1])
                pb, pbb = bank()
                pbv = pb[:].bitcast(BF16)
                for kc in range(8):
                    TR(pbv[:, kc * 128:(kc + 1) * 128], xn[sl][:, kc * 128:(kc + 1) * 128], identB[:], [xn_b[sl], CONST], [pbb])
                V("dve", "tensor_copy", [pbb], [hT_b[t]], out=hT[:, :, t * 128:(t + 1) * 128],
                  in_=pbv.rearrange("p (k c) -> p k c", k=8))
                rel(pbb)

            def hT_reads(c0, cw):
                return [hT_b[t] for t in range(c0 // 128, (c0 + cw + 127) // 128)]

            def silu_from(src_ap, src_reads, tmp, tmp_b, shape_sl):
                ACT(src_reads, [tmp_b], tmp, src_ap, AF.Exp, scale=-1.0)
                ACT([tmp_b], [tmp_b], tmp, tmp, AF.Ln, bias=1.0)
                ACT([tmp_b], [tmp_b], tmp, tmp, AF.Exp, scale=-1.0)

            def rsqrt_inplace(ap, b, scale, reads_extra=()):
                ACT([b] + list(reads_extra), [b], ap, ap, AF.Ln, scale=scale, bias=EPS)
                ACT([b], [b], ap, ap, AF.Exp, scale=-0.5)

            for s in range(NSEQ):
                bpool[0] = list(range(8))
                for t in range(NT):
                    sl = t % 2
                    load_x_tile(s, t, sl)
                    norm_to_hT(xin[sl][:], xin_b[sl], t, sl)
                S.barrier()
                if STOP == "A0":
                    raise _Stop()
                areset(0)
                wab = carve([128, 8, 16], BF16); wab_b = S.buf()
                ab = carve([128, NT, 16], F32); ab_b = S.buf()
                tm1 = carve([128, NT, 8], F32); tm1_b = S.buf()
                gcol = carve([128, NT, 8], F32); g_b = S.buf()
                lnb = carve([128, NT, 8], F32); lnb_b = S.buf()
                gc = carve([128, NT, 8], F32); gc_b = S.buf()
                ngc = carve([128, NT, 8], F32); egc = carve([128, NT, 8], F32); bls = carve([128, NT, 8], F32)
                beta = carve([128, NT, 8], F32); begc = carve([128, NT, 8], F32)
                der_b = S.buf()
                DMA("sp", wab, wview(w_in_b, 4096, 16), [], [wab_b], "wab")
                pb, pbb = bank()
                for t in range(NT):
                    for kc in range(8):
                        MM(pb[:, t * 16:(t + 1) * 16], hT[:, kc, t * 128:(t + 1) * 128], wab[:, kc, :], kc == 0, kc == 7,
                           [hT_b[t], wab_b], [pbb])
                V("dve", "tensor_copy", [pbb], [ab_b], out=ab, in_=pb[:, 0:NT * 16].rearrange("p (t c) -> p t c", c=16))
                rel(pbb)
                ACT([ab_b], [tm1_b], tm1, ab[:, :, 0:8], AF.Exp, scale=-1.0)
                ACT([tm1_b], [tm1_b], tm1, tm1, AF.Ln, bias=1.0)
                V("dve", "tensor_scalar_mul", [tm1_b], [lnb_b], out=lnb, in0=tm1, scalar1=-1.0)
                V("dve", "tensor_tensor", [ab_b, CONST], [tm1_b], out=tm1, in0=ab[:, :, 8:16],
                  in1=dtb[:].unsqueeze(1).to_broadcast([128, NT, 8]), op=ALU.add)
                ACT([tm1_b], [tm1_b], tm1, tm1, AF.Exp)
                ACT([tm1_b], [tm1_b], tm1, tm1, AF.Ln, bias=1.0)
                V("dve", "tensor_tensor", [tm1_b, CONST], [g_b], out=gcol, in0=tm1,
                  in1=negA[:].unsqueeze(1).to_broadcast([128, NT, 8]), op=ALU.mult)
                pb, pbb = bank()
                for t in range(NT):
                    MM(pb[:, t * 8:(t + 1) * 8], triF[:], gcol[:, t, :], True, True, [CONST, g_b], [pbb])
                V("dve", "tensor_copy", [pbb], [gc_b], out=gc, in_=pb[:, 0:NT * 8].rearrange("p (t c) -> p t c", c=8))
                rel(pbb)
                V("dve", "tensor_scalar_mul", [gc_b], [der_b], out=ngc, in0=gc, scalar1=-1.0)
                ACT([gc_b], [der_b], egc, gc, AF.Exp)
                V("dve", "tensor_tensor", [gc_b, lnb_b], [der_b], out=bls, in0=gc, in1=lnb, op=ALU.add)
                ACT([lnb_b], [der_b], beta, lnb, AF.Exp)
                ACT([der_b], [der_b], begc, bls, AF.Exp)

                if STOP == "Aab":
                    raise _Stop()
                a12_base = apos[0]
                for h in range(1):
                    areset(a12_base)
                    wst = [carve([128, 8, 128], BF16) for _ in range(2)]; wst_b = S.bufs(2)
                    zc = [carve([128, 515], F32) for _ in range(2)]; zc_b = S.bufs(2)
                    taL = [carve([128, 512], F32) for _ in range(2)]; taL_b = S.bufs(2)
                    tbL = [carve([128, 512], F32) for _ in range(2)]; tbL_b = S.bufs(2)
                    sqL = [carve([128, 512], BF16) for _ in range(2)]; sqL_b = S.bufs(2)
                    blkc = [0]
                    sT = [carve([128, T], BF16) for _ in range(3)]; sT_b = S.bufs(3)
                    Ktok = carve([128, NT, 128], BF16); Vtok = carve([128, NT, 128], BF16); KV_b = S.bufs(2)
                    Sst = carve([128, 128], F32); Sbf = carve([128, 128], BF16); S_b = S.buf(); Sb_b = S.buf()
                    NSL = 8
                    cm = []
                    for i in range(NSL):
                        cm.append(dict(
                            Eui=carve([128, 128], F32), Els=carve([128, 128], F32), Eus=carve([128, 128], F32),
                            Z=[carve([128, 256], CHAIN_DT) for _ in range(2)], P=[carve([128, 128], CHAIN_DT) for _ in range(2)],
                            attT=carve([128, 128], BF16), TmT=carve([128, 128], BF16), nWdT=carve([128, 128], BF16),
                            Bk=carve([128, 128], BF16), bV=carve([128, 128], BF16), Kd=carve([128, 128], BF16),
                            Ub=carve([128, 128], BF16), glc=carve([128, 1], F32), o1=carve([128, 128], F32),
                            b={k: S.buf() for k in ("Eui", "Els", "Eus", "Z0", "Z1", "P0", "P1", "attT", "TmT", "nWdT", "Bk", "bV",
                                                    "Kd", "Ub", "glc", "o1")}))
                for h in range(H):
                    for j in range(3):
                        fc = j * 8 + h
                        wsl = j % 2
                        DMA("sp", wst[wsl], wview(w_in_b, fc * 128, 128), [], [wst_b[wsl]], f"wst{wsl}")
                        for bi, (c0, cw) in enumerate(CB):
                            zs = bi % 2
                            bk_ = blkc[0] % 2; blkc[0] += 1
                            ta, ta_b, tb, tb_b, sq, sq_b = taL[bk_], taL_b[bk_], tbL[bk_], tbL_b[bk_], sqL[bk_], sqL_b[bk_]
                            pb, pbb = bank()
                            for kc in range(8):
                                MM(pb[:, :cw], wst[wsl][:, kc, :], hT[:, kc, c0:c0 + cw], kc == 0, kc == 7,
                                   [wst_b[wsl]] + hT_reads(c0, cw), [pbb])
                            if bi == 0:
                                V("pool", "memset", [], [zc_b[zs]], zc[zs][:, 0:3], 0.0)
                            else:
                                V("pool", "tensor_copy", [zc_b[1 - zs]], [zc_b[zs]], out=zc[zs][:, 0:3], in_=zc[1 - zs][:, 512:515])
                            S.op("act", (lambda o, i: (lambda e: e.copy(out=o, in_=i)))(zc[zs][:, 3:3 + cw], pb[:, :cw]),
                                 [pbb], [zc_b[zs]])
                            rel(pbb)
                            V("dve", "tensor_scalar_mul", [zc_b[zs], CONST], [ta_b], out=ta[:, :cw], in0=zc[zs][:, 3:3 + cw],
                              scalar1=convw[:, 3, fc:fc + 1])
                            for jj in (2, 1, 0):
                                V("dve", "scalar_tensor_tensor", [zc_b[zs], CONST, ta_b], [ta_b], out=ta[:, :cw],
                                  in0=zc[zs][:, jj:jj + cw], scalar=convw[:, jj, fc:fc + 1], in1=ta[:, :cw],
                                  op0=ALU.mult, op1=ALU.add)
                            silu_from(ta[:, :cw], [ta_b], tb[:, :cw], tb_b, None)
                            V("dve", "tensor_tensor", [ta_b, tb_b], [sT_b[j]], out=sT[j][:, c0:c0 + cw], in0=ta[:, :cw],
                              in1=tb[:, :cw], op=ALU.mult)
                            if j < 2:
                                ACT([sT_b[j]], [sq_b], sq[:, :cw], sT[j][:, c0:c0 + cw], AF.Square)
                                pb2, pbb2 = bank()
                                MM(pb2[:, :cw], onesB[:], sq[:, :cw], True, True, [CONST, sq_b], [pbb2])
                                ACT([pbb2], [tb_b], tb[:, :cw], pb2[:, :cw], AF.Ln, bias=EPS)
                                rel(pbb2)
                                ACT([tb_b], [tb_b], tb[:, :cw], tb[:, :cw], AF.Exp, scale=-0.5)
                                V("dve", "scalar_tensor_tensor", [sT_b[j], tb_b], [sT_b[j]], out=sT[j][:, c0:c0 + cw],
                                  in0=sT[j][:, c0:c0 + cw], scalar=(128.0 ** -0.5 if j == 0 else 1.0), in1=tb[:, :cw],
                                  op0=ALU.mult, op1=ALU.mult)
                    qT, kT, vT = sT
                    if STOP == "A12a":
                        raise _Stop()
                    for j, dst in ((1, Ktok), (2, Vtok)):
                        for t0 in range(0, NT, 8):
                            tn = min(8, NT - t0)
                            pb, pbb = bank()
                            pbv = pb[:].bitcast(BF16)
                            for i in range(tn):
                                TR(pbv[:, i * 128:(i + 1) * 128], sT[j][:, (t0 + i) * 128:(t0 + i + 1) * 128], identB[:],
                                   [sT_b[j], CONST], [pbb])
                            V("dve", "tensor_copy", [pbb], [KV_b[j - 1]], out=dst[:, t0:t0 + tn, :],
                              in_=pbv[:, 0:tn * 128].rearrange("p (t c) -> p t c", c=128))
                            rel(pbb)
                    V("pool", "memset", [], [S_b], Sst[:], 0.0)
                    V("pool", "memset", [], [Sb_b], Sbf[:], 0.0)
                    if STOP == "A12b":
                        raise _Stop()
                    def st_G(c):
                        m = cm[c % NSL]; mb = m["b"]
                        cs = slice(c * 128, (c + 1) * 128)
                        gb_l = gcol[:, c, h:h + 1].to_broadcast([128, 128])
                        lb_l = lnb[:, c, h:h + 1].to_broadcast([128, 128])
                        pg, pgb = bank()
                        pt, ptb = bank()
                        m["pg"], m["pgb"], m["pt"], m["ptb"] = pg, pgb, pt, ptb
                        MM(pg[:, 0:128], gb_l, triF[:], True, False, [g_b, CONST], [pgb])
                        MM(pg[:, 0:128], identF[:], muiF[:], False, True, [CONST], [pgb])
                        MM(pg[:, 128:256], gb_l, triF[:], True, False, [g_b, CONST], [pgb])
                        MM(pg[:, 128:256], identF[:], mlsF[:], False, True, [CONST], [pgb])
                        MM(pg[:, 256:384], kT[:, cs], kT[:, cs], True, True, [sT_b[1]], [pgb])
                        MM(pg[:, 384:512], kT[:, cs], qT[:, cs], True, True, [sT_b[1], sT_b[0]], [pgb])
                        MM(pt[:, 0:128], gb_l, triF[:], True, False, [g_b, CONST], [ptb])
                        MM(pt[:, 0:128], lb_l, identF[:], False, False, [lnb_b, CONST], [ptb])
                        MM(pt[:, 0:128], identF[:], musF[:], False, True, [CONST], [ptb])

                    def st_E(c):
                        m = cm[c % NSL]; mb = m["b"]
                        pg, pgb, pt, ptb = m["pg"], m["pgb"], m["pt"], m["ptb"]
                        ACT([pgb, der_b], [mb["Eui"]], m["Eui"], pg[:, 0:128], AF.Exp, bias=ngc[:, c, h:h + 1], scale=1.0)
                        ACT([pgb, der_b], [mb["Els"]], m["Els"], pg[:, 128:256], AF.Exp, bias=bls[:, c, h:h + 1], scale=-1.0)
                        ACT([pgb], [mb["glc"]], m["glc"], pg[:, 127:128], AF.Exp)
                        ACT([ptb, der_b], [mb["Eus"]], m["Eus"], pt[:, 0:128], AF.Exp, bias=ngc[:, c, h:h + 1], scale=1.0)
                        V("dve", "scalar_tensor_tensor", [pgb, mb["Els"]], [mb["Z0"]], out=m["Z"][0][:, 0:128],
                          in0=pg[:, 256:384], scalar=-1.0, in1=m["Els"], op0=ALU.mult, op1=ALU.mult)
                        V("dve", "scalar_tensor_tensor", [pgb, mb["Eus"]], [mb["Z0"]], out=m["Z"][0][:, 128:256],
                          in0=pg[:, 256:384], scalar=-1.0, in1=m["Eus"], op0=ALU.mult, op1=ALU.mult)
                        V("dve", "tensor_tensor", [pgb, mb["Eui"]], [mb["attT"]], out=m["attT"], in0=pg[:, 384:512],
                          in1=m["Eui"], op=ALU.mult)
                        V("dve", "tensor_tensor", [mb["Z0"], CONST], [mb["P0"]], out=m["P"][0], in0=m["Z"][0][:, 128:256],
                          in1=identF[:], op=ALU.add)
                        V("pool", "tensor_scalar_mul", [KV_b[0], der_b], [mb["Bk"]], out=m["Bk"], in0=Ktok[:, c, :],
                          scalar1=begc[:, c, h:h + 1])
                        V("pool", "tensor_scalar_mul", [KV_b[1], der_b], [mb["bV"]], out=m["bV"], in0=Vtok[:, c, :],
                          scalar1=beta[:, c, h:h + 1])
                        V("pool", "tensor_scalar_mul", [KV_b[0], mb["Eui"]], [mb["Kd"]], out=m["Kd"], in0=Ktok[:, c, :],
                          scalar1=m["Eui"][:, 127:128])
                        rel(pgb, ptb)

                    def st_Lmm(c, lev):
                        m = cm[c % NSL]; mb = m["b"]
                        zi = lev % 2
                        pi = (lev - 1) % 2
                        Zc, Zcb = m["Z"][zi], mb[f"Z{zi}"]
                        pk, pkb = bank()
                        m["pk"], m["pkb"] = pk, pkb
                        if lev >= 1:
                            MM(pk[:, 256:384], Zc[:, 0:128], m["P"][pi], True, True, [Zcb, mb[f"P{pi}"]], [pkb])
                        if lev < 6:
                            MM(pk[:, 0:128], Zc[:, 128:256], Zc[:, 0:128], True, True, [Zcb], [pkb])
                            MM(pk[:, 128:256], Zc[:, 0:128], Zc[:, 128:256], True, True, [Zcb], [pkb])

                    def st_Lev(c, lev):
                        m = cm[c % NSL]; mb = m["b"]
                        zi = lev % 2
                        pi = (lev - 1) % 2
                        Zn, Znb = m["Z"][1 - zi], mb[f"Z{1 - zi}"]
                        pk, pkb = m["pk"], m["pkb"]
                        if lev >= 1:
                            if lev < 6:
                                V("dve", "tensor_tensor", [pkb, mb[f"P{pi}"]], [mb[f"P{1 - pi}"]], out=m["P"][1 - pi],
                                  in0=pk[:, 256:384], in1=m["P"][pi], op=ALU.add)
                            else:
                                V("dve", "tensor_tensor", [pkb, mb[f"P{pi}"]], [mb["TmT"]], out=m["TmT"],
                                  in0=pk[:, 256:384], in1=m["P"][pi], op=ALU.add)
                        if lev < 6:
                            V("act", "copy", [pkb], [Znb], out=Zn[:, 0:256], in_=pk[:, 0:256])
                        rel(pkb)

                    def st_W(c):
                        m = cm[c % NSL]; mb = m["b"]
                        pw, pwb = bank()
                        MM(pw[:, 0:128], m["Bk"], m["TmT"], True, True, [mb["Bk"], mb["TmT"]], [pwb])
                        V("act", "mul", [pwb], [mb["nWdT"]], out=m["nWdT"], in_=pw[:, 0:128], mul=-1.0)
                        rel(pwb)

                    def st_R(c):
                        m = cm[c % NSL]; mb = m["b"]
                        cs = slice(c * 128, (c + 1) * 128)
                        pu, pub = bank()
                        MM(pu[:, 0:128], m["TmT"], m["bV"], True, c == 0, [mb["TmT"], mb["bV"]], [pub])
                        if c > 0:
                            MM(pu[:, 0:128], m["nWdT"], Sbf, False, True, [mb["nWdT"], Sb_b], [pub])
                        V("act", "copy", [pub], [mb["Ub"]], out=m["Ub"], in_=pu[:, 0:128])
                        rel(pub)
                        po, pob = bank()
                        po2, pob2 = bank()
                        MM(po[:, 0:128], m["Kd"], m["Ub"], True, True, [mb["Kd"], mb["Ub"]], [pob])
                        MM(po2[:, 0:128], qT[:, cs], Sbf, True, True, [sT_b[0], Sb_b], [pob2])
                        MM(po2[:, 128:256], m["attT"], m["Ub"], True, True, [mb["attT"], mb["Ub"]], [pob2])
                        V("dve", "scalar_tensor_tensor", [S_b, mb["glc"], pob], [Sb_b], out=Sbf[:], in0=Sst[:],
                          scalar=m["glc"][:, 0:1], in1=po[:, 0:128], op0=ALU.mult, op1=ALU.add)
                        V("dve", "scalar_tensor_tensor", [S_b, mb["glc"], pob], [S_b], out=Sst[:], in0=Sst[:],
                          scalar=m["glc"][:, 0:1], in1=po[:, 0:128], op0=ALU.mult, op1=ALU.add)
                        ACT([pob2, der_b], [mb["o1"]], m["o1"], po2[:, 0:128], AF.Copy, scale=egc[:, c, h:h + 1])
                        V("dve", "tensor_tensor", [pob2, mb["o1"]], [OV_b[c]], out=OV[:, c, h * 128:(h + 1) * 128],
                          in0=po2[:, 128:256], in1=m["o1"], op=ALU.add)
                        rel(pob, pob2)

                    GS = 4
                    groups = [list(range(i, min(i + GS, NT))) for i in range(0, NT, GS)]
                    pending = []
                    for grp in groups:
                        stages = [("GE", grp[0:2]), ("GE", grp[2:4])] + [("L", lev) for lev in range(7)] + [("W", None)]
                        for kind, lev in stages:
                            if kind == "GE":
                                for c in lev:
                                    st_G(c)
                                for c in lev:
                                    st_E(c)
                            elif kind == "L":
                                for c in grp:
                                    st_Lmm(c, lev)
                                for c in grp:
                                    st_Lev(c, lev)
                            else:
                                for c in grp:
                                    st_W(c)
                            if pending:
                                st_R(pending.pop(0))
                        while pending:
                            st_R(pending.pop(0))
                        pending = list(grp)
                    while pending:
                        st_R(pending.pop(0))
                S.barrier()
                if STOP == "A12":
                    raise _Stop()
                areset(0)
                gateW = carve([128, 8, 1024], BF16); outW = carve([128, 8, 1024], BF16); gw_b = S.buf(); ow_b = S.buf()
                sig = carve([128, 1024], F32); sig_b = S.buf()
                gs = carve([128, 1024], F32); gs_b = S.buf()
                osq = carve([128, 1024], F32); osq_b = S.buf()
                oss = carve([128, 8], F32); oss_b = S.buf()
                ogb = carve([128, 1024], BF16); ogb_b = S.buf()
                ogT = carve([128, 8, 128], BF16); ogT_b = S.buf()
                h1t = [carve([128, 1024], F32) for _ in range(2)]; h1t_b = S.bufs(2)
                DMA("sp", gateW, wview(w_in_b, 3072, 1024), [], [gw_b], "gw")
                DMA("sp", outW, wview(w_outa_b, 0, 1024), [], [ow_b], "ow")
                for t in range(NT):
                    sl = t % 2
                    load_x_tile(s, t, sl)
                    pgs = [bank(), bank()]
                    for hf in range(2):
                        pb, pbb = pgs[hf]
                        for kc in range(8):
                            MM(pb[:, :], hT[:, kc, t * 128:(t + 1) * 128], gateW[:, kc, hf * 512:(hf + 1) * 512], kc == 0, kc == 7,
                               [hT_b[t], gw_b], [pbb])
                        silu_from(pb[:, :], [pbb], sig[:, hf * 512:(hf + 1) * 512], sig_b, None)
                        V("dve", "tensor_tensor", [pbb, sig_b], [gs_b], out=gs[:, hf * 512:(hf + 1) * 512], in0=pb[:, :],
                          in1=sig[:, hf * 512:(hf + 1) * 512], op=ALU.mult)
                        rel(pbb)
                    ACT([OV_b[t]], [osq_b], osq, OV[:, t, :], AF.Square)
                    V("dve", "tensor_reduce", [osq_b], [oss_b], out=oss, in_=osq.rearrange("p (a b) -> p a b", a=8),
                      axis=AX.X, op=ALU.add)
                    ACT([oss_b], [oss_b], oss, oss, AF.Ln, scale=1.0 / 128, bias=EPS)
                    ACT([oss_b], [oss_b], oss, oss, AF.Exp, scale=-0.5)
                    V("dve", "tensor_tensor", [OV_b[t], oss_b], [osq_b], out=osq.rearrange("p (a b) -> p a b", a=8),
                      in0=OV[:, t, :].rearrange("p (a b) -> p a b", a=8), in1=oss.unsqueeze(2).to_broadcast([128, 8, 128]),
                      op=ALU.mult)
                    V("dve", "tensor_tensor", [osq_b, gs_b], [ogb_b], out=ogb, in0=osq, in1=gs, op=ALU.mult)
                    pb, pbb = bank()
                    pbv = pb[:].bitcast(BF16)
                    for kc in range(8):
                        TR(pbv[:, kc * 128:(kc + 1) * 128], ogb[:, kc * 128:(kc + 1) * 128], identB[:], [ogb_b, CONST], [pbb])
                    V("dve", "tensor_copy", [pbb], [ogT_b], out=ogT, in_=pbv.rearrange("p (k c) -> p k c", k=8))
                    rel(pbb)
                    for hf in range(2):
                        pb, pbb = bank()
                        for kc in range(8):
                            MM(pb[:, :], ogT[:, kc, :], outW[:, kc, hf * 512:(hf + 1) * 512], kc == 0, kc == 7, [ogT_b, ow_b], [pbb])
                        V("dve", "tensor_tensor", [pbb, xin_b[sl]], [h1t_b[sl]], out=h1t[sl][:, hf * 512:(hf + 1) * 512],
                          in0=pb[:, :], in1=xin[sl][:, hf * 512:(hf + 1) * 512], op=ALU.add)
                        rel(pbb)
                    DMA("act", h1_d[t * 128:(t + 1) * 128, :], h1t[sl], [h1t_b[sl]], [], f"h1s{sl}")
                    norm_to_hT(h1t[sl], h1t_b[sl], t, sl)
                S.barrier()
                if STOP == "A3":
                    raise _Stop()
                areset(0)
                KT = carve([128, 8, T], BF16); KT_b = S.buf()
                ksc = carve([128, NT, 8], F32); ksc_b = S.buf()
                kv_base = apos[0]
                bpool[0] = list(range(7))
                wdn = carve([128, 8, 320], BF16); wuk = carve([128, 2, 1024], BF16); wuv = carve([128, 2, 1024], BF16)
                wkv_b = S.buf()
                cTf = carve([128, 2, 512], F32); cT_b = S.buf()
                kpe = carve([64, 512], F32); kpe_b = S.buf()
                kpb = carve([64, 512], BF16); kpb_b = S.buf()
                csq = carve([128, 2, 512], BF16); csq_b = S.buf()
                rb = carve([128, 512], F32); rb_b = S.buf()
                ckv = carve([128, 2, 512], BF16); ckv_b = S.buf()
                sqK = [carve([128, 512], BF16) for _ in range(2)]; sqK_b = S.bufs(2)
                sqR = carve([64, 512], BF16); sqR_b = S.buf()
                t1 = carve([64, 512], F32); t1_b = S.buf()
                t2 = carve([64, 512], F32); t2_b = S.buf()
                DMA("sp", wdn, wview(w_down_b, 0, 320), [], [wkv_b], "wkv")
                DMA("sp", wuk, wview(w_uk_b, 0, 1024), [], [wkv_b], "wkv")
                DMA("sp", wuv, wview(w_uv_b, 0, 1024), [], [wkv_b], "wkv")
                pks, pksb = fbank(7)
                for (c0, cw) in CB:
                    rd = hT_reads(c0, cw)
                    pbs_ = [bank(), bank(), bank()]
                    for j, (lo, mw) in enumerate(((0, 128), (128, 128), (256, 64))):
                        pb, pbb = pbs_[j]
                        for kc in range(8):
                            MM(pb[0:mw, :cw], wdn[:, kc, lo:lo + mw], hT[:, kc, c0:c0 + cw], kc == 0, kc == 7, [wkv_b] + rd, [pbb])
                    for j in range(2):
                        pb, pbb = pbs_[j]
                        S.op("act", (lambda o, i: (lambda e: e.copy(out=o, in_=i)))(cTf[:, j, :cw], pb[:, :cw]), [pbb], [cT_b])
                        rel(pbb)
                    pb, pbb = pbs_[2]
                    V("dve", "tensor_scalar_mul", [pbb, CONST], [kpe_b], out=kpe[:, :cw], in0=pb[0:64, :cw], scalar1=kgr[:, 0:1])
                    ACT([pbb], [sqR_b], sqR[:, :cw], pb[0:64, :cw], AF.Square)
                    rel(pbb)
                    ACT([cT_b], [csq_b], csq[:, :, :cw], cTf[:, :, :cw], AF.Square)
                    pb, pbb = bank()
                    for j in range(2):
                        MM(pb[:, :cw], onesB[:], csq[:, j, :cw], j == 0, j == 1, [CONST, csq_b], [pbb])
                    ACT([pbb], [rb_b], rb[:, :cw], pb[:, :cw], AF.Ln, scale=1.0 / 256, bias=EPS)
                    rel(pbb)
                    ACT([rb_b], [rb_b], rb[:, :cw], rb[:, :cw], AF.Exp, scale=-0.5)
                    V("dve", "tensor_tensor", [cT_b, rb_b], [ckv_b], out=ckv[:, :, :cw], in0=cTf[:, :, :cw],
                      in1=rb[:, :cw].unsqueeze(1).to_broadcast([128, 2, cw]), op=ALU.mult)
                    V("pool", "tensor_copy", [kpe_b], [kpb_b], out=kpb[:, :cw], in_=kpe[:, :cw])
                    pb, pbb = bank()
                    MM(pb[0:64, :cw], rotB[:], kpb[:, :cw], True, True, [CONST, kpb_b], [pbb])
                    V("dve", "tensor_tensor", [pbb, CONST], [t1_b], out=t1[:, :cw], in0=pb[0:64, :cw], in1=sin2[:, c0:c0 + cw], op=ALU.mult)
                    rel(pbb)
                    V("dve", "tensor_tensor", [kpe_b, CONST], [t2_b], out=t2[:, :cw], in0=kpe[:, :cw], in1=cos2[:, c0:c0 + cw], op=ALU.mult)
                    V("dve", "tensor_tensor", [t1_b, t2_b], [RR_b], out=RR[:, c0:c0 + cw], in0=t1[:, :cw], in1=t2[:, :cw], op=ALU.add)
                    for h in range(H):
                        pb, pbb = bank()
                        for r in range(2):
                            MM(pb[:, :cw], wuk[:, r, h * 128:(h + 1) * 128], ckv[:, r, :cw], r == 0, r == 1, [wkv_b, ckv_b], [pbb])
                        V("dve", "tensor_scalar_mul", [pbb, CONST], [KT_b], out=KT[:, h, c0:c0 + cw], in0=pb[:, :cw], scalar1=kgn[:, 0:1])
                        q = h % 2
                        ACT([pbb], [sqK_b[q]], sqK[q][:, :cw], pb[:, :cw], AF.Square)
                        rel(pbb)
                        for ti in range(cw // 128):
                            t = c0 // 128 + ti
                            MM(pks[:, t * 8 + h:t * 8 + h + 1], sqK[q][:, ti * 128:(ti + 1) * 128], onesB[:, 0:1], True, False,
                               [sqK_b[q], CONST], [pksb])
                            MM(pks[:, t * 8 + h:t * 8 + h + 1], sqR[:, ti * 128:(ti + 1) * 128], onesB[0:64, 0:1], False, True,
                               [sqR_b, CONST], [pksb])
                    for ti in range(cw // 128):
                        t = c0 // 128 + ti
                        for hf in range(2):
                            pb, pbb = bank()
                            for r in range(2):
                                MM(pb[:, :], ckv[:, r, ti * 128:(ti + 1) * 128], wuv[:, r, hf * 512:(hf + 1) * 512], r == 0, r == 1,
                                   [ckv_b, wkv_b], [pbb])
                            S.op("act", (lambda o, i: (lambda e: e.copy(out=o, in_=i)))(OV[:, t, hf * 512:(hf + 1) * 512], pb[:, :]),
                                 [pbb], [OV_b[t]])
                            rel(pbb)
                ACT([pksb], [ksc_b], ksc, pks[:, 0:NT * 8].rearrange("p (t c) -> p t c", c=8), AF.Ln, scale=1.0 / 192, bias=EPS)
                ACT([ksc_b], [ksc_b], ksc, ksc, AF.Exp, scale=-0.5)
                V("dve", "tensor_scalar_mul", [ksc_b], [ksc_b], out=ksc, in0=ksc, scalar1=192.0 ** -0.5)
                S.barrier()
                if STOP == "KV":
                    raise _Stop()
                bpool[0] = list(range(4))
                b_base = kv_base
                for _once in range(1):
                    areset(b_base)
                    woutB = carve([128, 8, 1024], BF16); wo_b = S.buf()
                    wst = [carve([128, 8, 128], BF16) for _ in range(2)]; wst_b = S.bufs(2)
                    wuq = [carve([128, 3, 192], BF16) for _ in range(2)]; wuq_b = S.bufs(2)
                    cqT = carve([128, 3, 512], F32); cq_b = S.buf()
                    cqn = carve([128, 3, 512], BF16); cqn_b = S.buf()
                    cqs, cqs_b = cqn, cqn_b
                    rb = carve([128, 512], F32); rb_b = S.buf()
                    qn = carve([128, 512], F32); qn_b = S.buf()
                    qr = carve([64, 512], F32); qr_b = S.buf()
                    qrb = carve([64, 512], BF16); qrb_b = S.buf()
                    sqn = carve([128, 512], BF16); sqn_b = S.buf()
                    sqr = carve([64, 512], BF16); sqr_b = S.buf()
                    rq = carve([128, 512], F32); rq_b = S.buf()
                    QnT2 = [carve([128, 512], BF16) for _ in range(2)]; QrT2 = [carve([64, 512], BF16) for _ in range(2)]
                    Q2_b = S.bufs(2)
                    t1 = carve([64, 512], F32); t1_b = S.buf()
                    t2 = carve([64, 512], F32); t2_b = S.buf()
                    sig = carve([128, 512], F32); sig_b = S.buf()
                    gs2 = [carve([128, 512], F32) for _ in range(2)]; gs2_b = S.bufs(2)
                    PT = [carve([128, 512], BF16) for _ in range(3)]; PT_b = S.bufs(3)
                    rd_ = carve([128, 512], F32); rd_b = S.buf()
                    ot = carve([128, 512], F32); ot_b = S.buf()
                    OGT = carve([128, 8, 512], BF16); OGT_b = S.buf()
                for q0 in range(1, NT, 4):
                    q1 = min(q0 + 3, NT - 1)
                    bw = (q1 - q0 + 1) * 128
                    bc0 = q0 * 128
                    hrd = hT_reads(bc0, bw)
                    DMA("sp", woutB, wview(w_outb_b, 0, 1024), [], [wo_b], "wo")
                    wc = [0]
                    for fc in range(3):
                        wsl = wc[0] % 2; wc[0] += 1
                        DMA("sp", wst[wsl], wview(w_inb_b, fc * 128, 128), [], [wst_b[wsl]], f"wst{wsl}")
                        pb, pbb = bank()
                        for kc in range(8):
                            MM(pb[:, :bw], wst[wsl][:, kc, :], hT[:, kc, bc0:bc0 + bw], kc == 0, kc == 7, [wst_b[wsl]] + hrd, [pbb])
                        S.op("act", (lambda o, i: (lambda e: e.copy(out=o, in_=i)))(cqT[:, fc, :bw], pb[:, :bw]), [pbb], [cq_b])
                        rel(pbb)
                    ACT([cq_b], [cqs_b], cqs[:, :, :bw], cqT[:, :, :bw], AF.Square)
                    pb, pbb = bank()
                    for j in range(3):
                        MM(pb[:, :bw], onesB[:], cqs[:, j, :bw], j == 0, j == 2, [CONST, cqs_b], [pbb])
                    ACT([pbb], [rb_b], rb[:, :bw], pb[:, :bw], AF.Ln, scale=1.0 / 384, bias=EPS)
                    rel(pbb)
                    ACT([rb_b], [rb_b], rb[:, :bw], rb[:, :bw], AF.Exp, scale=-0.5)
                    V("dve", "tensor_tensor", [cq_b, rb_b], [cqn_b], out=cqn[:, :, :bw], in0=cqT[:, :, :bw],
                      in1=rb[:, :bw].unsqueeze(1).to_broadcast([128, 3, bw]), op=ALU.mult)
                    pctr = [0]

                    def prep(h):
                        us = h % 2
                        QnT, QrT, Q_b, gs, gs_b = QnT2[us], QrT2[us], Q2_b[us], gs2[us], gs2_b[us]
                        DMA("sp", wuq[us], wview(w_uq_b, h * 192, 192), [], [wuq_b[us]], f"wuq{us}")
                        pq, pqb = bank()
                        pr, prb = bank()
                        for j in range(3):
                            MM(pq[:, :bw], wuq[us][:, j, 0:128], cqn[:, j, :bw], j == 0, j == 2, [wuq_b[us], cqn_b], [pqb])
                        for j in range(3):
                            MM(pr[0:64, :bw], wuq[us][:, j, 128:192], cqn[:, j, :bw], j == 0, j == 2, [wuq_b[us], cqn_b], [prb])
                        yield
                        V("dve", "tensor_scalar_mul", [pqb, CONST], [qn_b], out=qn[:, :bw], in0=pq[:, :bw], scalar1=qgn[:, 0:1])
                        V("dve", "tensor_scalar_mul", [prb, CONST], [qr_b], out=qr[:, :bw], in0=pr[0:64, :bw], scalar1=qgr[:, 0:1])
                        ACT([pqb], [sqn_b], sqn[:, :bw], pq[:, :bw], AF.Square)
                        ACT([prb], [sqr_b], sqr[:, :bw], pr[0:64, :bw], AF.Square)
                        rel(pqb, prb)
                        yield
                        pb, pbb = bank()
                        MM(pb[:, :bw], onesB[:], sqn[:, :bw], True, False, [CONST, sqn_b], [pbb])
                        MM(pb[:, :bw], onesB[0:64, :], sqr[:, :bw], False, True, [CONST, sqr_b], [pbb])
                        yield
                        ACT([pbb], [rq_b], rq[:, :bw], pb[:, :bw], AF.Ln, scale=1.0 / 192, bias=EPS)
                        rel(pbb)
                        ACT([rq_b], [rq_b], rq[:, :bw], rq[:, :bw], AF.Exp, scale=-0.5)
                        V("pool", "tensor_copy", [qr_b], [qrb_b], out=qrb[:, :bw], in_=qr[:, :bw])
                        yield
                        V("dve", "tensor_tensor", [qn_b, rq_b], [Q_b], out=QnT[:, :bw], in0=qn[:, :bw], in1=rq[:, :bw], op=ALU.mult)
                        pb, pbb = bank()
                        MM(pb[0:64, :bw], rotB[:], qrb[:, :bw], True, True, [CONST, qrb_b], [pbb])
                        yield
                        V("dve", "tensor_tensor", [pbb, CONST], [t1_b], out=t1[:, :bw], in0=pb[0:64, :bw], in1=sin2[:, bc0:bc0 + bw], op=ALU.mult)
                        rel(pbb)
                        V("dve", "tensor_tensor", [qr_b, CONST], [t2_b], out=t2[:, :bw], in0=qr[:, :bw], in1=cos2[:, bc0:bc0 + bw], op=ALU.mult)
                        yield
                        V("dve", "tensor_tensor", [t1_b, t2_b], [t1_b], out=t1[:, :bw], in0=t1[:, :bw], in1=t2[:, :bw], op=ALU.add)
                        V("dve", "tensor_tensor", [t1_b, rq_b], [Q_b], out=QrT[:, :bw], in0=t1[:, :bw], in1=rq[0:64, :bw], op=ALU.mult)
                        wsl = wc[0] % 2; wc[0] += 1
                        DMA("sp", wst[wsl], wview(w_inb_b, 384 + h * 128, 128), [], [wst_b[wsl]], f"wst{wsl}")
                        pgt, pgtb = bank()
                        for kc in range(8):
                            MM(pgt[:, :bw], wst[wsl][:, kc, :], hT[:, kc, bc0:bc0 + bw], kc == 0, kc == 7, [wst_b[wsl]] + hrd, [pgtb])
                        yield
                        ACT([pgtb], [sig_b], sig[:, :bw], pgt[:, :bw], AF.Exp, scale=-1.0)
                        yield
                        ACT([sig_b], [sig_b], sig[:, :bw], sig[:, :bw], AF.Ln, bias=1.0)
                        ACT([sig_b], [sig_b], sig[:, :bw], sig[:, :bw], AF.Exp, scale=-1.0)
                        yield
                        V("dve", "tensor_tensor", [pgtb, sig_b], [gs_b], out=gs[:, :bw], in0=pgt[:, :bw], in1=sig[:, :bw], op=ALU.mult)
                        rel(pgtb)

                    def attn(h, nxt):
                        us = h % 2
                        QnT, QrT, Q_b, gs, gs_b = QnT2[us], QrT2[us], Q2_b[us], gs2[us], gs2_b[us]
                        pO, pOb = fbank(4 + 2 * (h % 2))
                        pD, pDb = fbank(5 + 2 * (h % 2))
                        for kt in range(0, q1 + 1):
                            o0 = (max(kt, q0) - q0) * 128
                            ks = slice(kt * 128, (kt + 1) * 128)
                            psc, pscb = bank()
                            diag = kt >= q0
                            MM(psc[:, o0:bw], KT[:, h, ks], QnT[:, o0:bw], True, False, [KT_b, Q_b], [pscb])
                            MM(psc[:, o0:bw], RR[:, ks], QrT[:, o0:bw], False, not diag, [RR_b, Q_b], [pscb])
                            if diag:
                                MM(psc[:, o0:o0 + 128], urowB[:], wrowB[:], False, True, [CONST], [pscb])
                            ps_ = pctr[0] % 3; pctr[0] += 1
                            ACT([pscb, ksc_b, CONST], [PT_b[ps_]], PT[ps_][:, o0:bw], psc[:, o0:bw], AF.Exp,
                                scale=ksc[:, kt, h:h + 1], bias=(padb[:, 0:1] if kt == 0 else zcol[:, 0:1]))
                            rel(pscb)
                            MM(pO[:, o0:bw], OV[:, kt, h * 128:(h + 1) * 128], PT[ps_][:, o0:bw], kt == 0, kt == q1, [OV_b[kt], PT_b[ps_]], [pOb])
                            MM(pD[:, o0:bw], onesB[:], PT[ps_][:, o0:bw], kt == 0, kt == q1, [CONST, PT_b[ps_]], [pDb])
                            if nxt is not None:
                                next(nxt, None)
                        if nxt is not None:
                            for _ in nxt:
                                pass
                        V("dve", "reciprocal", [pDb], [rd_b], out=rd_[:, :bw], in_=pD[:, :bw])
                        V("dve", "tensor_tensor", [pOb, rd_b], [ot_b], out=ot[:, :bw], in0=pO[:, :bw], in1=rd_[:, :bw], op=ALU.mult)
                        V("dve", "tensor_tensor", [ot_b, gs_b], [OGT_b], out=OGT[:, h, :bw], in0=ot[:, :bw], in1=gs[:, :bw], op=ALU.mult)

                    for _ in prep(0):
                        pass
                    for h in range(H):
                        attn(h, prep(h + 1) if h + 1 < H else None)
                    for qt in range(q0, q1 + 1):
                        sl = qt % 2
                        DMA("sp", xin[sl][:], h1_d[qt * 128:(qt + 1) * 128, :], [], [xin_b[sl]], f"xl{sl}")
                        lc = (qt - q0) * 128
                        for hf in range(2):
                            pb, pbb = bank()
                            for kc in range(8):
                                MM(pb[:, :], OGT[:, kc, lc:lc + 128], woutB[:, kc, hf * 512:(hf + 1) * 512], kc == 0, kc == 7,
                                   [OGT_b, wo_b], [pbb])
                            V("dve", "tensor_tensor", [pbb, xin_b[sl]], [xin_b[sl]], out=xin[sl][:, hf * 512:(hf + 1) * 512],
                              in0=pb[:, :], in1=xin[sl][:, hf * 512:(hf + 1) * 512], op=ALU.add)
                            rel(pbb)
                        r0 = s * SEQ + (qt - 1) * 128
                        DMA("act", out_d[r0:r0 + 128, :], xin[sl][:], [xin_b[sl]], [], f"os{sl}")
                S.barrier()
        except _Stop:
            S.barrier()
        S.emit(nc, st)
    nc._sched_info = S.info
    nc._sched = S
    return nc


_NC_CACHE = {}


def kernel(**inputs):
    x = np.ascontiguousarray(inputs["x"], dtype=np.float32)
    B, SEQ, _ = x.shape
    NSEQ = B // N_CORES
    key = (NSEQ, SEQ)
    if key not in _NC_CACHE:
        _NC_CACHE[key] = build(NSEQ, SEQ)
    nc = _NC_CACHE[key]
    consts = host_consts(SEQ + 128)
    shared = {}
    for k, v in inputs.items():
        if k == "x":
            continue
        a = np.ascontiguousarray(np.asarray(v, dtype=np.float32))
        if a.ndim >= 2 and a.shape[0] == 1:
            a = a[0]
        shared[k] = np.ascontiguousarray(a)
    shared.update(consts)
    in_maps = []
    for c in range(N_CORES):
        m = dict(shared)
        m["x"] = np.ascontiguousarray(x[c * NSEQ:(c + 1) * NSEQ].reshape(NSEQ * SEQ, D))
        in_maps.append(m)
    res = run_bass_kernel_spmd(nc, in_maps, core_ids=list(range(N_CORES)))
    out = np.concatenate([np.asarray(r["out"]).reshape(NSEQ, SEQ, D) for r in res.results], axis=0)
    return out.astype(np.float32)
```

```python
import numpy as np
from contextlib import ExitStack
import concourse.bass as bass
import concourse.mybir as mybir
from concourse.bass_utils import run_bass_kernel_spmd

F32 = mybir.dt.float32
BF16 = mybir.dt.bfloat16
AF = mybir.ActivationFunctionType
ALU = mybir.AluOpType
AX = mybir.AxisListType
ENGS = ("sp", "act", "pool", "dve", "pe")

D = 1024
H = 8
BIG = 30000.0
EPS = 1e-6
N_CORES = 8
CHAIN_DT = F32


class Buf:
    __slots__ = ("name", "w", "r", "excl")

    def __init__(self, name):
        self.name = name
        self.w = None
        self.r = {}
        self.excl = False


class Op:
    __slots__ = ("fn", "waits", "tok", "dma")

    def __init__(self, fn, waits, tok, dma):
        self.fn = fn
        self.waits = waits
        self.tok = tok
        self.dma = dma


class Sched:
    def __init__(self):
        self.ops = {e: [] for e in ENGS}
        self.count = {}
        self.known = {e: {} for e in ENGS}
        self.needed = set()
        self.nb = 0
        self.marks = []
        self.info = {}
        self.dbg = []
        self.names = {}

    def buf(self, name=None):
        self.nb += 1
        return Buf(name or f"b{self.nb}")

    def bufs(self, n, name="b"):
        return [self.buf(f"{name}{i}") for i in range(n)]

    def op(self, eng, fn, reads=(), writes=(), dma_dom=None):
        deps = {}

        def need(tok):
            if tok is not None and deps.get(tok[0], 0) < tok[1]:
                deps[tok[0]] = tok[1]

        own = dma_dom if dma_dom is not None else eng
        for b in reads:
            need(b.w)
            if b.excl:
                for d, i in b.r.items():
                    if d != own:
                        need((d, i))
        for b in writes:
            need(b.w)
            for d, i in b.r.items():
                need((d, i))
        is_dma = dma_dom is not None
        dom = dma_dom if is_dma else eng
        waits = {}
        kn = self.known[eng]
        for d, i in deps.items():
            if d == "pe" and eng == "pe" and not is_dma:
                continue
            if kn.get(d, 0) >= i:
                continue
            waits[d] = i
            kn[d] = i
            self.needed.add((d, i))
        c = self.count.get(dom, 0) + 1
        self.count[dom] = c
        tok = (dom, c)
        for b in reads:
            if b.r.get(dom, 0) < c:
                b.r[dom] = c
        for b in writes:
            b.w = tok
            b.r = {}
        o_ = Op(fn, waits, tok, is_dma)
        self.ops[eng].append(o_)
        self.dbg.append((o_, [b.name for b in reads], [b.name for b in writes]))
        return tok

    def barrier(self, label=None):
        self.marks.append((label, dict(self.count)))
        for e in ENGS:
            waits = {}
            kn = self.known[e]
            for d, c in self.count.items():
                if kn.get(d, 0) < c:
                    waits[d] = c
                    kn[d] = c
                    self.needed.add((d, c))
            if waits:
                self.ops[e].append(Op(None, waits, None, False))

    def emit(self, nc, stack):
        doms = sorted(self.count.keys())
        dma_doms = set()
        for e in ENGS:
            for o in self.ops[e]:
                if o.dma:
                    dma_doms.add(o.tok[0])
        sems = {d: stack.enter_context(nc.semaphore(f"s_{d}")) for d in doms}
        rank = {}
        for d in doms:
            if d in dma_doms:
                rank[d] = None
            else:
                idxs = sorted(i for (dd, i) in self.needed if dd == d)
                rank[d] = {i: k + 1 for k, i in enumerate(idxs)}
        block = stack.enter_context(nc.Block())
        ops = self.ops
        self.info = dict(sems={d: sems[d].num for d in doms},
                         marks=[(lab, {d: (rank[d][c] if rank[d] is not None else 16 * c) for d, c in cnt.items()
                                       if d in ("pe", "act", "dve", "pool")}) for lab, cnt in self.marks])

        def run(engh, lst):
            for o in lst:
                for d, i in o.waits.items():
                    engh.wait_ge(sems[d], 16 * i if rank[d] is None else rank[d][i])
                if o.fn is None:
                    continue
                ins = o.fn(engh)
                try:
                    self.names[ins.ins.name] = o
                except Exception:
                    pass
                d, i = o.tok
                if o.dma:
                    ins.then_inc(sems[d], 16)
                elif i in rank[d]:
                    ins.then_inc(sems[d], 1)

        @block.sync
        def _(e):
            run(e, ops["sp"])

        @block.scalar
        def _(e):
            run(e, ops["act"])

        @block.gpsimd
        def _(e):
            run(e, ops["pool"])

        @block.vector
        def _(e):
            run(e, ops["dve"])

        @block.tensor
        def _(e):
            run(e, ops["pe"])


def host_consts(T):
    p = np.arange(128)
    ident = np.eye(128, dtype=np.float32)
    tri = (p[:, None] <= p[None, :]).astype(np.float32)
    mui = np.where(p[None, :] < p[:, None], -BIG, 0.0).astype(np.float32)
    mls = np.where(p[None, :] >= p[:, None], BIG, 0.0).astype(np.float32)
    mus = np.where(p[None, :] <= p[:, None], -BIG, 0.0).astype(np.float32)
    rot = np.zeros((64, 64), np.float32)
    for m in range(32):
        rot[m + 32, m] = -1.0
        rot[m, m + 32] = 1.0
    pos = np.maximum(np.arange(T) - 112, 0).astype(np.float32)
    inv = (np.float32(10000.0) ** (-np.arange(32, dtype=np.float32) / np.float32(32))).astype(np.float32)
    ang = (pos[:, None] * inv[None, :]).astype(np.float32)
    cos = np.cos(ang).astype(np.float32).T
    sin = np.sin(ang).astype(np.float32).T
    cos2 = np.concatenate([cos, cos], 0).astype(np.float32)
    sin2 = np.concatenate([sin, sin], 0).astype(np.float32)
    padb = np.where(p < 112, -BIG, 0.0).astype(np.float32)[:, None]
    urow = np.where(p >= 64, 1.0, 0.0).astype(np.float32)[None, :]
    wrow = np.where(p < 64, -BIG, 0.0).astype(np.float32)[None, :]
    return dict(c_ident=ident, c_tri=tri, c_mui=mui, c_mls=mls, c_mus=mus, c_rot=rot, c_cos=cos2, c_sin=sin2,
                c_padb=padb, c_urow=urow, c_wrow=wrow)


class _Stop(Exception):
    pass


def build(NSEQ, SEQ, STOP=None):
    T = SEQ + 128
    NT = T // 128
    CB = [(c0, min(512, T - c0)) for c0 in range(0, T, 512)]
    nc = bass.Bass("TRN2", target_bir_lowering=False)

    def din(name, shape, dt=F32):
        return nc.dram_tensor(name, list(shape), dt, kind="ExternalInput").ap()

    x_d = din("x", [NSEQ * SEQ, D])
    meta_d = din("meta_tokens", [16, D])
    a_norm_d = din("a_norm", [D]); a_w_in_d = din("a_w_in", [D, 4112]); a_conv_d = din("a_conv", [4, 3072])
    a_log_d = din("a_log", [8]); a_dtb_d = din("a_dt_bias", [8]); a_og_d = din("a_o_gain", [128])
    a_w_out_d = din("a_w_out", [D, D]); kv_norm_d = din("kv_norm", [D]); kv_wd_d = din("kv_w_down", [D, 320])
    kv_ln_d = din("kv_latent_norm", [256]); kv_uk_d = din("kv_w_uk", [256, D]); kv_uv_d = din("kv_w_uv", [256, D])
    k_gain_d = din("k_gain", [192]); b_norm_d = din("b_norm", [D]); b_w_in_d = din("b_w_in", [D, 1408])
    b_qln_d = din("b_q_latent_norm", [384]); b_uq_d = din("b_w_uq", [384, 1536]); b_qg_d = din("b_q_gain", [192])
    b_w_out_d = din("b_w_out", [D, D])
    c_ident_d = din("c_ident", [128, 128]); c_tri_d = din("c_tri", [128, 128]); c_mui_d = din("c_mui", [128, 128])
    c_mls_d = din("c_mls", [128, 128]); c_mus_d = din("c_mus", [128, 128]); c_rot_d = din("c_rot", [64, 64]); c_cos_d = din("c_cos", [64, T])
    c_sin_d = din("c_sin", [64, T]); c_padb_d = din("c_padb", [128, 1]); c_urow_d = din("c_urow", [1, 128])
    c_wrow_d = din("c_wrow", [1, 128])
    out_d = nc.dram_tensor("out", [NSEQ * SEQ, D], F32, kind="ExternalOutput").ap()

    def dscr(name, shape, dt):
        return nc.dram_tensor(name, list(shape), dt).ap()

    w_in_b = dscr("w_in_b", [D, 4112], BF16); w_outa_b = dscr("w_outa_b", [D, D], BF16)
    w_down_b = dscr("w_down_b", [D, 320], BF16); w_uk_b = dscr("w_uk_b", [256, D], BF16)
    w_uv_b = dscr("w_uv_b", [256, D], BF16); w_inb_b = dscr("w_inb_b", [D, 1408], BF16)
    w_uq_b = dscr("w_uq_b", [384, 1536], BF16); w_outb_b = dscr("w_outb_b", [D, D], BF16)
    h1_d = dscr("h1_d", [T, D], F32)

    S = Sched()
    with ExitStack() as st:
        st.enter_context(nc.allow_non_contiguous_dma("small strided parameter loads"))

        def sb(name, shape, dt):
            return st.enter_context(nc.sbuf_tensor(name, list(shape), dt))

        def V(eng, fname, reads, writes, *a, **k):
            S.op(eng, lambda e: getattr(e, fname)(*a, **k), reads, writes)

        def ACT(reads, writes, out, in_, func, **k):
            S.op("act", lambda e: e.activation(out=out, in_=in_, func=func, **k), reads, writes)

        def MM(out, lhsT, rhs, start, stop, reads, writes):
            S.op("pe", lambda e: e.matmul(out, lhsT=lhsT, rhs=rhs, start=start, stop=stop), reads, writes)

        def TR(out, in_, ident, reads, writes):
            S.op("pe", lambda e: e.transpose(out=out, in_=in_, identity=ident), reads, writes)

        def DMA(q, out, in_, reads, writes, dom):
            S.op(q, lambda e: e.dma_start(out=out, in_=in_), reads, writes, dma_dom=dom)

        PB = [st.enter_context(nc.psum_tensor(f"pb{i}", [128, 512], F32)) for i in range(8)]
        PBb = S.bufs(8, "pb")
        for _b in PBb:
            _b.excl = True
        bctr = [0]
        bpool = [list(range(8))]

        live = set()

        def bank():
            pool_ = bpool[0]
            for k in range(len(pool_)):
                i = pool_[(bctr[0] + k) % len(pool_)]
                if i not in live:
                    bctr[0] += k + 1
                    live.add(i)
                    return PB[i], PBb[i]
            raise RuntimeError("no free PSUM bank")

        def rel(*bbs):
            for bb in bbs:
                live.discard(PBb.index(bb))

        def fbank(i):
            return PB[i], PBb[i]

        try:
            identF = sb("identF", [128, 128], F32); identB = sb("identB", [128, 128], BF16)
            triF = sb("triF", [128, 128], F32); muiF = sb("muiF", [128, 128], F32); mlsF = sb("mlsF", [128, 128], F32)
            musF = sb("musF", [128, 128], F32)
            onesB = sb("onesB", [128, 128], BF16); rotF = sb("rotF", [64, 64], F32); rotB = sb("rotB", [64, 64], BF16)
            cos2 = sb("cos2", [64, T], BF16); sin2 = sb("sin2", [64, T], BF16)
            padb = sb("padb", [128, 1], F32); zcol = sb("zcol", [128, 1], F32)
            urowF = sb("urowF", [1, 128], F32); wrowF = sb("wrowF", [1, 128], F32)
            urowB = sb("urowB", [1, 128], BF16); wrowB = sb("wrowB", [1, 128], BF16)
            g_anorm = sb("g_anorm", [128, 8], F32); g_kvnorm = sb("g_kvnorm", [128, 8], F32)
            g_bnorm = sb("g_bnorm", [128, 8], F32); g_kvln = sb("g_kvln", [128, 2], F32)
            g_qln = sb("g_qln", [128, 3], F32); g_og = sb("g_og", [128, 1], F32)
            kgn = sb("kgn", [128, 1], F32); kgr = sb("kgr", [64, 1], F32)
            qgn = sb("qgn", [128, 1], F32); qgr = sb("qgr", [64, 1], F32)
            convw = sb("convw", [128, 4, 24], F32)
            alog = sb("alog", [128, 8], F32); negA = sb("negA", [128, 8], F32); dtb = sb("dtb", [128, 8], F32)
            CONST = S.buf("const")

            def cload(dst, src):
                DMA("sp", dst, src, [], [CONST], "cst")

            cload(identF[:], c_ident_d); cload(triF[:], c_tri_d); cload(muiF[:], c_mui_d); cload(mlsF[:], c_mls_d); cload(musF[:], c_mus_d)
            cload(rotF[:], c_rot_d); cload(padb[:], c_padb_d)
            cload(urowF[:], c_urow_d); cload(wrowF[:], c_wrow_d)
            cload(g_anorm[:], a_norm_d.rearrange("(k p) -> p k", p=128))
            cload(g_kvnorm[:], kv_norm_d.rearrange("(k p) -> p k", p=128))
            cload(g_bnorm[:], b_norm_d.rearrange("(k p) -> p k", p=128))
            cload(g_kvln[:], kv_ln_d.rearrange("(k p) -> p k", p=128))
            cload(g_qln[:], b_qln_d.rearrange("(k p) -> p k", p=128))
            cload(g_og[:], a_og_d.rearrange("(p o) -> p o", o=1))
            cload(kgn[:], k_gain_d[0:128].rearrange("(p o) -> p o", o=1))
            cload(kgr[:], k_gain_d[128:192].rearrange("(p o) -> p o", o=1))
            cload(qgn[:], b_qg_d[0:128].rearrange("(p o) -> p o", o=1))
            cload(qgr[:], b_qg_d[128:192].rearrange("(p o) -> p o", o=1))
            for jj in range(4):
                cload(convw[:, jj, :], a_conv_d[jj, :].rearrange("(c p) -> p c", p=128))
            cload(alog[:], a_log_d.partition_broadcast(128))
            cload(dtb[:], a_dtb_d.partition_broadcast(128))
            CONST.w = ("cst", S.count["cst"])
            V("dve", "tensor_copy", [CONST], [CONST], out=identB[:], in_=identF[:])
            V("dve", "tensor_copy", [CONST], [CONST], out=rotB[:], in_=rotF[:])
            V("dve", "tensor_copy", [CONST], [CONST], out=urowB[:], in_=urowF[:])
            V("dve", "tensor_copy", [CONST], [CONST], out=wrowB[:], in_=wrowF[:])
            V("dve", "memset", [], [CONST], onesB[:], 1.0)
            V("dve", "memset", [], [CONST], zcol[:], 0.0)
            ACT([CONST], [CONST], negA[:], alog[:], AF.Exp)
            V("dve", "tensor_scalar_mul", [CONST], [CONST], out=negA[:], in0=negA[:], scalar1=-1.0)

            hT = sb("hT", [128, 8, T], BF16); hT_b = S.bufs(NT, "hT")
            OV = sb("OV", [128, NT, 1024], BF16); OV_b = S.bufs(NT, "OV")
            RR = sb("RR", [64, T], BF16); RR_b = S.buf("RR")
            xin = [sb(f"xin{i}", [128, 1024], F32) for i in range(2)]; xin_b = S.bufs(2, "xin")
            xn = [sb(f"xn{i}", [128, 1024], BF16) for i in range(2)]; xn_b = S.bufs(2, "xn")
            junk = sb("junk", [128, 1024], BF16); junk_b = S.buf("junk")
            ssq = [sb(f"ssq{i}", [128, 1], F32) for i in range(2)]; ssq_b = S.bufs(2, "ssq")
            ARENA_F32 = 27136
            arena = sb("arena", [128, ARENA_F32], F32)
            apos = [0]

            def areset(off=0):
                apos[0] = off

            def carve(shape, dt):
                P = shape[0]
                n = int(np.prod(shape[1:]))
                nbytes = n * (4 if dt == F32 else 2)
                n32 = (nbytes + 3) // 4
                o = apos[0]
                apos[0] += (n32 + 7) // 8 * 8
                assert apos[0] <= ARENA_F32, ("arena overflow", apos[0])
                v = arena[0:P, o:o + n32]
                if dt != F32:
                    v = v.bitcast(dt)[:, 0:n]
                if len(shape) == 3:
                    v = v.rearrange("p (a b) -> p a b", a=shape[1], b=shape[2])
                return v

            areset(0)
            wl = [carve([128, 1024], F32) for i in range(2)]; wl_b = S.bufs(2, "wl")
            ws = [carve([128, 1024], BF16) for i in range(2)]; ws_b = S.bufs(2, "ws")
            for tab_d, tab in ((c_cos_d, cos2), (c_sin_d, sin2)):
                for c0 in range(0, T, 1024):
                    cw = min(1024, T - c0)
                    sl = wctr_ = 0
                    DMA("sp", wl[0][0:64, :cw], tab_d[:, c0:c0 + cw], [], [wl_b[0]], "wl0")
                    V("dve", "tensor_copy", [wl_b[0]], [CONST], out=tab[:, c0:c0 + cw], in_=wl[0][0:64, :cw])
            wctr = [0]

            def prep_weight(src, K, N, dst, gcol):
                for kc in range(K // 128):
                    for c0 in range(0, N, 1024):
                        cw = min(1024, N - c0)
                        sl = wctr[0] % 2
                        wctr[0] += 1
                        DMA("sp", wl[sl][:, :cw], src[kc * 128:(kc + 1) * 128, c0:c0 + cw], [], [wl_b[sl]], f"wl{sl}")
                        if gcol is not None:
                            V("dve", "tensor_scalar_mul", [wl_b[sl], CONST], [ws_b[sl]], out=ws[sl][:, :cw],
                              in0=wl[sl][:, :cw], scalar1=gcol(kc))
                        else:
                            V("dve", "tensor_copy", [wl_b[sl]], [ws_b[sl]], out=ws[sl][:, :cw], in_=wl[sl][:, :cw])
                        DMA("act", dst[kc * 128:(kc + 1) * 128, c0:c0 + cw], ws[sl][:, :cw], [ws_b[sl]], [], f"ws{sl}")

            prep_weight(a_w_in_d, D, 4112, w_in_b, lambda kc: g_anorm[:, kc:kc + 1])
            prep_weight(a_w_out_d, D, D, w_outa_b, lambda kc: g_og[:, 0:1])
            prep_weight(kv_wd_d, D, 320, w_down_b, lambda kc: g_kvnorm[:, kc:kc + 1])
            prep_weight(kv_uk_d, 256, D, w_uk_b, lambda kc: g_kvln[:, kc:kc + 1])
            prep_weight(kv_uv_d, 256, D, w_uv_b, lambda kc: g_kvln[:, kc:kc + 1])
            prep_weight(b_w_in_d, D, 1408, w_inb_b, lambda kc: g_bnorm[:, kc:kc + 1])
            prep_weight(b_uq_d, 384, 1536, w_uq_b, lambda kc: g_qln[:, kc:kc + 1])
            prep_weight(b_w_out_d, D, D, w_outb_b, None)
            S.barrier()
            if STOP == "W":
                raise _Stop()

            def wview(wb, c0, cw):
                return wb[:, c0:c0 + cw].rearrange("(k p) n -> p k n", p=128)

            def load_x_tile(s, t, sl):
                if t == 0:
                    V("pool", "memset", [], [xin_b[sl]], xin[sl][:], 0.0)
                    DMA("sp", xin[sl][112:128, :], meta_d, [], [xin_b[sl]], f"xl{sl}")
                else:
                    r0 = s * SEQ + (t - 1) * 128
                    DMA("sp", xin[sl][:], x_d[r0:r0 + 128, :], [], [xin_b[sl]], f"xl{sl}")

            def norm_to_hT(src, src_b, t, sl):
                ACT([src_b], [junk_b, ssq_b[sl]], junk[:], src, AF.Square, accum_out=ssq[sl][:])
                ACT([ssq_b[sl]], [ssq_b[sl]], ssq[sl][:], ssq[sl][:], AF.Ln, scale=1.0 / D, bias=EPS)
                ACT([ssq_b[sl]], [ssq_b[sl]], ssq[sl][:], ssq[sl][:], AF.Exp, scale=-0.5)
                V("dve", "tensor_scalar_mul", [src_b, ssq_b[sl]], [xn_b[sl]], out=xn[sl][:], in0=src, scalar1=ssq[sl][:, 0:1])
                pb, pbb = bank()
                pbv = pb[:].bitcast(BF16)
                for kc in range(8):
                    TR(pbv[:, kc * 128:(kc + 1) * 128], xn[sl][:, kc * 128:(kc + 1) * 128], identB[:], [xn_b[sl], CONST], [pbb])
                V("dve", "tensor_copy", [pbb], [hT_b[t]], out=hT[:, :, t * 128:(t + 1) * 128],
                  in_=pbv.rearrange("p (k c) -> p k c", k=8))
                rel(pbb)

            def hT_reads(c0, cw):
                return [hT_b[t] for t in range(c0 // 128, (c0 + cw + 127) // 128)]

            def silu_from(src_ap, src_reads, tmp, tmp_b, shape_sl):
                ACT(src_reads, [tmp_b], tmp, src_ap, AF.Exp, scale=-1.0)
                ACT([tmp_b], [tmp_b], tmp, tmp, AF.Ln, bias=1.0)
                ACT([tmp_b], [tmp_b], tmp, tmp, AF.Exp, scale=-1.0)

            def rsqrt_inplace(ap, b, scale, reads_extra=()):
                ACT([b] + list(reads_extra), [b], ap, ap, AF.Ln, scale=scale, bias=EPS)
                ACT([b], [b], ap, ap, AF.Exp, scale=-0.5)

            for s in range(NSEQ):
                bpool[0] = list(range(8))
                for t in range(NT):
                    sl = t % 2
                    load_x_tile(s, t, sl)
                    norm_to_hT(xin[sl][:], xin_b[sl], t, sl)
                S.barrier()
                if STOP == "A0":
                    raise _Stop()
                areset(0)
                wab = carve([128, 8, 16], BF16); wab_b = S.buf()
                ab = carve([128, NT, 16], F32); ab_b = S.buf()
                tm1 = carve([128, NT, 8], F32); tm1_b = S.buf()
                gcol = carve([128, NT, 8], F32); g_b = S.buf()
                lnb = carve([128, NT, 8], F32); lnb_b = S.buf()
                gc = carve([128, NT, 8], F32); gc_b = S.buf()
                ngc = carve([128, NT, 8], F32); egc = carve([128, NT, 8], F32); bls = carve([128, NT, 8], F32)
                beta = carve([128, NT, 8], F32); begc = carve([128, NT, 8], F32)
                der_b = S.buf()
                DMA("sp", wab, wview(w_in_b, 4096, 16), [], [wab_b], "wab")
                pb, pbb = bank()
                for t in range(NT):
                    for kc in range(8):
                        MM(pb[:, t * 16:(t + 1) * 16], hT[:, kc, t * 128:(t + 1) * 128], wab[:, kc, :], kc == 0, kc == 7,
                           [hT_b[t], wab_b], [pbb])
                V("dve", "tensor_copy", [pbb], [ab_b], out=ab, in_=pb[:, 0:NT * 16].rearrange("p (t c) -> p t c", c=16))
                rel(pbb)
                ACT([ab_b], [tm1_b], tm1, ab[:, :, 0:8], AF.Exp, scale=-1.0)
                ACT([tm1_b], [tm1_b], tm1, tm1, AF.Ln, bias=1.0)
                V("dve", "tensor_scalar_mul", [tm1_b], [lnb_b], out=lnb, in0=tm1, scalar1=-1.0)
                V("dve", "tensor_tensor", [ab_b, CONST], [tm1_b], out=tm1, in0=ab[:, :, 8:16],
                  in1=dtb[:].unsqueeze(1).to_broadcast([128, NT, 8]), op=ALU.add)
                ACT([tm1_b], [tm1_b], tm1, tm1, AF.Exp)
                ACT([tm1_b], [tm1_b], tm1, tm1, AF.Ln, bias=1.0)
                V("dve", "tensor_tensor", [tm1_b, CONST], [g_b], out=gcol, in0=tm1,
                  in1=negA[:].unsqueeze(1).to_broadcast([128, NT, 8]), op=ALU.mult)
                pb, pbb = bank()
                for t in range(NT):
                    MM(pb[:, t * 8:(t + 1) * 8], triF[:], gcol[:, t, :], True, True, [CONST, g_b], [pbb])
                V("dve", "tensor_copy", [pbb], [gc_b], out=gc, in_=pb[:, 0:NT * 8].rearrange("p (t c) -> p t c", c=8))
                rel(pbb)
                V("dve", "tensor_scalar_mul", [gc_b], [der_b], out=ngc, in0=gc, scalar1=-1.0)
                ACT([gc_b], [der_b], egc, gc, AF.Exp)
                V("dve", "tensor_tensor", [gc_b, lnb_b], [der_b], out=bls, in0=gc, in1=lnb, op=ALU.add)
                ACT([lnb_b], [der_b], beta, lnb, AF.Exp)
                ACT([der_b], [der_b], begc, bls, AF.Exp)

                if STOP == "Aab":
                    raise _Stop()
                a12_base = apos[0]
                for h in range(1):
                    areset(a12_base)
                    wst = [carve([128, 8, 128], BF16) for _ in range(2)]; wst_b = S.bufs(2)
                    zc = [carve([128, 515], F32) for _ in range(2)]; zc_b = S.bufs(2)
                    taL = [carve([128, 512], F32) for _ in range(2)]; taL_b = S.bufs(2)
                    tbL = [carve([128, 512], F32) for _ in range(2)]; tbL_b = S.bufs(2)
                    sqL = [carve([128, 512], BF16) for _ in range(2)]; sqL_b = S.bufs(2)
                    blkc = [0]
                    sT = [carve([128, T], BF16) for _ in range(3)]; sT_b = S.bufs(3)
                    Ktok = carve([128, NT, 128], BF16); Vtok = carve([128, NT, 128], BF16); KV_b = S.bufs(2)
                    Sst = carve([128, 128], F32); Sbf = carve([128, 128], BF16); S_b = S.buf(); Sb_b = S.buf()
                    NSL = 8
                    cm = []
                    for i in range(NSL):
                        cm.append(dict(
                            Eui=carve([128, 128], F32), Els=carve([128, 128], F32), Eus=carve([128, 128], F32),
                            Z=[carve([128, 256], CHAIN_DT) for _ in range(2)], P=[carve([128, 128], CHAIN_DT) for _ in range(2)],
                            attT=carve([128, 128], BF16), TmT=carve([128, 128], BF16), nWdT=carve([128, 128], BF16),
                            Bk=carve([128, 128], BF16), bV=carve([128, 128], BF16), Kd=carve([128, 128], BF16),
                            Ub=carve([128, 128], BF16), glc=carve([128, 1], F32), o1=carve([128, 128], F32),
                            b={k: S.buf() for k in ("Eui", "Els", "Eus", "Z0", "Z1", "P0", "P1", "attT", "TmT", "nWdT", "Bk", "bV",
                                                    "Kd", "Ub", "glc", "o1")}))
                for h in range(H):
                    for j in range(3):
                        fc = j * 8 + h
                        wsl = j % 2
                        DMA("sp", wst[wsl], wview(w_in_b, fc * 128, 128), [], [wst_b[wsl]], f"wst{wsl}")
                        for bi, (c0, cw) in enumerate(CB):
                            zs = bi % 2
                            bk_ = blkc[0] % 2; blkc[0] += 1
                            ta, ta_b, tb, tb_b, sq, sq_b = taL[bk_], taL_b[bk_], tbL[bk_], tbL_b[bk_], sqL[bk_], sqL_b[bk_]
                            pb, pbb = bank()
                            for kc in range(8):
                                MM(pb[:, :cw], wst[wsl][:, kc, :], hT[:, kc, c0:c0 + cw], kc == 0, kc == 7,
                                   [wst_b[wsl]] + hT_reads(c0, cw), [pbb])
                            if bi == 0:
                                V("pool", "memset", [], [zc_b[zs]], zc[zs][:, 0:3], 0.0)
                            else:
                                V("pool", "tensor_copy", [zc_b[1 - zs]], [zc_b[zs]], out=zc[zs][:, 0:3], in_=zc[1 - zs][:, 512:515])
                            S.op("act", (lambda o, i: (lambda e: e.copy(out=o, in_=i)))(zc[zs][:, 3:3 + cw], pb[:, :cw]),
                                 [pbb], [zc_b[zs]])
                            rel(pbb)
                            V("dve", "tensor_scalar_mul", [zc_b[zs], CONST], [ta_b], out=ta[:, :cw], in0=zc[zs][:, 3:3 + cw],
                              scalar1=convw[:, 3, fc:fc + 1])
                            for jj in (2, 1, 0):
                                V("dve", "scalar_tensor_tensor", [zc_b[zs], CONST, ta_b], [ta_b], out=ta[:, :cw],
                                  in0=zc[zs][:, jj:jj + cw], scalar=convw[:, jj, fc:fc + 1], in1=ta[:, :cw],
                                  op0=ALU.mult, op1=ALU.add)
                            silu_from(ta[:, :cw], [ta_b], tb[:, :cw], tb_b, None)
                            V("dve", "tensor_tensor", [ta_b, tb_b], [sT_b[j]], out=sT[j][:, c0:c0 + cw], in0=ta[:, :cw],
                              in1=tb[:, :cw], op=ALU.mult)
                            if j < 2:
                                ACT([sT_b[j]], [sq_b], sq[:, :cw], sT[j][:, c0:c0 + cw], AF.Square)
                                pb2, pbb2 = bank()
                                MM(pb2[:, :cw], onesB[:], sq[:, :cw], True, True, [CONST, sq_b], [pbb2])
                                ACT([pbb2], [tb_b], tb[:, :cw], pb2[:, :cw], AF.Ln, bias=EPS)
                                rel(pbb2)
                                ACT([tb_b], [tb_b], tb[:, :cw], tb[:, :cw], AF.Exp, scale=-0.5)
                                V("dve", "scalar_tensor_tensor", [sT_b[j], tb_b], [sT_b[j]], out=sT[j][:, c0:c0 + cw],
                                  in0=sT[j][:, c0:c0 + cw], scalar=(128.0 ** -0.5 if j == 0 else 1.0), in1=tb[:, :cw],
                                  op0=ALU.mult, op1=ALU.mult)
                    qT, kT, vT = sT
                    if STOP == "A12a":
                        raise _Stop()
                    for j, dst in ((1, Ktok), (2, Vtok)):
                        for t0 in range(0, NT, 8):
                            tn = min(8, NT - t0)
                            pb, pbb = bank()
                            pbv = pb[:].bitcast(BF16)
                            for i in range(tn):
                                TR(pbv[:, i * 128:(i + 1) * 128], sT[j][:, (t0 + i) * 128:(t0 + i + 1) * 128], identB[:],
                                   [sT_b[j], CONST], [pbb])
                            V("dve", "tensor_copy", [pbb], [KV_b[j - 1]], out=dst[:, t0:t0 + tn, :],
                              in_=pbv[:, 0:tn * 128].rearrange("p (t c) -> p t c", c=128))
                            rel(pbb)
                    V("pool", "memset", [], [S_b], Sst[:], 0.0)
                    V("pool", "memset", [], [Sb_b], Sbf[:], 0.0)
                    if STOP == "A12b":
                        raise _Stop()
                    def st_G(c):
                        m = cm[c % NSL]; mb = m["b"]
                        cs = slice(c * 128, (c + 1) * 128)
                        gb_l = gcol[:, c, h:h + 1].to_broadcast([128, 128])
                        lb_l = lnb[:, c, h:h + 1].to_broadcast([128, 128])
                        pg, pgb = bank()
                        pt, ptb = bank()
                        m["pg"], m["pgb"], m["pt"], m["ptb"] = pg, pgb, pt, ptb
                        MM(pg[:, 0:128], gb_l, triF[:], True, False, [g_b, CONST], [pgb])
                        MM(pg[:, 0:128], identF[:], muiF[:], False, True, [CONST], [pgb])
                        MM(pg[:, 128:256], gb_l, triF[:], True, False, [g_b, CONST], [pgb])
                        MM(pg[:, 128:256], identF[:], mlsF[:], False, True, [CONST], [pgb])
                        MM(pg[:, 256:384], kT[:, cs], kT[:, cs], True, True, [sT_b[1]], [pgb])
                        MM(pg[:, 384:512], kT[:, cs], qT[:, cs], True, True, [sT_b[1], sT_b[0]], [pgb])
                        MM(pt[:, 0:128], gb_l, triF[:], True, False, [g_b, CONST], [ptb])
                        MM(pt[:, 0:128], lb_l, identF[:], False, False, [lnb_b, CONST], [ptb])
                        MM(pt[:, 0:128], identF[:], musF[:], False, True, [CONST], [ptb])

                    def st_E(c):
                        m = cm[c % NSL]; mb = m["b"]
                        pg, pgb, pt, ptb = m["pg"], m["pgb"], m["pt"], m["ptb"]
                        ACT([pgb, der_b], [mb["Eui"]], m["Eui"], pg[:, 0:128], AF.Exp, bias=ngc[:, c, h:h + 1], scale=1.0)
                        ACT([pgb, der_b], [mb["Els"]], m["Els"], pg[:, 128:256], AF.Exp, bias=bls[:, c, h:h + 1], scale=-1.0)
                        ACT([pgb], [mb["glc"]], m["glc"], pg[:, 127:128], AF.Exp)
                        ACT([ptb, der_b], [mb["Eus"]], m["Eus"], pt[:, 0:128], AF.Exp, bias=ngc[:, c, h:h + 1], scale=1.0)
                        V("dve", "scalar_tensor_tensor", [pgb, mb["Els"]], [mb["Z0"]], out=m["Z"][0][:, 0:128],
                          in0=pg[:, 256:384], scalar=-1.0, in1=m["Els"], op0=ALU.mult, op1=ALU.mult)
                        V("dve", "scalar_tensor_tensor", [pgb, mb["Eus"]], [mb["Z0"]], out=m["Z"][0][:, 128:256],
                          in0=pg[:, 256:384], scalar=-1.0, in1=m["Eus"], op0=ALU.mult, op1=ALU.mult)
                        V("dve", "tensor_tensor", [pgb, mb["Eui"]], [mb["attT"]], out=m["attT"], in0=pg[:, 384:512],
                          in1=m["Eui"], op=ALU.mult)
                        V("dve", "tensor_tensor", [mb["Z0"], CONST], [mb["P0"]], out=m["P"][0], in0=m["Z"][0][:, 128:256],
                          in1=identF[:], op=ALU.add)
                        V("dve", "tensor_scalar_mul", [KV_b[0], der_b], [mb["Bk"]], out=m["Bk"], in0=Ktok[:, c, :],
                          scalar1=begc[:, c, h:h + 1])
                        V("dve", "tensor_scalar_mul", [KV_b[1], der_b], [mb["bV"]], out=m["bV"], in0=Vtok[:, c, :],
                          scalar1=beta[:, c, h:h + 1])
                        V("dve", "tensor_scalar_mul", [KV_b[0], mb["Eui"]], [mb["Kd"]], out=m["Kd"], in0=Ktok[:, c, :],
                          scalar1=m["Eui"][:, 127:128])
                        rel(pgb, ptb)

                    def st_Lmm(c, lev):
                        m = cm[c % NSL]; mb = m["b"]
                        zi = lev % 2
                        pi = (lev - 1) % 2
                        Zc, Zcb = m["Z"][zi], mb[f"Z{zi}"]
                        pk, pkb = bank()
                        m["pk"], m["pkb"] = pk, pkb
                        if lev >= 1:
                            MM(pk[:, 256:384], Zc[:, 0:128], m["P"][pi], True, True, [Zcb, mb[f"P{pi}"]], [pkb])
                        if lev < 6:
                            MM(pk[:, 0:128], Zc[:, 128:256], Zc[:, 0:128], True, True, [Zcb], [pkb])
                            MM(pk[:, 128:256], Zc[:, 0:128], Zc[:, 128:256], True, True, [Zcb], [pkb])

                    def st_Lev(c, lev):
                        m = cm[c % NSL]; mb = m["b"]
                        zi = lev % 2
                        pi = (lev - 1) % 2
                        Zn, Znb = m["Z"][1 - zi], mb[f"Z{1 - zi}"]
                        pk, pkb = m["pk"], m["pkb"]
                        if lev >= 1:
                            if lev < 6:
                                V("dve", "tensor_tensor", [pkb, mb[f"P{pi}"]], [mb[f"P{1 - pi}"]], out=m["P"][1 - pi],
                                  in0=pk[:, 256:384], in1=m["P"][pi], op=ALU.add)
                            else:
                                V("dve", "tensor_tensor", [pkb, mb[f"P{pi}"]], [mb["TmT"]], out=m["TmT"],
                                  in0=pk[:, 256:384], in1=m["P"][pi], op=ALU.add)
                        if lev < 6:
                            V("act", "copy", [pkb], [Znb], out=Zn[:, 0:256], in_=pk[:, 0:256])
                        rel(pkb)

                    def st_W(c):
                        m = cm[c % NSL]; mb = m["b"]
                        pw, pwb = bank()
                        MM(pw[:, 0:128], m["Bk"], m["TmT"], True, True, [mb["Bk"], mb["TmT"]], [pwb])
                        V("act", "mul", [pwb], [mb["nWdT"]], out=m["nWdT"], in_=pw[:, 0:128], mul=-1.0)
                        rel(pwb)

                    def st_R(c):
                        m = cm[c % NSL]; mb = m["b"]
                        cs = slice(c * 128, (c + 1) * 128)
                        pu, pub = bank()
                        MM(pu[:, 0:128], m["TmT"], m["bV"], True, c == 0, [mb["TmT"], mb["bV"]], [pub])
                        if c > 0:
                            MM(pu[:, 0:128], m["nWdT"], Sbf, False, True, [mb["nWdT"], Sb_b], [pub])
                        V("act", "copy", [pub], [mb["Ub"]], out=m["Ub"], in_=pu[:, 0:128])
                        rel(pub)
                        po, pob = bank()
                        po2, pob2 = bank()
                        MM(po[:, 0:128], m["Kd"], m["Ub"], True, True, [mb["Kd"], mb["Ub"]], [pob])
                        MM(po2[:, 0:128], qT[:, cs], Sbf, True, True, [sT_b[0], Sb_b], [pob2])
                        MM(po2[:, 128:256], m["attT"], m["Ub"], True, True, [mb["attT"], mb["Ub"]], [pob2])
                        V("dve", "scalar_tensor_tensor", [S_b, mb["glc"], pob], [Sb_b], out=Sbf[:], in0=Sst[:],
                          scalar=m["glc"][:, 0:1], in1=po[:, 0:128], op0=ALU.mult, op1=ALU.add)
                        V("dve", "scalar_tensor_tensor", [S_b, mb["glc"], pob], [S_b], out=Sst[:], in0=Sst[:],
                          scalar=m["glc"][:, 0:1], in1=po[:, 0:128], op0=ALU.mult, op1=ALU.add)
                        ACT([pob2, der_b], [mb["o1"]], m["o1"], po2[:, 0:128], AF.Copy, scale=egc[:, c, h:h + 1])
                        V("dve", "tensor_tensor", [pob2, mb["o1"]], [OV_b[c]], out=OV[:, c, h * 128:(h + 1) * 128],
                          in0=po2[:, 128:256], in1=m["o1"], op=ALU.add)
                        rel(pob, pob2)

                    GS = 4
                    groups = [list(range(i, min(i + GS, NT))) for i in range(0, NT, GS)]
                    pending = []
                    for grp in groups:
                        stages = [("GE", grp[0:2]), ("GE", grp[2:4])] + [("L", lev) for lev in range(7)] + [("W", None)]
                        for kind, lev in stages:
                            if kind == "GE":
                                for c in lev:
                                    st_G(c)
                                for c in lev:
                                    st_E(c)
                            elif kind == "L":
                                for c in grp:
                                    st_Lmm(c, lev)
                                for c in grp:
                                    st_Lev(c, lev)
                            else:
                                for c in grp:
                                    st_W(c)
                            if pending:
                                st_R(pending.pop(0))
                        while pending:
                            st_R(pending.pop(0))
                        pending = list(grp)
                    while pending:
                        st_R(pending.pop(0))
                S.barrier()
                if STOP == "A12":
                    raise _Stop()
                areset(0)
                gateW = carve([128, 8, 1024], BF16); outW = carve([128, 8, 1024], BF16); gw_b = S.buf(); ow_b = S.buf()
                sig = carve([128, 1024], F32); sig_b = S.buf()
                gs = carve([128, 1024], F32); gs_b = S.buf()
                osq = carve([128, 1024], F32); osq_b = S.buf()
                oss = carve([128, 8], F32); oss_b = S.buf()
                ogb = carve([128, 1024], BF16); ogb_b = S.buf()
                ogT = carve([128, 8, 128], BF16); ogT_b = S.buf()
                h1t = [carve([128, 1024], F32) for _ in range(2)]; h1t_b = S.bufs(2)
                DMA("sp", gateW, wview(w_in_b, 3072, 1024), [], [gw_b], "gw")
                DMA("sp", outW, wview(w_outa_b, 0, 1024), [], [ow_b], "ow")
                for t in range(NT):
                    sl = t % 2
                    load_x_tile(s, t, sl)
                    pgs = [bank(), bank()]
                    for hf in range(2):
                        pb, pbb = pgs[hf]
                        for kc in range(8):
                            MM(pb[:, :], hT[:, kc, t * 128:(t + 1) * 128], gateW[:, kc, hf * 512:(hf + 1) * 512], kc == 0, kc == 7,
                               [hT_b[t], gw_b], [pbb])
                        silu_from(pb[:, :], [pbb], sig[:, hf * 512:(hf + 1) * 512], sig_b, None)
                        V("dve", "tensor_tensor", [pbb, sig_b], [gs_b], out=gs[:, hf * 512:(hf + 1) * 512], in0=pb[:, :],
                          in1=sig[:, hf * 512:(hf + 1) * 512], op=ALU.mult)
                        rel(pbb)
                    ACT([OV_b[t]], [osq_b], osq, OV[:, t, :], AF.Square)
                    V("dve", "tensor_reduce", [osq_b], [oss_b], out=oss, in_=osq.rearrange("p (a b) -> p a b", a=8),
                      axis=AX.X, op=ALU.add)
                    ACT([oss_b], [oss_b], oss, oss, AF.Ln, scale=1.0 / 128, bias=EPS)
                    ACT([oss_b], [oss_b], oss, oss, AF.Exp, scale=-0.5)
                    V("dve", "tensor_tensor", [OV_b[t], oss_b], [osq_b], out=osq.rearrange("p (a b) -> p a b", a=8),
                      in0=OV[:, t, :].rearrange("p (a b) -> p a b", a=8), in1=oss.unsqueeze(2).to_broadcast([128, 8, 128]),
                      op=ALU.mult)
                    V("dve", "tensor_tensor", [osq_b, gs_b], [ogb_b], out=ogb, in0=osq, in1=gs, op=ALU.mult)
                    pb, pbb = bank()
                    pbv = pb[:].bitcast(BF16)
                    for kc in range(8):
                        TR(pbv[:, kc * 128:(kc + 1) * 128], ogb[:, kc * 128:(kc + 1) * 128], identB[:], [ogb_b, CONST], [pbb])
                    V("dve", "tensor_copy", [pbb], [ogT_b], out=ogT, in_=pbv.rearrange("p (k c) -> p k c", k=8))
                    rel(pbb)
                    for hf in range(2):
                        pb, pbb = bank()
                        for kc in range(8):
                            MM(pb[:, :], ogT[:, kc, :], outW[:, kc, hf * 512:(hf + 1) * 512], kc == 0, kc == 7, [ogT_b, ow_b], [pbb])
                        V("dve", "tensor_tensor", [pbb, xin_b[sl]], [h1t_b[sl]], out=h1t[sl][:, hf * 512:(hf + 1) * 512],
                          in0=pb[:, :], in1=xin[sl][:, hf * 512:(hf + 1) * 512], op=ALU.add)
                        rel(pbb)
                    DMA("act", h1_d[t * 128:(t + 1) * 128, :], h1t[sl], [h1t_b[sl]], [], f"h1s{sl}")
                    norm_to_hT(h1t[sl], h1t_b[sl], t, sl)
                S.barrier()
                if STOP == "A3":
                    raise _Stop()
                areset(0)
                KT = carve([128, 8, T], BF16); KT_b = S.buf()
                ksc = carve([128, NT, 8], F32); ksc_b = S.buf()
                kv_base = apos[0]
                bpool[0] = list(range(7))
                wdn = carve([128, 8, 320], BF16); wuk = carve([128, 2, 1024], BF16); wuv = carve([128, 2, 1024], BF16)
                wkv_b = S.buf()
                cTf = carve([128, 2, 512], F32); cT_b = S.buf()
                kpe = carve([64, 512], F32); kpe_b = S.buf()
                kpb = carve([64, 512], BF16); kpb_b = S.buf()
                csq = carve([128, 2, 512], BF16); csq_b = S.buf()
                rb = carve([128, 512], F32); rb_b = S.buf()
                ckv = carve([128, 2, 512], BF16); ckv_b = S.buf()
                sqK = [carve([128, 512], BF16) for _ in range(2)]; sqK_b = S.bufs(2)
                sqR = carve([64, 512], BF16); sqR_b = S.buf()
                t1 = carve([64, 512], F32); t1_b = S.buf()
                t2 = carve([64, 512], F32); t2_b = S.buf()
                DMA("sp", wdn, wview(w_down_b, 0, 320), [], [wkv_b], "wkv")
                DMA("sp", wuk, wview(w_uk_b, 0, 1024), [], [wkv_b], "wkv")
                DMA("sp", wuv, wview(w_uv_b, 0, 1024), [], [wkv_b], "wkv")
                pks, pksb = fbank(7)
                for (c0, cw) in CB:
                    rd = hT_reads(c0, cw)
                    pbs_ = [bank(), bank(), bank()]
                    for j, (lo, mw) in enumerate(((0, 128), (128, 128), (256, 64))):
                        pb, pbb = pbs_[j]
                        for kc in range(8):
                            MM(pb[0:mw, :cw], wdn[:, kc, lo:lo + mw], hT[:, kc, c0:c0 + cw], kc == 0, kc == 7, [wkv_b] + rd, [pbb])
                    for j in range(2):
                        pb, pbb = pbs_[j]
                        S.op("act", (lambda o, i: (lambda e: e.copy(out=o, in_=i)))(cTf[:, j, :cw], pb[:, :cw]), [pbb], [cT_b])
                        rel(pbb)
                    pb, pbb = pbs_[2]
                    V("dve", "tensor_scalar_mul", [pbb, CONST], [kpe_b], out=kpe[:, :cw], in0=pb[0:64, :cw], scalar1=kgr[:, 0:1])
                    ACT([pbb], [sqR_b], sqR[:, :cw], pb[0:64, :cw], AF.Square)
                    rel(pbb)
                    ACT([cT_b], [csq_b], csq[:, :, :cw], cTf[:, :, :cw], AF.Square)
                    pb, pbb = bank()
                    for j in range(2):
                        MM(pb[:, :cw], onesB[:], csq[:, j, :cw], j == 0, j == 1, [CONST, csq_b], [pbb])
                    ACT([pbb], [rb_b], rb[:, :cw], pb[:, :cw], AF.Ln, scale=1.0 / 256, bias=EPS)
                    rel(pbb)
                    ACT([rb_b], [rb_b], rb[:, :cw], rb[:, :cw], AF.Exp, scale=-0.5)
                    V("dve", "tensor_tensor", [cT_b, rb_b], [ckv_b], out=ckv[:, :, :cw], in0=cTf[:, :, :cw],
                      in1=rb[:, :cw].unsqueeze(1).to_broadcast([128, 2, cw]), op=ALU.mult)
                    V("pool", "tensor_copy", [kpe_b], [kpb_b], out=kpb[:, :cw], in_=kpe[:, :cw])
                    pb, pbb = bank()
                    MM(pb[0:64, :cw], rotB[:], kpb[:, :cw], True, True, [CONST, kpb_b], [pbb])
                    V("dve", "tensor_tensor", [pbb, CONST], [t1_b], out=t1[:, :cw], in0=pb[0:64, :cw], in1=sin2[:, c0:c0 + cw], op=ALU.mult)
                    rel(pbb)
                    V("dve", "tensor_tensor", [kpe_b, CONST], [t2_b], out=t2[:, :cw], in0=kpe[:, :cw], in1=cos2[:, c0:c0 + cw], op=ALU.mult)
                    V("dve", "tensor_tensor", [t1_b, t2_b], [RR_b], out=RR[:, c0:c0 + cw], in0=t1[:, :cw], in1=t2[:, :cw], op=ALU.add)
                    for h in range(H):
                        pb, pbb = bank()
                        for r in range(2):
                            MM(pb[:, :cw], wuk[:, r, h * 128:(h + 1) * 128], ckv[:, r, :cw], r == 0, r == 1, [wkv_b, ckv_b], [pbb])
                        V("dve", "tensor_scalar_mul", [pbb, CONST], [KT_b], out=KT[:, h, c0:c0 + cw], in0=pb[:, :cw], scalar1=kgn[:, 0:1])
                        q = h % 2
                        ACT([pbb], [sqK_b[q]], sqK[q][:, :cw], pb[:, :cw], AF.Square)
                        rel(pbb)
                        for ti in range(cw // 128):
                            t = c0 // 128 + ti
                            MM(pks[:, t * 8 + h:t * 8 + h + 1], sqK[q][:, ti * 128:(ti + 1) * 128], onesB[:, 0:1], True, False,
                               [sqK_b[q], CONST], [pksb])
                            MM(pks[:, t * 8 + h:t * 8 + h + 1], sqR[:, ti * 128:(ti + 1) * 128], onesB[0:64, 0:1], False, True,
                               [sqR_b, CONST], [pksb])
                    for ti in range(cw // 128):
                        t = c0 // 128 + ti
                        for hf in range(2):
                            pb, pbb = bank()
                            for r in range(2):
                                MM(pb[:, :], ckv[:, r, ti * 128:(ti + 1) * 128], wuv[:, r, hf * 512:(hf + 1) * 512], r == 0, r == 1,
                                   [ckv_b, wkv_b], [pbb])
                            S.op("act", (lambda o, i: (lambda e: e.copy(out=o, in_=i)))(OV[:, t, hf * 512:(hf + 1) * 512], pb[:, :]),
                                 [pbb], [OV_b[t]])
                            rel(pbb)
                ACT([pksb], [ksc_b], ksc, pks[:, 0:NT * 8].rearrange("p (t c) -> p t c", c=8), AF.Ln, scale=1.0 / 192, bias=EPS)
                ACT([ksc_b], [ksc_b], ksc, ksc, AF.Exp, scale=-0.5)
                V("dve", "tensor_scalar_mul", [ksc_b], [ksc_b], out=ksc, in0=ksc, scalar1=192.0 ** -0.5)
                S.barrier()
                if STOP == "KV":
                    raise _Stop()
                bpool[0] = list(range(4))
                b_base = kv_base
                for _once in range(1):
                    areset(b_base)
                    woutB = carve([128, 8, 1024], BF16); wo_b = S.buf()
                    wst = [carve([128, 8, 128], BF16) for _ in range(2)]; wst_b = S.bufs(2)
                    wuq = [carve([128, 3, 192], BF16) for _ in range(2)]; wuq_b = S.bufs(2)
                    cqT = carve([128, 3, 512], F32); cq_b = S.buf()
                    cqn = carve([128, 3, 512], BF16); cqn_b = S.buf()
                    cqs, cqs_b = cqn, cqn_b
                    rb = carve([128, 512], F32); rb_b = S.buf()
                    qn = carve([128, 512], F32); qn_b = S.buf()
                    qr = carve([64, 512], F32); qr_b = S.buf()
                    qrb = carve([64, 512], BF16); qrb_b = S.buf()
                    sqn = carve([128, 512], BF16); sqn_b = S.buf()
                    sqr = carve([64, 512], BF16); sqr_b = S.buf()
                    rq = carve([128, 512], F32); rq_b = S.buf()
                    QnT2 = [carve([128, 512], BF16) for _ in range(2)]; QrT2 = [carve([64, 512], BF16) for _ in range(2)]
                    Q2_b = S.bufs(2)
                    t1 = carve([64, 512], F32); t1_b = S.buf()
                    t2 = carve([64, 512], F32); t2_b = S.buf()
                    sig = carve([128, 512], F32); sig_b = S.buf()
                    gs2 = [carve([128, 512], F32) for _ in range(2)]; gs2_b = S.bufs(2)
                    PT = [carve([128, 512], BF16) for _ in range(3)]; PT_b = S.bufs(3)
                    rd_ = carve([128, 512], F32); rd_b = S.buf()
                    ot = carve([128, 512], F32); ot_b = S.buf()
                    OGT = carve([128, 8, 512], BF16); OGT_b = S.buf()
                for q0 in range(1, NT, 4):
                    q1 = min(q0 + 3, NT - 1)
                    bw = (q1 - q0 + 1) * 128
                    bc0 = q0 * 128
                    hrd = hT_reads(bc0, bw)
                    DMA("sp", woutB, wview(w_outb_b, 0, 1024), [], [wo_b], "wo")
                    wc = [0]
                    for fc in range(3):
                        wsl = wc[0] % 2; wc[0] += 1
                        DMA("sp", wst[wsl], wview(w_inb_b, fc * 128, 128), [], [wst_b[wsl]], f"wst{wsl}")
                        pb, pbb = bank()
                        for kc in range(8):
                            MM(pb[:, :bw], wst[wsl][:, kc, :], hT[:, kc, bc0:bc0 + bw], kc == 0, kc == 7, [wst_b[wsl]] + hrd, [pbb])
                        S.op("act", (lambda o, i: (lambda e: e.copy(out=o, in_=i)))(cqT[:, fc, :bw], pb[:, :bw]), [pbb], [cq_b])
                        rel(pbb)
                    ACT([cq_b], [cqs_b], cqs[:, :, :bw], cqT[:, :, :bw], AF.Square)
                    pb, pbb = bank()
                    for j in range(3):
                        MM(pb[:, :bw], onesB[:], cqs[:, j, :bw], j == 0, j == 2, [CONST, cqs_b], [pbb])
                    ACT([pbb], [rb_b], rb[:, :bw], pb[:, :bw], AF.Ln, scale=1.0 / 384, bias=EPS)
                    rel(pbb)
                    ACT([rb_b], [rb_b], rb[:, :bw], rb[:, :bw], AF.Exp, scale=-0.5)
                    V("dve", "tensor_tensor", [cq_b, rb_b], [cqn_b], out=cqn[:, :, :bw], in0=cqT[:, :, :bw],
                      in1=rb[:, :bw].unsqueeze(1).to_broadcast([128, 3, bw]), op=ALU.mult)
                    pctr = [0]

                    def prep(h):
                        us = h % 2
                        QnT, QrT, Q_b, gs, gs_b = QnT2[us], QrT2[us], Q2_b[us], gs2[us], gs2_b[us]
                        DMA("sp", wuq[us], wview(w_uq_b, h * 192, 192), [], [wuq_b[us]], f"wuq{us}")
                        pq, pqb = bank()
                        pr, prb = bank()
                        for j in range(3):
                            MM(pq[:, :bw], wuq[us][:, j, 0:128], cqn[:, j, :bw], j == 0, j == 2, [wuq_b[us], cqn_b], [pqb])
                        for j in range(3):
                            MM(pr[0:64, :bw], wuq[us][:, j, 128:192], cqn[:, j, :bw], j == 0, j == 2, [wuq_b[us], cqn_b], [prb])
                        yield
                        V("dve", "tensor_scalar_mul", [pqb, CONST], [qn_b], out=qn[:, :bw], in0=pq[:, :bw], scalar1=qgn[:, 0:1])
                        V("dve", "tensor_scalar_mul", [prb, CONST], [qr_b], out=qr[:, :bw], in0=pr[0:64, :bw], scalar1=qgr[:, 0:1])
                        ACT([pqb], [sqn_b], sqn[:, :bw], pq[:, :bw], AF.Square)
                        ACT([prb], [sqr_b], sqr[:, :bw], pr[0:64, :bw], AF.Square)
                        rel(pqb, prb)
                        yield
                        pb, pbb = bank()
                        MM(pb[:, :bw], onesB[:], sqn[:, :bw], True, False, [CONST, sqn_b], [pbb])
                        MM(pb[:, :bw], onesB[0:64, :], sqr[:, :bw], False, True, [CONST, sqr_b], [pbb])
                        yield
                        ACT([pbb], [rq_b], rq[:, :bw], pb[:, :bw], AF.Ln, scale=1.0 / 192, bias=EPS)
                        rel(pbb)
                        ACT([rq_b], [rq_b], rq[:, :bw], rq[:, :bw], AF.Exp, scale=-0.5)
                        V("pool", "tensor_copy", [qr_b], [qrb_b], out=qrb[:, :bw], in_=qr[:, :bw])
                        yield
                        V("dve", "tensor_tensor", [qn_b, rq_b], [Q_b], out=QnT[:, :bw], in0=qn[:, :bw], in1=rq[:, :bw], op=ALU.mult)
                        pb, pbb = bank()
                        MM(pb[0:64, :bw], rotB[:], qrb[:, :bw], True, True, [CONST, qrb_b], [pbb])
                        yield
                        V("dve", "tensor_tensor", [pbb, CONST], [t1_b], out=t1[:, :bw], in0=pb[0:64, :bw], in1=sin2[:, bc0:bc0 + bw], op=ALU.mult)
                        rel(pbb)
                        V("dve", "tensor_tensor", [qr_b, CONST], [t2_b], out=t2[:, :bw], in0=qr[:, :bw], in1=cos2[:, bc0:bc0 + bw], op=ALU.mult)
                        yield
                        V("dve", "tensor_tensor", [t1_b, t2_b], [t1_b], out=t1[:, :bw], in0=t1[:, :bw], in1=t2[:, :bw], op=ALU.add)
                        V("dve", "tensor_tensor", [t1_b, rq_b], [Q_b], out=QrT[:, :bw], in0=t1[:, :bw], in1=rq[0:64, :bw], op=ALU.mult)
                        wsl = wc[0] % 2; wc[0] += 1
                        DMA("sp", wst[wsl], wview(w_inb_b, 384 + h * 128, 128), [], [wst_b[wsl]], f"wst{wsl}")
                        pgt, pgtb = bank()
                        for kc in range(8):
                            MM(pgt[:, :bw], wst[wsl][:, kc, :], hT[:, kc, bc0:bc0 + bw], kc == 0, kc == 7, [wst_b[wsl]] + hrd, [pgtb])
                        yield
                        ACT([pgtb], [sig_b], sig[:, :bw], pgt[:, :bw], AF.Exp, scale=-1.0)
                        yield
                        ACT([sig_b], [sig_b], sig[:, :bw], sig[:, :bw], AF.Ln, bias=1.0)
                        ACT([sig_b], [sig_b], sig[:, :bw], sig[:, :bw], AF.Exp, scale=-1.0)
                        yield
                        V("dve", "tensor_tensor", [pgtb, sig_b], [gs_b], out=gs[:, :bw], in0=pgt[:, :bw], in1=sig[:, :bw], op=ALU.mult)
                        rel(pgtb)

                    def attn(h, nxt):
                        us = h % 2
                        QnT, QrT, Q_b, gs, gs_b = QnT2[us], QrT2[us], Q2_b[us], gs2[us], gs2_b[us]
                        pO, pOb = fbank(4 + 2 * (h % 2))
                        pD, pDb = fbank(5 + 2 * (h % 2))
                        def s_mm(kt):
                            o0 = (max(kt, q0) - q0) * 128
                            ks = slice(kt * 128, (kt + 1) * 128)
                            psc, pscb = bank()
                            diag = kt >= q0
                            MM(psc[:, o0:bw], KT[:, h, ks], QnT[:, o0:bw], True, False, [KT_b, Q_b], [pscb])
                            MM(psc[:, o0:bw], RR[:, ks], QrT[:, o0:bw], False, not diag, [RR_b, Q_b], [pscb])
                            if diag:
                                MM(psc[:, o0:o0 + 128], urowB[:], wrowB[:], False, True, [CONST], [pscb])
                            return psc, pscb

                        cur = s_mm(0)
                        for kt in range(0, q1 + 1):
                            o0 = (max(kt, q0) - q0) * 128
                            psc, pscb = cur
                            if kt + 1 <= q1:
                                cur = s_mm(kt + 1)
                            ps_ = pctr[0] % 3; pctr[0] += 1
                            ACT([pscb, ksc_b, CONST], [PT_b[ps_]], PT[ps_][:, o0:bw], psc[:, o0:bw], AF.Exp,
                                scale=ksc[:, kt, h:h + 1], bias=(padb[:, 0:1] if kt == 0 else zcol[:, 0:1]))
                            rel(pscb)
                            MM(pO[:, o0:bw], OV[:, kt, h * 128:(h + 1) * 128], PT[ps_][:, o0:bw], kt == 0, kt == q1, [OV_b[kt], PT_b[ps_]], [pOb])
                            MM(pD[:, o0:bw], onesB[:], PT[ps_][:, o0:bw], kt == 0, kt == q1, [CONST, PT_b[ps_]], [pDb])
                            if nxt is not None:
                                next(nxt, None)
                        if nxt is not None:
                            for _ in nxt:
                                pass
                        V("dve", "reciprocal", [pDb], [rd_b], out=rd_[:, :bw], in_=pD[:, :bw])
                        V("dve", "tensor_tensor", [pOb, rd_b], [ot_b], out=ot[:, :bw], in0=pO[:, :bw], in1=rd_[:, :bw], op=ALU.mult)
                        V("dve", "tensor_tensor", [ot_b, gs_b], [OGT_b], out=OGT[:, h, :bw], in0=ot[:, :bw], in1=gs[:, :bw], op=ALU.mult)

                    for _ in prep(0):
                        pass
                    for h in range(H):
                        attn(h, prep(h + 1) if h + 1 < H else None)
                    for qt in range(q0, q1 + 1):
                        sl = qt % 2
                        DMA("sp", xin[sl][:], h1_d[qt * 128:(qt + 1) * 128, :], [], [xin_b[sl]], f"xl{sl}")
                        lc = (qt - q0) * 128
                        for hf in range(2):
                            pb, pbb = bank()
                            for kc in range(8):
                                MM(pb[:, :], OGT[:, kc, lc:lc + 128], woutB[:, kc, hf * 512:(hf + 1) * 512], kc == 0, kc == 7,
                                   [OGT_b, wo_b], [pbb])
                            V("dve", "tensor_tensor", [pbb, xin_b[sl]], [xin_b[sl]], out=xin[sl][:, hf * 512:(hf + 1) * 512],
                              in0=pb[:, :], in1=xin[sl][:, hf * 512:(hf + 1) * 512], op=ALU.add)
                            rel(pbb)
                        r0 = s * SEQ + (qt - 1) * 128
                        DMA("act", out_d[r0:r0 + 128, :], xin[sl][:], [xin_b[sl]], [], f"os{sl}")
                S.barrier()
        except _Stop:
            S.barrier()
        S.emit(nc, st)
    nc._sched_info = S.info
    nc._sched = S
    return nc


_NC_CACHE = {}


def kernel(**inputs):
    x = np.ascontiguousarray(inputs["x"], dtype=np.float32)
    B, SEQ, _ = x.shape
    NSEQ = B // N_CORES
    key = (NSEQ, SEQ)
    if key not in _NC_CACHE:
        _NC_CACHE[key] = build(NSEQ, SEQ)
    nc = _NC_CACHE[key]
    consts = host_consts(SEQ + 128)
    shared = {}
    for k, v in inputs.items():
        if k == "x":
            continue
        a = np.ascontiguousarray(np.asarray(v, dtype=np.float32))
        if a.ndim >= 2 and a.shape[0] == 1:
            a = a[0]
        shared[k] = np.ascontiguousarray(a)
    shared.update(consts)
    in_maps = []
    for c in range(N_CORES):
        m = dict(shared)
        m["x"] = np.ascontiguousarray(x[c * NSEQ:(c + 1) * NSEQ].reshape(NSEQ * SEQ, D))
        in_maps.append(m)
    res = run_bass_kernel_spmd(nc, in_maps, core_ids=list(range(N_CORES)))
    out = np.concatenate([np.asarray(r["out"]).reshape(NSEQ, SEQ, D) for r in res.results], axis=0)
    return out.astype(np.float32)
```

```python
import numpy as np
from contextlib import ExitStack
import concourse.bass as bass
import concourse.mybir as mybir
from concourse.bass_utils import run_bass_kernel_spmd

F32 = mybir.dt.float32
BF16 = mybir.dt.bfloat16
AF = mybir.ActivationFunctionType
ALU = mybir.AluOpType
AX = mybir.AxisListType
ENGS = ("sp", "act", "pool", "dve", "pe")

D = 1024
H = 8
BIG = 30000.0
EPS = 1e-6
N_CORES = 8
CHAIN_DT = F32


class Buf:
    __slots__ = ("name", "w", "r", "excl")

    def __init__(self, name):
        self.name = name
        self.w = None
        self.r = {}
        self.excl = False


class Op:
    __slots__ = ("fn", "waits", "tok", "dma")

    def __init__(self, fn, waits, tok, dma):
        self.fn = fn
        self.waits = waits
        self.tok = tok
        self.dma = dma


class Sched:
    def __init__(self):
        self.ops = {e: [] for e in ENGS}
        self.count = {}
        self.known = {e: {} for e in ENGS}
        self.needed = set()
        self.nb = 0
        self.marks = []
        self.info = {}
        self.dbg = []
        self.names = {}

    def buf(self, name=None):
        self.nb += 1
        return Buf(name or f"b{self.nb}")

    def bufs(self, n, name="b"):
        return [self.buf(f"{name}{i}") for i in range(n)]

    def op(self, eng, fn, reads=(), writes=(), dma_dom=None):
        deps = {}

        def need(tok):
            if tok is not None and deps.get(tok[0], 0) < tok[1]:
                deps[tok[0]] = tok[1]

        own = dma_dom if dma_dom is not None else eng
        for b in reads:
            need(b.w)
            if b.excl:
                for d, i in b.r.items():
                    if d != own:
                        need((d, i))
        for b in writes:
            need(b.w)
            for d, i in b.r.items():
                need((d, i))
        is_dma = dma_dom is not None
        dom = dma_dom if is_dma else eng
        waits = {}
        kn = self.known[eng]
        for d, i in deps.items():
            if d == "pe" and eng == "pe" and not is_dma:
                continue
            if kn.get(d, 0) >= i:
                continue
            waits[d] = i
            kn[d] = i
            self.needed.add((d, i))
        c = self.count.get(dom, 0) + 1
        self.count[dom] = c
        tok = (dom, c)
        for b in reads:
            if b.r.get(dom, 0) < c:
                b.r[dom] = c
        for b in writes:
            b.w = tok
            b.r = {}
        o_ = Op(fn, waits, tok, is_dma)
        self.ops[eng].append(o_)
        self.dbg.append((o_, [b.name for b in reads], [b.name for b in writes]))
        return tok

    def barrier(self, label=None):
        self.marks.append((label, dict(self.count)))
        for e in ENGS:
            waits = {}
            kn = self.known[e]
            for d, c in self.count.items():
                if kn.get(d, 0) < c:
                    waits[d] = c
                    kn[d] = c
                    self.needed.add((d, c))
            if waits:
                self.ops[e].append(Op(None, waits, None, False))

    def emit(self, nc, stack):
        doms = sorted(self.count.keys())
        dma_doms = set()
        for e in ENGS:
            for o in self.ops[e]:
                if o.dma:
                    dma_doms.add(o.tok[0])
        sems = {d: stack.enter_context(nc.semaphore(f"s_{d}")) for d in doms}
        rank = {}
        for d in doms:
            if d in dma_doms:
                rank[d] = None
            else:
                idxs = sorted(i for (dd, i) in self.needed if dd == d)
                rank[d] = {i: k + 1 for k, i in enumerate(idxs)}
        block = stack.enter_context(nc.Block())
        ops = self.ops
        self.info = dict(sems={d: sems[d].num for d in doms},
                         marks=[(lab, {d: (rank[d][c] if rank[d] is not None else 16 * c) for d, c in cnt.items()
                                       if d in ("pe", "act", "dve", "pool")}) for lab, cnt in self.marks])

        def run(engh, lst):
            for o in lst:
                for d, i in o.waits.items():
                    engh.wait_ge(sems[d], 16 * i if rank[d] is None else rank[d][i])
                if o.fn is None:
                    continue
                ins = o.fn(engh)
                try:
                    self.names[ins.ins.name] = o
                except Exception:
                    pass
                d, i = o.tok
                if o.dma:
                    ins.then_inc(sems[d], 16)
                elif i in rank[d]:
                    ins.then_inc(sems[d], 1)

        @block.sync
        def _(e):
            run(e, ops["sp"])

        @block.scalar
        def _(e):
            run(e, ops["act"])

        @block.gpsimd
        def _(e):
            run(e, ops["pool"])

        @block.vector
        def _(e):
            run(e, ops["dve"])

        @block.tensor
        def _(e):
            run(e, ops["pe"])


def host_consts(T):
    p = np.arange(128)
    ident = np.eye(128, dtype=np.float32)
    tri = (p[:, None] <= p[None, :]).astype(np.float32)
    mui = np.where(p[None, :] < p[:, None], -BIG, 0.0).astype(np.float32)
    mls = np.where(p[None, :] >= p[:, None], BIG, 0.0).astype(np.float32)
    mus = np.where(p[None, :] <= p[:, None], -BIG, 0.0).astype(np.float32)
    rot = np.zeros((64, 64), np.float32)
    for m in range(32):
        rot[m + 32, m] = -1.0
        rot[m, m + 32] = 1.0
    pos = np.maximum(np.arange(T) - 112, 0).astype(np.float32)
    inv = (np.float32(10000.0) ** (-np.arange(32, dtype=np.float32) / np.float32(32))).astype(np.float32)
    ang = (pos[:, None] * inv[None, :]).astype(np.float32)
    cos = np.cos(ang).astype(np.float32).T
    sin = np.sin(ang).astype(np.float32).T
    cos2 = np.concatenate([cos, cos], 0).astype(np.float32)
    sin2 = np.concatenate([sin, sin], 0).astype(np.float32)
    padb = np.where(p < 112, -BIG, 0.0).astype(np.float32)[:, None]
    urow = np.where(p >= 64, 1.0, 0.0).astype(np.float32)[None, :]
    wrow = np.where(p < 64, -BIG, 0.0).astype(np.float32)[None, :]
    return dict(c_ident=ident, c_tri=tri, c_mui=mui, c_mls=mls, c_mus=mus, c_rot=rot, c_cos=cos2, c_sin=sin2,
                c_padb=padb, c_urow=urow, c_wrow=wrow)


class _Stop(Exception):
    pass


def build(NSEQ, SEQ, STOP=None):
    T = SEQ + 128
    NT = T // 128
    CB = [(c0, min(512, T - c0)) for c0 in range(0, T, 512)]
    nc = bass.Bass("TRN2", target_bir_lowering=False)

    def din(name, shape, dt=F32):
        return nc.dram_tensor(name, list(shape), dt, kind="ExternalInput").ap()

    x_d = din("x", [NSEQ * SEQ, D])
    meta_d = din("meta_tokens", [16, D])
    a_norm_d = din("a_norm", [D]); a_w_in_d = din("a_w_in", [D, 4112]); a_conv_d = din("a_conv", [4, 3072])
    a_log_d = din("a_log", [8]); a_dtb_d = din("a_dt_bias", [8]); a_og_d = din("a_o_gain", [128])
    a_w_out_d = din("a_w_out", [D, D]); kv_norm_d = din("kv_norm", [D]); kv_wd_d = din("kv_w_down", [D, 320])
    kv_ln_d = din("kv_latent_norm", [256]); kv_uk_d = din("kv_w_uk", [256, D]); kv_uv_d = din("kv_w_uv", [256, D])
    k_gain_d = din("k_gain", [192]); b_norm_d = din("b_norm", [D]); b_w_in_d = din("b_w_in", [D, 1408])
    b_qln_d = din("b_q_latent_norm", [384]); b_uq_d = din("b_w_uq", [384, 1536]); b_qg_d = din("b_q_gain", [192])
    b_w_out_d = din("b_w_out", [D, D])
    c_ident_d = din("c_ident", [128, 128]); c_tri_d = din("c_tri", [128, 128]); c_mui_d = din("c_mui", [128, 128])
    c_mls_d = din("c_mls", [128, 128]); c_mus_d = din("c_mus", [128, 128]); c_rot_d = din("c_rot", [64, 64]); c_cos_d = din("c_cos", [64, T])
    c_sin_d = din("c_sin", [64, T]); c_padb_d = din("c_padb", [128, 1]); c_urow_d = din("c_urow", [1, 128])
    c_wrow_d = din("c_wrow", [1, 128])
    out_d = nc.dram_tensor("out", [NSEQ * SEQ, D], F32, kind="ExternalOutput").ap()

    def dscr(name, shape, dt):
        return nc.dram_tensor(name, list(shape), dt).ap()

    w_in_b = dscr("w_in_b", [D, 4112], BF16); w_outa_b = dscr("w_outa_b", [D, D], BF16)
    w_down_b = dscr("w_down_b", [D, 320], BF16); w_uk_b = dscr("w_uk_b", [256, D], BF16)
    w_uv_b = dscr("w_uv_b", [256, D], BF16); w_inb_b = dscr("w_inb_b", [D, 1408], BF16)
    w_uq_b = dscr("w_uq_b", [384, 1536], BF16); w_outb_b = dscr("w_outb_b", [D, D], BF16)
    h1_d = dscr("h1_d", [T, D], F32)

    S = Sched()
    with ExitStack() as st:
        st.enter_context(nc.allow_non_contiguous_dma("small strided parameter loads"))

        def sb(name, shape, dt):
            return st.enter_context(nc.sbuf_tensor(name, list(shape), dt))

        def V(eng, fname, reads, writes, *a, **k):
            S.op(eng, lambda e: getattr(e, fname)(*a, **k), reads, writes)

        def ACT(reads, writes, out, in_, func, **k):
            S.op("act", lambda e: e.activation(out=out, in_=in_, func=func, **k), reads, writes)

        def MM(out, lhsT, rhs, start, stop, reads, writes):
            S.op("pe", lambda e: e.matmul(out, lhsT=lhsT, rhs=rhs, start=start, stop=stop), reads, writes)

        def TR(out, in_, ident, reads, writes):
            S.op("pe", lambda e: e.transpose(out=out, in_=in_, identity=ident), reads, writes)

        def DMA(q, out, in_, reads, writes, dom):
            S.op(q, lambda e: e.dma_start(out=out, in_=in_), reads, writes, dma_dom=dom)

        PB = [st.enter_context(nc.psum_tensor(f"pb{i}", [128, 512], F32)) for i in range(8)]
        PBb = S.bufs(8, "pb")
        for _b in PBb:
            _b.excl = True
        bctr = [0]
        bpool = [list(range(8))]

        live = set()

        def bank():
            pool_ = bpool[0]
            for k in range(len(pool_)):
                i = pool_[(bctr[0] + k) % len(pool_)]
                if i not in live:
                    bctr[0] += k + 1
                    live.add(i)
                    return PB[i], PBb[i]
            raise RuntimeError("no free PSUM bank")

        def rel(*bbs):
            for bb in bbs:
                live.discard(PBb.index(bb))

        def fbank(i):
            return PB[i], PBb[i]

        try:
            identF = sb("identF", [128, 128], F32); identB = sb("identB", [128, 128], BF16)
            triF = sb("triF", [128, 128], F32); muiF = sb("muiF", [128, 128], F32); mlsF = sb("mlsF", [128, 128], F32)
            musF = sb("musF", [128, 128], F32)
            onesB = sb("onesB", [128, 128], BF16); rotF = sb("rotF", [64, 64], F32); rotB = sb("rotB", [64, 64], BF16)
            cos2 = sb("cos2", [64, T], BF16); sin2 = sb("sin2", [64, T], BF16)
            padb = sb("padb", [128, 1], F32); zcol = sb("zcol", [128, 1], F32)
            urowF = sb("urowF", [1, 128], F32); wrowF = sb("wrowF", [1, 128], F32)
            urowB = sb("urowB", [1, 128], BF16); wrowB = sb("wrowB", [1, 128], BF16)
            g_anorm = sb("g_anorm", [128, 8], F32); g_kvnorm = sb("g_kvnorm", [128, 8], F32)
            g_bnorm = sb("g_bnorm", [128, 8], F32); g_kvln = sb("g_kvln", [128, 2], F32)
            g_qln = sb("g_qln", [128, 3], F32); g_og = sb("g_og", [128, 1], F32)
            kgn = sb("kgn", [128, 1], F32); kgr = sb("kgr", [64, 1], F32)
            qgn = sb("qgn", [128, 1], F32); qgr = sb("qgr", [64, 1], F32)
            convw = sb("convw", [128, 4, 24], F32)
            alog = sb("alog", [128, 8], F32); negA = sb("negA", [128, 8], F32); dtb = sb("dtb", [128, 8], F32)
            CONST = S.buf("const")

            def cload(dst, src):
                DMA("sp", dst, src, [], [CONST], "cst")

            cload(identF[:], c_ident_d); cload(triF[:], c_tri_d); cload(muiF[:], c_mui_d); cload(mlsF[:], c_mls_d); cload(musF[:], c_mus_d)
            cload(rotF[:], c_rot_d); cload(padb[:], c_padb_d)
            cload(urowF[:], c_urow_d); cload(wrowF[:], c_wrow_d)
            cload(g_anorm[:], a_norm_d.rearrange("(k p) -> p k", p=128))
            cload(g_kvnorm[:], kv_norm_d.rearrange("(k p) -> p k", p=128))
            cload(g_bnorm[:], b_norm_d.rearrange("(k p) -> p k", p=128))
            cload(g_kvln[:], kv_ln_d.rearrange("(k p) -> p k", p=128))
            cload(g_qln[:], b_qln_d.rearrange("(k p) -> p k", p=128))
            cload(g_og[:], a_og_d.rearrange("(p o) -> p o", o=1))
            cload(kgn[:], k_gain_d[0:128].rearrange("(p o) -> p o", o=1))
            cload(kgr[:], k_gain_d[128:192].rearrange("(p o) -> p o", o=1))
            cload(qgn[:], b_qg_d[0:128].rearrange("(p o) -> p o", o=1))
            cload(qgr[:], b_qg_d[128:192].rearrange("(p o) -> p o", o=1))
            for jj in range(4):
                cload(convw[:, jj, :], a_conv_d[jj, :].rearrange("(c p) -> p c", p=128))
            cload(alog[:], a_log_d.partition_broadcast(128))
            cload(dtb[:], a_dtb_d.partition_broadcast(128))
            CONST.w = ("cst", S.count["cst"])
            V("dve", "tensor_copy", [CONST], [CONST], out=identB[:], in_=identF[:])
            V("dve", "tensor_copy", [CONST], [CONST], out=rotB[:], in_=rotF[:])
            V("dve", "tensor_copy", [CONST], [CONST], out=urowB[:], in_=urowF[:])
            V("dve", "tensor_copy", [CONST], [CONST], out=wrowB[:], in_=wrowF[:])
            V("dve", "memset", [], [CONST], onesB[:], 1.0)
            V("dve", "memset", [], [CONST], zcol[:], 0.0)
            ACT([CONST], [CONST], negA[:], alog[:], AF.Exp)
            V("dve", "tensor_scalar_mul", [CONST], [CONST], out=negA[:], in0=negA[:], scalar1=-1.0)

            hT = sb("hT", [128, 8, T], BF16); hT_b = S.bufs(NT, "hT")
            OV = sb("OV", [128, NT, 1024], BF16); OV_b = S.bufs(NT, "OV")
            RR = sb("RR", [64, T], BF16); RR_b = S.buf("RR")
            xin = [sb(f"xin{i}", [128, 1024], F32) for i in range(2)]; xin_b = S.bufs(2, "xin")
            xn = [sb(f"xn{i}", [128, 1024], BF16) for i in range(2)]; xn_b = S.bufs(2, "xn")
            junk = sb("junk", [128, 1024], BF16); junk_b = S.buf("junk")
            ssq = [sb(f"ssq{i}", [128, 1], F32) for i in range(2)]; ssq_b = S.bufs(2, "ssq")
            ARENA_F32 = 27136
            arena = sb("arena", [128, ARENA_F32], F32)
            apos = [0]

            def areset(off=0):
                apos[0] = off

            def carve(shape, dt):
                P = shape[0]
                n = int(np.prod(shape[1:]))
                nbytes = n * (4 if dt == F32 else 2)
                n32 = (nbytes + 3) // 4
                o = apos[0]
                apos[0] += (n32 + 7) // 8 * 8
                assert apos[0] <= ARENA_F32, ("arena overflow", apos[0])
                v = arena[0:P, o:o + n32]
                if dt != F32:
                    v = v.bitcast(dt)[:, 0:n]
                if len(shape) == 3:
                    v = v.rearrange("p (a b) -> p a b", a=shape[1], b=shape[2])
                return v

            areset(0)
            wl = [carve([128, 1024], F32) for i in range(2)]; wl_b = S.bufs(2, "wl")
            ws = [carve([128, 1024], BF16) for i in range(2)]; ws_b = S.bufs(2, "ws")
            for tab_d, tab in ((c_cos_d, cos2), (c_sin_d, sin2)):
                for c0 in range(0, T, 1024):
                    cw = min(1024, T - c0)
                    sl = wctr_ = 0
                    DMA("sp", wl[0][0:64, :cw], tab_d[:, c0:c0 + cw], [], [wl_b[0]], "wl0")
                    V("dve", "tensor_copy", [wl_b[0]], [CONST], out=tab[:, c0:c0 + cw], in_=wl[0][0:64, :cw])
            wctr = [0]

            def prep_weight(src, K, N, dst, gcol):
                for kc in range(K // 128):
                    for c0 in range(0, N, 1024):
                        cw = min(1024, N - c0)
                        sl = wctr[0] % 2
                        wctr[0] += 1
                        DMA("sp", wl[sl][:, :cw], src[kc * 128:(kc + 1) * 128, c0:c0 + cw], [], [wl_b[sl]], f"wl{sl}")
                        if gcol is not None:
                            V("dve", "tensor_scalar_mul", [wl_b[sl], CONST], [ws_b[sl]], out=ws[sl][:, :cw],
                              in0=wl[sl][:, :cw], scalar1=gcol(kc))
                        else:
                            V("dve", "tensor_copy", [wl_b[sl]], [ws_b[sl]], out=ws[sl][:, :cw], in_=wl[sl][:, :cw])
                        DMA("act", dst[kc * 128:(kc + 1) * 128, c0:c0 + cw], ws[sl][:, :cw], [ws_b[sl]], [], f"ws{sl}")

            prep_weight(a_w_in_d, D, 4112, w_in_b, lambda kc: g_anorm[:, kc:kc + 1])
            prep_weight(a_w_out_d, D, D, w_outa_b, lambda kc: g_og[:, 0:1])
            prep_weight(kv_wd_d, D, 320, w_down_b, lambda kc: g_kvnorm[:, kc:kc + 1])
            prep_weight(kv_uk_d, 256, D, w_uk_b, lambda kc: g_kvln[:, kc:kc + 1])
            prep_weight(kv_uv_d, 256, D, w_uv_b, lambda kc: g_kvln[:, kc:kc + 1])
            prep_weight(b_w_in_d, D, 1408, w_inb_b, lambda kc: g_bnorm[:, kc:kc + 1])
            prep_weight(b_uq_d, 384, 1536, w_uq_b, lambda kc: g_qln[:, kc:kc + 1])
            prep_weight(b_w_out_d, D, D, w_outb_b, None)
            S.barrier()
            if STOP == "W":
                raise _Stop()

            def wview(wb, c0, cw):
                return wb[:, c0:c0 + cw].rearrange("(k p) n -> p k n", p=128)

            def load_x_tile(s, t, sl):
                if t == 0:
                    V("pool", "memset", [], [xin_b[sl]], xin[sl][:], 0.0)
                    DMA("sp", xin[sl][112:128, :], meta_d, [], [xin_b[sl]], f"xl{sl}")
                else:
                    r0 = s * SEQ + (t - 1) * 128
                    DMA("sp", xin[sl][:], x_d[r0:r0 + 128, :], [], [xin_b[sl]], f"xl{sl}")

            def norm_to_hT(src, src_b, t, sl):
                ACT([src_b], [junk_b, ssq_b[sl]], junk[:], src, AF.Square, accum_out=ssq[sl][:])
                ACT([ssq_b[sl]], [ssq_b[sl]], ssq[sl][:], ssq[sl][:], AF.Ln, scale=1.0 / D, bias=EPS)
                ACT([ssq_b[sl]], [ssq_b[sl]], ssq[sl][:], ssq[sl][:], AF.Exp, scale=-0.5)
                V("dve", "tensor_scalar_mul", [src_b, ssq_b[sl]], [xn_b[sl]], out=xn[sl][:], in0=src, scalar1=ssq[sl][:, 0:1])
                pb, pbb = bank()
                pbv = pb[:].bitcast(BF16)
                for kc in range(8):
                    TR(pbv[:, kc * 128:(kc + 1) * 128], xn[sl][:, kc * 128:(kc + 1) * 128], identB[:], [xn_b[sl], CONST], [pbb])
                V("dve", "tensor_copy", [pbb], [hT_b[t]], out=hT[:, :, t * 128:(t + 1) * 128],
                  in_=pbv.rearrange("p (k c) -> p k c", k=8))
                rel(pbb)

            def hT_reads(c0, cw):
                return [hT_b[t] for t in range(c0 // 128, (c0 + cw + 127) // 128)]

            def silu_from(src_ap, src_reads, tmp, tmp_b, shape_sl):
                ACT(src_reads, [tmp_b], tmp, src_ap, AF.Exp, scale=-1.0)
                ACT([tmp_b], [tmp_b], tmp, tmp, AF.Ln, bias=1.0)
                ACT([tmp_b], [tmp_b], tmp, tmp, AF.Exp, scale=-1.0)

            def rsqrt_inplace(ap, b, scale, reads_extra=()):
                ACT([b] + list(reads_extra), [b], ap, ap, AF.Ln, scale=scale, bias=EPS)
                ACT([b], [b], ap, ap, AF.Exp, scale=-0.5)

            for s in range(NSEQ):
                bpool[0] = list(range(8))
                for t in range(NT):
                    sl = t % 2
                    load_x_tile(s, t, sl)
                    norm_to_hT(xin[sl][:], xin_b[sl], t, sl)
                S.barrier()
                if STOP == "A0":
                    raise _Stop()
                areset(0)
                wab = carve([128, 8, 16], BF16); wab_b = S.buf()
                ab = carve([128, NT, 16], F32); ab_b = S.buf()
                tm1 = carve([128, NT, 8], F32); tm1_b = S.buf()
                gcol = carve([128, NT, 8], F32); g_b = S.buf()
                lnb = carve([128, NT, 8], F32); lnb_b = S.buf()
                gc = carve([128, NT, 8], F32); gc_b = S.buf()
                ngc = carve([128, NT, 8], F32); egc = carve([128, NT, 8], F32); bls = carve([128, NT, 8], F32)
                beta = carve([128, NT, 8], F32); begc = carve([128, NT, 8], F32)
                der_b = S.buf()
                DMA("sp", wab, wview(w_in_b, 4096, 16), [], [wab_b], "wab")
                pb, pbb = bank()
                for t in range(NT):
                    for kc in range(8):
                        MM(pb[:, t * 16:(t + 1) * 16], hT[:, kc, t * 128:(t + 1) * 128], wab[:, kc, :], kc == 0, kc == 7,
                           [hT_b[t], wab_b], [pbb])
                V("dve", "tensor_copy", [pbb], [ab_b], out=ab, in_=pb[:, 0:NT * 16].rearrange("p (t c) -> p t c", c=16))
                rel(pbb)
                ACT([ab_b], [tm1_b], tm1, ab[:, :, 0:8], AF.Exp, scale=-1.0)
                ACT([tm1_b], [tm1_b], tm1, tm1, AF.Ln, bias=1.0)
                V("dve", "tensor_scalar_mul", [tm1_b], [lnb_b], out=lnb, in0=tm1, scalar1=-1.0)
                V("dve", "tensor_tensor", [ab_b, CONST], [tm1_b], out=tm1, in0=ab[:, :, 8:16],
                  in1=dtb[:].unsqueeze(1).to_broadcast([128, NT, 8]), op=ALU.add)
                ACT([tm1_b], [tm1_b], tm1, tm1, AF.Exp)
                ACT([tm1_b], [tm1_b], tm1, tm1, AF.Ln, bias=1.0)
                V("dve", "tensor_tensor", [tm1_b, CONST], [g_b], out=gcol, in0=tm1,
                  in1=negA[:].unsqueeze(1).to_broadcast([128, NT, 8]), op=ALU.mult)
                pb, pbb = bank()
                for t in range(NT):
                    MM(pb[:, t * 8:(t + 1) * 8], triF[:], gcol[:, t, :], True, True, [CONST, g_b], [pbb])
                V("dve", "tensor_copy", [pbb], [gc_b], out=gc, in_=pb[:, 0:NT * 8].rearrange("p (t c) -> p t c", c=8))
                rel(pbb)
                V("dve", "tensor_scalar_mul", [gc_b], [der_b], out=ngc, in0=gc, scalar1=-1.0)
                ACT([gc_b], [der_b], egc, gc, AF.Exp)
                V("dve", "tensor_tensor", [gc_b, lnb_b], [der_b], out=bls, in0=gc, in1=lnb, op=ALU.add)
                ACT([lnb_b], [der_b], beta, lnb, AF.Exp)
                ACT([der_b], [der_b], begc, bls, AF.Exp)

                if STOP == "Aab":
                    raise _Stop()
                a12_base = apos[0]
                for h in range(1):
                    areset(a12_base)
                    wst = [carve([128, 8, 128], BF16) for _ in range(2)]; wst_b = S.bufs(2)
                    zc = [carve([128, 515], F32) for _ in range(2)]; zc_b = S.bufs(2)
                    taL = [carve([128, 512], F32) for _ in range(2)]; taL_b = S.bufs(2)
                    tbL = [carve([128, 512], F32) for _ in range(2)]; tbL_b = S.bufs(2)
                    sqL = [carve([128, 512], BF16) for _ in range(2)]; sqL_b = S.bufs(2)
                    blkc = [0]
                    sT = [carve([128, T], BF16) for _ in range(3)]; sT_b = S.bufs(3)
                    Ktok = carve([128, NT, 128], BF16); Vtok = carve([128, NT, 128], BF16); KV_b = S.bufs(2)
                    Sst = carve([128, 128], F32); Sbf = carve([128, 128], BF16); S_b = S.buf(); Sb_b = S.buf()
                    NSL = 8
                    cm = []
                    for i in range(NSL):
                        cm.append(dict(
                            Eui=carve([128, 128], F32), Els=carve([128, 128], F32), Eus=carve([128, 128], F32),
                            Z=[carve([128, 256], CHAIN_DT) for _ in range(2)], P=[carve([128, 128], CHAIN_DT) for _ in range(2)],
                            attT=carve([128, 128], BF16), TmT=carve([128, 128], BF16), nWdT=carve([128, 128], BF16),
                            Bk=carve([128, 128], BF16), bV=carve([128, 128], BF16), Kd=carve([128, 128], BF16),
                            Ub=carve([128, 128], BF16), glc=carve([128, 1], F32), o1=carve([128, 128], F32),
                            b={k: S.buf() for k in ("Eui", "Els", "Eus", "Z0", "Z1", "P0", "P1", "attT", "TmT", "nWdT", "Bk", "bV",
                                                    "Kd", "Ub", "glc", "o1")}))
                for h in range(H):
                    for j in range(3):
                        fc = j * 8 + h
                        wsl = j % 2
                        DMA("sp", wst[wsl], wview(w_in_b, fc * 128, 128), [], [wst_b[wsl]], f"wst{wsl}")
                        for bi, (c0, cw) in enumerate(CB):
                            zs = bi % 2
                            bk_ = blkc[0] % 2; blkc[0] += 1
                            ta, ta_b, tb, tb_b, sq, sq_b = taL[bk_], taL_b[bk_], tbL[bk_], tbL_b[bk_], sqL[bk_], sqL_b[bk_]
                            pb, pbb = bank()
                            for kc in range(8):
                                MM(pb[:, :cw], wst[wsl][:, kc, :], hT[:, kc, c0:c0 + cw], kc == 0, kc == 7,
                                   [wst_b[wsl]] + hT_reads(c0, cw), [pbb])
                            if bi == 0:
                                V("pool", "memset", [], [zc_b[zs]], zc[zs][:, 0:3], 0.0)
                            else:
                                V("pool", "tensor_copy", [zc_b[1 - zs]], [zc_b[zs]], out=zc[zs][:, 0:3], in_=zc[1 - zs][:, 512:515])
                            S.op("act", (lambda o, i: (lambda e: e.copy(out=o, in_=i)))(zc[zs][:, 3:3 + cw], pb[:, :cw]),
                                 [pbb], [zc_b[zs]])
                            rel(pbb)
                            V("dve", "tensor_scalar_mul", [zc_b[zs], CONST], [ta_b], out=ta[:, :cw], in0=zc[zs][:, 3:3 + cw],
                              scalar1=convw[:, 3, fc:fc + 1])
                            for jj in (2, 1, 0):
                                V("dve", "scalar_tensor_tensor", [zc_b[zs], CONST, ta_b], [ta_b], out=ta[:, :cw],
                                  in0=zc[zs][:, jj:jj + cw], scalar=convw[:, jj, fc:fc + 1], in1=ta[:, :cw],
                                  op0=ALU.mult, op1=ALU.add)
                            silu_from(ta[:, :cw], [ta_b], tb[:, :cw], tb_b, None)
                            V("dve", "tensor_tensor", [ta_b, tb_b], [sT_b[j]], out=sT[j][:, c0:c0 + cw], in0=ta[:, :cw],
                              in1=tb[:, :cw], op=ALU.mult)
                            if j < 2:
                                ACT([sT_b[j]], [sq_b], sq[:, :cw], sT[j][:, c0:c0 + cw], AF.Square)
                                pb2, pbb2 = bank()
                                MM(pb2[:, :cw], onesB[:], sq[:, :cw], True, True, [CONST, sq_b], [pbb2])
                                ACT([pbb2], [tb_b], tb[:, :cw], pb2[:, :cw], AF.Ln, bias=EPS)
                                rel(pbb2)
                                ACT([tb_b], [tb_b], tb[:, :cw], tb[:, :cw], AF.Exp, scale=-0.5)
                                V("dve", "scalar_tensor_tensor", [sT_b[j], tb_b], [sT_b[j]], out=sT[j][:, c0:c0 + cw],
                                  in0=sT[j][:, c0:c0 + cw], scalar=(128.0 ** -0.5 if j == 0 else 1.0), in1=tb[:, :cw],
                                  op0=ALU.mult, op1=ALU.mult)
                    qT, kT, vT = sT
                    if STOP == "A12a":
                        raise _Stop()
                    for j, dst in ((1, Ktok), (2, Vtok)):
                        for t0 in range(0, NT, 8):
                            tn = min(8, NT - t0)
                            pb, pbb = bank()
                            pbv = pb[:].bitcast(BF16)
                            for i in range(tn):
                                TR(pbv[:, i * 128:(i + 1) * 128], sT[j][:, (t0 + i) * 128:(t0 + i + 1) * 128], identB[:],
                                   [sT_b[j], CONST], [pbb])
                            V("dve", "tensor_copy", [pbb], [KV_b[j - 1]], out=dst[:, t0:t0 + tn, :],
                              in_=pbv[:, 0:tn * 128].rearrange("p (t c) -> p t c", c=128))
                            rel(pbb)
                    V("pool", "memset", [], [S_b], Sst[:], 0.0)
                    V("pool", "memset", [], [Sb_b], Sbf[:], 0.0)
                    if STOP == "A12b":
                        raise _Stop()
                    def st_G(c):
                        m = cm[c % NSL]; mb = m["b"]
                        cs = slice(c * 128, (c + 1) * 128)
                        gb_l = gcol[:, c, h:h + 1].to_broadcast([128, 128])
                        lb_l = lnb[:, c, h:h + 1].to_broadcast([128, 128])
                        pg, pgb = bank()
                        pt, ptb = bank()
                        m["pg"], m["pgb"], m["pt"], m["ptb"] = pg, pgb, pt, ptb
                        MM(pg[:, 0:128], gb_l, triF[:], True, False, [g_b, CONST], [pgb])
                        MM(pg[:, 0:128], identF[:], muiF[:], False, True, [CONST], [pgb])
                        MM(pg[:, 128:256], gb_l, triF[:], True, False, [g_b, CONST], [pgb])
                        MM(pg[:, 128:256], identF[:], mlsF[:], False, True, [CONST], [pgb])
                        MM(pg[:, 256:384], kT[:, cs], kT[:, cs], True, True, [sT_b[1]], [pgb])
                        MM(pg[:, 384:512], kT[:, cs], qT[:, cs], True, True, [sT_b[1], sT_b[0]], [pgb])
                        MM(pt[:, 0:128], gb_l, triF[:], True, False, [g_b, CONST], [ptb])
                        MM(pt[:, 0:128], lb_l, identF[:], False, False, [lnb_b, CONST], [ptb])
                        MM(pt[:, 0:128], identF[:], musF[:], False, True, [CONST], [ptb])

                    def st_E(c):
                        m = cm[c % NSL]; mb = m["b"]
                        pg, pgb, pt, ptb = m["pg"], m["pgb"], m["pt"], m["ptb"]
                        ACT([pgb, der_b], [mb["Eui"]], m["Eui"], pg[:, 0:128], AF.Exp, bias=ngc[:, c, h:h + 1], scale=1.0)
                        ACT([pgb, der_b], [mb["Els"]], m["Els"], pg[:, 128:256], AF.Exp, bias=bls[:, c, h:h + 1], scale=-1.0)
                        ACT([pgb], [mb["glc"]], m["glc"], pg[:, 127:128], AF.Exp)
                        ACT([ptb, der_b], [mb["Eus"]], m["Eus"], pt[:, 0:128], AF.Exp, bias=ngc[:, c, h:h + 1], scale=1.0)
                        V("dve", "scalar_tensor_tensor", [pgb, mb["Els"]], [mb["Z0"]], out=m["Z"][0][:, 0:128],
                          in0=pg[:, 256:384], scalar=-1.0, in1=m["Els"], op0=ALU.mult, op1=ALU.mult)
                        V("dve", "scalar_tensor_tensor", [pgb, mb["Eus"]], [mb["Z0"]], out=m["Z"][0][:, 128:256],
                          in0=pg[:, 256:384], scalar=-1.0, in1=m["Eus"], op0=ALU.mult, op1=ALU.mult)
                        V("dve", "tensor_tensor", [pgb, mb["Eui"]], [mb["attT"]], out=m["attT"], in0=pg[:, 384:512],
                          in1=m["Eui"], op=ALU.mult)
                        V("dve", "tensor_tensor", [mb["Z0"], CONST], [mb["P0"]], out=m["P"][0], in0=m["Z"][0][:, 128:256],
                          in1=identF[:], op=ALU.add)
                        V("dve", "tensor_scalar_mul", [KV_b[0], der_b], [mb["Bk"]], out=m["Bk"], in0=Ktok[:, c, :],
                          scalar1=begc[:, c, h:h + 1])
                        V("dve", "tensor_scalar_mul", [KV_b[1], der_b], [mb["bV"]], out=m["bV"], in0=Vtok[:, c, :],
                          scalar1=beta[:, c, h:h + 1])
                        V("dve", "tensor_scalar_mul", [KV_b[0], mb["Eui"]], [mb["Kd"]], out=m["Kd"], in0=Ktok[:, c, :],
                          scalar1=m["Eui"][:, 127:128])
                        rel(pgb, ptb)

                    def st_Lmm(c, lev):
                        m = cm[c % NSL]; mb = m["b"]
                        zi = lev % 2
                        pi = (lev - 1) % 2
                        Zc, Zcb = m["Z"][zi], mb[f"Z{zi}"]
                        pk, pkb = bank()
                        m["pk"], m["pkb"] = pk, pkb
                        if lev >= 1:
                            MM(pk[:, 256:384], Zc[:, 0:128], m["P"][pi], True, True, [Zcb, mb[f"P{pi}"]], [pkb])
                        if lev < 6:
                            MM(pk[:, 0:128], Zc[:, 128:256], Zc[:, 0:128], True, True, [Zcb], [pkb])
                            MM(pk[:, 128:256], Zc[:, 0:128], Zc[:, 128:256], True, True, [Zcb], [pkb])

                    def st_Lev(c, lev):
                        m = cm[c % NSL]; mb = m["b"]
                        zi = lev % 2
                        pi = (lev - 1) % 2
                        Zn, Znb = m["Z"][1 - zi], mb[f"Z{1 - zi}"]
                        pk, pkb = m["pk"], m["pkb"]
                        if lev >= 1:
                            if lev < 6:
                                V("dve", "tensor_tensor", [pkb, mb[f"P{pi}"]], [mb[f"P{1 - pi}"]], out=m["P"][1 - pi],
                                  in0=pk[:, 256:384], in1=m["P"][pi], op=ALU.add)
                            else:
                                V("dve", "tensor_tensor", [pkb, mb[f"P{pi}"]], [mb["TmT"]], out=m["TmT"],
                                  in0=pk[:, 256:384], in1=m["P"][pi], op=ALU.add)
                        if lev < 6:
                            V("act", "copy", [pkb], [Znb], out=Zn[:, 0:256], in_=pk[:, 0:256])
                        rel(pkb)

                    def st_W(c):
                        m = cm[c % NSL]; mb = m["b"]
                        pw, pwb = bank()
                        MM(pw[:, 0:128], m["Bk"], m["TmT"], True, True, [mb["Bk"], mb["TmT"]], [pwb])
                        V("act", "mul", [pwb], [mb["nWdT"]], out=m["nWdT"], in_=pw[:, 0:128], mul=-1.0)
                        rel(pwb)

                    def st_R(c):
                        m = cm[c % NSL]; mb = m["b"]
                        cs = slice(c * 128, (c + 1) * 128)
                        pu, pub = bank()
                        MM(pu[:, 0:128], m["TmT"], m["bV"], True, c == 0, [mb["TmT"], mb["bV"]], [pub])
                        if c > 0:
                            MM(pu[:, 0:128], m["nWdT"], Sbf, False, True, [mb["nWdT"], Sb_b], [pub])
                        V("act", "copy", [pub], [mb["Ub"]], out=m["Ub"], in_=pu[:, 0:128])
                        rel(pub)
                        po, pob = bank()
                        po2, pob2 = bank()
                        MM(po[:, 0:128], m["Kd"], m["Ub"], True, True, [mb["Kd"], mb["Ub"]], [pob])
                        MM(po2[:, 0:128], qT[:, cs], Sbf, True, True, [sT_b[0], Sb_b], [pob2])
                        MM(po2[:, 128:256], m["attT"], m["Ub"], True, True, [mb["attT"], mb["Ub"]], [pob2])
                        V("dve", "scalar_tensor_tensor", [S_b, mb["glc"], pob], [Sb_b], out=Sbf[:], in0=Sst[:],
                          scalar=m["glc"][:, 0:1], in1=po[:, 0:128], op0=ALU.mult, op1=ALU.add)
                        V("dve", "scalar_tensor_tensor", [S_b, mb["glc"], pob], [S_b], out=Sst[:], in0=Sst[:],
                          scalar=m["glc"][:, 0:1], in1=po[:, 0:128], op0=ALU.mult, op1=ALU.add)
                        ACT([pob2, der_b], [mb["o1"]], m["o1"], po2[:, 0:128], AF.Copy, scale=egc[:, c, h:h + 1])
                        V("dve", "tensor_tensor", [pob2, mb["o1"]], [OV_b[c]], out=OV[:, c, h * 128:(h + 1) * 128],
                          in0=po2[:, 128:256], in1=m["o1"], op=ALU.add)
                        rel(pob, pob2)

                    GS = 4
                    groups = [list(range(i, min(i + GS, NT))) for i in range(0, NT, GS)]
                    pending = []
                    for grp in groups:
                        stages = [("GE", grp[0:2]), ("GE", grp[2:4])] + [("L", lev) for lev in range(7)] + [("W", None)]
                        for kind, lev in stages:
                            if kind == "GE":
                                for c in lev:
                                    st_G(c)
                                for c in lev:
                                    st_E(c)
                            elif kind == "L":
                                for c in grp:
                                    st_Lmm(c, lev)
                                for c in grp:
                                    st_Lev(c, lev)
                            else:
                                for c in grp:
                                    st_W(c)
                            if pending:
                                st_R(pending.pop(0))
                        while pending:
                            st_R(pending.pop(0))
                        pending = list(grp)
                    while pending:
                        st_R(pending.pop(0))
                S.barrier()
                if STOP == "A12":
                    raise _Stop()
                areset(0)
                gateW = carve([128, 8, 1024], BF16); outW = carve([128, 8, 1024], BF16); gw_b = S.buf(); ow_b = S.buf()
                sigL = [carve([128, 1024], F32) for _ in range(2)]; sigL_b = S.bufs(2)
                gsL = [carve([128, 1024], F32) for _ in range(2)]; gsL_b = S.bufs(2)
                osqL = [carve([128, 1024], F32) for _ in range(2)]; osqL_b = S.bufs(2)
                ossL = [carve([128, 8], F32) for _ in range(2)]; ossL_b = S.bufs(2)
                ogbL = [carve([128, 1024], BF16) for _ in range(2)]; ogbL_b = S.bufs(2)
                ogTL = [carve([128, 8, 128], BF16) for _ in range(2)]; ogTL_b = S.bufs(2)
                h1t = [carve([128, 1024], F32) for _ in range(2)]; h1t_b = S.bufs(2)
                DMA("sp", gateW, wview(w_in_b, 3072, 1024), [], [gw_b], "gw")
                DMA("sp", outW, wview(w_outa_b, 0, 1024), [], [ow_b], "ow")
                for t in range(NT):
                    sl = t % 2
                    sig, sig_b, gs, gs_b, osq, osq_b = sigL[sl], sigL_b[sl], gsL[sl], gsL_b[sl], osqL[sl], osqL_b[sl]
                    oss, oss_b, ogb, ogb_b, ogT, ogT_b = ossL[sl], ossL_b[sl], ogbL[sl], ogbL_b[sl], ogTL[sl], ogTL_b[sl]
                    load_x_tile(s, t, sl)
                    pgs = [bank(), bank()]
                    for hf in range(2):
                        pb, pbb = pgs[hf]
                        for kc in range(8):
                            MM(pb[:, :], hT[:, kc, t * 128:(t + 1) * 128], gateW[:, kc, hf * 512:(hf + 1) * 512], kc == 0, kc == 7,
                               [hT_b[t], gw_b], [pbb])
                        silu_from(pb[:, :], [pbb], sig[:, hf * 512:(hf + 1) * 512], sig_b, None)
                        V("dve", "tensor_tensor", [pbb, sig_b], [gs_b], out=gs[:, hf * 512:(hf + 1) * 512], in0=pb[:, :],
                          in1=sig[:, hf * 512:(hf + 1) * 512], op=ALU.mult)
                        rel(pbb)
                    ACT([OV_b[t]], [osq_b], osq, OV[:, t, :], AF.Square)
                    V("dve", "tensor_reduce", [osq_b], [oss_b], out=oss, in_=osq.rearrange("p (a b) -> p a b", a=8),
                      axis=AX.X, op=ALU.add)
                    ACT([oss_b], [oss_b], oss, oss, AF.Ln, scale=1.0 / 128, bias=EPS)
                    ACT([oss_b], [oss_b], oss, oss, AF.Exp, scale=-0.5)
                    V("dve", "tensor_tensor", [OV_b[t], oss_b], [osq_b], out=osq.rearrange("p (a b) -> p a b", a=8),
                      in0=OV[:, t, :].rearrange("p (a b) -> p a b", a=8), in1=oss.unsqueeze(2).to_broadcast([128, 8, 128]),
                      op=ALU.mult)
                    V("dve", "tensor_tensor", [osq_b, gs_b], [ogb_b], out=ogb, in0=osq, in1=gs, op=ALU.mult)
                    pb, pbb = bank()
                    pbv = pb[:].bitcast(BF16)
                    for kc in range(8):
                        TR(pbv[:, kc * 128:(kc + 1) * 128], ogb[:, kc * 128:(kc + 1) * 128], identB[:], [ogb_b, CONST], [pbb])
                    V("dve", "tensor_copy", [pbb], [ogT_b], out=ogT, in_=pbv.rearrange("p (k c) -> p k c", k=8))
                    rel(pbb)
                    for hf in range(2):
                        pb, pbb = bank()
                        for kc in range(8):
                            MM(pb[:, :], ogT[:, kc, :], outW[:, kc, hf * 512:(hf + 1) * 512], kc == 0, kc == 7, [ogT_b, ow_b], [pbb])
                        V("dve", "tensor_tensor", [pbb, xin_b[sl]], [h1t_b[sl]], out=h1t[sl][:, hf * 512:(hf + 1) * 512],
                          in0=pb[:, :], in1=xin[sl][:, hf * 512:(hf + 1) * 512], op=ALU.add)
                        rel(pbb)
                    DMA("act", h1_d[t * 128:(t + 1) * 128, :], h1t[sl], [h1t_b[sl]], [], f"h1s{sl}")
                    norm_to_hT(h1t[sl], h1t_b[sl], t, sl)
                S.barrier()
                if STOP == "A3":
                    raise _Stop()
                areset(0)
                KT = carve([128, 8, T], BF16); KT_b = S.buf()
                ksc = carve([128, NT, 8], F32); ksc_b = S.buf()
                kv_base = apos[0]
                bpool[0] = list(range(7))
                wdn = carve([128, 8, 320], BF16); wuk = carve([128, 2, 1024], BF16); wuv = carve([128, 2, 1024], BF16)
                wkv_b = S.buf()
                cTf = carve([128, 2, 512], F32); cT_b = S.buf()
                kpe = carve([64, 512], F32); kpe_b = S.buf()
                kpb = carve([64, 512], BF16); kpb_b = S.buf()
                csq = carve([128, 2, 512], BF16); csq_b = S.buf()
                rb = carve([128, 512], F32); rb_b = S.buf()
                ckv = carve([128, 2, 512], BF16); ckv_b = S.buf()
                sqK = [carve([128, 512], BF16) for _ in range(2)]; sqK_b = S.bufs(2)
                sqR = carve([64, 512], BF16); sqR_b = S.buf()
                t1 = carve([64, 512], F32); t1_b = S.buf()
                t2 = carve([64, 512], F32); t2_b = S.buf()
                DMA("sp", wdn, wview(w_down_b, 0, 320), [], [wkv_b], "wkv")
                DMA("sp", wuk, wview(w_uk_b, 0, 1024), [], [wkv_b], "wkv")
                DMA("sp", wuv, wview(w_uv_b, 0, 1024), [], [wkv_b], "wkv")
                pks, pksb = fbank(7)
                for (c0, cw) in CB:
                    rd = hT_reads(c0, cw)
                    pbs_ = [bank(), bank(), bank()]
                    for j, (lo, mw) in enumerate(((0, 128), (128, 128), (256, 64))):
                        pb, pbb = pbs_[j]
                        for kc in range(8):
                            MM(pb[0:mw, :cw], wdn[:, kc, lo:lo + mw], hT[:, kc, c0:c0 + cw], kc == 0, kc == 7, [wkv_b] + rd, [pbb])
                    for j in range(2):
                        pb, pbb = pbs_[j]
                        S.op("act", (lambda o, i: (lambda e: e.copy(out=o, in_=i)))(cTf[:, j, :cw], pb[:, :cw]), [pbb], [cT_b])
                        rel(pbb)
                    pb, pbb = pbs_[2]
                    V("dve", "tensor_scalar_mul", [pbb, CONST], [kpe_b], out=kpe[:, :cw], in0=pb[0:64, :cw], scalar1=kgr[:, 0:1])
                    ACT([pbb], [sqR_b], sqR[:, :cw], pb[0:64, :cw], AF.Square)
                    rel(pbb)
                    ACT([cT_b], [csq_b], csq[:, :, :cw], cTf[:, :, :cw], AF.Square)
                    pb, pbb = bank()
                    for j in range(2):
                        MM(pb[:, :cw], onesB[:], csq[:, j, :cw], j == 0, j == 1, [CONST, csq_b], [pbb])
                    ACT([pbb], [rb_b], rb[:, :cw], pb[:, :cw], AF.Ln, scale=1.0 / 256, bias=EPS)
                    rel(pbb)
                    ACT([rb_b], [rb_b], rb[:, :cw], rb[:, :cw], AF.Exp, scale=-0.5)
                    V("dve", "tensor_tensor", [cT_b, rb_b], [ckv_b], out=ckv[:, :, :cw], in0=cTf[:, :, :cw],
                      in1=rb[:, :cw].unsqueeze(1).to_broadcast([128, 2, cw]), op=ALU.mult)
                    V("pool", "tensor_copy", [kpe_b], [kpb_b], out=kpb[:, :cw], in_=kpe[:, :cw])
                    pb, pbb = bank()
                    MM(pb[0:64, :cw], rotB[:], kpb[:, :cw], True, True, [CONST, kpb_b], [pbb])
                    V("dve", "tensor_tensor", [pbb, CONST], [t1_b], out=t1[:, :cw], in0=pb[0:64, :cw], in1=sin2[:, c0:c0 + cw], op=ALU.mult)
                    rel(pbb)
                    V("dve", "tensor_tensor", [kpe_b, CONST], [t2_b], out=t2[:, :cw], in0=kpe[:, :cw], in1=cos2[:, c0:c0 + cw], op=ALU.mult)
                    V("dve", "tensor_tensor", [t1_b, t2_b], [RR_b], out=RR[:, c0:c0 + cw], in0=t1[:, :cw], in1=t2[:, :cw], op=ALU.add)
                    for h in range(H):
                        pb, pbb = bank()
                        for r in range(2):
                            MM(pb[:, :cw], wuk[:, r, h * 128:(h + 1) * 128], ckv[:, r, :cw], r == 0, r == 1, [wkv_b, ckv_b], [pbb])
                        V("dve", "tensor_scalar_mul", [pbb, CONST], [KT_b], out=KT[:, h, c0:c0 + cw], in0=pb[:, :cw], scalar1=kgn[:, 0:1])
                        q = h % 2
                        ACT([pbb], [sqK_b[q]], sqK[q][:, :cw], pb[:, :cw], AF.Square)
                        rel(pbb)
                        for ti in range(cw // 128):
                            t = c0 // 128 + ti
                            MM(pks[:, t * 8 + h:t * 8 + h + 1], sqK[q][:, ti * 128:(ti + 1) * 128], onesB[:, 0:1], True, False,
                               [sqK_b[q], CONST], [pksb])
                            MM(pks[:, t * 8 + h:t * 8 + h + 1], sqR[:, ti * 128:(ti + 1) * 128], onesB[0:64, 0:1], False, True,
                               [sqR_b, CONST], [pksb])
                    for ti in range(cw // 128):
                        t = c0 // 128 + ti
                        for hf in range(2):
                            pb, pbb = bank()
                            for r in range(2):
                                MM(pb[:, :], ckv[:, r, ti * 128:(ti + 1) * 128], wuv[:, r, hf * 512:(hf + 1) * 512], r == 0, r == 1,
                                   [ckv_b, wkv_b], [pbb])
                            S.op("act", (lambda o, i: (lambda e: e.copy(out=o, in_=i)))(OV[:, t, hf * 512:(hf + 1) * 512], pb[:, :]),
                                 [pbb], [OV_b[t]])
                            rel(pbb)
                ACT([pksb], [ksc_b], ksc, pks[:, 0:NT * 8].rearrange("p (t c) -> p t c", c=8), AF.Ln, scale=1.0 / 192, bias=EPS)
                ACT([ksc_b], [ksc_b], ksc, ksc, AF.Exp, scale=-0.5)
                V("dve", "tensor_scalar_mul", [ksc_b], [ksc_b], out=ksc, in0=ksc, scalar1=192.0 ** -0.5)
                S.barrier()
                if STOP == "KV":
                    raise _Stop()
                bpool[0] = list(range(4))
                b_base = kv_base
                for _once in range(1):
                    areset(b_base)
                    woutB = carve([128, 8, 1024], BF16); wo_b = S.buf()
                    wst = [carve([128, 8, 128], BF16) for _ in range(2)]; wst_b = S.bufs(2)
                    wuq = [carve([128, 3, 192], BF16) for _ in range(2)]; wuq_b = S.bufs(2)
                    cqT = carve([128, 3, 512], F32); cq_b = S.buf()
                    cqn = carve([128, 3, 512], BF16); cqn_b = S.buf()
                    cqs, cqs_b = cqn, cqn_b
                    rb = carve([128, 512], F32); rb_b = S.buf()
                    qn = carve([128, 512], F32); qn_b = S.buf()
                    qr = carve([64, 512], F32); qr_b = S.buf()
                    qrb = carve([64, 512], BF16); qrb_b = S.buf()
                    sqn = carve([128, 512], BF16); sqn_b = S.buf()
                    sqr = carve([64, 512], BF16); sqr_b = S.buf()
                    rq = carve([128, 512], F32); rq_b = S.buf()
                    QnT2 = [carve([128, 512], BF16) for _ in range(2)]; QrT2 = [carve([64, 512], BF16) for _ in range(2)]
                    Q2_b = S.bufs(2)
                    t1 = carve([64, 512], F32); t1_b = S.buf()
                    t2 = carve([64, 512], F32); t2_b = S.buf()
                    sig = carve([128, 512], F32); sig_b = S.buf()
                    gs2 = [carve([128, 512], F32) for _ in range(2)]; gs2_b = S.bufs(2)
                    PT = [carve([128, 512], BF16) for _ in range(3)]; PT_b = S.bufs(3)
                    rd_ = carve([128, 512], F32); rd_b = S.buf()
                    ot = carve([128, 512], F32); ot_b = S.buf()
                    OGT = carve([128, 8, 512], BF16); OGT_b = S.buf()
                for q0 in range(1, NT, 4):
                    q1 = min(q0 + 3, NT - 1)
                    bw = (q1 - q0 + 1) * 128
                    bc0 = q0 * 128
                    hrd = hT_reads(bc0, bw)
                    DMA("sp", woutB, wview(w_outb_b, 0, 1024), [], [wo_b], "wo")
                    wc = [0]
                    for fc in range(3):
                        wsl = wc[0] % 2; wc[0] += 1
                        DMA("sp", wst[wsl], wview(w_inb_b, fc * 128, 128), [], [wst_b[wsl]], f"wst{wsl}")
                        pb, pbb = bank()
                        for kc in range(8):
                            MM(pb[:, :bw], wst[wsl][:, kc, :], hT[:, kc, bc0:bc0 + bw], kc == 0, kc == 7, [wst_b[wsl]] + hrd, [pbb])
                        S.op("act", (lambda o, i: (lambda e: e.copy(out=o, in_=i)))(cqT[:, fc, :bw], pb[:, :bw]), [pbb], [cq_b])
                        rel(pbb)
                    ACT([cq_b], [cqs_b], cqs[:, :, :bw], cqT[:, :, :bw], AF.Square)
                    pb, pbb = bank()
                    for j in range(3):
                        MM(pb[:, :bw], onesB[:], cqs[:, j, :bw], j == 0, j == 2, [CONST, cqs_b], [pbb])
                    ACT([pbb], [rb_b], rb[:, :bw], pb[:, :bw], AF.Ln, scale=1.0 / 384, bias=EPS)
                    rel(pbb)
                    ACT([rb_b], [rb_b], rb[:, :bw], rb[:, :bw], AF.Exp, scale=-0.5)
                    V("dve", "tensor_tensor", [cq_b, rb_b], [cqn_b], out=cqn[:, :, :bw], in0=cqT[:, :, :bw],
                      in1=rb[:, :bw].unsqueeze(1).to_broadcast([128, 3, bw]), op=ALU.mult)
                    pctr = [0]

                    def prep(h):
                        us = h % 2
                        QnT, QrT, Q_b, gs, gs_b = QnT2[us], QrT2[us], Q2_b[us], gs2[us], gs2_b[us]
                        DMA("sp", wuq[us], wview(w_uq_b, h * 192, 192), [], [wuq_b[us]], f"wuq{us}")
                        pq, pqb = bank()
                        pr, prb = bank()
                        for j in range(3):
                            MM(pq[:, :bw], wuq[us][:, j, 0:128], cqn[:, j, :bw], j == 0, j == 2, [wuq_b[us], cqn_b], [pqb])
                        for j in range(3):
                            MM(pr[0:64, :bw], wuq[us][:, j, 128:192], cqn[:, j, :bw], j == 0, j == 2, [wuq_b[us], cqn_b], [prb])
                        yield
                        V("dve", "tensor_scalar_mul", [pqb, CONST], [qn_b], out=qn[:, :bw], in0=pq[:, :bw], scalar1=qgn[:, 0:1])
                        V("dve", "tensor_scalar_mul", [prb, CONST], [qr_b], out=qr[:, :bw], in0=pr[0:64, :bw], scalar1=qgr[:, 0:1])
                        ACT([pqb], [sqn_b], sqn[:, :bw], pq[:, :bw], AF.Square)
                        ACT([prb], [sqr_b], sqr[:, :bw], pr[0:64, :bw], AF.Square)
                        rel(pqb, prb)
                        yield
                        pb, pbb = bank()
                        MM(pb[:, :bw], onesB[:], sqn[:, :bw], True, False, [CONST, sqn_b], [pbb])
                        MM(pb[:, :bw], onesB[0:64, :], sqr[:, :bw], False, True, [CONST, sqr_b], [pbb])
                        yield
                        ACT([pbb], [rq_b], rq[:, :bw], pb[:, :bw], AF.Ln, scale=1.0 / 192, bias=EPS)
                        rel(pbb)
                        ACT([rq_b], [rq_b], rq[:, :bw], rq[:, :bw], AF.Exp, scale=-0.5)
                        V("pool", "tensor_copy", [qr_b], [qrb_b], out=qrb[:, :bw], in_=qr[:, :bw])
                        yield
                        V("dve", "tensor_tensor", [qn_b, rq_b], [Q_b], out=QnT[:, :bw], in0=qn[:, :bw], in1=rq[:, :bw], op=ALU.mult)
                        pb, pbb = bank()
                        MM(pb[0:64, :bw], rotB[:], qrb[:, :bw], True, True, [CONST, qrb_b], [pbb])
                        yield
                        V("dve", "tensor_tensor", [pbb, CONST], [t1_b], out=t1[:, :bw], in0=pb[0:64, :bw], in1=sin2[:, bc0:bc0 + bw], op=ALU.mult)
                        rel(pbb)
                        V("dve", "tensor_tensor", [qr_b, CONST], [t2_b], out=t2[:, :bw], in0=qr[:, :bw], in1=cos2[:, bc0:bc0 + bw], op=ALU.mult)
                        yield
                        V("dve", "tensor_tensor", [t1_b, t2_b], [t1_b], out=t1[:, :bw], in0=t1[:, :bw], in1=t2[:, :bw], op=ALU.add)
                        V("dve", "tensor_tensor", [t1_b, rq_b], [Q_b], out=QrT[:, :bw], in0=t1[:, :bw], in1=rq[0:64, :bw], op=ALU.mult)
                        wsl = wc[0] % 2; wc[0] += 1
                        DMA("sp", wst[wsl], wview(w_inb_b, 384 + h * 128, 128), [], [wst_b[wsl]], f"wst{wsl}")
                        pgt, pgtb = bank()
                        for kc in range(8):
                            MM(pgt[:, :bw], wst[wsl][:, kc, :], hT[:, kc, bc0:bc0 + bw], kc == 0, kc == 7, [wst_b[wsl]] + hrd, [pgtb])
                        yield
                        ACT([pgtb], [sig_b], sig[:, :bw], pgt[:, :bw], AF.Exp, scale=-1.0)
                        yield
                        ACT([sig_b], [sig_b], sig[:, :bw], sig[:, :bw], AF.Ln, bias=1.0)
                        ACT([sig_b], [sig_b], sig[:, :bw], sig[:, :bw], AF.Exp, scale=-1.0)
                        yield
                        V("dve", "tensor_tensor", [pgtb, sig_b], [gs_b], out=gs[:, :bw], in0=pgt[:, :bw], in1=sig[:, :bw], op=ALU.mult)
                        rel(pgtb)

                    def attn(h, nxt):
                        us = h % 2
                        QnT, QrT, Q_b, gs, gs_b = QnT2[us], QrT2[us], Q2_b[us], gs2[us], gs2_b[us]
                        pO, pOb = fbank(4 + 2 * (h % 2))
                        pD, pDb = fbank(5 + 2 * (h % 2))
                        def s_mm(kt):
                            o0 = (max(kt, q0) - q0) * 128
                            ks = slice(kt * 128, (kt + 1) * 128)
                            psc, pscb = bank()
                            diag = kt >= q0
                            MM(psc[:, o0:bw], KT[:, h, ks], QnT[:, o0:bw], True, False, [KT_b, Q_b], [pscb])
                            MM(psc[:, o0:bw], RR[:, ks], QrT[:, o0:bw], False, not diag, [RR_b, Q_b], [pscb])
                            if diag:
                                MM(psc[:, o0:o0 + 128], urowB[:], wrowB[:], False, True, [CONST], [pscb])
                            return psc, pscb

                        cur = s_mm(0)
                        for kt in range(0, q1 + 1):
                            o0 = (max(kt, q0) - q0) * 128
                            psc, pscb = cur
                            if kt + 1 <= q1:
                                cur = s_mm(kt + 1)
                            ps_ = pctr[0] % 3; pctr[0] += 1
                            ACT([pscb, ksc_b, CONST], [PT_b[ps_]], PT[ps_][:, o0:bw], psc[:, o0:bw], AF.Exp,
                                scale=ksc[:, kt, h:h + 1], bias=(padb[:, 0:1] if kt == 0 else zcol[:, 0:1]))
                            rel(pscb)
                            MM(pO[:, o0:bw], OV[:, kt, h * 128:(h + 1) * 128], PT[ps_][:, o0:bw], kt == 0, kt == q1, [OV_b[kt], PT_b[ps_]], [pOb])
                            MM(pD[:, o0:bw], onesB[:], PT[ps_][:, o0:bw], kt == 0, kt == q1, [CONST, PT_b[ps_]], [pDb])
                            if nxt is not None:
                                next(nxt, None)
                        if nxt is not None:
                            for _ in nxt:
                                pass
                        V("dve", "reciprocal", [pDb], [rd_b], out=rd_[:, :bw], in_=pD[:, :bw])
                        V("dve", "tensor_tensor", [pOb, rd_b], [ot_b], out=ot[:, :bw], in0=pO[:, :bw], in1=rd_[:, :bw], op=ALU.mult)
                        V("dve", "tensor_tensor", [ot_b, gs_b], [OGT_b], out=OGT[:, h, :bw], in0=ot[:, :bw], in1=gs[:, :bw], op=ALU.mult)

                    for _ in prep(0):
                        pass
                    for h in range(H):
                        attn(h, prep(h + 1) if h + 1 < H else None)
                    for qt in range(q0, q1 + 1):
                        sl = qt % 2
                        DMA("sp", xin[sl][:], h1_d[qt * 128:(qt + 1) * 128, :], [], [xin_b[sl]], f"xl{sl}")
                        lc = (qt - q0) * 128
                        for hf in range(2):
                            pb, pbb = bank()
                            for kc in range(8):
                                MM(pb[:, :], OGT[:, kc, lc:lc + 128], woutB[:, kc, hf * 512:(hf + 1) * 512], kc == 0, kc == 7,
                                   [OGT_b, wo_b], [pbb])
                            V("dve", "tensor_tensor", [pbb, xin_b[sl]], [xin_b[sl]], out=xin[sl][:, hf * 512:(hf + 1) * 512],
                              in0=pb[:, :], in1=xin[sl][:, hf * 512:(hf + 1) * 512], op=ALU.add)
                            rel(pbb)
                        r0 = s * SEQ + (qt - 1) * 128
                        DMA("act", out_d[r0:r0 + 128, :], xin[sl][:], [xin_b[sl]], [], f"os{sl}")
                S.barrier()
        except _Stop:
            S.barrier()
        S.emit(nc, st)
    nc._sched_info = S.info
    nc._sched = S
    return nc


_NC_CACHE = {}


def kernel(**inputs):
    x = np.ascontiguousarray(inputs["x"], dtype=np.float32)
    B, SEQ, _ = x.shape
    NSEQ = B // N_CORES
    key = (NSEQ, SEQ)
    if key not in _NC_CACHE:
        _NC_CACHE[key] = build(NSEQ, SEQ)
    nc = _NC_CACHE[key]
    consts = host_consts(SEQ + 128)
    shared = {}
    for k, v in inputs.items():
        if k == "x":
            continue
        a = np.ascontiguousarray(np.asarray(v, dtype=np.float32))
        if a.ndim >= 2 and a.shape[0] == 1:
            a = a[0]
        shared[k] = np.ascontiguousarray(a)
    shared.update(consts)
    in_maps = []
    for c in range(N_CORES):
        m = dict(shared)
        m["x"] = np.ascontiguousarray(x[c * NSEQ:(c + 1) * NSEQ].reshape(NSEQ * SEQ, D))
        in_maps.append(m)
    res = run_bass_kernel_spmd(nc, in_maps, core_ids=list(range(N_CORES)))
    out = np.concatenate([np.asarray(r["out"]).reshape(NSEQ, SEQ, D) for r in res.results], axis=0)
    return out.astype(np.float32)
```

```python
import numpy as np
from contextlib import ExitStack
import concourse.bass as bass
import concourse.mybir as mybir
from concourse.bass_utils import run_bass_kernel_spmd

F32 = mybir.dt.float32
BF16 = mybir.dt.bfloat16
AF = mybir.ActivationFunctionType
ALU = mybir.AluOpType
AX = mybir.AxisListType
ENGS = ("sp", "act", "pool", "dve", "pe")

D = 1024
H = 8
BIG = 30000.0
EPS = 1e-6
N_CORES = 8
CHAIN_DT = F32


class Buf:
    __slots__ = ("name", "w", "r", "excl")

    def __init__(self, name):
        self.name = name
        self.w = None
        self.r = {}
        self.excl = False


class Op:
    __slots__ = ("fn", "waits", "tok", "dma")

    def __init__(self, fn, waits, tok, dma):
        self.fn = fn
        self.waits = waits
        self.tok = tok
        self.dma = dma


class Sched:
    def __init__(self):
        self.ops = {e: [] for e in ENGS}
        self.count = {}
        self.known = {e: {} for e in ENGS}
        self.needed = set()
        self.nb = 0
        self.marks = []
        self.info = {}
        self.dbg = []
        self.names = {}

    def buf(self, name=None):
        self.nb += 1
        return Buf(name or f"b{self.nb}")

    def bufs(self, n, name="b"):
        return [self.buf(f"{name}{i}") for i in range(n)]

    def op(self, eng, fn, reads=(), writes=(), dma_dom=None):
        deps = {}

        def need(tok):
            if tok is not None and deps.get(tok[0], 0) < tok[1]:
                deps[tok[0]] = tok[1]

        own = dma_dom if dma_dom is not None else eng
        for b in reads:
            need(b.w)
            if b.excl:
                for d, i in b.r.items():
                    if d != own:
                        need((d, i))
        for b in writes:
            need(b.w)
            for d, i in b.r.items():
                need((d, i))
        is_dma = dma_dom is not None
        dom = dma_dom if is_dma else eng
        waits = {}
        kn = self.known[eng]
        for d, i in deps.items():
            if d == "pe" and eng == "pe" and not is_dma:
                continue
            if kn.get(d, 0) >= i:
                continue
            waits[d] = i
            kn[d] = i
            self.needed.add((d, i))
        c = self.count.get(dom, 0) + 1
        self.count[dom] = c
        tok = (dom, c)
        for b in reads:
            if b.r.get(dom, 0) < c:
                b.r[dom] = c
        for b in writes:
            b.w = tok
            b.r = {}
        o_ = Op(fn, waits, tok, is_dma)
        self.ops[eng].append(o_)
        self.dbg.append((o_, [b.name for b in reads], [b.name for b in writes]))
        return tok

    def barrier(self, label=None):
        self.marks.append((label, dict(self.count)))
        for e in ENGS:
            waits = {}
            kn = self.known[e]
            for d, c in self.count.items():
                if kn.get(d, 0) < c:
                    waits[d] = c
                    kn[d] = c
                    self.needed.add((d, c))
            if waits:
                self.ops[e].append(Op(None, waits, None, False))

    def emit(self, nc, stack):
        doms = sorted(self.count.keys())
        dma_doms = set()
        for e in ENGS:
            for o in self.ops[e]:
                if o.dma:
                    dma_doms.add(o.tok[0])
        sems = {d: stack.enter_context(nc.semaphore(f"s_{d}")) for d in doms}
        rank = {}
        for d in doms:
            if d in dma_doms:
                rank[d] = None
            else:
                idxs = sorted(i for (dd, i) in self.needed if dd == d)
                rank[d] = {i: k + 1 for k, i in enumerate(idxs)}
        block = stack.enter_context(nc.Block())
        ops = self.ops
        self.info = dict(sems={d: sems[d].num for d in doms},
                         marks=[(lab, {d: (rank[d][c] if rank[d] is not None else 16 * c) for d, c in cnt.items()
                                       if d in ("pe", "act", "dve", "pool")}) for lab, cnt in self.marks])

        def run(engh, lst):
            for o in lst:
                for d, i in o.waits.items():
                    engh.wait_ge(sems[d], 16 * i if rank[d] is None else rank[d][i])
                if o.fn is None:
                    continue
                ins = o.fn(engh)
                try:
                    self.names[ins.ins.name] = o
                except Exception:
                    pass
                d, i = o.tok
                if o.dma:
                    ins.then_inc(sems[d], 16)
                elif i in rank[d]:
                    ins.then_inc(sems[d], 1)

        @block.sync
        def _(e):
            run(e, ops["sp"])

        @block.scalar
        def _(e):
            run(e, ops["act"])

        @block.gpsimd
        def _(e):
            run(e, ops["pool"])

        @block.vector
        def _(e):
            run(e, ops["dve"])

        @block.tensor
        def _(e):
            run(e, ops["pe"])


def host_consts(T):
    p = np.arange(128)
    ident = np.eye(128, dtype=np.float32)
    tri = (p[:, None] <= p[None, :]).astype(np.float32)
    mui = np.where(p[None, :] < p[:, None], -BIG, 0.0).astype(np.float32)
    mls = np.where(p[None, :] >= p[:, None], BIG, 0.0).astype(np.float32)
    mus = np.where(p[None, :] <= p[:, None], -BIG, 0.0).astype(np.float32)
    rot = np.zeros((64, 64), np.float32)
    for m in range(32):
        rot[m + 32, m] = -1.0
        rot[m, m + 32] = 1.0
    pos = np.maximum(np.arange(T) - 112, 0).astype(np.float32)
    inv = (np.float32(10000.0) ** (-np.arange(32, dtype=np.float32) / np.float32(32))).astype(np.float32)
    ang = (pos[:, None] * inv[None, :]).astype(np.float32)
    cos = np.cos(ang).astype(np.float32).T
    sin = np.sin(ang).astype(np.float32).T
    cos2 = np.concatenate([cos, cos], 0).astype(np.float32)
    sin2 = np.concatenate([sin, sin], 0).astype(np.float32)
    padb = np.where(p < 112, -BIG, 0.0).astype(np.float32)[:, None]
    urow = np.where(p >= 64, 1.0, 0.0).astype(np.float32)[None, :]
    wrow = np.where(p < 64, -BIG, 0.0).astype(np.float32)[None, :]
    return dict(c_ident=ident, c_tri=tri, c_mui=mui, c_mls=mls, c_mus=mus, c_rot=rot, c_cos=cos2, c_sin=sin2,
                c_padb=padb, c_urow=urow, c_wrow=wrow)


class _Stop(Exception):
    pass


def build(NSEQ, SEQ, STOP=None):
    T = SEQ + 128
    NT = T // 128
    CB = [(c0, min(512, T - c0)) for c0 in range(0, T, 512)]
    nc = bass.Bass("TRN2", target_bir_lowering=False)

    def din(name, shape, dt=F32):
        return nc.dram_tensor(name, list(shape), dt, kind="ExternalInput").ap()

    x_d = din("x", [NSEQ * SEQ, D])
    meta_d = din("meta_tokens", [16, D])
    a_norm_d = din("a_norm", [D]); a_w_in_d = din("a_w_in", [D, 4112]); a_conv_d = din("a_conv", [4, 3072])
    a_log_d = din("a_log", [8]); a_dtb_d = din("a_dt_bias", [8]); a_og_d = din("a_o_gain", [128])
    a_w_out_d = din("a_w_out", [D, D]); kv_norm_d = din("kv_norm", [D]); kv_wd_d = din("kv_w_down", [D, 320])
    kv_ln_d = din("kv_latent_norm", [256]); kv_uk_d = din("kv_w_uk", [256, D]); kv_uv_d = din("kv_w_uv", [256, D])
    k_gain_d = din("k_gain", [192]); b_norm_d = din("b_norm", [D]); b_w_in_d = din("b_w_in", [D, 1408])
    b_qln_d = din("b_q_latent_norm", [384]); b_uq_d = din("b_w_uq", [384, 1536]); b_qg_d = din("b_q_gain", [192])
    b_w_out_d = din("b_w_out", [D, D])
    c_ident_d = din("c_ident", [128, 128]); c_tri_d = din("c_tri", [128, 128]); c_mui_d = din("c_mui", [128, 128])
    c_mls_d = din("c_mls", [128, 128]); c_mus_d = din("c_mus", [128, 128]); c_rot_d = din("c_rot", [64, 64]); c_cos_d = din("c_cos", [64, T])
    c_sin_d = din("c_sin", [64, T]); c_padb_d = din("c_padb", [128, 1]); c_urow_d = din("c_urow", [1, 128])
    c_wrow_d = din("c_wrow", [1, 128])
    out_d = nc.dram_tensor("out", [NSEQ * SEQ, D], F32, kind="ExternalOutput").ap()

    def dscr(name, shape, dt):
        return nc.dram_tensor(name, list(shape), dt).ap()

    w_in_b = dscr("w_in_b", [D, 4112], BF16); w_outa_b = dscr("w_outa_b", [D, D], BF16)
    w_down_b = dscr("w_down_b", [D, 320], BF16); w_uk_b = dscr("w_uk_b", [256, D], BF16)
    w_uv_b = dscr("w_uv_b", [256, D], BF16); w_inb_b = dscr("w_inb_b", [D, 1408], BF16)
    w_uq_b = dscr("w_uq_b", [384, 1536], BF16); w_outb_b = dscr("w_outb_b", [D, D], BF16)
    h1_d = dscr("h1_d", [T, D], F32)

    S = Sched()
    with ExitStack() as st:
        st.enter_context(nc.allow_non_contiguous_dma("small strided parameter loads"))

        def sb(name, shape, dt):
            return st.enter_context(nc.sbuf_tensor(name, list(shape), dt))

        def V(eng, fname, reads, writes, *a, **k):
            S.op(eng, lambda e: getattr(e, fname)(*a, **k), reads, writes)

        def ACT(reads, writes, out, in_, func, **k):
            S.op("act", lambda e: e.activation(out=out, in_=in_, func=func, **k), reads, writes)

        def MM(out, lhsT, rhs, start, stop, reads, writes):
            S.op("pe", lambda e: e.matmul(out, lhsT=lhsT, rhs=rhs, start=start, stop=stop), reads, writes)

        def TR(out, in_, ident, reads, writes):
            S.op("pe", lambda e: e.transpose(out=out, in_=in_, identity=ident), reads, writes)

        def DMA(q, out, in_, reads, writes, dom):
            S.op(q, lambda e: e.dma_start(out=out, in_=in_), reads, writes, dma_dom=dom)

        PB = [st.enter_context(nc.psum_tensor(f"pb{i}", [128, 512], F32)) for i in range(8)]
        PBb = S.bufs(8, "pb")
        for _b in PBb:
            _b.excl = True
        bctr = [0]
        bpool = [list(range(8))]

        live = set()

        def bank():
            pool_ = bpool[0]
            for k in range(len(pool_)):
                i = pool_[(bctr[0] + k) % len(pool_)]
                if i not in live:
                    bctr[0] += k + 1
                    live.add(i)
                    return PB[i], PBb[i]
            raise RuntimeError("no free PSUM bank")

        def rel(*bbs):
            for bb in bbs:
                live.discard(PBb.index(bb))

        def fbank(i):
            return PB[i], PBb[i]

        try:
            identF = sb("identF", [128, 128], F32); identB = sb("identB", [128, 128], BF16)
            triF = sb("triF", [128, 128], F32); muiF = sb("muiF", [128, 128], F32); mlsF = sb("mlsF", [128, 128], F32)
            musF = sb("musF", [128, 128], F32)
            onesB = sb("onesB", [128, 128], BF16); rotF = sb("rotF", [64, 64], F32); rotB = sb("rotB", [64, 64], BF16)
            cos2 = sb("cos2", [64, T], BF16); sin2 = sb("sin2", [64, T], BF16)
            padb = sb("padb", [128, 1], F32); zcol = sb("zcol", [128, 1], F32)
            urowF = sb("urowF", [1, 128], F32); wrowF = sb("wrowF", [1, 128], F32)
            urowB = sb("urowB", [1, 128], BF16); wrowB = sb("wrowB", [1, 128], BF16)
            g_anorm = sb("g_anorm", [128, 8], F32); g_kvnorm = sb("g_kvnorm", [128, 8], F32)
            g_bnorm = sb("g_bnorm", [128, 8], F32); g_kvln = sb("g_kvln", [128, 2], F32)
            g_qln = sb("g_qln", [128, 3], F32); g_og = sb("g_og", [128, 1], F32)
            kgn = sb("kgn", [128, 1], F32); kgr = sb("kgr", [64, 1], F32)
            qgn = sb("qgn", [128, 1], F32); qgr = sb("qgr", [64, 1], F32)
            convw = sb("convw", [128, 4, 24], F32)
            alog = sb("alog", [128, 8], F32); negA = sb("negA", [128, 8], F32); dtb = sb("dtb", [128, 8], F32)
            CONST = S.buf("const")

            def cload(dst, src):
                DMA("sp", dst, src, [], [CONST], "cst")

            cload(identF[:], c_ident_d); cload(triF[:], c_tri_d); cload(muiF[:], c_mui_d); cload(mlsF[:], c_mls_d); cload(musF[:], c_mus_d)
            cload(rotF[:], c_rot_d); cload(padb[:], c_padb_d)
            cload(urowF[:], c_urow_d); cload(wrowF[:], c_wrow_d)
            cload(g_anorm[:], a_norm_d.rearrange("(k p) -> p k", p=128))
            cload(g_kvnorm[:], kv_norm_d.rearrange("(k p) -> p k", p=128))
            cload(g_bnorm[:], b_norm_d.rearrange("(k p) -> p k", p=128))
            cload(g_kvln[:], kv_ln_d.rearrange("(k p) -> p k", p=128))
            cload(g_qln[:], b_qln_d.rearrange("(k p) -> p k", p=128))
            cload(g_og[:], a_og_d.rearrange("(p o) -> p o", o=1))
            cload(kgn[:], k_gain_d[0:128].rearrange("(p o) -> p o", o=1))
            cload(kgr[:], k_gain_d[128:192].rearrange("(p o) -> p o", o=1))
            cload(qgn[:], b_qg_d[0:128].rearrange("(p o) -> p o", o=1))
            cload(qgr[:], b_qg_d[128:192].rearrange("(p o) -> p o", o=1))
            for jj in range(4):
                cload(convw[:, jj, :], a_conv_d[jj, :].rearrange("(c p) -> p c", p=128))
            cload(alog[:], a_log_d.partition_broadcast(128))
            cload(dtb[:], a_dtb_d.partition_broadcast(128))
            CONST.w = ("cst", S.count["cst"])
            V("dve", "tensor_copy", [CONST], [CONST], out=identB[:], in_=identF[:])
            V("dve", "tensor_copy", [CONST], [CONST], out=rotB[:], in_=rotF[:])
            V("dve", "tensor_copy", [CONST], [CONST], out=urowB[:], in_=urowF[:])
            V("dve", "tensor_copy", [CONST], [CONST], out=wrowB[:], in_=wrowF[:])
            V("dve", "memset", [], [CONST], onesB[:], 1.0)
            V("dve", "memset", [], [CONST], zcol[:], 0.0)
            ACT([CONST], [CONST], negA[:], alog[:], AF.Exp)
            V("dve", "tensor_scalar_mul", [CONST], [CONST], out=negA[:], in0=negA[:], scalar1=-1.0)

            hT = sb("hT", [128, 8, T], BF16); hT_b = S.bufs(NT, "hT")
            OV = sb("OV", [128, NT, 1024], BF16); OV_b = S.bufs(NT, "OV")
            RR = sb("RR", [64, T], BF16); RR_b = S.buf("RR")
            xin = [sb(f"xin{i}", [128, 1024], F32) for i in range(2)]; xin_b = S.bufs(2, "xin")
            xn = [sb(f"xn{i}", [128, 1024], BF16) for i in range(2)]; xn_b = S.bufs(2, "xn")
            junk = sb("junk", [128, 1024], BF16); junk_b = S.buf("junk")
            ssq = [sb(f"ssq{i}", [128, 1], F32) for i in range(2)]; ssq_b = S.bufs(2, "ssq")
            ARENA_F32 = 27136
            arena = sb("arena", [128, ARENA_F32], F32)
            apos = [0]

            def areset(off=0):
                apos[0] = off

            def carve(shape, dt):
                P = shape[0]
                n = int(np.prod(shape[1:]))
                nbytes = n * (4 if dt == F32 else 2)
                n32 = (nbytes + 3) // 4
                o = apos[0]
                apos[0] += (n32 + 7) // 8 * 8
                assert apos[0] <= ARENA_F32, ("arena overflow", apos[0])
                v = arena[0:P, o:o + n32]
                if dt != F32:
                    v = v.bitcast(dt)[:, 0:n]
                if len(shape) == 3:
                    v = v.rearrange("p (a b) -> p a b", a=shape[1], b=shape[2])
                return v

            areset(0)
            wl = [carve([128, 1024], F32) for i in range(2)]; wl_b = S.bufs(2, "wl")
            ws = [carve([128, 1024], BF16) for i in range(2)]; ws_b = S.bufs(2, "ws")
            for tab_d, tab in ((c_cos_d, cos2), (c_sin_d, sin2)):
                for c0 in range(0, T, 1024):
                    cw = min(1024, T - c0)
                    sl = wctr_ = 0
                    DMA("sp", wl[0][0:64, :cw], tab_d[:, c0:c0 + cw], [], [wl_b[0]], "wl0")
                    V("dve", "tensor_copy", [wl_b[0]], [CONST], out=tab[:, c0:c0 + cw], in_=wl[0][0:64, :cw])
            wctr = [0]

            def prep_weight(src, K, N, dst, gcol):
                for kc in range(K // 128):
                    for c0 in range(0, N, 1024):
                        cw = min(1024, N - c0)
                        sl = wctr[0] % 2
                        wctr[0] += 1
                        DMA("sp", wl[sl][:, :cw], src[kc * 128:(kc + 1) * 128, c0:c0 + cw], [], [wl_b[sl]], f"wl{sl}")
                        if gcol is not None:
                            V("dve", "tensor_scalar_mul", [wl_b[sl], CONST], [ws_b[sl]], out=ws[sl][:, :cw],
                              in0=wl[sl][:, :cw], scalar1=gcol(kc))
                        else:
                            V("dve", "tensor_copy", [wl_b[sl]], [ws_b[sl]], out=ws[sl][:, :cw], in_=wl[sl][:, :cw])
                        DMA("act", dst[kc * 128:(kc + 1) * 128, c0:c0 + cw], ws[sl][:, :cw], [ws_b[sl]], [], f"ws{sl}")

            prep_weight(a_w_in_d, D, 4112, w_in_b, lambda kc: g_anorm[:, kc:kc + 1])
            prep_weight(a_w_out_d, D, D, w_outa_b, lambda kc: g_og[:, 0:1])
            prep_weight(kv_wd_d, D, 320, w_down_b, lambda kc: g_kvnorm[:, kc:kc + 1])
            prep_weight(kv_uk_d, 256, D, w_uk_b, lambda kc: g_kvln[:, kc:kc + 1])
            prep_weight(kv_uv_d, 256, D, w_uv_b, lambda kc: g_kvln[:, kc:kc + 1])
            prep_weight(b_w_in_d, D, 1408, w_inb_b, lambda kc: g_bnorm[:, kc:kc + 1])
            prep_weight(b_uq_d, 384, 1536, w_uq_b, lambda kc: g_qln[:, kc:kc + 1])
            prep_weight(b_w_out_d, D, D, w_outb_b, None)
            S.barrier()
            if STOP == "W":
                raise _Stop()

            def wview(wb, c0, cw):
                return wb[:, c0:c0 + cw].rearrange("(k p) n -> p k n", p=128)

            def load_x_tile(s, t, sl):
                if t == 0:
                    V("pool", "memset", [], [xin_b[sl]], xin[sl][:], 0.0)
                    DMA("sp", xin[sl][112:128, :], meta_d, [], [xin_b[sl]], f"xl{sl}")
                else:
                    r0 = s * SEQ + (t - 1) * 128
                    DMA("sp", xin[sl][:], x_d[r0:r0 + 128, :], [], [xin_b[sl]], f"xl{sl}")

            def norm_to_hT(src, src_b, t, sl):
                ACT([src_b], [junk_b, ssq_b[sl]], junk[:], src, AF.Square, accum_out=ssq[sl][:])
                ACT([ssq_b[sl]], [ssq_b[sl]], ssq[sl][:], ssq[sl][:], AF.Ln, scale=1.0 / D, bias=EPS)
                ACT([ssq_b[sl]], [ssq_b[sl]], ssq[sl][:], ssq[sl][:], AF.Exp, scale=-0.5)
                V("dve", "tensor_scalar_mul", [src_b, ssq_b[sl]], [xn_b[sl]], out=xn[sl][:], in0=src, scalar1=ssq[sl][:, 0:1])
                pb, pbb = bank()
                pbv = pb[:].bitcast(BF16)
                for kc in range(8):
                    TR(pbv[:, kc * 128:(kc + 1) * 128], xn[sl][:, kc * 128:(kc + 1) * 128], identB[:], [xn_b[sl], CONST], [pbb])
                V("dve", "tensor_copy", [pbb], [hT_b[t]], out=hT[:, :, t * 128:(t + 1) * 128],
                  in_=pbv.rearrange("p (k c) -> p k c", k=8))
                rel(pbb)

            def hT_reads(c0, cw):
                return [hT_b[t] for t in range(c0 // 128, (c0 + cw + 127) // 128)]

            def silu_from(src_ap, src_reads, tmp, tmp_b, shape_sl):
                ACT(src_reads, [tmp_b], tmp, src_ap, AF.Exp, scale=-1.0)
                ACT([tmp_b], [tmp_b], tmp, tmp, AF.Ln, bias=1.0)
                ACT([tmp_b], [tmp_b], tmp, tmp, AF.Exp, scale=-1.0)

            def rsqrt_inplace(ap, b, scale, reads_extra=()):
                ACT([b] + list(reads_extra), [b], ap, ap, AF.Ln, scale=scale, bias=EPS)
                ACT([b], [b], ap, ap, AF.Exp, scale=-0.5)

            for s in range(NSEQ):
                bpool[0] = list(range(8))
                for t in range(NT):
                    sl = t % 2
                    load_x_tile(s, t, sl)
                    norm_to_hT(xin[sl][:], xin_b[sl], t, sl)
                S.barrier()
                if STOP == "A0":
                    raise _Stop()
                areset(0)
                wab = carve([128, 8, 16], BF16); wab_b = S.buf()
                ab = carve([128, NT, 16], F32); ab_b = S.buf()
                tm1 = carve([128, NT, 8], F32); tm1_b = S.buf()
                gcol = carve([128, NT, 8], F32); g_b = S.buf()
                lnb = carve([128, NT, 8], F32); lnb_b = S.buf()
                gc = carve([128, NT, 8], F32); gc_b = S.buf()
                ngc = carve([128, NT, 8], F32); egc = carve([128, NT, 8], F32); bls = carve([128, NT, 8], F32)
                beta = carve([128, NT, 8], F32); begc = carve([128, NT, 8], F32)
                der_b = S.buf()
                DMA("sp", wab, wview(w_in_b, 4096, 16), [], [wab_b], "wab")
                pb, pbb = bank()
                for t in range(NT):
                    for kc in range(8):
                        MM(pb[:, t * 16:(t + 1) * 16], hT[:, kc, t * 128:(t + 1) * 128], wab[:, kc, :], kc == 0, kc == 7,
                           [hT_b[t], wab_b], [pbb])
                V("dve", "tensor_copy", [pbb], [ab_b], out=ab, in_=pb[:, 0:NT * 16].rearrange("p (t c) -> p t c", c=16))
                rel(pbb)
                ACT([ab_b], [tm1_b], tm1, ab[:, :, 0:8], AF.Exp, scale=-1.0)
                ACT([tm1_b], [tm1_b], tm1, tm1, AF.Ln, bias=1.0)
                V("dve", "tensor_scalar_mul", [tm1_b], [lnb_b], out=lnb, in0=tm1, scalar1=-1.0)
                V("dve", "tensor_tensor", [ab_b, CONST], [tm1_b], out=tm1, in0=ab[:, :, 8:16],
                  in1=dtb[:].unsqueeze(1).to_broadcast([128, NT, 8]), op=ALU.add)
                ACT([tm1_b], [tm1_b], tm1, tm1, AF.Exp)
                ACT([tm1_b], [tm1_b], tm1, tm1, AF.Ln, bias=1.0)
                V("dve", "tensor_tensor", [tm1_b, CONST], [g_b], out=gcol, in0=tm1,
                  in1=negA[:].unsqueeze(1).to_broadcast([128, NT, 8]), op=ALU.mult)
                pb, pbb = bank()
                for t in range(NT):
                    MM(pb[:, t * 8:(t + 1) * 8], triF[:], gcol[:, t, :], True, True, [CONST, g_b], [pbb])
                V("dve", "tensor_copy", [pbb], [gc_b], out=gc, in_=pb[:, 0:NT * 8].rearrange("p (t c) -> p t c", c=8))
                rel(pbb)
                V("dve", "tensor_scalar_mul", [gc_b], [der_b], out=ngc, in0=gc, scalar1=-1.0)
                ACT([gc_b], [der_b], egc, gc, AF.Exp)
                V("dve", "tensor_tensor", [gc_b, lnb_b], [der_b], out=bls, in0=gc, in1=lnb, op=ALU.add)
                ACT([lnb_b], [der_b], beta, lnb, AF.Exp)
                ACT([der_b], [der_b], begc, bls, AF.Exp)

                if STOP == "Aab":
                    raise _Stop()
                a12_base = apos[0]
                for h in range(1):
                    areset(a12_base)
                    wst = [carve([128, 8, 128], BF16) for _ in range(2)]; wst_b = S.bufs(2)
                    zc = [carve([128, 515], F32) for _ in range(2)]; zc_b = S.bufs(2)
                    taL = [carve([128, 512], F32) for _ in range(2)]; taL_b = S.bufs(2)
                    tbL = [carve([128, 512], F32) for _ in range(2)]; tbL_b = S.bufs(2)
                    sqL = [carve([128, 512], BF16) for _ in range(2)]; sqL_b = S.bufs(2)
                    blkc = [0]
                    sT = [carve([128, T], BF16) for _ in range(3)]; sT_b = S.bufs(3)
                    Ktok = carve([128, NT, 128], BF16); Vtok = carve([128, NT, 128], BF16); KV_b = S.bufs(2)
                    Sst = carve([128, 128], F32); Sbf = carve([128, 128], BF16); S_b = S.buf(); Sb_b = S.buf()
                    NSL = 8
                    cm = []
                    for i in range(NSL):
                        cm.append(dict(
                            Eui=carve([128, 128], F32), Els=carve([128, 128], F32), Eus=carve([128, 128], F32),
                            Z=[carve([128, 256], CHAIN_DT) for _ in range(2)], P=[carve([128, 128], CHAIN_DT) for _ in range(2)],
                            attT=carve([128, 128], BF16), TmT=carve([128, 128], BF16), nWdT=carve([128, 128], BF16),
                            Bk=carve([128, 128], BF16), bV=carve([128, 128], BF16), Kd=carve([128, 128], BF16),
                            Ub=carve([128, 128], BF16), glc=carve([128, 1], F32), o1=carve([128, 128], F32),
                            b={k: S.buf() for k in ("Eui", "Els", "Eus", "Z0", "Z1", "P0", "P1", "attT", "TmT", "nWdT", "Bk", "bV",
                                                    "Kd", "Ub", "glc", "o1")}))
                for h in range(H):
                    for j in range(3):
                        fc = j * 8 + h
                        wsl = j % 2
                        DMA("sp", wst[wsl], wview(w_in_b, fc * 128, 128), [], [wst_b[wsl]], f"wst{wsl}")
                        for bi, (c0, cw) in enumerate(CB):
                            zs = bi % 2
                            bk_ = blkc[0] % 2; blkc[0] += 1
                            ta, ta_b, tb, tb_b, sq, sq_b = taL[bk_], taL_b[bk_], tbL[bk_], tbL_b[bk_], sqL[bk_], sqL_b[bk_]
                            pb, pbb = bank()
                            for kc in range(8):
                                MM(pb[:, :cw], wst[wsl][:, kc, :], hT[:, kc, c0:c0 + cw], kc == 0, kc == 7,
                                   [wst_b[wsl]] + hT_reads(c0, cw), [pbb])
                            if bi == 0:
                                V("pool", "memset", [], [zc_b[zs]], zc[zs][:, 0:3], 0.0)
                            else:
                                V("pool", "tensor_copy", [zc_b[1 - zs]], [zc_b[zs]], out=zc[zs][:, 0:3], in_=zc[1 - zs][:, 512:515])
                            S.op("act", (lambda o, i: (lambda e: e.copy(out=o, in_=i)))(zc[zs][:, 3:3 + cw], pb[:, :cw]),
                                 [pbb], [zc_b[zs]])
                            rel(pbb)
                            V("dve", "tensor_scalar_mul", [zc_b[zs], CONST], [ta_b], out=ta[:, :cw], in0=zc[zs][:, 3:3 + cw],
                              scalar1=convw[:, 3, fc:fc + 1])
                            for jj in (2, 1, 0):
                                V("dve", "scalar_tensor_tensor", [zc_b[zs], CONST, ta_b], [ta_b], out=ta[:, :cw],
                                  in0=zc[zs][:, jj:jj + cw], scalar=convw[:, jj, fc:fc + 1], in1=ta[:, :cw],
                                  op0=ALU.mult, op1=ALU.add)
                            silu_from(ta[:, :cw], [ta_b], tb[:, :cw], tb_b, None)
                            V("dve", "tensor_tensor", [ta_b, tb_b], [sT_b[j]], out=sT[j][:, c0:c0 + cw], in0=ta[:, :cw],
                              in1=tb[:, :cw], op=ALU.mult)
                            if j < 2:
                                ACT([sT_b[j]], [sq_b], sq[:, :cw], sT[j][:, c0:c0 + cw], AF.Square)
                                pb2, pbb2 = bank()
                                MM(pb2[:, :cw], onesB[:], sq[:, :cw], True, True, [CONST, sq_b], [pbb2])
                                ACT([pbb2], [tb_b], tb[:, :cw], pb2[:, :cw], AF.Ln, bias=EPS)
                                rel(pbb2)
                                ACT([tb_b], [tb_b], tb[:, :cw], tb[:, :cw], AF.Exp, scale=-0.5)
                                V("dve", "scalar_tensor_tensor", [sT_b[j], tb_b], [sT_b[j]], out=sT[j][:, c0:c0 + cw],
                                  in0=sT[j][:, c0:c0 + cw], scalar=(128.0 ** -0.5 if j == 0 else 1.0), in1=tb[:, :cw],
                                  op0=ALU.mult, op1=ALU.mult)
                    qT, kT, vT = sT
                    if STOP == "A12a":
                        raise _Stop()
                    for j, dst in ((1, Ktok), (2, Vtok)):
                        for t0 in range(0, NT, 8):
                            tn = min(8, NT - t0)
                            pb, pbb = bank()
                            pbv = pb[:].bitcast(BF16)
                            for i in range(tn):
                                TR(pbv[:, i * 128:(i + 1) * 128], sT[j][:, (t0 + i) * 128:(t0 + i + 1) * 128], identB[:],
                                   [sT_b[j], CONST], [pbb])
                            V("dve", "tensor_copy", [pbb], [KV_b[j - 1]], out=dst[:, t0:t0 + tn, :],
                              in_=pbv[:, 0:tn * 128].rearrange("p (t c) -> p t c", c=128))
                            rel(pbb)
                    V("pool", "memset", [], [S_b], Sst[:], 0.0)
                    V("pool", "memset", [], [Sb_b], Sbf[:], 0.0)
                    if STOP == "A12b":
                        raise _Stop()
                    def st_G(c):
                        m = cm[c % NSL]; mb = m["b"]
                        cs = slice(c * 128, (c + 1) * 128)
                        gb_l = gcol[:, c, h:h + 1].to_broadcast([128, 128])
                        lb_l = lnb[:, c, h:h + 1].to_broadcast([128, 128])
                        pg, pgb = bank()
                        pt, ptb = bank()
                        m["pg"], m["pgb"], m["pt"], m["ptb"] = pg, pgb, pt, ptb
                        MM(pg[:, 0:128], gb_l, triF[:], True, False, [g_b, CONST], [pgb])
                        MM(pg[:, 0:128], identF[:], muiF[:], False, True, [CONST], [pgb])
                        MM(pg[:, 128:256], gb_l, triF[:], True, False, [g_b, CONST], [pgb])
                        MM(pg[:, 128:256], identF[:], mlsF[:], False, True, [CONST], [pgb])
                        MM(pg[:, 256:384], kT[:, cs], kT[:, cs], True, True, [sT_b[1]], [pgb])
                        MM(pg[:, 384:512], kT[:, cs], qT[:, cs], True, True, [sT_b[1], sT_b[0]], [pgb])
                        MM(pt[:, 0:128], gb_l, triF[:], True, False, [g_b, CONST], [ptb])
                        MM(pt[:, 0:128], lb_l, identF[:], False, False, [lnb_b, CONST], [ptb])
                        MM(pt[:, 0:128], identF[:], musF[:], False, True, [CONST], [ptb])

                    def st_E(c):
                        m = cm[c % NSL]; mb = m["b"]
                        pg, pgb, pt, ptb = m["pg"], m["pgb"], m["pt"], m["ptb"]
                        ACT([pgb, der_b], [mb["Eui"]], m["Eui"], pg[:, 0:128], AF.Exp, bias=ngc[:, c, h:h + 1], scale=1.0)
                        ACT([pgb, der_b], [mb["Els"]], m["Els"], pg[:, 128:256], AF.Exp, bias=bls[:, c, h:h + 1], scale=-1.0)
                        ACT([pgb], [mb["glc"]], m["glc"], pg[:, 127:128], AF.Exp)
                        ACT([ptb, der_b], [mb["Eus"]], m["Eus"], pt[:, 0:128], AF.Exp, bias=ngc[:, c, h:h + 1], scale=1.0)
                        V("dve", "scalar_tensor_tensor", [pgb, mb["Els"]], [mb["Z0"]], out=m["Z"][0][:, 0:128],
                          in0=pg[:, 256:384], scalar=-1.0, in1=m["Els"], op0=ALU.mult, op1=ALU.mult)
                        V("dve", "scalar_tensor_tensor", [pgb, mb["Eus"]], [mb["Z0"]], out=m["Z"][0][:, 128:256],
                          in0=pg[:, 256:384], scalar=-1.0, in1=m["Eus"], op0=ALU.mult, op1=ALU.mult)
                        V("dve", "tensor_tensor", [pgb, mb["Eui"]], [mb["attT"]], out=m["attT"], in0=pg[:, 384:512],
                          in1=m["Eui"], op=ALU.mult)
                        V("dve", "tensor_tensor", [mb["Z0"], CONST], [mb["P0"]], out=m["P"][0], in0=m["Z"][0][:, 128:256],
                          in1=identF[:], op=ALU.add)
                        V("dve", "tensor_scalar_mul", [KV_b[0], der_b], [mb["Bk"]], out=m["Bk"], in0=Ktok[:, c, :],
                          scalar1=begc[:, c, h:h + 1])
                        V("dve", "tensor_scalar_mul", [KV_b[1], der_b], [mb["bV"]], out=m["bV"], in0=Vtok[:, c, :],
                          scalar1=beta[:, c, h:h + 1])
                        V("dve", "tensor_scalar_mul", [KV_b[0], mb["Eui"]], [mb["Kd"]], out=m["Kd"], in0=Ktok[:, c, :],
                          scalar1=m["Eui"][:, 127:128])
                        rel(pgb, ptb)

                    def st_Lmm(c, lev):
                        m = cm[c % NSL]; mb = m["b"]
                        zi = lev % 2
                        pi = (lev - 1) % 2
                        Zc, Zcb = m["Z"][zi], mb[f"Z{zi}"]
                        pk, pkb = bank()
                        m["pk"], m["pkb"] = pk, pkb
                        if lev >= 1:
                            MM(pk[:, 256:384], Zc[:, 0:128], m["P"][pi], True, True, [Zcb, mb[f"P{pi}"]], [pkb])
                        if lev < 6:
                            MM(pk[:, 0:128], Zc[:, 128:256], Zc[:, 0:128], True, True, [Zcb], [pkb])
                            MM(pk[:, 128:256], Zc[:, 0:128], Zc[:, 128:256], True, True, [Zcb], [pkb])

                    def st_Lev(c, lev):
                        m = cm[c % NSL]; mb = m["b"]
                        zi = lev % 2
                        pi = (lev - 1) % 2
                        Zn, Znb = m["Z"][1 - zi], mb[f"Z{1 - zi}"]
                        pk, pkb = m["pk"], m["pkb"]
                        if lev >= 1:
                            if lev < 6:
                                V("dve", "tensor_tensor", [pkb, mb[f"P{pi}"]], [mb[f"P{1 - pi}"]], out=m["P"][1 - pi],
                                  in0=pk[:, 256:384], in1=m["P"][pi], op=ALU.add)
                            else:
                                V("dve", "tensor_tensor", [pkb, mb[f"P{pi}"]], [mb["TmT"]], out=m["TmT"],
                                  in0=pk[:, 256:384], in1=m["P"][pi], op=ALU.add)
                        if lev < 6:
                            V("act", "copy", [pkb], [Znb], out=Zn[:, 0:256], in_=pk[:, 0:256])
                        rel(pkb)

                    def st_W(c):
                        m = cm[c % NSL]; mb = m["b"]
                        pw, pwb = bank()
                        MM(pw[:, 0:128], m["Bk"], m["TmT"], True, True, [mb["Bk"], mb["TmT"]], [pwb])
                        V("act", "mul", [pwb], [mb["nWdT"]], out=m["nWdT"], in_=pw[:, 0:128], mul=-1.0)
                        rel(pwb)

                    def st_Ra(c):
                        m = cm[c % NSL]; mb = m["b"]
                        pu, pub = bank()
                        MM(pu[:, 0:128], m["TmT"], m["bV"], True, c == 0, [mb["TmT"], mb["bV"]], [pub])
                        if c > 0:
                            MM(pu[:, 0:128], m["nWdT"], Sbf, False, True, [mb["nWdT"], Sb_b], [pub])
                        V("act", "copy", [pub], [mb["Ub"]], out=m["Ub"], in_=pu[:, 0:128])
                        rel(pub)

                    def st_Rb(c):
                        m = cm[c % NSL]; mb = m["b"]
                        cs = slice(c * 128, (c + 1) * 128)
                        po, pob = bank()
                        po2, pob2 = bank()
                        MM(po[:, 0:128], m["Kd"], m["Ub"], True, True, [mb["Kd"], mb["Ub"]], [pob])
                        MM(po2[:, 0:128], qT[:, cs], Sbf, True, True, [sT_b[0], Sb_b], [pob2])
                        MM(po2[:, 128:256], m["attT"], m["Ub"], True, True, [mb["attT"], mb["Ub"]], [pob2])
                        V("dve", "scalar_tensor_tensor", [S_b, mb["glc"], pob], [Sb_b], out=Sbf[:], in0=Sst[:],
                          scalar=m["glc"][:, 0:1], in1=po[:, 0:128], op0=ALU.mult, op1=ALU.add)
                        V("dve", "scalar_tensor_tensor", [S_b, mb["glc"], pob], [S_b], out=Sst[:], in0=Sst[:],
                          scalar=m["glc"][:, 0:1], in1=po[:, 0:128], op0=ALU.mult, op1=ALU.add)
                        ACT([pob2, der_b], [mb["o1"]], m["o1"], po2[:, 0:128], AF.Copy, scale=egc[:, c, h:h + 1])
                        V("dve", "tensor_tensor", [pob2, mb["o1"]], [OV_b[c]], out=OV[:, c, h * 128:(h + 1) * 128],
                          in0=po2[:, 128:256], in1=m["o1"], op=ALU.add)
                        rel(pob, pob2)

                    def st_R(item):
                        (st_Ra if item[0] == "a" else st_Rb)(item[1])

                    GS = 4
                    groups = [list(range(i, min(i + GS, NT))) for i in range(0, NT, GS)]
                    pending = []
                    for grp in groups:
                        stages = [("GE", grp[0:2]), ("GE", grp[2:4])] + [("L", lev) for lev in range(7)] + [("W", None)]
                        for kind, lev in stages:
                            if kind == "GE":
                                for c in lev:
                                    st_G(c)
                                for c in lev:
                                    st_E(c)
                            elif kind == "L":
                                for c in grp:
                                    st_Lmm(c, lev)
                                for c in grp:
                                    st_Lev(c, lev)
                            else:
                                for c in grp:
                                    st_W(c)
                            if pending:
                                st_R(pending.pop(0))
                        while pending:
                            st_R(pending.pop(0))
                        pending = [(ph, c) for c in grp for ph in ("a", "b")]
                    while pending:
                        st_R(pending.pop(0))
                S.barrier()
                if STOP == "A12":
                    raise _Stop()
                areset(0)
                gateW = carve([128, 8, 1024], BF16); outW = carve([128, 8, 1024], BF16); gw_b = S.buf(); ow_b = S.buf()
                sigL = [carve([128, 1024], F32) for _ in range(2)]; sigL_b = S.bufs(2)
                gsL = [carve([128, 1024], F32) for _ in range(2)]; gsL_b = S.bufs(2)
                osqL = [carve([128, 1024], F32) for _ in range(2)]; osqL_b = S.bufs(2)
                ossL = [carve([128, 8], F32) for _ in range(2)]; ossL_b = S.bufs(2)
                ogbL = [carve([128, 1024], BF16) for _ in range(2)]; ogbL_b = S.bufs(2)
                ogTL = [carve([128, 8, 128], BF16) for _ in range(2)]; ogTL_b = S.bufs(2)
                h1t = [carve([128, 1024], F32) for _ in range(2)]; h1t_b = S.bufs(2)
                DMA("sp", gateW, wview(w_in_b, 3072, 1024), [], [gw_b], "gw")
                DMA("sp", outW, wview(w_outa_b, 0, 1024), [], [ow_b], "ow")
                for t in range(NT):
                    sl = t % 2
                    sig, sig_b, gs, gs_b, osq, osq_b = sigL[sl], sigL_b[sl], gsL[sl], gsL_b[sl], osqL[sl], osqL_b[sl]
                    oss, oss_b, ogb, ogb_b, ogT, ogT_b = ossL[sl], ossL_b[sl], ogbL[sl], ogbL_b[sl], ogTL[sl], ogTL_b[sl]
                    load_x_tile(s, t, sl)
                    pgs = [bank(), bank()]
                    for hf in range(2):
                        pb, pbb = pgs[hf]
                        for kc in range(8):
                            MM(pb[:, :], hT[:, kc, t * 128:(t + 1) * 128], gateW[:, kc, hf * 512:(hf + 1) * 512], kc == 0, kc == 7,
                               [hT_b[t], gw_b], [pbb])
                        silu_from(pb[:, :], [pbb], sig[:, hf * 512:(hf + 1) * 512], sig_b, None)
                        V("dve", "tensor_tensor", [pbb, sig_b], [gs_b], out=gs[:, hf * 512:(hf + 1) * 512], in0=pb[:, :],
                          in1=sig[:, hf * 512:(hf + 1) * 512], op=ALU.mult)
                        rel(pbb)
                    ACT([OV_b[t]], [osq_b], osq, OV[:, t, :], AF.Square)
                    V("dve", "tensor_reduce", [osq_b], [oss_b], out=oss, in_=osq.rearrange("p (a b) -> p a b", a=8),
                      axis=AX.X, op=ALU.add)
                    ACT([oss_b], [oss_b], oss, oss, AF.Ln, scale=1.0 / 128, bias=EPS)
                    ACT([oss_b], [oss_b], oss, oss, AF.Exp, scale=-0.5)
                    V("dve", "tensor_tensor", [OV_b[t], oss_b], [osq_b], out=osq.rearrange("p (a b) -> p a b", a=8),
                      in0=OV[:, t, :].rearrange("p (a b) -> p a b", a=8), in1=oss.unsqueeze(2).to_broadcast([128, 8, 128]),
                      op=ALU.mult)
                    V("dve", "tensor_tensor", [osq_b, gs_b], [ogb_b], out=ogb, in0=osq, in1=gs, op=ALU.mult)
                    pb, pbb = bank()
                    pbv = pb[:].bitcast(BF16)
                    for kc in range(8):
                        TR(pbv[:, kc * 128:(kc + 1) * 128], ogb[:, kc * 128:(kc + 1) * 128], identB[:], [ogb_b, CONST], [pbb])
                    V("dve", "tensor_copy", [pbb], [ogT_b], out=ogT, in_=pbv.rearrange("p (k c) -> p k c", k=8))
                    rel(pbb)
                    for hf in range(2):
                        pb, pbb = bank()
                        for kc in range(8):
                            MM(pb[:, :], ogT[:, kc, :], outW[:, kc, hf * 512:(hf + 1) * 512], kc == 0, kc == 7, [ogT_b, ow_b], [pbb])
                        V("dve", "tensor_tensor", [pbb, xin_b[sl]], [h1t_b[sl]], out=h1t[sl][:, hf * 512:(hf + 1) * 512],
                          in0=pb[:, :], in1=xin[sl][:, hf * 512:(hf + 1) * 512], op=ALU.add)
                        rel(pbb)
                    DMA("act", h1_d[t * 128:(t + 1) * 128, :], h1t[sl], [h1t_b[sl]], [], f"h1s{sl}")
                    norm_to_hT(h1t[sl], h1t_b[sl], t, sl)
                S.barrier()
                if STOP == "A3":
                    raise _Stop()
                areset(0)
                KT = carve([128, 8, T], BF16); KT_b = S.buf()
                ksc = carve([128, NT, 8], F32); ksc_b = S.buf()
                kv_base = apos[0]
                bpool[0] = list(range(7))
                wdn = carve([128, 8, 320], BF16); wuk = carve([128, 2, 1024], BF16); wuv = carve([128, 2, 1024], BF16)
                wkv_b = S.buf()
                cTf = carve([128, 2, 512], F32); cT_b = S.buf()
                kpe = carve([64, 512], F32); kpe_b = S.buf()
                kpb = carve([64, 512], BF16); kpb_b = S.buf()
                csq = carve([128, 2, 512], BF16); csq_b = S.buf()
                rb = carve([128, 512], F32); rb_b = S.buf()
                ckv = carve([128, 2, 512], BF16); ckv_b = S.buf()
                sqK = [carve([128, 512], BF16) for _ in range(2)]; sqK_b = S.bufs(2)
                sqR = carve([64, 512], BF16); sqR_b = S.buf()
                t1 = carve([64, 512], F32); t1_b = S.buf()
                t2 = carve([64, 512], F32); t2_b = S.buf()
                DMA("sp", wdn, wview(w_down_b, 0, 320), [], [wkv_b], "wkv")
                DMA("sp", wuk, wview(w_uk_b, 0, 1024), [], [wkv_b], "wkv")
                DMA("sp", wuv, wview(w_uv_b, 0, 1024), [], [wkv_b], "wkv")
                pks, pksb = fbank(7)
                for (c0, cw) in CB:
                    rd = hT_reads(c0, cw)
                    pbs_ = [bank(), bank(), bank()]
                    for j, (lo, mw) in enumerate(((0, 128), (128, 128), (256, 64))):
                        pb, pbb = pbs_[j]
                        for kc in range(8):
                            MM(pb[0:mw, :cw], wdn[:, kc, lo:lo + mw], hT[:, kc, c0:c0 + cw], kc == 0, kc == 7, [wkv_b] + rd, [pbb])
                    for j in range(2):
                        pb, pbb = pbs_[j]
                        S.op("act", (lambda o, i: (lambda e: e.copy(out=o, in_=i)))(cTf[:, j, :cw], pb[:, :cw]), [pbb], [cT_b])
                        rel(pbb)
                    pb, pbb = pbs_[2]
                    V("dve", "tensor_scalar_mul", [pbb, CONST], [kpe_b], out=kpe[:, :cw], in0=pb[0:64, :cw], scalar1=kgr[:, 0:1])
                    ACT([pbb], [sqR_b], sqR[:, :cw], pb[0:64, :cw], AF.Square)
                    rel(pbb)
                    ACT([cT_b], [csq_b], csq[:, :, :cw], cTf[:, :, :cw], AF.Square)
                    pb, pbb = bank()
                    for j in range(2):
                        MM(pb[:, :cw], onesB[:], csq[:, j, :cw], j == 0, j == 1, [CONST, csq_b], [pbb])
                    ACT([pbb], [rb_b], rb[:, :cw], pb[:, :cw], AF.Ln, scale=1.0 / 256, bias=EPS)
                    rel(pbb)
                    ACT([rb_b], [rb_b], rb[:, :cw], rb[:, :cw], AF.Exp, scale=-0.5)
                    V("dve", "tensor_tensor", [cT_b, rb_b], [ckv_b], out=ckv[:, :, :cw], in0=cTf[:, :, :cw],
                      in1=rb[:, :cw].unsqueeze(1).to_broadcast([128, 2, cw]), op=ALU.mult)
                    V("pool", "tensor_copy", [kpe_b], [kpb_b], out=kpb[:, :cw], in_=kpe[:, :cw])
                    pb, pbb = bank()
                    MM(pb[0:64, :cw], rotB[:], kpb[:, :cw], True, True, [CONST, kpb_b], [pbb])
                    V("dve", "tensor_tensor", [pbb, CONST], [t1_b], out=t1[:, :cw], in0=pb[0:64, :cw], in1=sin2[:, c0:c0 + cw], op=ALU.mult)
                    rel(pbb)
                    V("dve", "tensor_tensor", [kpe_b, CONST], [t2_b], out=t2[:, :cw], in0=kpe[:, :cw], in1=cos2[:, c0:c0 + cw], op=ALU.mult)
                    V("dve", "tensor_tensor", [t1_b, t2_b], [RR_b], out=RR[:, c0:c0 + cw], in0=t1[:, :cw], in1=t2[:, :cw], op=ALU.add)
                    for h in range(H):
                        pb, pbb = bank()
                        for r in range(2):
                            MM(pb[:, :cw], wuk[:, r, h * 128:(h + 1) * 128], ckv[:, r, :cw], r == 0, r == 1, [wkv_b, ckv_b], [pbb])
                        V("dve", "tensor_scalar_mul", [pbb, CONST], [KT_b], out=KT[:, h, c0:c0 + cw], in0=pb[:, :cw], scalar1=kgn[:, 0:1])
                        q = h % 2
                        ACT([pbb], [sqK_b[q]], sqK[q][:, :cw], pb[:, :cw], AF.Square)
                        rel(pbb)
                        for ti in range(cw // 128):
                            t = c0 // 128 + ti
                            MM(pks[:, t * 8 + h:t * 8 + h + 1], sqK[q][:, ti * 128:(ti + 1) * 128], onesB[:, 0:1], True, False,
                               [sqK_b[q], CONST], [pksb])
                            MM(pks[:, t * 8 + h:t * 8 + h + 1], sqR[:, ti * 128:(ti + 1) * 128], onesB[0:64, 0:1], False, True,
                               [sqR_b, CONST], [pksb])
                    for ti in range(cw // 128):
                        t = c0 // 128 + ti
                        for hf in range(2):
                            pb, pbb = bank()
                            for r in range(2):
                                MM(pb[:, :], ckv[:, r, ti * 128:(ti + 1) * 128], wuv[:, r, hf * 512:(hf + 1) * 512], r == 0, r == 1,
                                   [ckv_b, wkv_b], [pbb])
                            S.op("act", (lambda o, i: (lambda e: e.copy(out=o, in_=i)))(OV[:, t, hf * 512:(hf + 1) * 512], pb[:, :]),
                                 [pbb], [OV_b[t]])
                            rel(pbb)
                ACT([pksb], [ksc_b], ksc, pks[:, 0:NT * 8].rearrange("p (t c) -> p t c", c=8), AF.Ln, scale=1.0 / 192, bias=EPS)
                ACT([ksc_b], [ksc_b], ksc, ksc, AF.Exp, scale=-0.5)
                V("dve", "tensor_scalar_mul", [ksc_b], [ksc_b], out=ksc, in0=ksc, scalar1=192.0 ** -0.5)
                S.barrier()
                if STOP == "KV":
                    raise _Stop()
                bpool[0] = list(range(4))
                b_base = kv_base
                for _once in range(1):
                    areset(b_base)
                    woutB = carve([128, 8, 1024], BF16); wo_b = S.buf()
                    wst = [carve([128, 8, 128], BF16) for _ in range(2)]; wst_b = S.bufs(2)
                    wuq = [carve([128, 3, 192], BF16) for _ in range(2)]; wuq_b = S.bufs(2)
                    cqT = carve([128, 3, 512], F32); cq_b = S.buf()
                    cqn = carve([128, 3, 512], BF16); cqn_b = S.buf()
                    cqs, cqs_b = cqn, cqn_b
                    rb = carve([128, 512], F32); rb_b = S.buf()
                    qn = carve([128, 512], F32); qn_b = S.buf()
                    qr = carve([64, 512], F32); qr_b = S.buf()
                    qrb = carve([64, 512], BF16); qrb_b = S.buf()
                    sqn = carve([128, 512], BF16); sqn_b = S.buf()
                    sqr = carve([64, 512], BF16); sqr_b = S.buf()
                    rq = carve([128, 512], F32); rq_b = S.buf()
                    QnT2 = [carve([128, 512], BF16) for _ in range(2)]; QrT2 = [carve([64, 512], BF16) for _ in range(2)]
                    Q2_b = S.bufs(2)
                    t1 = carve([64, 512], F32); t1_b = S.buf()
                    t2 = carve([64, 512], F32); t2_b = S.buf()
                    sig = carve([128, 512], F32); sig_b = S.buf()
                    gs2 = [carve([128, 512], F32) for _ in range(2)]; gs2_b = S.bufs(2)
                    PT = [carve([128, 512], BF16) for _ in range(3)]; PT_b = S.bufs(3)
                    rd_ = carve([128, 512], F32); rd_b = S.buf()
                    ot = carve([128, 512], F32); ot_b = S.buf()
                    OGT = carve([128, 8, 512], BF16); OGT_b = S.buf()
                for q0 in range(1, NT, 4):
                    q1 = min(q0 + 3, NT - 1)
                    bw = (q1 - q0 + 1) * 128
                    bc0 = q0 * 128
                    hrd = hT_reads(bc0, bw)
                    DMA("sp", woutB, wview(w_outb_b, 0, 1024), [], [wo_b], "wo")
                    wc = [0]
                    for fc in range(3):
                        wsl = wc[0] % 2; wc[0] += 1
                        DMA("sp", wst[wsl], wview(w_inb_b, fc * 128, 128), [], [wst_b[wsl]], f"wst{wsl}")
                        pb, pbb = bank()
                        for kc in range(8):
                            MM(pb[:, :bw], wst[wsl][:, kc, :], hT[:, kc, bc0:bc0 + bw], kc == 0, kc == 7, [wst_b[wsl]] + hrd, [pbb])
                        S.op("act", (lambda o, i: (lambda e: e.copy(out=o, in_=i)))(cqT[:, fc, :bw], pb[:, :bw]), [pbb], [cq_b])
                        rel(pbb)
                    ACT([cq_b], [cqs_b], cqs[:, :, :bw], cqT[:, :, :bw], AF.Square)
                    pb, pbb = bank()
                    for j in range(3):
                        MM(pb[:, :bw], onesB[:], cqs[:, j, :bw], j == 0, j == 2, [CONST, cqs_b], [pbb])
                    ACT([pbb], [rb_b], rb[:, :bw], pb[:, :bw], AF.Ln, scale=1.0 / 384, bias=EPS)
                    rel(pbb)
                    ACT([rb_b], [rb_b], rb[:, :bw], rb[:, :bw], AF.Exp, scale=-0.5)
                    V("dve", "tensor_tensor", [cq_b, rb_b], [cqn_b], out=cqn[:, :, :bw], in0=cqT[:, :, :bw],
                      in1=rb[:, :bw].unsqueeze(1).to_broadcast([128, 3, bw]), op=ALU.mult)
                    pctr = [0]

                    def prep(h):
                        us = h % 2
                        QnT, QrT, Q_b, gs, gs_b = QnT2[us], QrT2[us], Q2_b[us], gs2[us], gs2_b[us]
                        DMA("sp", wuq[us], wview(w_uq_b, h * 192, 192), [], [wuq_b[us]], f"wuq{us}")
                        pq, pqb = bank()
                        pr, prb = bank()
                        for j in range(3):
                            MM(pq[:, :bw], wuq[us][:, j, 0:128], cqn[:, j, :bw], j == 0, j == 2, [wuq_b[us], cqn_b], [pqb])
                        for j in range(3):
                            MM(pr[0:64, :bw], wuq[us][:, j, 128:192], cqn[:, j, :bw], j == 0, j == 2, [wuq_b[us], cqn_b], [prb])
                        yield
                        V("dve", "tensor_scalar_mul", [pqb, CONST], [qn_b], out=qn[:, :bw], in0=pq[:, :bw], scalar1=qgn[:, 0:1])
                        V("dve", "tensor_scalar_mul", [prb, CONST], [qr_b], out=qr[:, :bw], in0=pr[0:64, :bw], scalar1=qgr[:, 0:1])
                        ACT([pqb], [sqn_b], sqn[:, :bw], pq[:, :bw], AF.Square)
                        ACT([prb], [sqr_b], sqr[:, :bw], pr[0:64, :bw], AF.Square)
                        rel(pqb, prb)
                        yield
                        pb, pbb = bank()
                        MM(pb[:, :bw], onesB[:], sqn[:, :bw], True, False, [CONST, sqn_b], [pbb])
                        MM(pb[:, :bw], onesB[0:64, :], sqr[:, :bw], False, True, [CONST, sqr_b], [pbb])
                        yield
                        ACT([pbb], [rq_b], rq[:, :bw], pb[:, :bw], AF.Ln, scale=1.0 / 192, bias=EPS)
                        rel(pbb)
                        ACT([rq_b], [rq_b], rq[:, :bw], rq[:, :bw], AF.Exp, scale=-0.5)
                        V("pool", "tensor_copy", [qr_b], [qrb_b], out=qrb[:, :bw], in_=qr[:, :bw])
                        yield
                        V("dve", "tensor_tensor", [qn_b, rq_b], [Q_b], out=QnT[:, :bw], in0=qn[:, :bw], in1=rq[:, :bw], op=ALU.mult)
                        pb, pbb = bank()
                        MM(pb[0:64, :bw], rotB[:], qrb[:, :bw], True, True, [CONST, qrb_b], [pbb])
                        yield
                        V("dve", "tensor_tensor", [pbb, CONST], [t1_b], out=t1[:, :bw], in0=pb[0:64, :bw], in1=sin2[:, bc0:bc0 + bw], op=ALU.mult)
                        rel(pbb)
                        V("dve", "tensor_tensor", [qr_b, CONST], [t2_b], out=t2[:, :bw], in0=qr[:, :bw], in1=cos2[:, bc0:bc0 + bw], op=ALU.mult)
                        yield
                        V("dve", "tensor_tensor", [t1_b, t2_b], [t1_b], out=t1[:, :bw], in0=t1[:, :bw], in1=t2[:, :bw], op=ALU.add)
                        V("dve", "tensor_tensor", [t1_b, rq_b], [Q_b], out=QrT[:, :bw], in0=t1[:, :bw], in1=rq[0:64, :bw], op=ALU.mult)
                        wsl = wc[0] % 2; wc[0] += 1
                        DMA("sp", wst[wsl], wview(w_inb_b, 384 + h * 128, 128), [], [wst_b[wsl]], f"wst{wsl}")
                        pgt, pgtb = bank()
                        for kc in range(8):
                            MM(pgt[:, :bw], wst[wsl][:, kc, :], hT[:, kc, bc0:bc0 + bw], kc == 0, kc == 7, [wst_b[wsl]] + hrd, [pgtb])
                        yield
                        ACT([pgtb], [sig_b], sig[:, :bw], pgt[:, :bw], AF.Exp, scale=-1.0)
                        yield
                        ACT([sig_b], [sig_b], sig[:, :bw], sig[:, :bw], AF.Ln, bias=1.0)
                        ACT([sig_b], [sig_b], sig[:, :bw], sig[:, :bw], AF.Exp, scale=-1.0)
                        yield
                        V("dve", "tensor_tensor", [pgtb, sig_b], [gs_b], out=gs[:, :bw], in0=pgt[:, :bw], in1=sig[:, :bw], op=ALU.mult)
                        rel(pgtb)

                    def attn(h, nxt):
                        us = h % 2
                        QnT, QrT, Q_b, gs, gs_b = QnT2[us], QrT2[us], Q2_b[us], gs2[us], gs2_b[us]
                        pO, pOb = fbank(4 + 2 * (h % 2))
                        pD, pDb = fbank(5 + 2 * (h % 2))
                        def s_mm(kt):
                            o0 = (max(kt, q0) - q0) * 128
                            ks = slice(kt * 128, (kt + 1) * 128)
                            psc, pscb = bank()
                            diag = kt >= q0
                            MM(psc[:, o0:bw], KT[:, h, ks], QnT[:, o0:bw], True, False, [KT_b, Q_b], [pscb])
                            MM(psc[:, o0:bw], RR[:, ks], QrT[:, o0:bw], False, not diag, [RR_b, Q_b], [pscb])
                            if diag:
                                MM(psc[:, o0:o0 + 128], urowB[:], wrowB[:], False, True, [CONST], [pscb])
                            return psc, pscb

                        cur = s_mm(0)
                        for kt in range(0, q1 + 1):
                            o0 = (max(kt, q0) - q0) * 128
                            psc, pscb = cur
                            if kt + 1 <= q1:
                                cur = s_mm(kt + 1)
                            ps_ = pctr[0] % 3; pctr[0] += 1
                            ACT([pscb, ksc_b, CONST], [PT_b[ps_]], PT[ps_][:, o0:bw], psc[:, o0:bw], AF.Exp,
                                scale=ksc[:, kt, h:h + 1], bias=(padb[:, 0:1] if kt == 0 else zcol[:, 0:1]))
                            rel(pscb)
                            MM(pO[:, o0:bw], OV[:, kt, h * 128:(h + 1) * 128], PT[ps_][:, o0:bw], kt == 0, kt == q1, [OV_b[kt], PT_b[ps_]], [pOb])
                            MM(pD[:, o0:bw], onesB[:], PT[ps_][:, o0:bw], kt == 0, kt == q1, [CONST, PT_b[ps_]], [pDb])
                            if nxt is not None:
                                next(nxt, None)
                        if nxt is not None:
                            for _ in nxt:
                                pass
                        V("dve", "reciprocal", [pDb], [rd_b], out=rd_[:, :bw], in_=pD[:, :bw])
                        V("dve", "tensor_tensor", [pOb, rd_b], [ot_b], out=ot[:, :bw], in0=pO[:, :bw], in1=rd_[:, :bw], op=ALU.mult)
                        V("dve", "tensor_tensor", [ot_b, gs_b], [OGT_b], out=OGT[:, h, :bw], in0=ot[:, :bw], in1=gs[:, :bw], op=ALU.mult)

                    for _ in prep(0):
                        pass
                    for h in range(H):
                        attn(h, prep(h + 1) if h + 1 < H else None)
                    for qt in range(q0, q1 + 1):
                        sl = qt % 2
                        DMA("sp", xin[sl][:], h1_d[qt * 128:(qt + 1) * 128, :], [], [xin_b[sl]], f"xl{sl}")
                        lc = (qt - q0) * 128
                        for hf in range(2):
                            pb, pbb = bank()
                            for kc in range(8):
                                MM(pb[:, :], OGT[:, kc, lc:lc + 128], woutB[:, kc, hf * 512:(hf + 1) * 512], kc == 0, kc == 7,
                                   [OGT_b, wo_b], [pbb])
                            V("dve", "tensor_tensor", [pbb, xin_b[sl]], [xin_b[sl]], out=xin[sl][:, hf * 512:(hf + 1) * 512],
                              in0=pb[:, :], in1=xin[sl][:, hf * 512:(hf + 1) * 512], op=ALU.add)
                            rel(pbb)
                        r0 = s * SEQ + (qt - 1) * 128
                        DMA("act", out_d[r0:r0 + 128, :], xin[sl][:], [xin_b[sl]], [], f"os{sl}")
                S.barrier()
        except _Stop:
            S.barrier()
        S.emit(nc, st)
    nc._sched_info = S.info
    nc._sched = S
    return nc


_NC_CACHE = {}


def kernel(**inputs):
    x = np.ascontiguousarray(inputs["x"], dtype=np.float32)
    B, SEQ, _ = x.shape
    NSEQ = B // N_CORES
    key = (NSEQ, SEQ)
    if key not in _NC_CACHE:
        _NC_CACHE[key] = build(NSEQ, SEQ)
    nc = _NC_CACHE[key]
    consts = host_consts(SEQ + 128)
    shared = {}
    for k, v in inputs.items():
        if k == "x":
            continue
        a = np.ascontiguousarray(np.asarray(v, dtype=np.float32))
        if a.ndim >= 2 and a.shape[0] == 1:
            a = a[0]
        shared[k] = np.ascontiguousarray(a)
    shared.update(consts)
    in_maps = []
    for c in range(N_CORES):
        m = dict(shared)
        m["x"] = np.ascontiguousarray(x[c * NSEQ:(c + 1) * NSEQ].reshape(NSEQ * SEQ, D))
        in_maps.append(m)
    res = run_bass_kernel_spmd(nc, in_maps, core_ids=list(range(N_CORES)))
    out = np.concatenate([np.asarray(r["out"]).reshape(NSEQ, SEQ, D) for r in res.results], axis=0)
    return out.astype(np.float32)
```

```python
import numpy as np
from contextlib import ExitStack
import concourse.bass as bass
import concourse.mybir as mybir
from concourse.bass_utils import run_bass_kernel_spmd

F32 = mybir.dt.float32
BF16 = mybir.dt.bfloat16
AF = mybir.ActivationFunctionType
ALU = mybir.AluOpType
AX = mybir.AxisListType
ENGS = ("sp", "act", "pool", "dve", "pe")

D = 1024
H = 8
BIG = 30000.0
EPS = 1e-6
N_CORES = 8
CHAIN_DT = F32


class Buf:
    __slots__ = ("name", "w", "r", "excl")

    def __init__(self, name):
        self.name = name
        self.w = None
        self.r = {}
        self.excl = False


class Op:
    __slots__ = ("fn", "waits", "tok", "dma")

    def __init__(self, fn, waits, tok, dma):
        self.fn = fn
        self.waits = waits
        self.tok = tok
        self.dma = dma


class Sched:
    def __init__(self):
        self.ops = {e: [] for e in ENGS}
        self.count = {}
        self.known = {e: {} for e in ENGS}
        self.needed = set()
        self.nb = 0
        self.marks = []
        self.info = {}
        self.dbg = []
        self.names = {}

    def buf(self, name=None):
        self.nb += 1
        return Buf(name or f"b{self.nb}")

    def bufs(self, n, name="b"):
        return [self.buf(f"{name}{i}") for i in range(n)]

    def op(self, eng, fn, reads=(), writes=(), dma_dom=None):
        deps = {}

        def need(tok):
            if tok is not None and deps.get(tok[0], 0) < tok[1]:
                deps[tok[0]] = tok[1]

        own = dma_dom if dma_dom is not None else eng
        for b in reads:
            need(b.w)
            if b.excl:
                for d, i in b.r.items():
                    if d != own:
                        need((d, i))
        for b in writes:
            need(b.w)
            for d, i in b.r.items():
                need((d, i))
        is_dma = dma_dom is not None
        dom = dma_dom if is_dma else eng
        waits = {}
        kn = self.known[eng]
        for d, i in deps.items():
            if d == "pe" and eng == "pe" and not is_dma:
                continue
            if kn.get(d, 0) >= i:
                continue
            waits[d] = i
            kn[d] = i
            self.needed.add((d, i))
        c = self.count.get(dom, 0) + 1
        self.count[dom] = c
        tok = (dom, c)
        for b in reads:
            if b.r.get(dom, 0) < c:
                b.r[dom] = c
        for b in writes:
            b.w = tok
            b.r = {}
        o_ = Op(fn, waits, tok, is_dma)
        self.ops[eng].append(o_)
        self.dbg.append((o_, [b.name for b in reads], [b.name for b in writes]))
        return tok

    def barrier(self, label=None):
        self.marks.append((label, dict(self.count)))
        for e in ENGS:
            waits = {}
            kn = self.known[e]
            for d, c in self.count.items():
                if kn.get(d, 0) < c:
                    waits[d] = c
                    kn[d] = c
                    self.needed.add((d, c))
            if waits:
                self.ops[e].append(Op(None, waits, None, False))

    def emit(self, nc, stack):
        doms = sorted(self.count.keys())
        dma_doms = set()
        for e in ENGS:
            for o in self.ops[e]:
                if o.dma:
                    dma_doms.add(o.tok[0])
        sems = {d: stack.enter_context(nc.semaphore(f"s_{d}")) for d in doms}
        rank = {}
        for d in doms:
            if d in dma_doms:
                rank[d] = None
            else:
                idxs = sorted(i for (dd, i) in self.needed if dd == d)
                rank[d] = {i: k + 1 for k, i in enumerate(idxs)}
        block = stack.enter_context(nc.Block())
        ops = self.ops
        self.info = dict(sems={d: sems[d].num for d in doms},
                         marks=[(lab, {d: (rank[d][c] if rank[d] is not None else 16 * c) for d, c in cnt.items()
                                       if d in ("pe", "act", "dve", "pool")}) for lab, cnt in self.marks])

        def run(engh, lst):
            for o in lst:
                for d, i in o.waits.items():
                    engh.wait_ge(sems[d], 16 * i if rank[d] is None else rank[d][i])
                if o.fn is None:
                    continue
                ins = o.fn(engh)
                try:
                    self.names[ins.ins.name] = o
                except Exception:
                    pass
                d, i = o.tok
                if o.dma:
                    ins.then_inc(sems[d], 16)
                elif i in rank[d]:
                    ins.then_inc(sems[d], 1)

        @block.sync
        def _(e):
            run(e, ops["sp"])

        @block.scalar
        def _(e):
            run(e, ops["act"])

        @block.gpsimd
        def _(e):
            run(e, ops["pool"])

        @block.vector
        def _(e):
            run(e, ops["dve"])

        @block.tensor
        def _(e):
            run(e, ops["pe"])


def host_consts(T):
    p = np.arange(128)
    ident = np.eye(128, dtype=np.float32)
    tri = (p[:, None] <= p[None, :]).astype(np.float32)
    mui = np.where(p[None, :] < p[:, None], -BIG, 0.0).astype(np.float32)
    mls = np.where(p[None, :] >= p[:, None], BIG, 0.0).astype(np.float32)
    mus = np.where(p[None, :] <= p[:, None], -BIG, 0.0).astype(np.float32)
    rot = np.zeros((64, 64), np.float32)
    for m in range(32):
        rot[m + 32, m] = -1.0
        rot[m, m + 32] = 1.0
    pos = np.maximum(np.arange(T) - 112, 0).astype(np.float32)
    inv = (np.float32(10000.0) ** (-np.arange(32, dtype=np.float32) / np.float32(32))).astype(np.float32)
    ang = (pos[:, None] * inv[None, :]).astype(np.float32)
    cos = np.cos(ang).astype(np.float32).T
    sin = np.sin(ang).astype(np.float32).T
    cos2 = np.concatenate([cos, cos], 0).astype(np.float32)
    sin2 = np.concatenate([sin, sin], 0).astype(np.float32)
    padb = np.where(p < 112, -BIG, 0.0).astype(np.float32)[:, None]
    urow = np.where(p >= 64, 1.0, 0.0).astype(np.float32)[None, :]
    wrow = np.where(p < 64, -BIG, 0.0).astype(np.float32)[None, :]
    return dict(c_ident=ident, c_tri=tri, c_mui=mui, c_mls=mls, c_mus=mus, c_rot=rot, c_cos=cos2, c_sin=sin2,
                c_padb=padb, c_urow=urow, c_wrow=wrow)


class _Stop(Exception):
    pass


def build(NSEQ, SEQ, STOP=None):
    T = SEQ + 128
    NT = T // 128
    CB = [(c0, min(512, T - c0)) for c0 in range(0, T, 512)]
    nc = bass.Bass("TRN2", target_bir_lowering=False)

    def din(name, shape, dt=F32):
        return nc.dram_tensor(name, list(shape), dt, kind="ExternalInput").ap()

    x_d = din("x", [NSEQ * SEQ, D])
    meta_d = din("meta_tokens", [16, D])
    a_norm_d = din("a_norm", [D]); a_w_in_d = din("a_w_in", [D, 4112]); a_conv_d = din("a_conv", [4, 3072])
    a_log_d = din("a_log", [8]); a_dtb_d = din("a_dt_bias", [8]); a_og_d = din("a_o_gain", [128])
    a_w_out_d = din("a_w_out", [D, D]); kv_norm_d = din("kv_norm", [D]); kv_wd_d = din("kv_w_down", [D, 320])
    kv_ln_d = din("kv_latent_norm", [256]); kv_uk_d = din("kv_w_uk", [256, D]); kv_uv_d = din("kv_w_uv", [256, D])
    k_gain_d = din("k_gain", [192]); b_norm_d = din("b_norm", [D]); b_w_in_d = din("b_w_in", [D, 1408])
    b_qln_d = din("b_q_latent_norm", [384]); b_uq_d = din("b_w_uq", [384, 1536]); b_qg_d = din("b_q_gain", [192])
    b_w_out_d = din("b_w_out", [D, D])
    c_ident_d = din("c_ident", [128, 128]); c_tri_d = din("c_tri", [128, 128]); c_mui_d = din("c_mui", [128, 128])
    c_mls_d = din("c_mls", [128, 128]); c_mus_d = din("c_mus", [128, 128]); c_rot_d = din("c_rot", [64, 64]); c_cos_d = din("c_cos", [64, T])
    c_sin_d = din("c_sin", [64, T]); c_padb_d = din("c_padb", [128, 1]); c_urow_d = din("c_urow", [1, 128])
    c_wrow_d = din("c_wrow", [1, 128])
    out_d = nc.dram_tensor("out", [NSEQ * SEQ, D], F32, kind="ExternalOutput").ap()

    def dscr(name, shape, dt):
        return nc.dram_tensor(name, list(shape), dt).ap()

    w_in_b = dscr("w_in_b", [D, 4112], BF16); w_outa_b = dscr("w_outa_b", [D, D], BF16)
    w_down_b = dscr("w_down_b", [D, 320], BF16); w_uk_b = dscr("w_uk_b", [256, D], BF16)
    w_uv_b = dscr("w_uv_b", [256, D], BF16); w_inb_b = dscr("w_inb_b", [D, 1408], BF16)
    w_uq_b = dscr("w_uq_b", [384, 1536], BF16); w_outb_b = dscr("w_outb_b", [D, D], BF16)
    h1_d = dscr("h1_d", [T, D], F32)

    S = Sched()
    with ExitStack() as st:
        st.enter_context(nc.allow_non_contiguous_dma("small strided parameter loads"))

        def sb(name, shape, dt):
            return st.enter_context(nc.sbuf_tensor(name, list(shape), dt))

        def V(eng, fname, reads, writes, *a, **k):
            S.op(eng, lambda e: getattr(e, fname)(*a, **k), reads, writes)

        def ACT(reads, writes, out, in_, func, **k):
            S.op("act", lambda e: e.activation(out=out, in_=in_, func=func, **k), reads, writes)

        def MM(out, lhsT, rhs, start, stop, reads, writes):
            S.op("pe", lambda e: e.matmul(out, lhsT=lhsT, rhs=rhs, start=start, stop=stop), reads, writes)

        def TR(out, in_, ident, reads, writes):
            S.op("pe", lambda e: e.transpose(out=out, in_=in_, identity=ident), reads, writes)

        def DMA(q, out, in_, reads, writes, dom):
            S.op(q, lambda e: e.dma_start(out=out, in_=in_), reads, writes, dma_dom=dom)

        PB = [st.enter_context(nc.psum_tensor(f"pb{i}", [128, 512], F32)) for i in range(8)]
        PBb = S.bufs(8, "pb")
        for _b in PBb:
            _b.excl = True
        bctr = [0]
        bpool = [list(range(8))]

        live = set()

        def bank():
            pool_ = bpool[0]
            for k in range(len(pool_)):
                i = pool_[(bctr[0] + k) % len(pool_)]
                if i not in live:
                    bctr[0] += k + 1
                    live.add(i)
                    return PB[i], PBb[i]
            raise RuntimeError("no free PSUM bank")

        def rel(*bbs):
            for bb in bbs:
                live.discard(PBb.index(bb))

        def fbank(i):
            return PB[i], PBb[i]

        try:
            identF = sb("identF", [128, 128], F32); identB = sb("identB", [128, 128], BF16)
            triF = sb("triF", [128, 128], F32); muiF = sb("muiF", [128, 128], F32); mlsF = sb("mlsF", [128, 128], F32)
            musF = sb("musF", [128, 128], F32)
            onesB = sb("onesB", [128, 128], BF16); rotF = sb("rotF", [64, 64], F32); rotB = sb("rotB", [64, 64], BF16)
            cos2 = sb("cos2", [64, T], BF16); sin2 = sb("sin2", [64, T], BF16)
            padb = sb("padb", [128, 1], F32); zcol = sb("zcol", [128, 1], F32)
            urowF = sb("urowF", [1, 128], F32); wrowF = sb("wrowF", [1, 128], F32)
            urowB = sb("urowB", [1, 128], BF16); wrowB = sb("wrowB", [1, 128], BF16)
            g_anorm = sb("g_anorm", [128, 8], F32); g_kvnorm = sb("g_kvnorm", [128, 8], F32)
            g_bnorm = sb("g_bnorm", [128, 8], F32); g_kvln = sb("g_kvln", [128, 2], F32)
            g_qln = sb("g_qln", [128, 3], F32); g_og = sb("g_og", [128, 1], F32)
            kgn = sb("kgn", [128, 1], F32); kgr = sb("kgr", [64, 1], F32)
            qgn = sb("qgn", [128, 1], F32); qgr = sb("qgr", [64, 1], F32)
            convw = sb("convw", [128, 4, 24], F32)
            alog = sb("alog", [128, 8], F32); negA = sb("negA", [128, 8], F32); dtb = sb("dtb", [128, 8], F32)
            CONST = S.buf("const")

            def cload(dst, src):
                DMA("sp", dst, src, [], [CONST], "cst")

            cload(identF[:], c_ident_d); cload(triF[:], c_tri_d); cload(muiF[:], c_mui_d); cload(mlsF[:], c_mls_d); cload(musF[:], c_mus_d)
            cload(rotF[:], c_rot_d); cload(padb[:], c_padb_d)
            cload(urowF[:], c_urow_d); cload(wrowF[:], c_wrow_d)
            cload(g_anorm[:], a_norm_d.rearrange("(k p) -> p k", p=128))
            cload(g_kvnorm[:], kv_norm_d.rearrange("(k p) -> p k", p=128))
            cload(g_bnorm[:], b_norm_d.rearrange("(k p) -> p k", p=128))
            cload(g_kvln[:], kv_ln_d.rearrange("(k p) -> p k", p=128))
            cload(g_qln[:], b_qln_d.rearrange("(k p) -> p k", p=128))
            cload(g_og[:], a_og_d.rearrange("(p o) -> p o", o=1))
            cload(kgn[:], k_gain_d[0:128].rearrange("(p o) -> p o", o=1))
            cload(kgr[:], k_gain_d[128:192].rearrange("(p o) -> p o", o=1))
            cload(qgn[:], b_qg_d[0:128].rearrange("(p o) -> p o", o=1))
            cload(qgr[:], b_qg_d[128:192].rearrange("(p o) -> p o", o=1))
            for jj in range(4):
                cload(convw[:, jj, :], a_conv_d[jj, :].rearrange("(c p) -> p c", p=128))
            cload(alog[:], a_log_d.partition_broadcast(128))
            cload(dtb[:], a_dtb_d.partition_broadcast(128))
            CONST.w = ("cst", S.count["cst"])
            V("dve", "tensor_copy", [CONST], [CONST], out=identB[:], in_=identF[:])
            V("dve", "tensor_copy", [CONST], [CONST], out=rotB[:], in_=rotF[:])
            V("dve", "tensor_copy", [CONST], [CONST], out=urowB[:], in_=urowF[:])
            V("dve", "tensor_copy", [CONST], [CONST], out=wrowB[:], in_=wrowF[:])
            V("dve", "memset", [], [CONST], onesB[:], 1.0)
            V("dve", "memset", [], [CONST], zcol[:], 0.0)
            ACT([CONST], [CONST], negA[:], alog[:], AF.Exp)
            V("dve", "tensor_scalar_mul", [CONST], [CONST], out=negA[:], in0=negA[:], scalar1=-1.0)

            hT = sb("hT", [128, 8, T], BF16); hT_b = S.bufs(NT, "hT")
            OV = sb("OV", [128, NT, 1024], BF16); OV_b = S.bufs(NT, "OV")
            RR = sb("RR", [64, T], BF16); RR_b = S.buf("RR")
            xin = [sb(f"xin{i}", [128, 1024], F32) for i in range(2)]; xin_b = S.bufs(2, "xin")
            xn = [sb(f"xn{i}", [128, 1024], BF16) for i in range(2)]; xn_b = S.bufs(2, "xn")
            junk = sb("junk", [128, 1024], BF16); junk_b = S.buf("junk")
            ssq = [sb(f"ssq{i}", [128, 1], F32) for i in range(2)]; ssq_b = S.bufs(2, "ssq")
            ARENA_F32 = 27136
            arena = sb("arena", [128, ARENA_F32], F32)
            apos = [0]

            def areset(off=0):
                apos[0] = off

            def carve(shape, dt):
                P = shape[0]
                n = int(np.prod(shape[1:]))
                nbytes = n * (4 if dt == F32 else 2)
                n32 = (nbytes + 3) // 4
                o = apos[0]
                apos[0] += (n32 + 7) // 8 * 8
                assert apos[0] <= ARENA_F32, ("arena overflow", apos[0])
                v = arena[0:P, o:o + n32]
                if dt != F32:
                    v = v.bitcast(dt)[:, 0:n]
                if len(shape) == 3:
                    v = v.rearrange("p (a b) -> p a b", a=shape[1], b=shape[2])
                return v

            areset(0)
            wl = [carve([128, 1024], F32) for i in range(2)]; wl_b = S.bufs(2, "wl")
            ws = [carve([128, 1024], BF16) for i in range(2)]; ws_b = S.bufs(2, "ws")
            for tab_d, tab in ((c_cos_d, cos2), (c_sin_d, sin2)):
                for c0 in range(0, T, 1024):
                    cw = min(1024, T - c0)
                    sl = wctr_ = 0
                    DMA("sp", wl[0][0:64, :cw], tab_d[:, c0:c0 + cw], [], [wl_b[0]], "wl0")
                    V("dve", "tensor_copy", [wl_b[0]], [CONST], out=tab[:, c0:c0 + cw], in_=wl[0][0:64, :cw])
            wctr = [0]

            def prep_weight(src, K, N, dst, gcol):
                for kc in range(K // 128):
                    for c0 in range(0, N, 1024):
                        cw = min(1024, N - c0)
                        sl = wctr[0] % 2
                        wctr[0] += 1
                        DMA("sp", wl[sl][:, :cw], src[kc * 128:(kc + 1) * 128, c0:c0 + cw], [], [wl_b[sl]], f"wl{sl}")
                        if gcol is not None:
                            V("dve", "tensor_scalar_mul", [wl_b[sl], CONST], [ws_b[sl]], out=ws[sl][:, :cw],
                              in0=wl[sl][:, :cw], scalar1=gcol(kc))
                        else:
                            V("dve", "tensor_copy", [wl_b[sl]], [ws_b[sl]], out=ws[sl][:, :cw], in_=wl[sl][:, :cw])
                        DMA("act", dst[kc * 128:(kc + 1) * 128, c0:c0 + cw], ws[sl][:, :cw], [ws_b[sl]], [], f"ws{sl}")

            prep_weight(a_w_in_d, D, 4112, w_in_b, lambda kc: g_anorm[:, kc:kc + 1])
            prep_weight(a_w_out_d, D, D, w_outa_b, lambda kc: g_og[:, 0:1])
            prep_weight(kv_wd_d, D, 320, w_down_b, lambda kc: g_kvnorm[:, kc:kc + 1])
            prep_weight(kv_uk_d, 256, D, w_uk_b, lambda kc: g_kvln[:, kc:kc + 1])
            prep_weight(kv_uv_d, 256, D, w_uv_b, lambda kc: g_kvln[:, kc:kc + 1])
            prep_weight(b_w_in_d, D, 1408, w_inb_b, lambda kc: g_bnorm[:, kc:kc + 1])
            prep_weight(b_uq_d, 384, 1536, w_uq_b, lambda kc: g_qln[:, kc:kc + 1])
            prep_weight(b_w_out_d, D, D, w_outb_b, None)
            S.barrier()
            if STOP == "W":
                raise _Stop()

            def wview(wb, c0, cw):
                return wb[:, c0:c0 + cw].rearrange("(k p) n -> p k n", p=128)

            def load_x_tile(s, t, sl):
                if t == 0:
                    V("pool", "memset", [], [xin_b[sl]], xin[sl][:], 0.0)
                    DMA("sp", xin[sl][112:128, :], meta_d, [], [xin_b[sl]], f"xl{sl}")
                else:
                    r0 = s * SEQ + (t - 1) * 128
                    DMA("sp", xin[sl][:], x_d[r0:r0 + 128, :], [], [xin_b[sl]], f"xl{sl}")

            def norm_to_hT(src, src_b, t, sl):
                ACT([src_b], [junk_b, ssq_b[sl]], junk[:], src, AF.Square, accum_out=ssq[sl][:])
                ACT([ssq_b[sl]], [ssq_b[sl]], ssq[sl][:], ssq[sl][:], AF.Ln, scale=1.0 / D, bias=EPS)
                ACT([ssq_b[sl]], [ssq_b[sl]], ssq[sl][:], ssq[sl][:], AF.Exp, scale=-0.5)
                V("dve", "tensor_scalar_mul", [src_b, ssq_b[sl]], [xn_b[sl]], out=xn[sl][:], in0=src, scalar1=ssq[sl][:, 0:1])
                pb, pbb = bank()
                pbv = pb[:].bitcast(BF16)
                for kc in range(8):
                    TR(pbv[:, kc * 128:(kc + 1) * 128], xn[sl][:, kc * 128:(kc + 1) * 128], identB[:], [xn_b[sl], CONST], [pbb])
                V("dve", "tensor_copy", [pbb], [hT_b[t]], out=hT[:, :, t * 128:(t + 1) * 128],
                  in_=pbv.rearrange("p (k c) -> p k c", k=8))
                rel(pbb)

            def hT_reads(c0, cw):
                return [hT_b[t] for t in range(c0 // 128, (c0 + cw + 127) // 128)]

            def silu_from(src_ap, src_reads, tmp, tmp_b, shape_sl):
                ACT(src_reads, [tmp_b], tmp, src_ap, AF.Exp, scale=-1.0)
                ACT([tmp_b], [tmp_b], tmp, tmp, AF.Ln, bias=1.0)
                ACT([tmp_b], [tmp_b], tmp, tmp, AF.Exp, scale=-1.0)

            def rsqrt_inplace(ap, b, scale, reads_extra=()):
                ACT([b] + list(reads_extra), [b], ap, ap, AF.Ln, scale=scale, bias=EPS)
                ACT([b], [b], ap, ap, AF.Exp, scale=-0.5)

            for s in range(NSEQ):
                bpool[0] = list(range(8))
                for t in range(NT):
                    sl = t % 2
                    load_x_tile(s, t, sl)
                    norm_to_hT(xin[sl][:], xin_b[sl], t, sl)
                S.barrier()
                if STOP == "A0":
                    raise _Stop()
                areset(0)
                wab = carve([128, 8, 16], BF16); wab_b = S.buf()
                ab = carve([128, NT, 16], F32); ab_b = S.buf()
                tm1 = carve([128, NT, 8], F32); tm1_b = S.buf()
                gcol = carve([128, NT, 8], F32); g_b = S.buf()
                lnb = carve([128, NT, 8], F32); lnb_b = S.buf()
                gc = carve([128, NT, 8], F32); gc_b = S.buf()
                ngc = carve([128, NT, 8], F32); egc = carve([128, NT, 8], F32); bls = carve([128, NT, 8], F32)
                beta = carve([128, NT, 8], F32); begc = carve([128, NT, 8], F32)
                der_b = S.buf()
                DMA("sp", wab, wview(w_in_b, 4096, 16), [], [wab_b], "wab")
                pb, pbb = bank()
                for t in range(NT):
                    for kc in range(8):
                        MM(pb[:, t * 16:(t + 1) * 16], hT[:, kc, t * 128:(t + 1) * 128], wab[:, kc, :], kc == 0, kc == 7,
                           [hT_b[t], wab_b], [pbb])
                V("dve", "tensor_copy", [pbb], [ab_b], out=ab, in_=pb[:, 0:NT * 16].rearrange("p (t c) -> p t c", c=16))
                rel(pbb)
                ACT([ab_b], [tm1_b], tm1, ab[:, :, 0:8], AF.Exp, scale=-1.0)
                ACT([tm1_b], [tm1_b], tm1, tm1, AF.Ln, bias=1.0)
                V("dve", "tensor_scalar_mul", [tm1_b], [lnb_b], out=lnb, in0=tm1, scalar1=-1.0)
                V("dve", "tensor_tensor", [ab_b, CONST], [tm1_b], out=tm1, in0=ab[:, :, 8:16],
                  in1=dtb[:].unsqueeze(1).to_broadcast([128, NT, 8]), op=ALU.add)
                ACT([tm1_b], [tm1_b], tm1, tm1, AF.Exp)
                ACT([tm1_b], [tm1_b], tm1, tm1, AF.Ln, bias=1.0)
                V("dve", "tensor_tensor", [tm1_b, CONST], [g_b], out=gcol, in0=tm1,
                  in1=negA[:].unsqueeze(1).to_broadcast([128, NT, 8]), op=ALU.mult)
                pb, pbb = bank()
                for t in range(NT):
                    MM(pb[:, t * 8:(t + 1) * 8], triF[:], gcol[:, t, :], True, True, [CONST, g_b], [pbb])
                V("dve", "tensor_copy", [pbb], [gc_b], out=gc, in_=pb[:, 0:NT * 8].rearrange("p (t c) -> p t c", c=8))
                rel(pbb)
                V("dve", "tensor_scalar_mul", [gc_b], [der_b], out=ngc, in0=gc, scalar1=-1.0)
                ACT([gc_b], [der_b], egc, gc, AF.Exp)
                V("dve", "tensor_tensor", [gc_b, lnb_b], [der_b], out=bls, in0=gc, in1=lnb, op=ALU.add)
                ACT([lnb_b], [der_b], beta, lnb, AF.Exp)
                ACT([der_b], [der_b], begc, bls, AF.Exp)

                if STOP == "Aab":
                    raise _Stop()
                a12_base = apos[0]
                for h in range(1):
                    areset(a12_base)
                    wst = [carve([128, 8, 128], BF16) for _ in range(2)]; wst_b = S.bufs(2)
                    zc = [carve([128, 515], F32) for _ in range(2)]; zc_b = S.bufs(2)
                    taL = [carve([128, 512], F32) for _ in range(2)]; taL_b = S.bufs(2)
                    tbL = [carve([128, 512], F32) for _ in range(2)]; tbL_b = S.bufs(2)
                    sqL = [carve([128, 512], BF16) for _ in range(2)]; sqL_b = S.bufs(2)
                    blkc = [0]
                    deferred = [None]
                    sT = [carve([128, T], BF16) for _ in range(3)]; sT_b = S.bufs(3)
                    Ktok = carve([128, NT, 128], BF16); Vtok = carve([128, NT, 128], BF16); KV_b = S.bufs(2)
                    Sst = carve([128, 128], F32); Sbf = carve([128, 128], BF16); S_b = S.buf(); Sb_b = S.buf()
                    NSL = 8
                    cm = []
                    for i in range(NSL):
                        cm.append(dict(
                            Eui=carve([128, 128], F32), Els=carve([128, 128], F32), Eus=carve([128, 128], F32),
                            Z=[carve([128, 256], CHAIN_DT) for _ in range(2)], P=[carve([128, 128], CHAIN_DT) for _ in range(2)],
                            attT=carve([128, 128], BF16), TmT=carve([128, 128], BF16), nWdT=carve([128, 128], BF16),
                            Bk=carve([128, 128], BF16), bV=carve([128, 128], BF16), Kd=carve([128, 128], BF16),
                            Ub=carve([128, 128], BF16), glc=carve([128, 1], F32), o1=carve([128, 128], F32),
                            b={k: S.buf() for k in ("Eui", "Els", "Eus", "Z0", "Z1", "P0", "P1", "attT", "TmT", "nWdT", "Bk", "bV",
                                                    "Kd", "Ub", "glc", "o1")}))
                for h in range(H):
                    for j in range(3):
                        fc = j * 8 + h
                        wsl = j % 2
                        DMA("sp", wst[wsl], wview(w_in_b, fc * 128, 128), [], [wst_b[wsl]], f"wst{wsl}")
                        for bi, (c0, cw) in enumerate(CB):
                            zs = bi % 2
                            bk_ = blkc[0] % 2; blkc[0] += 1
                            ta, ta_b, tb, tb_b, sq, sq_b = taL[bk_], taL_b[bk_], tbL[bk_], tbL_b[bk_], sqL[bk_], sqL_b[bk_]
                            pb, pbb = bank()
                            for kc in range(8):
                                MM(pb[:, :cw], wst[wsl][:, kc, :], hT[:, kc, c0:c0 + cw], kc == 0, kc == 7,
                                   [wst_b[wsl]] + hT_reads(c0, cw), [pbb])
                            if bi == 0:
                                V("pool", "memset", [], [zc_b[zs]], zc[zs][:, 0:3], 0.0)
                            else:
                                V("pool", "tensor_copy", [zc_b[1 - zs]], [zc_b[zs]], out=zc[zs][:, 0:3], in_=zc[1 - zs][:, 512:515])
                            S.op("act", (lambda o, i: (lambda e: e.copy(out=o, in_=i)))(zc[zs][:, 3:3 + cw], pb[:, :cw]),
                                 [pbb], [zc_b[zs]])
                            rel(pbb)
                            V("dve", "tensor_scalar_mul", [zc_b[zs], CONST], [ta_b], out=ta[:, :cw], in0=zc[zs][:, 3:3 + cw],
                              scalar1=convw[:, 3, fc:fc + 1])
                            for jj in (2, 1, 0):
                                V("dve", "scalar_tensor_tensor", [zc_b[zs], CONST, ta_b], [ta_b], out=ta[:, :cw],
                                  in0=zc[zs][:, jj:jj + cw], scalar=convw[:, jj, fc:fc + 1], in1=ta[:, :cw],
                                  op0=ALU.mult, op1=ALU.add)
                            silu_from(ta[:, :cw], [ta_b], tb[:, :cw], tb_b, None)
                            V("dve", "tensor_tensor", [ta_b, tb_b], [sT_b[j]], out=sT[j][:, c0:c0 + cw], in0=ta[:, :cw],
                              in1=tb[:, :cw], op=ALU.mult)
                            if j < 2:
                                ACT([sT_b[j]], [sq_b], sq[:, :cw], sT[j][:, c0:c0 + cw], AF.Square)
                                pb2, pbb2 = bank()
                                MM(pb2[:, :cw], onesB[:], sq[:, :cw], True, True, [CONST, sq_b], [pbb2])

                                def stage_b(pb2=pb2, pbb2=pbb2, tb=tb, tb_b=tb_b, j=j, c0=c0, cw=cw):
                                    ACT([pbb2], [tb_b], tb[:, :cw], pb2[:, :cw], AF.Ln, bias=EPS)
                                    rel(pbb2)
                                    ACT([tb_b], [tb_b], tb[:, :cw], tb[:, :cw], AF.Exp, scale=-0.5)
                                    V("dve", "scalar_tensor_tensor", [sT_b[j], tb_b], [sT_b[j]], out=sT[j][:, c0:c0 + cw],
                                      in0=sT[j][:, c0:c0 + cw], scalar=(128.0 ** -0.5 if j == 0 else 1.0), in1=tb[:, :cw],
                                      op0=ALU.mult, op1=ALU.mult)

                                prev_b, deferred[0] = deferred[0], stage_b
                            else:
                                prev_b, deferred[0] = deferred[0], None
                            if prev_b is not None:
                                prev_b()
                    if deferred[0] is not None:
                        deferred[0]()
                        deferred[0] = None
                    qT, kT, vT = sT
                    if STOP == "A12a":
                        raise _Stop()
                    for j, dst in ((1, Ktok), (2, Vtok)):
                        for t0 in range(0, NT, 8):
                            tn = min(8, NT - t0)
                            pb, pbb = bank()
                            pbv = pb[:].bitcast(BF16)
                            for i in range(tn):
                                TR(pbv[:, i * 128:(i + 1) * 128], sT[j][:, (t0 + i) * 128:(t0 + i + 1) * 128], identB[:],
                                   [sT_b[j], CONST], [pbb])
                            V("dve", "tensor_copy", [pbb], [KV_b[j - 1]], out=dst[:, t0:t0 + tn, :],
                              in_=pbv[:, 0:tn * 128].rearrange("p (t c) -> p t c", c=128))
                            rel(pbb)
                    V("pool", "memset", [], [S_b], Sst[:], 0.0)
                    V("pool", "memset", [], [Sb_b], Sbf[:], 0.0)
                    if STOP == "A12b":
                        raise _Stop()
                    def st_G(c):
                        m = cm[c % NSL]; mb = m["b"]
                        cs = slice(c * 128, (c + 1) * 128)
                        gb_l = gcol[:, c, h:h + 1].to_broadcast([128, 128])
                        lb_l = lnb[:, c, h:h + 1].to_broadcast([128, 128])
                        pg, pgb = bank()
                        pt, ptb = bank()
                        m["pg"], m["pgb"], m["pt"], m["ptb"] = pg, pgb, pt, ptb
                        MM(pg[:, 0:128], gb_l, triF[:], True, False, [g_b, CONST], [pgb])
                        MM(pg[:, 0:128], identF[:], muiF[:], False, True, [CONST], [pgb])
                        MM(pg[:, 128:256], gb_l, triF[:], True, False, [g_b, CONST], [pgb])
                        MM(pg[:, 128:256], identF[:], mlsF[:], False, True, [CONST], [pgb])
                        MM(pg[:, 256:384], kT[:, cs], kT[:, cs], True, True, [sT_b[1]], [pgb])
                        MM(pg[:, 384:512], kT[:, cs], qT[:, cs], True, True, [sT_b[1], sT_b[0]], [pgb])
                        MM(pt[:, 0:128], gb_l, triF[:], True, False, [g_b, CONST], [ptb])
                        MM(pt[:, 0:128], lb_l, identF[:], False, False, [lnb_b, CONST], [ptb])
                        MM(pt[:, 0:128], identF[:], musF[:], False, True, [CONST], [ptb])

                    def st_E(c):
                        m = cm[c % NSL]; mb = m["b"]
                        pg, pgb, pt, ptb = m["pg"], m["pgb"], m["pt"], m["ptb"]
                        ACT([pgb, der_b], [mb["Eui"]], m["Eui"], pg[:, 0:128], AF.Exp, bias=ngc[:, c, h:h + 1], scale=1.0)
                        ACT([pgb, der_b], [mb["Els"]], m["Els"], pg[:, 128:256], AF.Exp, bias=bls[:, c, h:h + 1], scale=-1.0)
                        ACT([pgb], [mb["glc"]], m["glc"], pg[:, 127:128], AF.Exp)
                        ACT([ptb, der_b], [mb["Eus"]], m["Eus"], pt[:, 0:128], AF.Exp, bias=ngc[:, c, h:h + 1], scale=1.0)
                        V("dve", "scalar_tensor_tensor", [pgb, mb["Els"]], [mb["Z0"]], out=m["Z"][0][:, 0:128],
                          in0=pg[:, 256:384], scalar=-1.0, in1=m["Els"], op0=ALU.mult, op1=ALU.mult)
                        V("dve", "scalar_tensor_tensor", [pgb, mb["Eus"]], [mb["Z0"]], out=m["Z"][0][:, 128:256],
                          in0=pg[:, 256:384], scalar=-1.0, in1=m["Eus"], op0=ALU.mult, op1=ALU.mult)
                        V("dve", "tensor_tensor", [pgb, mb["Eui"]], [mb["attT"]], out=m["attT"], in0=pg[:, 384:512],
                          in1=m["Eui"], op=ALU.mult)
                        V("dve", "tensor_tensor", [mb["Z0"], CONST], [mb["P0"]], out=m["P"][0], in0=m["Z"][0][:, 128:256],
                          in1=identF[:], op=ALU.add)
                        V("dve", "tensor_scalar_mul", [KV_b[0], der_b], [mb["Bk"]], out=m["Bk"], in0=Ktok[:, c, :],
                          scalar1=begc[:, c, h:h + 1])
                        V("dve", "tensor_scalar_mul", [KV_b[1], der_b], [mb["bV"]], out=m["bV"], in0=Vtok[:, c, :],
                          scalar1=beta[:, c, h:h + 1])
                        V("dve", "tensor_scalar_mul", [KV_b[0], mb["Eui"]], [mb["Kd"]], out=m["Kd"], in0=Ktok[:, c, :],
                          scalar1=m["Eui"][:, 127:128])
                        rel(pgb, ptb)

                    def st_Lmm(c, lev):
                        m = cm[c % NSL]; mb = m["b"]
                        zi = lev % 2
                        pi = (lev - 1) % 2
                        Zc, Zcb = m["Z"][zi], mb[f"Z{zi}"]
                        pk, pkb = bank()
                        m["pk"], m["pkb"] = pk, pkb
                        if lev >= 1:
                            MM(pk[:, 256:384], Zc[:, 0:128], m["P"][pi], True, True, [Zcb, mb[f"P{pi}"]], [pkb])
                        if lev < 6:
                            MM(pk[:, 0:128], Zc[:, 128:256], Zc[:, 0:128], True, True, [Zcb], [pkb])
                            MM(pk[:, 128:256], Zc[:, 0:128], Zc[:, 128:256], True, True, [Zcb], [pkb])

                    def st_Lev(c, lev):
                        m = cm[c % NSL]; mb = m["b"]
                        zi = lev % 2
                        pi = (lev - 1) % 2
                        Zn, Znb = m["Z"][1 - zi], mb[f"Z{1 - zi}"]
                        pk, pkb = m["pk"], m["pkb"]
                        if lev >= 1:
                            if lev < 6:
                                V("dve", "tensor_tensor", [pkb, mb[f"P{pi}"]], [mb[f"P{1 - pi}"]], out=m["P"][1 - pi],
                                  in0=pk[:, 256:384], in1=m["P"][pi], op=ALU.add)
                            else:
                                V("dve", "tensor_tensor", [pkb, mb[f"P{pi}"]], [mb["TmT"]], out=m["TmT"],
                                  in0=pk[:, 256:384], in1=m["P"][pi], op=ALU.add)
                        if lev < 6:
                            V("act", "copy", [pkb], [Znb], out=Zn[:, 0:256], in_=pk[:, 0:256])
                        rel(pkb)

                    def st_W(c):
                        m = cm[c % NSL]; mb = m["b"]
                        pw, pwb = bank()
                        MM(pw[:, 0:128], m["Bk"], m["TmT"], True, True, [mb["Bk"], mb["TmT"]], [pwb])
                        V("act", "mul", [pwb], [mb["nWdT"]], out=m["nWdT"], in_=pw[:, 0:128], mul=-1.0)
                        rel(pwb)

                    def st_Ra(c):
                        m = cm[c % NSL]; mb = m["b"]
                        pu, pub = bank()
                        MM(pu[:, 0:128], m["TmT"], m["bV"], True, c == 0, [mb["TmT"], mb["bV"]], [pub])
                        if c > 0:
                            MM(pu[:, 0:128], m["nWdT"], Sbf, False, True, [mb["nWdT"], Sb_b], [pub])
                        V("act", "copy", [pub], [mb["Ub"]], out=m["Ub"], in_=pu[:, 0:128])
                        rel(pub)

                    def st_Rb(c):
                        m = cm[c % NSL]; mb = m["b"]
                        cs = slice(c * 128, (c + 1) * 128)
                        po, pob = bank()
                        po2, pob2 = bank()
                        MM(po[:, 0:128], m["Kd"], m["Ub"], True, True, [mb["Kd"], mb["Ub"]], [pob])
                        MM(po2[:, 0:128], qT[:, cs], Sbf, True, True, [sT_b[0], Sb_b], [pob2])
                        MM(po2[:, 128:256], m["attT"], m["Ub"], True, True, [mb["attT"], mb["Ub"]], [pob2])
                        V("dve", "scalar_tensor_tensor", [S_b, mb["glc"], pob], [Sb_b], out=Sbf[:], in0=Sst[:],
                          scalar=m["glc"][:, 0:1], in1=po[:, 0:128], op0=ALU.mult, op1=ALU.add)
                        V("dve", "scalar_tensor_tensor", [S_b, mb["glc"], pob], [S_b], out=Sst[:], in0=Sst[:],
                          scalar=m["glc"][:, 0:1], in1=po[:, 0:128], op0=ALU.mult, op1=ALU.add)
                        ACT([pob2, der_b], [mb["o1"]], m["o1"], po2[:, 0:128], AF.Copy, scale=egc[:, c, h:h + 1])
                        V("dve", "tensor_tensor", [pob2, mb["o1"]], [OV_b[c]], out=OV[:, c, h * 128:(h + 1) * 128],
                          in0=po2[:, 128:256], in1=m["o1"], op=ALU.add)
                        rel(pob, pob2)

                    def st_R(item):
                        (st_Ra if item[0] == "a" else st_Rb)(item[1])

                    GS = 4
                    groups = [list(range(i, min(i + GS, NT))) for i in range(0, NT, GS)]
                    pending = []
                    for grp in groups:
                        stages = [("GE", grp[0:2]), ("GE", grp[2:4])] + [("L", lev) for lev in range(7)] + [("W", None)]
                        for kind, lev in stages:
                            if kind == "GE":
                                for c in lev:
                                    st_G(c)
                                for c in lev:
                                    st_E(c)
                            elif kind == "L":
                                for c in grp:
                                    st_Lmm(c, lev)
                                for c in grp:
                                    st_Lev(c, lev)
                            else:
                                for c in grp:
                                    st_W(c)
                            if pending:
                                st_R(pending.pop(0))
                        while pending:
                            st_R(pending.pop(0))
                        pending = [(ph, c) for c in grp for ph in ("a", "b")]
                    while pending:
                        st_R(pending.pop(0))
                S.barrier()
                if STOP == "A12":
                    raise _Stop()
                areset(0)
                gateW = carve([128, 8, 1024], BF16); outW = carve([128, 8, 1024], BF16); gw_b = S.buf(); ow_b = S.buf()
                sigL = [carve([128, 1024], F32) for _ in range(2)]; sigL_b = S.bufs(2)
                gsL = [carve([128, 1024], F32) for _ in range(2)]; gsL_b = S.bufs(2)
                osqL = [carve([128, 1024], F32) for _ in range(2)]; osqL_b = S.bufs(2)
                ossL = [carve([128, 8], F32) for _ in range(2)]; ossL_b = S.bufs(2)
                ogbL = [carve([128, 1024], BF16) for _ in range(2)]; ogbL_b = S.bufs(2)
                ogTL = [carve([128, 8, 128], BF16) for _ in range(2)]; ogTL_b = S.bufs(2)
                h1t = [carve([128, 1024], F32) for _ in range(2)]; h1t_b = S.bufs(2)
                DMA("sp", gateW, wview(w_in_b, 3072, 1024), [], [gw_b], "gw")
                DMA("sp", outW, wview(w_outa_b, 0, 1024), [], [ow_b], "ow")
                for t in range(NT):
                    sl = t % 2
                    sig, sig_b, gs, gs_b, osq, osq_b = sigL[sl], sigL_b[sl], gsL[sl], gsL_b[sl], osqL[sl], osqL_b[sl]
                    oss, oss_b, ogb, ogb_b, ogT, ogT_b = ossL[sl], ossL_b[sl], ogbL[sl], ogbL_b[sl], ogTL[sl], ogTL_b[sl]
                    load_x_tile(s, t, sl)
                    pgs = [bank(), bank()]
                    for hf in range(2):
                        pb, pbb = pgs[hf]
                        for kc in range(8):
                            MM(pb[:, :], hT[:, kc, t * 128:(t + 1) * 128], gateW[:, kc, hf * 512:(hf + 1) * 512], kc == 0, kc == 7,
                               [hT_b[t], gw_b], [pbb])
                        silu_from(pb[:, :], [pbb], sig[:, hf * 512:(hf + 1) * 512], sig_b, None)
                        V("dve", "tensor_tensor", [pbb, sig_b], [gs_b], out=gs[:, hf * 512:(hf + 1) * 512], in0=pb[:, :],
                          in1=sig[:, hf * 512:(hf + 1) * 512], op=ALU.mult)
                        rel(pbb)
                    ACT([OV_b[t]], [osq_b], osq, OV[:, t, :], AF.Square)
                    V("dve", "tensor_reduce", [osq_b], [oss_b], out=oss, in_=osq.rearrange("p (a b) -> p a b", a=8),
                      axis=AX.X, op=ALU.add)
                    ACT([oss_b], [oss_b], oss, oss, AF.Ln, scale=1.0 / 128, bias=EPS)
                    ACT([oss_b], [oss_b], oss, oss, AF.Exp, scale=-0.5)
                    V("dve", "tensor_tensor", [OV_b[t], oss_b], [osq_b], out=osq.rearrange("p (a b) -> p a b", a=8),
                      in0=OV[:, t, :].rearrange("p (a b) -> p a b", a=8), in1=oss.unsqueeze(2).to_broadcast([128, 8, 128]),
                      op=ALU.mult)
                    V("dve", "tensor_tensor", [osq_b, gs_b], [ogb_b], out=ogb, in0=osq, in1=gs, op=ALU.mult)
                    pb, pbb = bank()
                    pbv = pb[:].bitcast(BF16)
                    for kc in range(8):
                        TR(pbv[:, kc * 128:(kc + 1) * 128], ogb[:, kc * 128:(kc + 1) * 128], identB[:], [ogb_b, CONST], [pbb])
                    V("dve", "tensor_copy", [pbb], [ogT_b], out=ogT, in_=pbv.rearrange("p (k c) -> p k c", k=8))
                    rel(pbb)
                    for hf in range(2):
                        pb, pbb = bank()
                        for kc in range(8):
                            MM(pb[:, :], ogT[:, kc, :], outW[:, kc, hf * 512:(hf + 1) * 512], kc == 0, kc == 7, [ogT_b, ow_b], [pbb])
                        V("dve", "tensor_tensor", [pbb, xin_b[sl]], [h1t_b[sl]], out=h1t[sl][:, hf * 512:(hf + 1) * 512],
                          in0=pb[:, :], in1=xin[sl][:, hf * 512:(hf + 1) * 512], op=ALU.add)
                        rel(pbb)
                    DMA("act", h1_d[t * 128:(t + 1) * 128, :], h1t[sl], [h1t_b[sl]], [], f"h1s{sl}")
                    norm_to_hT(h1t[sl], h1t_b[sl], t, sl)
                S.barrier()
                if STOP == "A3":
                    raise _Stop()
                areset(0)
                KT = carve([128, 8, T], BF16); KT_b = S.buf()
                ksc = carve([128, NT, 8], F32); ksc_b = S.buf()
                kv_base = apos[0]
                bpool[0] = list(range(7))
                wdn = carve([128, 8, 320], BF16); wuk = carve([128, 2, 1024], BF16); wuv = carve([128, 2, 1024], BF16)
                wkv_b = S.buf()
                cTf = carve([128, 2, 512], F32); cT_b = S.buf()
                kpe = carve([64, 512], F32); kpe_b = S.buf()
                kpb = carve([64, 512], BF16); kpb_b = S.buf()
                csq = carve([128, 2, 512], BF16); csq_b = S.buf()
                rb = carve([128, 512], F32); rb_b = S.buf()
                ckv = carve([128, 2, 512], BF16); ckv_b = S.buf()
                sqK = [carve([128, 512], BF16) for _ in range(2)]; sqK_b = S.bufs(2)
                sqR = carve([64, 512], BF16); sqR_b = S.buf()
                t1 = carve([64, 512], F32); t1_b = S.buf()
                t2 = carve([64, 512], F32); t2_b = S.buf()
                DMA("sp", wdn, wview(w_down_b, 0, 320), [], [wkv_b], "wkv")
                DMA("sp", wuk, wview(w_uk_b, 0, 1024), [], [wkv_b], "wkv")
                DMA("sp", wuv, wview(w_uv_b, 0, 1024), [], [wkv_b], "wkv")
                pks, pksb = fbank(7)
                for (c0, cw) in CB:
                    rd = hT_reads(c0, cw)
                    pbs_ = [bank(), bank(), bank()]
                    for j, (lo, mw) in enumerate(((0, 128), (128, 128), (256, 64))):
                        pb, pbb = pbs_[j]
                        for kc in range(8):
                            MM(pb[0:mw, :cw], wdn[:, kc, lo:lo + mw], hT[:, kc, c0:c0 + cw], kc == 0, kc == 7, [wkv_b] + rd, [pbb])
                    for j in range(2):
                        pb, pbb = pbs_[j]
                        S.op("act", (lambda o, i: (lambda e: e.copy(out=o, in_=i)))(cTf[:, j, :cw], pb[:, :cw]), [pbb], [cT_b])
                        rel(pbb)
                    pb, pbb = pbs_[2]
                    V("dve", "tensor_scalar_mul", [pbb, CONST], [kpe_b], out=kpe[:, :cw], in0=pb[0:64, :cw], scalar1=kgr[:, 0:1])
                    ACT([pbb], [sqR_b], sqR[:, :cw], pb[0:64, :cw], AF.Square)
                    rel(pbb)
                    ACT([cT_b], [csq_b], csq[:, :, :cw], cTf[:, :, :cw], AF.Square)
                    pb, pbb = bank()
                    for j in range(2):
                        MM(pb[:, :cw], onesB[:], csq[:, j, :cw], j == 0, j == 1, [CONST, csq_b], [pbb])
                    ACT([pbb], [rb_b], rb[:, :cw], pb[:, :cw], AF.Ln, scale=1.0 / 256, bias=EPS)
                    rel(pbb)
                    ACT([rb_b], [rb_b], rb[:, :cw], rb[:, :cw], AF.Exp, scale=-0.5)
                    V("dve", "tensor_tensor", [cT_b, rb_b], [ckv_b], out=ckv[:, :, :cw], in0=cTf[:, :, :cw],
                      in1=rb[:, :cw].unsqueeze(1).to_broadcast([128, 2, cw]), op=ALU.mult)
                    V("pool", "tensor_copy", [kpe_b], [kpb_b], out=kpb[:, :cw], in_=kpe[:, :cw])
                    pb, pbb = bank()
                    MM(pb[0:64, :cw], rotB[:], kpb[:, :cw], True, True, [CONST, kpb_b], [pbb])
                    V("dve", "tensor_tensor", [pbb, CONST], [t1_b], out=t1[:, :cw], in0=pb[0:64, :cw], in1=sin2[:, c0:c0 + cw], op=ALU.mult)
                    rel(pbb)
                    V("dve", "tensor_tensor", [kpe_b, CONST], [t2_b], out=t2[:, :cw], in0=kpe[:, :cw], in1=cos2[:, c0:c0 + cw], op=ALU.mult)
                    V("dve", "tensor_tensor", [t1_b, t2_b], [RR_b], out=RR[:, c0:c0 + cw], in0=t1[:, :cw], in1=t2[:, :cw], op=ALU.add)
                    for h in range(H):
                        pb, pbb = bank()
                        for r in range(2):
                            MM(pb[:, :cw], wuk[:, r, h * 128:(h + 1) * 128], ckv[:, r, :cw], r == 0, r == 1, [wkv_b, ckv_b], [pbb])
                        V("dve", "tensor_scalar_mul", [pbb, CONST], [KT_b], out=KT[:, h, c0:c0 + cw], in0=pb[:, :cw], scalar1=kgn[:, 0:1])
                        q = h % 2
                        ACT([pbb], [sqK_b[q]], sqK[q][:, :cw], pb[:, :cw], AF.Square)
                        rel(pbb)
                        for ti in range(cw // 128):
                            t = c0 // 128 + ti
                            MM(pks[:, t * 8 + h:t * 8 + h + 1], sqK[q][:, ti * 128:(ti + 1) * 128], onesB[:, 0:1], True, False,
                               [sqK_b[q], CONST], [pksb])
                            MM(pks[:, t * 8 + h:t * 8 + h + 1], sqR[:, ti * 128:(ti + 1) * 128], onesB[0:64, 0:1], False, True,
                               [sqR_b, CONST], [pksb])
                    for ti in range(cw // 128):
                        t = c0 // 128 + ti
                        for hf in range(2):
                            pb, pbb = bank()
                            for r in range(2):
                                MM(pb[:, :], ckv[:, r, ti * 128:(ti + 1) * 128], wuv[:, r, hf * 512:(hf + 1) * 512], r == 0, r == 1,
                                   [ckv_b, wkv_b], [pbb])
                            S.op("act", (lambda o, i: (lambda e: e.copy(out=o, in_=i)))(OV[:, t, hf * 512:(hf + 1) * 512], pb[:, :]),
                                 [pbb], [OV_b[t]])
                            rel(pbb)
                ACT([pksb], [ksc_b], ksc, pks[:, 0:NT * 8].rearrange("p (t c) -> p t c", c=8), AF.Ln, scale=1.0 / 192, bias=EPS)
                ACT([ksc_b], [ksc_b], ksc, ksc, AF.Exp, scale=-0.5)
                V("dve", "tensor_scalar_mul", [ksc_b], [ksc_b], out=ksc, in0=ksc, scalar1=192.0 ** -0.5)
                S.barrier()
                if STOP == "KV":
                    raise _Stop()
                bpool[0] = list(range(4))
                b_base = kv_base
                for _once in range(1):
                    areset(b_base)
                    woutB = carve([128, 8, 1024], BF16); wo_b = S.buf()
                    wst = [carve([128, 8, 128], BF16) for _ in range(2)]; wst_b = S.bufs(2)
                    wuq = [carve([128, 3, 192], BF16) for _ in range(2)]; wuq_b = S.bufs(2)
                    cqT = carve([128, 3, 512], F32); cq_b = S.buf()
                    cqn = carve([128, 3, 512], BF16); cqn_b = S.buf()
                    cqs, cqs_b = cqn, cqn_b
                    rb = carve([128, 512], F32); rb_b = S.buf()
                    qn = carve([128, 512], F32); qn_b = S.buf()
                    qr = carve([64, 512], F32); qr_b = S.buf()
                    qrb = carve([64, 512], BF16); qrb_b = S.buf()
                    sqn = carve([128, 512], BF16); sqn_b = S.buf()
                    sqr = carve([64, 512], BF16); sqr_b = S.buf()
                    rq = carve([128, 512], F32); rq_b = S.buf()
                    QnT2 = [carve([128, 512], BF16) for _ in range(2)]; QrT2 = [carve([64, 512], BF16) for _ in range(2)]
                    Q2_b = S.bufs(2)
                    t1 = carve([64, 512], F32); t1_b = S.buf()
                    t2 = carve([64, 512], F32); t2_b = S.buf()
                    sig = carve([128, 512], F32); sig_b = S.buf()
                    gs2 = [carve([128, 512], F32) for _ in range(2)]; gs2_b = S.bufs(2)
                    PT = [carve([128, 512], BF16) for _ in range(3)]; PT_b = S.bufs(3)
                    rd_ = carve([128, 512], F32); rd_b = S.buf()
                    ot = carve([128, 512], F32); ot_b = S.buf()
                    OGT = carve([128, 8, 512], BF16); OGT_b = S.buf()
                for q0 in range(1, NT, 4):
                    q1 = min(q0 + 3, NT - 1)
                    bw = (q1 - q0 + 1) * 128
                    bc0 = q0 * 128
                    hrd = hT_reads(bc0, bw)
                    DMA("sp", woutB, wview(w_outb_b, 0, 1024), [], [wo_b], "wo")
                    wc = [0]
                    for fc in range(3):
                        wsl = wc[0] % 2; wc[0] += 1
                        DMA("sp", wst[wsl], wview(w_inb_b, fc * 128, 128), [], [wst_b[wsl]], f"wst{wsl}")
                        pb, pbb = bank()
                        for kc in range(8):
                            MM(pb[:, :bw], wst[wsl][:, kc, :], hT[:, kc, bc0:bc0 + bw], kc == 0, kc == 7, [wst_b[wsl]] + hrd, [pbb])
                        S.op("act", (lambda o, i: (lambda e: e.copy(out=o, in_=i)))(cqT[:, fc, :bw], pb[:, :bw]), [pbb], [cq_b])
                        rel(pbb)
                    ACT([cq_b], [cqs_b], cqs[:, :, :bw], cqT[:, :, :bw], AF.Square)
                    pb, pbb = bank()
                    for j in range(3):
                        MM(pb[:, :bw], onesB[:], cqs[:, j, :bw], j == 0, j == 2, [CONST, cqs_b], [pbb])
                    ACT([pbb], [rb_b], rb[:, :bw], pb[:, :bw], AF.Ln, scale=1.0 / 384, bias=EPS)
                    rel(pbb)
                    ACT([rb_b], [rb_b], rb[:, :bw], rb[:, :bw], AF.Exp, scale=-0.5)
                    V("dve", "tensor_tensor", [cq_b, rb_b], [cqn_b], out=cqn[:, :, :bw], in0=cqT[:, :, :bw],
                      in1=rb[:, :bw].unsqueeze(1).to_broadcast([128, 3, bw]), op=ALU.mult)
                    pctr = [0]

                    def prep(h):
                        us = h % 2
                        QnT, QrT, Q_b, gs, gs_b = QnT2[us], QrT2[us], Q2_b[us], gs2[us], gs2_b[us]
                        DMA("sp", wuq[us], wview(w_uq_b, h * 192, 192), [], [wuq_b[us]], f"wuq{us}")
                        pq, pqb = bank()
                        pr, prb = bank()
                        for j in range(3):
                            MM(pq[:, :bw], wuq[us][:, j, 0:128], cqn[:, j, :bw], j == 0, j == 2, [wuq_b[us], cqn_b], [pqb])
                        for j in range(3):
                            MM(pr[0:64, :bw], wuq[us][:, j, 128:192], cqn[:, j, :bw], j == 0, j == 2, [wuq_b[us], cqn_b], [prb])
                        yield
                        V("dve", "tensor_scalar_mul", [pqb, CONST], [qn_b], out=qn[:, :bw], in0=pq[:, :bw], scalar1=qgn[:, 0:1])
                        V("dve", "tensor_scalar_mul", [prb, CONST], [qr_b], out=qr[:, :bw], in0=pr[0:64, :bw], scalar1=qgr[:, 0:1])
                        ACT([pqb], [sqn_b], sqn[:, :bw], pq[:, :bw], AF.Square)
                        ACT([prb], [sqr_b], sqr[:, :bw], pr[0:64, :bw], AF.Square)
                        rel(pqb, prb)
                        yield
                        pb, pbb = bank()
                        MM(pb[:, :bw], onesB[:], sqn[:, :bw], True, False, [CONST, sqn_b], [pbb])
                        MM(pb[:, :bw], onesB[0:64, :], sqr[:, :bw], False, True, [CONST, sqr_b], [pbb])
                        yield
                        ACT([pbb], [rq_b], rq[:, :bw], pb[:, :bw], AF.Ln, scale=1.0 / 192, bias=EPS)
                        rel(pbb)
                        ACT([rq_b], [rq_b], rq[:, :bw], rq[:, :bw], AF.Exp, scale=-0.5)
                        V("pool", "tensor_copy", [qr_b], [qrb_b], out=qrb[:, :bw], in_=qr[:, :bw])
                        yield
                        V("dve", "tensor_tensor", [qn_b, rq_b], [Q_b], out=QnT[:, :bw], in0=qn[:, :bw], in1=rq[:, :bw], op=ALU.mult)
                        pb, pbb = bank()
                        MM(pb[0:64, :bw], rotB[:], qrb[:, :bw], True, True, [CONST, qrb_b], [pbb])
                        yield
                        V("dve", "tensor_tensor", [pbb, CONST], [t1_b], out=t1[:, :bw], in0=pb[0:64, :bw], in1=sin2[:, bc0:bc0 + bw], op=ALU.mult)
                        rel(pbb)
                        V("dve", "tensor_tensor", [qr_b, CONST], [t2_b], out=t2[:, :bw], in0=qr[:, :bw], in1=cos2[:, bc0:bc0 + bw], op=ALU.mult)
                        yield
                        V("dve", "tensor_tensor", [t1_b, t2_b], [t1_b], out=t1[:, :bw], in0=t1[:, :bw], in1=t2[:, :bw], op=ALU.add)
                        V("dve", "tensor_tensor", [t1_b, rq_b], [Q_b], out=QrT[:, :bw], in0=t1[:, :bw], in1=rq[0:64, :bw], op=ALU.mult)
                        wsl = wc[0] % 2; wc[0] += 1
                        DMA("sp", wst[wsl], wview(w_inb_b, 384 + h * 128, 128), [], [wst_b[wsl]], f"wst{wsl}")
                        pgt, pgtb = bank()
                        for kc in range(8):
                            MM(pgt[:, :bw], wst[wsl][:, kc, :], hT[:, kc, bc0:bc0 + bw], kc == 0, kc == 7, [wst_b[wsl]] + hrd, [pgtb])
                        yield
                        ACT([pgtb], [sig_b], sig[:, :bw], pgt[:, :bw], AF.Exp, scale=-1.0)
                        yield
                        ACT([sig_b], [sig_b], sig[:, :bw], sig[:, :bw], AF.Ln, bias=1.0)
                        ACT([sig_b], [sig_b], sig[:, :bw], sig[:, :bw], AF.Exp, scale=-1.0)
                        yield
                        V("dve", "tensor_tensor", [pgtb, sig_b], [gs_b], out=gs[:, :bw], in0=pgt[:, :bw], in1=sig[:, :bw], op=ALU.mult)
                        rel(pgtb)

                    def attn(h, nxt):
                        us = h % 2
                        QnT, QrT, Q_b, gs, gs_b = QnT2[us], QrT2[us], Q2_b[us], gs2[us], gs2_b[us]
                        pO, pOb = fbank(4 + 2 * (h % 2))
                        pD, pDb = fbank(5 + 2 * (h % 2))
                        def s_mm(kt):
                            o0 = (max(kt, q0) - q0) * 128
                            ks = slice(kt * 128, (kt + 1) * 128)
                            psc, pscb = bank()
                            diag = kt >= q0
                            MM(psc[:, o0:bw], KT[:, h, ks], QnT[:, o0:bw], True, False, [KT_b, Q_b], [pscb])
                            MM(psc[:, o0:bw], RR[:, ks], QrT[:, o0:bw], False, not diag, [RR_b, Q_b], [pscb])
                            if diag:
                                MM(psc[:, o0:o0 + 128], urowB[:], wrowB[:], False, True, [CONST], [pscb])
                            return psc, pscb

                        cur = s_mm(0)
                        for kt in range(0, q1 + 1):
                            o0 = (max(kt, q0) - q0) * 128
                            psc, pscb = cur
                            if kt + 1 <= q1:
                                cur = s_mm(kt + 1)
                            ps_ = pctr[0] % 3; pctr[0] += 1
                            ACT([pscb, ksc_b, CONST], [PT_b[ps_]], PT[ps_][:, o0:bw], psc[:, o0:bw], AF.Exp,
                                scale=ksc[:, kt, h:h + 1], bias=(padb[:, 0:1] if kt == 0 else zcol[:, 0:1]))
                            rel(pscb)
                            MM(pO[:, o0:bw], OV[:, kt, h * 128:(h + 1) * 128], PT[ps_][:, o0:bw], kt == 0, kt == q1, [OV_b[kt], PT_b[ps_]], [pOb])
                            MM(pD[:, o0:bw], onesB[:], PT[ps_][:, o0:bw], kt == 0, kt == q1, [CONST, PT_b[ps_]], [pDb])
                            if nxt is not None:
                                next(nxt, None)
                        if nxt is not None:
                            for _ in nxt:
                                pass
                        V("dve", "reciprocal", [pDb], [rd_b], out=rd_[:, :bw], in_=pD[:, :bw])
                        V("dve", "tensor_tensor", [pOb, rd_b], [ot_b], out=ot[:, :bw], in0=pO[:, :bw], in1=rd_[:, :bw], op=ALU.mult)
                        V("dve", "tensor_tensor", [ot_b, gs_b], [OGT_b], out=OGT[:, h, :bw], in0=ot[:, :bw], in1=gs[:, :bw], op=ALU.mult)

                    for _ in prep(0):
                        pass
                    for h in range(H):
                        attn(h, prep(h + 1) if h + 1 < H else None)
                    for qt in range(q0, q1 + 1):
                        sl = qt % 2
                        DMA("sp", xin[sl][:], h1_d[qt * 128:(qt + 1) * 128, :], [], [xin_b[sl]], f"xl{sl}")
                        lc = (qt - q0) * 128
                        for hf in range(2):
                            pb, pbb = bank()
                            for kc in range(8):
                                MM(pb[:, :], OGT[:, kc, lc:lc + 128], woutB[:, kc, hf * 512:(hf + 1) * 512], kc == 0, kc == 7,
                                   [OGT_b, wo_b], [pbb])
                            V("dve", "tensor_tensor", [pbb, xin_b[sl]], [xin_b[sl]], out=xin[sl][:, hf * 512:(hf + 1) * 512],
                              in0=pb[:, :], in1=xin[sl][:, hf * 512:(hf + 1) * 512], op=ALU.add)
                            rel(pbb)
                        r0 = s * SEQ + (qt - 1) * 128
                        DMA("act", out_d[r0:r0 + 128, :], xin[sl][:], [xin_b[sl]], [], f"os{sl}")
                S.barrier()
        except _Stop:
            S.barrier()
        S.emit(nc, st)
    nc._sched_info = S.info
    nc._sched = S
    return nc


_NC_CACHE = {}


def kernel(**inputs):
    x = np.ascontiguousarray(inputs["x"], dtype=np.float32)
    B, SEQ, _ = x.shape
    NSEQ = B // N_CORES
    key = (NSEQ, SEQ)
    if key not in _NC_CACHE:
        _NC_CACHE[key] = build(NSEQ, SEQ)
    nc = _NC_CACHE[key]
    consts = host_consts(SEQ + 128)
    shared = {}
    for k, v in inputs.items():
        if k == "x":
            continue
        a = np.ascontiguousarray(np.asarray(v, dtype=np.float32))
        if a.ndim >= 2 and a.shape[0] == 1:
            a = a[0]
        shared[k] = np.ascontiguousarray(a)
    shared.update(consts)
    in_maps = []
    for c in range(N_CORES):
        m = dict(shared)
        m["x"] = np.ascontiguousarray(x[c * NSEQ:(c + 1) * NSEQ].reshape(NSEQ * SEQ, D))
        in_maps.append(m)
    res = run_bass_kernel_spmd(nc, in_maps, core_ids=list(range(N_CORES)))
    out = np.concatenate([np.asarray(r["out"]).reshape(NSEQ, SEQ, D) for r in res.results], axis=0)
    return out.astype(np.float32)
```

```python
import numpy as np
from contextlib import ExitStack
import concourse.bass as bass
import concourse.mybir as mybir
from concourse.bass_utils import run_bass_kernel_spmd

F32 = mybir.dt.float32
BF16 = mybir.dt.bfloat16
AF = mybir.ActivationFunctionType
ALU = mybir.AluOpType
AX = mybir.AxisListType
ENGS = ("sp", "act", "pool", "dve", "pe")

D = 1024
H = 8
BIG = 30000.0
EPS = 1e-6
N_CORES = 8
CHAIN_DT = F32


class Buf:
    __slots__ = ("name", "w", "r", "excl")

    def __init__(self, name):
        self.name = name
        self.w = None
        self.r = {}
        self.excl = False


class Op:
    __slots__ = ("fn", "waits", "tok", "dma")

    def __init__(self, fn, waits, tok, dma):
        self.fn = fn
        self.waits = waits
        self.tok = tok
        self.dma = dma


class Sched:
    def __init__(self):
        self.ops = {e: [] for e in ENGS}
        self.count = {}
        self.known = {e: {} for e in ENGS}
        self.needed = set()
        self.nb = 0
        self.marks = []
        self.info = {}
        self.dbg = []
        self.names = {}

    def buf(self, name=None):
        self.nb += 1
        return Buf(name or f"b{self.nb}")

    def bufs(self, n, name="b"):
        return [self.buf(f"{name}{i}") for i in range(n)]

    def op(self, eng, fn, reads=(), writes=(), dma_dom=None):
        deps = {}

        def need(tok):
            if tok is not None and deps.get(tok[0], 0) < tok[1]:
                deps[tok[0]] = tok[1]

        own = dma_dom if dma_dom is not None else eng
        for b in reads:
            need(b.w)
            if b.excl:
                for d, i in b.r.items():
                    if d != own:
                        need((d, i))
        for b in writes:
            need(b.w)
            for d, i in b.r.items():
                need((d, i))
        is_dma = dma_dom is not None
        dom = dma_dom if is_dma else eng
        waits = {}
        kn = self.known[eng]
        for d, i in deps.items():
            if d == "pe" and eng == "pe" and not is_dma:
                continue
            if kn.get(d, 0) >= i:
                continue
            waits[d] = i
            kn[d] = i
            self.needed.add((d, i))
        c = self.count.get(dom, 0) + 1
        self.count[dom] = c
        tok = (dom, c)
        for b in reads:
            if b.r.get(dom, 0) < c:
                b.r[dom] = c
        for b in writes:
            b.w = tok
            b.r = {}
        o_ = Op(fn, waits, tok, is_dma)
        self.ops[eng].append(o_)
        self.dbg.append((o_, [b.name for b in reads], [b.name for b in writes]))
        return tok

    def barrier(self, label=None):
        self.marks.append((label, dict(self.count)))
        for e in ENGS:
            waits = {}
            kn = self.known[e]
            for d, c in self.count.items():
                if kn.get(d, 0) < c:
                    waits[d] = c
                    kn[d] = c
                    self.needed.add((d, c))
            if waits:
                self.ops[e].append(Op(None, waits, None, False))

    def emit(self, nc, stack):
        doms = sorted(self.count.keys())
        dma_doms = set()
        for e in ENGS:
            for o in self.ops[e]:
                if o.dma:
                    dma_doms.add(o.tok[0])
        sems = {d: stack.enter_context(nc.semaphore(f"s_{d}")) for d in doms}
        rank = {}
        for d in doms:
            if d in dma_doms:
                rank[d] = None
            else:
                idxs = sorted(i for (dd, i) in self.needed if dd == d)
                rank[d] = {i: k + 1 for k, i in enumerate(idxs)}
        block = stack.enter_context(nc.Block())
        ops = self.ops
        self.info = dict(sems={d: sems[d].num for d in doms},
                         marks=[(lab, {d: (rank[d][c] if rank[d] is not None else 16 * c) for d, c in cnt.items()
                                       if d in ("pe", "act", "dve", "pool")}) for lab, cnt in self.marks])

        def run(engh, lst):
            for o in lst:
                for d, i in o.waits.items():
                    engh.wait_ge(sems[d], 16 * i if rank[d] is None else rank[d][i])
                if o.fn is None:
                    continue
                ins = o.fn(engh)
                try:
                    self.names[ins.ins.name] = o
                except Exception:
                    pass
                d, i = o.tok
                if o.dma:
                    ins.then_inc(sems[d], 16)
                elif i in rank[d]:
                    ins.then_inc(sems[d], 1)

        @block.sync
        def _(e):
            run(e, ops["sp"])

        @block.scalar
        def _(e):
            run(e, ops["act"])

        @block.gpsimd
        def _(e):
            run(e, ops["pool"])

        @block.vector
        def _(e):
            run(e, ops["dve"])

        @block.tensor
        def _(e):
            run(e, ops["pe"])


def host_consts(T):
    p = np.arange(128)
    ident = np.eye(128, dtype=np.float32)
    tri = (p[:, None] <= p[None, :]).astype(np.float32)
    mui = np.where(p[None, :] < p[:, None], -BIG, 0.0).astype(np.float32)
    mls = np.where(p[None, :] >= p[:, None], BIG, 0.0).astype(np.float32)
    mus = np.where(p[None, :] <= p[:, None], -BIG, 0.0).astype(np.float32)
    rot = np.zeros((64, 64), np.float32)
    for m in range(32):
        rot[m + 32, m] = -1.0
        rot[m, m + 32] = 1.0
    pos = np.maximum(np.arange(T) - 112, 0).astype(np.float32)
    inv = (np.float32(10000.0) ** (-np.arange(32, dtype=np.float32) / np.float32(32))).astype(np.float32)
    ang = (pos[:, None] * inv[None, :]).astype(np.float32)
    cos = np.cos(ang).astype(np.float32).T
    sin = np.sin(ang).astype(np.float32).T
    cos2 = np.concatenate([cos, cos], 0).astype(np.float32)
    sin2 = np.concatenate([sin, sin], 0).astype(np.float32)
    padb = np.where(p < 112, -BIG, 0.0).astype(np.float32)[:, None]
    urow = np.where(p >= 64, 1.0, 0.0).astype(np.float32)[None, :]
    wrow = np.where(p < 64, -BIG, 0.0).astype(np.float32)[None, :]
    return dict(c_ident=ident, c_tri=tri, c_mui=mui, c_mls=mls, c_mus=mus, c_rot=rot, c_cos=cos2, c_sin=sin2,
                c_padb=padb, c_urow=urow, c_wrow=wrow)


class _Stop(Exception):
    pass


def build(NSEQ, SEQ, STOP=None):
    T = SEQ + 128
    NT = T // 128
    CB = [(c0, min(512, T - c0)) for c0 in range(0, T, 512)]
    nc = bass.Bass("TRN2", target_bir_lowering=False)

    def din(name, shape, dt=F32):
        return nc.dram_tensor(name, list(shape), dt, kind="ExternalInput").ap()

    x_d = din("x", [NSEQ * SEQ, D])
    meta_d = din("meta_tokens", [16, D])
    a_norm_d = din("a_norm", [D]); a_w_in_d = din("a_w_in", [D, 4112]); a_conv_d = din("a_conv", [4, 3072])
    a_log_d = din("a_log", [8]); a_dtb_d = din("a_dt_bias", [8]); a_og_d = din("a_o_gain", [128])
    a_w_out_d = din("a_w_out", [D, D]); kv_norm_d = din("kv_norm", [D]); kv_wd_d = din("kv_w_down", [D, 320])
    kv_ln_d = din("kv_latent_norm", [256]); kv_uk_d = din("kv_w_uk", [256, D]); kv_uv_d = din("kv_w_uv", [256, D])
    k_gain_d = din("k_gain", [192]); b_norm_d = din("b_norm", [D]); b_w_in_d = din("b_w_in", [D, 1408])
    b_qln_d = din("b_q_latent_norm", [384]); b_uq_d = din("b_w_uq", [384, 1536]); b_qg_d = din("b_q_gain", [192])
    b_w_out_d = din("b_w_out", [D, D])
    c_ident_d = din("c_ident", [128, 128]); c_tri_d = din("c_tri", [128, 128]); c_mui_d = din("c_mui", [128, 128])
    c_mls_d = din("c_mls", [128, 128]); c_mus_d = din("c_mus", [128, 128]); c_rot_d = din("c_rot", [64, 64]); c_cos_d = din("c_cos", [64, T])
    c_sin_d = din("c_sin", [64, T]); c_padb_d = din("c_padb", [128, 1]); c_urow_d = din("c_urow", [1, 128])
    c_wrow_d = din("c_wrow", [1, 128])
    out_d = nc.dram_tensor("out", [NSEQ * SEQ, D], F32, kind="ExternalOutput").ap()

    def dscr(name, shape, dt):
        return nc.dram_tensor(name, list(shape), dt).ap()

    w_in_b = dscr("w_in_b", [D, 4112], BF16); w_outa_b = dscr("w_outa_b", [D, D], BF16)
    w_down_b = dscr("w_down_b", [D, 320], BF16); w_uk_b = dscr("w_uk_b", [256, D], BF16)
    w_uv_b = dscr("w_uv_b", [256, D], BF16); w_inb_b = dscr("w_inb_b", [D, 1408], BF16)
    w_uq_b = dscr("w_uq_b", [384, 1536], BF16); w_outb_b = dscr("w_outb_b", [D, D], BF16)
    h1_d = dscr("h1_d", [T, D], F32)

    S = Sched()
    with ExitStack() as st:
        st.enter_context(nc.allow_non_contiguous_dma("small strided parameter loads"))

        def sb(name, shape, dt):
            return st.enter_context(nc.sbuf_tensor(name, list(shape), dt))

        def V(eng, fname, reads, writes, *a, **k):
            S.op(eng, lambda e: getattr(e, fname)(*a, **k), reads, writes)

        def ACT(reads, writes, out, in_, func, **k):
            S.op("act", lambda e: e.activation(out=out, in_=in_, func=func, **k), reads, writes)

        def MM(out, lhsT, rhs, start, stop, reads, writes):
            S.op("pe", lambda e: e.matmul(out, lhsT=lhsT, rhs=rhs, start=start, stop=stop), reads, writes)

        def TR(out, in_, ident, reads, writes):
            S.op("pe", lambda e: e.transpose(out=out, in_=in_, identity=ident), reads, writes)

        def DMA(q, out, in_, reads, writes, dom):
            S.op(q, lambda e: e.dma_start(out=out, in_=in_), reads, writes, dma_dom=dom)

        PB = [st.enter_context(nc.psum_tensor(f"pb{i}", [128, 512], F32)) for i in range(8)]
        PBb = S.bufs(8, "pb")
        for _b in PBb:
            _b.excl = True
        bctr = [0]
        bpool = [list(range(8))]

        live = set()

        def bank():
            pool_ = bpool[0]
            for k in range(len(pool_)):
                i = pool_[(bctr[0] + k) % len(pool_)]
                if i not in live:
                    bctr[0] += k + 1
                    live.add(i)
                    return PB[i], PBb[i]
            raise RuntimeError("no free PSUM bank")

        def rel(*bbs):
            for bb in bbs:
                live.discard(PBb.index(bb))

        def fbank(i):
            return PB[i], PBb[i]

        try:
            identF = sb("identF", [128, 128], F32); identB = sb("identB", [128, 128], BF16)
            triF = sb("triF", [128, 128], F32); muiF = sb("muiF", [128, 128], F32); mlsF = sb("mlsF", [128, 128], F32)
            musF = sb("musF", [128, 128], F32)
            muiB = sb("muiB", [128, 128], BF16); mlsB = sb("mlsB", [128, 128], BF16); musB = sb("musB", [128, 128], BF16)
            onesB = sb("onesB", [128, 128], BF16); rotF = sb("rotF", [64, 64], F32); rotB = sb("rotB", [64, 64], BF16)
            cos2 = sb("cos2", [64, T], BF16); sin2 = sb("sin2", [64, T], BF16)
            padb = sb("padb", [128, 1], F32); zcol = sb("zcol", [128, 1], F32)
            urowF = sb("urowF", [1, 128], F32); wrowF = sb("wrowF", [1, 128], F32)
            urowB = sb("urowB", [1, 128], BF16); wrowB = sb("wrowB", [1, 128], BF16)
            g_anorm = sb("g_anorm", [128, 8], F32); g_kvnorm = sb("g_kvnorm", [128, 8], F32)
            g_bnorm = sb("g_bnorm", [128, 8], F32); g_kvln = sb("g_kvln", [128, 2], F32)
            g_qln = sb("g_qln", [128, 3], F32); g_og = sb("g_og", [128, 1], F32)
            kgn = sb("kgn", [128, 1], F32); kgr = sb("kgr", [64, 1], F32)
            qgn = sb("qgn", [128, 1], F32); qgr = sb("qgr", [64, 1], F32)
            convw = sb("convw", [128, 4, 24], F32)
            alog = sb("alog", [128, 8], F32); negA = sb("negA", [128, 8], F32); dtb = sb("dtb", [128, 8], F32)
            CONST = S.buf("const")

            def cload(dst, src):
                DMA("sp", dst, src, [], [CONST], "cst")

            cload(identF[:], c_ident_d); cload(triF[:], c_tri_d); cload(muiF[:], c_mui_d); cload(mlsF[:], c_mls_d); cload(musF[:], c_mus_d)
            cload(rotF[:], c_rot_d); cload(padb[:], c_padb_d)
            cload(urowF[:], c_urow_d); cload(wrowF[:], c_wrow_d)
            cload(g_anorm[:], a_norm_d.rearrange("(k p) -> p k", p=128))
            cload(g_kvnorm[:], kv_norm_d.rearrange("(k p) -> p k", p=128))
            cload(g_bnorm[:], b_norm_d.rearrange("(k p) -> p k", p=128))
            cload(g_kvln[:], kv_ln_d.rearrange("(k p) -> p k", p=128))
            cload(g_qln[:], b_qln_d.rearrange("(k p) -> p k", p=128))
            cload(g_og[:], a_og_d.rearrange("(p o) -> p o", o=1))
            cload(kgn[:], k_gain_d[0:128].rearrange("(p o) -> p o", o=1))
            cload(kgr[:], k_gain_d[128:192].rearrange("(p o) -> p o", o=1))
            cload(qgn[:], b_qg_d[0:128].rearrange("(p o) -> p o", o=1))
            cload(qgr[:], b_qg_d[128:192].rearrange("(p o) -> p o", o=1))
            for jj in range(4):
                cload(convw[:, jj, :], a_conv_d[jj, :].rearrange("(c p) -> p c", p=128))
            cload(alog[:], a_log_d.partition_broadcast(128))
            cload(dtb[:], a_dtb_d.partition_broadcast(128))
            CONST.w = ("cst", S.count["cst"])
            V("dve", "tensor_copy", [CONST], [CONST], out=identB[:], in_=identF[:])
            V("dve", "tensor_copy", [CONST], [CONST], out=rotB[:], in_=rotF[:])
            V("dve", "tensor_copy", [CONST], [CONST], out=muiB[:], in_=muiF[:])
            V("dve", "tensor_copy", [CONST], [CONST], out=mlsB[:], in_=mlsF[:])
            V("dve", "tensor_copy", [CONST], [CONST], out=musB[:], in_=musF[:])
            V("dve", "tensor_copy", [CONST], [CONST], out=urowB[:], in_=urowF[:])
            V("dve", "tensor_copy", [CONST], [CONST], out=wrowB[:], in_=wrowF[:])
            V("dve", "memset", [], [CONST], onesB[:], 1.0)
            V("dve", "memset", [], [CONST], zcol[:], 0.0)
            ACT([CONST], [CONST], negA[:], alog[:], AF.Exp)
            V("dve", "tensor_scalar_mul", [CONST], [CONST], out=negA[:], in0=negA[:], scalar1=-1.0)

            hT = sb("hT", [128, 8, T], BF16); hT_b = S.bufs(NT, "hT")
            OV = sb("OV", [128, NT, 1024], BF16); OV_b = S.bufs(NT, "OV")
            RR = sb("RR", [64, T], BF16); RR_b = S.buf("RR")
            xin = [sb(f"xin{i}", [128, 1024], F32) for i in range(2)]; xin_b = S.bufs(2, "xin")
            xn = [sb(f"xn{i}", [128, 1024], BF16) for i in range(2)]; xn_b = S.bufs(2, "xn")
            junk = sb("junk", [128, 1024], BF16); junk_b = S.buf("junk")
            ssq = [sb(f"ssq{i}", [128, 1], F32) for i in range(2)]; ssq_b = S.bufs(2, "ssq")
            ARENA_F32 = 27136
            arena = sb("arena", [128, ARENA_F32], F32)
            apos = [0]

            def areset(off=0):
                apos[0] = off

            def carve(shape, dt):
                P = shape[0]
                n = int(np.prod(shape[1:]))
                nbytes = n * (4 if dt == F32 else 2)
                n32 = (nbytes + 3) // 4
                o = apos[0]
                apos[0] += (n32 + 7) // 8 * 8
                assert apos[0] <= ARENA_F32, ("arena overflow", apos[0])
                v = arena[0:P, o:o + n32]
                if dt != F32:
                    v = v.bitcast(dt)[:, 0:n]
                if len(shape) == 3:
                    v = v.rearrange("p (a b) -> p a b", a=shape[1], b=shape[2])
                return v

            areset(0)
            wl = [carve([128, 1024], F32) for i in range(2)]; wl_b = S.bufs(2, "wl")
            ws = [carve([128, 1024], BF16) for i in range(2)]; ws_b = S.bufs(2, "ws")
            for tab_d, tab in ((c_cos_d, cos2), (c_sin_d, sin2)):
                for c0 in range(0, T, 1024):
                    cw = min(1024, T - c0)
                    sl = wctr_ = 0
                    DMA("sp", wl[0][0:64, :cw], tab_d[:, c0:c0 + cw], [], [wl_b[0]], "wl0")
                    V("dve", "tensor_copy", [wl_b[0]], [CONST], out=tab[:, c0:c0 + cw], in_=wl[0][0:64, :cw])
            wctr = [0]

            def prep_weight(src, K, N, dst, gcol):
                for kc in range(K // 128):
                    for c0 in range(0, N, 1024):
                        cw = min(1024, N - c0)
                        sl = wctr[0] % 2
                        wctr[0] += 1
                        DMA("sp", wl[sl][:, :cw], src[kc * 128:(kc + 1) * 128, c0:c0 + cw], [], [wl_b[sl]], f"wl{sl}")
                        if gcol is not None:
                            V("dve", "tensor_scalar_mul", [wl_b[sl], CONST], [ws_b[sl]], out=ws[sl][:, :cw],
                              in0=wl[sl][:, :cw], scalar1=gcol(kc))
                        else:
                            V("dve", "tensor_copy", [wl_b[sl]], [ws_b[sl]], out=ws[sl][:, :cw], in_=wl[sl][:, :cw])
                        DMA("act", dst[kc * 128:(kc + 1) * 128, c0:c0 + cw], ws[sl][:, :cw], [ws_b[sl]], [], f"ws{sl}")

            prep_weight(a_w_in_d, D, 4112, w_in_b, lambda kc: g_anorm[:, kc:kc + 1])
            prep_weight(a_w_out_d, D, D, w_outa_b, lambda kc: g_og[:, 0:1])
            prep_weight(kv_wd_d, D, 320, w_down_b, lambda kc: g_kvnorm[:, kc:kc + 1])
            prep_weight(kv_uk_d, 256, D, w_uk_b, lambda kc: g_kvln[:, kc:kc + 1])
            prep_weight(kv_uv_d, 256, D, w_uv_b, lambda kc: g_kvln[:, kc:kc + 1])
            prep_weight(b_w_in_d, D, 1408, w_inb_b, lambda kc: g_bnorm[:, kc:kc + 1])
            prep_weight(b_uq_d, 384, 1536, w_uq_b, lambda kc: g_qln[:, kc:kc + 1])
            prep_weight(b_w_out_d, D, D, w_outb_b, None)
            S.barrier()
            if STOP == "W":
                raise _Stop()

            def wview(wb, c0, cw):
                return wb[:, c0:c0 + cw].rearrange("(k p) n -> p k n", p=128)

            def load_x_tile(s, t, sl):
                if t == 0:
                    V("pool", "memset", [], [xin_b[sl]], xin[sl][:], 0.0)
                    DMA("sp", xin[sl][112:128, :], meta_d, [], [xin_b[sl]], f"xl{sl}")
                else:
                    r0 = s * SEQ + (t - 1) * 128
                    DMA("sp", xin[sl][:], x_d[r0:r0 + 128, :], [], [xin_b[sl]], f"xl{sl}")

            def norm_to_hT(src, src_b, t, sl):
                ACT([src_b], [junk_b, ssq_b[sl]], junk[:], src, AF.Square, accum_out=ssq[sl][:])
                ACT([ssq_b[sl]], [ssq_b[sl]], ssq[sl][:], ssq[sl][:], AF.Ln, scale=1.0 / D, bias=EPS)
                ACT([ssq_b[sl]], [ssq_b[sl]], ssq[sl][:], ssq[sl][:], AF.Exp, scale=-0.5)
                V("dve", "tensor_scalar_mul", [src_b, ssq_b[sl]], [xn_b[sl]], out=xn[sl][:], in0=src, scalar1=ssq[sl][:, 0:1])
                pb, pbb = bank()
                pbv = pb[:].bitcast(BF16)
                for kc in range(8):
                    TR(pbv[:, kc * 128:(kc + 1) * 128], xn[sl][:, kc * 128:(kc + 1) * 128], identB[:], [xn_b[sl], CONST], [pbb])
                V("dve", "tensor_copy", [pbb], [hT_b[t]], out=hT[:, :, t * 128:(t + 1) * 128],
                  in_=pbv.rearrange("p (k c) -> p k c", k=8))
                rel(pbb)

            def hT_reads(c0, cw):
                return [hT_b[t] for t in range(c0 // 128, (c0 + cw + 127) // 128)]

            def silu_from(src_ap, src_reads, tmp, tmp_b, shape_sl):
                ACT(src_reads, [tmp_b], tmp, src_ap, AF.Exp, scale=-1.0)
                ACT([tmp_b], [tmp_b], tmp, tmp, AF.Ln, bias=1.0)
                ACT([tmp_b], [tmp_b], tmp, tmp, AF.Exp, scale=-1.0)

            def rsqrt_inplace(ap, b, scale, reads_extra=()):
                ACT([b] + list(reads_extra), [b], ap, ap, AF.Ln, scale=scale, bias=EPS)
                ACT([b], [b], ap, ap, AF.Exp, scale=-0.5)

            for s in range(NSEQ):
                bpool[0] = list(range(8))
                for t in range(NT):
                    sl = t % 2
                    load_x_tile(s, t, sl)
                    norm_to_hT(xin[sl][:], xin_b[sl], t, sl)
                S.barrier()
                if STOP == "A0":
                    raise _Stop()
                areset(0)
                wab = carve([128, 8, 16], BF16); wab_b = S.buf()
                ab = carve([128, NT, 16], F32); ab_b = S.buf()
                tm1 = carve([128, NT, 8], F32); tm1_b = S.buf()
                gcol = carve([128, NT, 8], F32); g_b = S.buf()
                lnb = carve([128, NT, 8], F32); lnb_b = S.buf()
                gc = carve([128, NT, 8], F32); gc_b = S.buf()
                ngc = carve([128, NT, 8], F32); egc = carve([128, NT, 8], F32); bls = carve([128, NT, 8], F32)
                beta = carve([128, NT, 8], F32); begc = carve([128, NT, 8], F32)
                der_b = S.buf()
                DMA("sp", wab, wview(w_in_b, 4096, 16), [], [wab_b], "wab")
                pb, pbb = bank()
                for t in range(NT):
                    for kc in range(8):
                        MM(pb[:, t * 16:(t + 1) * 16], hT[:, kc, t * 128:(t + 1) * 128], wab[:, kc, :], kc == 0, kc == 7,
                           [hT_b[t], wab_b], [pbb])
                V("dve", "tensor_copy", [pbb], [ab_b], out=ab, in_=pb[:, 0:NT * 16].rearrange("p (t c) -> p t c", c=16))
                rel(pbb)
                ACT([ab_b], [tm1_b], tm1, ab[:, :, 0:8], AF.Exp, scale=-1.0)
                ACT([tm1_b], [tm1_b], tm1, tm1, AF.Ln, bias=1.0)
                V("dve", "tensor_scalar_mul", [tm1_b], [lnb_b], out=lnb, in0=tm1, scalar1=-1.0)
                V("dve", "tensor_tensor", [ab_b, CONST], [tm1_b], out=tm1, in0=ab[:, :, 8:16],
                  in1=dtb[:].unsqueeze(1).to_broadcast([128, NT, 8]), op=ALU.add)
                ACT([tm1_b], [tm1_b], tm1, tm1, AF.Exp)
                ACT([tm1_b], [tm1_b], tm1, tm1, AF.Ln, bias=1.0)
                V("dve", "tensor_tensor", [tm1_b, CONST], [g_b], out=gcol, in0=tm1,
                  in1=negA[:].unsqueeze(1).to_broadcast([128, NT, 8]), op=ALU.mult)
                pb, pbb = bank()
                for t in range(NT):
                    MM(pb[:, t * 8:(t + 1) * 8], triF[:], gcol[:, t, :], True, True, [CONST, g_b], [pbb])
                V("dve", "tensor_copy", [pbb], [gc_b], out=gc, in_=pb[:, 0:NT * 8].rearrange("p (t c) -> p t c", c=8))
                rel(pbb)
                V("dve", "tensor_scalar_mul", [gc_b], [der_b], out=ngc, in0=gc, scalar1=-1.0)
                ACT([gc_b], [der_b], egc, gc, AF.Exp)
                V("dve", "tensor_tensor", [gc_b, lnb_b], [der_b], out=bls, in0=gc, in1=lnb, op=ALU.add)
                ACT([lnb_b], [der_b], beta, lnb, AF.Exp)
                ACT([der_b], [der_b], begc, bls, AF.Exp)

                if STOP == "Aab":
                    raise _Stop()
                a12_base = apos[0]
                for h in range(1):
                    areset(a12_base)
                    wst = [carve([128, 8, 128], BF16) for _ in range(2)]; wst_b = S.bufs(2)
                    zc = [carve([128, 515], F32) for _ in range(2)]; zc_b = S.bufs(2)
                    taL = [carve([128, 512], F32) for _ in range(2)]; taL_b = S.bufs(2)
                    tbL = [carve([128, 512], F32) for _ in range(2)]; tbL_b = S.bufs(2)
                    sqL = [carve([128, 512], BF16) for _ in range(2)]; sqL_b = S.bufs(2)
                    blkc = [0]
                    deferred = [None]
                    sT = [carve([128, T], BF16) for _ in range(3)]; sT_b = S.bufs(3)
                    Ktok = carve([128, NT, 128], BF16); Vtok = carve([128, NT, 128], BF16); KV_b = S.bufs(2)
                    Sst = carve([128, 128], F32); Sbf = carve([128, 128], BF16); S_b = S.buf(); Sb_b = S.buf()
                    NSL = 8
                    cm = []
                    for i in range(NSL):
                        cm.append(dict(
                            Eui=carve([128, 128], F32), Els=carve([128, 128], F32), Eus=carve([128, 128], F32),
                            Z=[carve([128, 256], CHAIN_DT) for _ in range(2)], P=[carve([128, 128], CHAIN_DT) for _ in range(2)],
                            attT=carve([128, 128], BF16), TmT=carve([128, 128], BF16), nWdT=carve([128, 128], BF16),
                            Bk=carve([128, 128], BF16), bV=carve([128, 128], BF16), Kd=carve([128, 128], BF16),
                            Ub=carve([128, 128], BF16), glc=carve([128, 1], F32), o1=carve([128, 128], F32),
                            b={k: S.buf() for k in ("Eui", "Els", "Eus", "Z0", "Z1", "P0", "P1", "attT", "TmT", "nWdT", "Bk", "bV",
                                                    "Kd", "Ub", "glc", "o1")}))
                for h in range(H):
                    for j in range(3):
                        fc = j * 8 + h
                        wsl = j % 2
                        DMA("sp", wst[wsl], wview(w_in_b, fc * 128, 128), [], [wst_b[wsl]], f"wst{wsl}")
                        for bi, (c0, cw) in enumerate(CB):
                            zs = bi % 2
                            bk_ = blkc[0] % 2; blkc[0] += 1
                            ta, ta_b, tb, tb_b, sq, sq_b = taL[bk_], taL_b[bk_], tbL[bk_], tbL_b[bk_], sqL[bk_], sqL_b[bk_]
                            pb, pbb = bank()
                            for kc in range(8):
                                MM(pb[:, :cw], wst[wsl][:, kc, :], hT[:, kc, c0:c0 + cw], kc == 0, kc == 7,
                                   [wst_b[wsl]] + hT_reads(c0, cw), [pbb])
                            if bi == 0:
                                V("pool", "memset", [], [zc_b[zs]], zc[zs][:, 0:3], 0.0)
                            else:
                                V("pool", "tensor_copy", [zc_b[1 - zs]], [zc_b[zs]], out=zc[zs][:, 0:3], in_=zc[1 - zs][:, 512:515])
                            S.op("act", (lambda o, i: (lambda e: e.copy(out=o, in_=i)))(zc[zs][:, 3:3 + cw], pb[:, :cw]),
                                 [pbb], [zc_b[zs]])
                            rel(pbb)
                            V("dve", "tensor_scalar_mul", [zc_b[zs], CONST], [ta_b], out=ta[:, :cw], in0=zc[zs][:, 3:3 + cw],
                              scalar1=convw[:, 3, fc:fc + 1])
                            for jj in (2, 1, 0):
                                V("dve", "scalar_tensor_tensor", [zc_b[zs], CONST, ta_b], [ta_b], out=ta[:, :cw],
                                  in0=zc[zs][:, jj:jj + cw], scalar=convw[:, jj, fc:fc + 1], in1=ta[:, :cw],
                                  op0=ALU.mult, op1=ALU.add)
                            silu_from(ta[:, :cw], [ta_b], tb[:, :cw], tb_b, None)
                            V("dve", "tensor_tensor", [ta_b, tb_b], [sT_b[j]], out=sT[j][:, c0:c0 + cw], in0=ta[:, :cw],
                              in1=tb[:, :cw], op=ALU.mult)
                            if j < 2:
                                ACT([sT_b[j]], [sq_b], sq[:, :cw], sT[j][:, c0:c0 + cw], AF.Square)
                                pb2, pbb2 = bank()
                                MM(pb2[:, :cw], onesB[:], sq[:, :cw], True, True, [CONST, sq_b], [pbb2])

                                def stage_b(pb2=pb2, pbb2=pbb2, tb=tb, tb_b=tb_b, j=j, c0=c0, cw=cw):
                                    ACT([pbb2], [tb_b], tb[:, :cw], pb2[:, :cw], AF.Ln, bias=EPS)
                                    rel(pbb2)
                                    ACT([tb_b], [tb_b], tb[:, :cw], tb[:, :cw], AF.Exp, scale=-0.5)
                                    V("dve", "scalar_tensor_tensor", [sT_b[j], tb_b], [sT_b[j]], out=sT[j][:, c0:c0 + cw],
                                      in0=sT[j][:, c0:c0 + cw], scalar=(128.0 ** -0.5 if j == 0 else 1.0), in1=tb[:, :cw],
                                      op0=ALU.mult, op1=ALU.mult)

                                prev_b, deferred[0] = deferred[0], stage_b
                            else:
                                prev_b, deferred[0] = deferred[0], None
                            if prev_b is not None:
                                prev_b()
                    if deferred[0] is not None:
                        deferred[0]()
                        deferred[0] = None
                    qT, kT, vT = sT
                    if STOP == "A12a":
                        raise _Stop()
                    for j, dst in ((1, Ktok), (2, Vtok)):
                        for t0 in range(0, NT, 8):
                            tn = min(8, NT - t0)
                            pb, pbb = bank()
                            pbv = pb[:].bitcast(BF16)
                            for i in range(tn):
                                TR(pbv[:, i * 128:(i + 1) * 128], sT[j][:, (t0 + i) * 128:(t0 + i + 1) * 128], identB[:],
                                   [sT_b[j], CONST], [pbb])
                            V("dve", "tensor_copy", [pbb], [KV_b[j - 1]], out=dst[:, t0:t0 + tn, :],
                              in_=pbv[:, 0:tn * 128].rearrange("p (t c) -> p t c", c=128))
                            rel(pbb)
                    V("pool", "memset", [], [S_b], Sst[:], 0.0)
                    V("pool", "memset", [], [Sb_b], Sbf[:], 0.0)
                    if STOP == "A12b":
                        raise _Stop()
                    def st_G(c):
                        m = cm[c % NSL]; mb = m["b"]
                        cs = slice(c * 128, (c + 1) * 128)
                        gb_l = gcol[:, c, h:h + 1].to_broadcast([128, 128])
                        lb_l = lnb[:, c, h:h + 1].to_broadcast([128, 128])
                        pg, pgb = bank()
                        pt, ptb = bank()
                        m["pg"], m["pgb"], m["pt"], m["ptb"] = pg, pgb, pt, ptb
                        MM(pg[:, 0:128], gb_l, triF[:], True, False, [g_b, CONST], [pgb])
                        MM(pg[:, 0:128], identB[:], muiB[:], False, True, [CONST], [pgb])
                        MM(pg[:, 128:256], gb_l, triF[:], True, False, [g_b, CONST], [pgb])
                        MM(pg[:, 128:256], identB[:], mlsB[:], False, True, [CONST], [pgb])
                        MM(pg[:, 256:384], kT[:, cs], kT[:, cs], True, True, [sT_b[1]], [pgb])
                        MM(pg[:, 384:512], kT[:, cs], qT[:, cs], True, True, [sT_b[1], sT_b[0]], [pgb])
                        MM(pt[:, 0:128], gb_l, triF[:], True, False, [g_b, CONST], [ptb])
                        MM(pt[:, 0:128], lb_l, identF[:], False, False, [lnb_b, CONST], [ptb])
                        MM(pt[:, 0:128], identB[:], musB[:], False, True, [CONST], [ptb])

                    def st_E(c):
                        m = cm[c % NSL]; mb = m["b"]
                        pg, pgb, pt, ptb = m["pg"], m["pgb"], m["pt"], m["ptb"]
                        ACT([pgb, der_b], [mb["Eui"]], m["Eui"], pg[:, 0:128], AF.Exp, bias=ngc[:, c, h:h + 1], scale=1.0)
                        ACT([pgb, der_b], [mb["Els"]], m["Els"], pg[:, 128:256], AF.Exp, bias=bls[:, c, h:h + 1], scale=-1.0)
                        ACT([pgb], [mb["glc"]], m["glc"], pg[:, 127:128], AF.Exp)
                        ACT([ptb, der_b], [mb["Eus"]], m["Eus"], pt[:, 0:128], AF.Exp, bias=ngc[:, c, h:h + 1], scale=1.0)
                        V("dve", "scalar_tensor_tensor", [pgb, mb["Els"]], [mb["Z0"]], out=m["Z"][0][:, 0:128],
                          in0=pg[:, 256:384], scalar=-1.0, in1=m["Els"], op0=ALU.mult, op1=ALU.mult)
                        V("dve", "scalar_tensor_tensor", [pgb, mb["Eus"]], [mb["Z0"]], out=m["Z"][0][:, 128:256],
                          in0=pg[:, 256:384], scalar=-1.0, in1=m["Eus"], op0=ALU.mult, op1=ALU.mult)
                        V("dve", "tensor_tensor", [pgb, mb["Eui"]], [mb["attT"]], out=m["attT"], in0=pg[:, 384:512],
                          in1=m["Eui"], op=ALU.mult)
                        V("dve", "tensor_tensor", [mb["Z0"], CONST], [mb["P0"]], out=m["P"][0], in0=m["Z"][0][:, 128:256],
                          in1=identF[:], op=ALU.add)
                        V("dve", "tensor_scalar_mul", [KV_b[0], der_b], [mb["Bk"]], out=m["Bk"], in0=Ktok[:, c, :],
                          scalar1=begc[:, c, h:h + 1])
                        V("dve", "tensor_scalar_mul", [KV_b[1], der_b], [mb["bV"]], out=m["bV"], in0=Vtok[:, c, :],
                          scalar1=beta[:, c, h:h + 1])
                        V("dve", "tensor_scalar_mul", [KV_b[0], mb["Eui"]], [mb["Kd"]], out=m["Kd"], in0=Ktok[:, c, :],
                          scalar1=m["Eui"][:, 127:128])
                        rel(pgb, ptb)

                    def st_Lmm(c, lev):
                        m = cm[c % NSL]; mb = m["b"]
                        zi = lev % 2
                        pi = (lev - 1) % 2
                        Zc, Zcb = m["Z"][zi], mb[f"Z{zi}"]
                        pk, pkb = bank()
                        m["pk"], m["pkb"] = pk, pkb
                        if lev >= 1:
                            MM(pk[:, 256:384], Zc[:, 0:128], m["P"][pi], True, True, [Zcb, mb[f"P{pi}"]], [pkb])
                        if lev < 6:
                            MM(pk[:, 0:128], Zc[:, 128:256], Zc[:, 0:128], True, True, [Zcb], [pkb])
                            MM(pk[:, 128:256], Zc[:, 0:128], Zc[:, 128:256], True, True, [Zcb], [pkb])

                    def st_Lev(c, lev):
                        m = cm[c % NSL]; mb = m["b"]
                        zi = lev % 2
                        pi = (lev - 1) % 2
                        Zn, Znb = m["Z"][1 - zi], mb[f"Z{1 - zi}"]
                        pk, pkb = m["pk"], m["pkb"]
                        if lev >= 1:
                            if lev < 6:
                                V("dve", "tensor_tensor", [pkb, mb[f"P{pi}"]], [mb[f"P{1 - pi}"]], out=m["P"][1 - pi],
                                  in0=pk[:, 256:384], in1=m["P"][pi], op=ALU.add)
                            else:
                                V("dve", "tensor_tensor", [pkb, mb[f"P{pi}"]], [mb["TmT"]], out=m["TmT"],
                                  in0=pk[:, 256:384], in1=m["P"][pi], op=ALU.add)
                        if lev < 6:
                            V("act", "copy", [pkb], [Znb], out=Zn[:, 0:256], in_=pk[:, 0:256])
                        rel(pkb)

                    def st_W(c):
                        m = cm[c % NSL]; mb = m["b"]
                        pw, pwb = bank()
                        MM(pw[:, 0:128], m["Bk"], m["TmT"], True, True, [mb["Bk"], mb["TmT"]], [pwb])
                        V("act", "mul", [pwb], [mb["nWdT"]], out=m["nWdT"], in_=pw[:, 0:128], mul=-1.0)
                        rel(pwb)

                    def st_Ra(c):
                        m = cm[c % NSL]; mb = m["b"]
                        pu, pub = bank()
                        MM(pu[:, 0:128], m["TmT"], m["bV"], True, c == 0, [mb["TmT"], mb["bV"]], [pub])
                        if c > 0:
                            MM(pu[:, 0:128], m["nWdT"], Sbf, False, True, [mb["nWdT"], Sb_b], [pub])
                        V("act", "copy", [pub], [mb["Ub"]], out=m["Ub"], in_=pu[:, 0:128])
                        rel(pub)

                    def st_Rb(c):
                        m = cm[c % NSL]; mb = m["b"]
                        cs = slice(c * 128, (c + 1) * 128)
                        po, pob = bank()
                        po2, pob2 = bank()
                        MM(po[:, 0:128], m["Kd"], m["Ub"], True, True, [mb["Kd"], mb["Ub"]], [pob])
                        MM(po2[:, 0:128], qT[:, cs], Sbf, True, True, [sT_b[0], Sb_b], [pob2])
                        MM(po2[:, 128:256], m["attT"], m["Ub"], True, True, [mb["attT"], mb["Ub"]], [pob2])
                        V("dve", "scalar_tensor_tensor", [S_b, mb["glc"], pob], [Sb_b], out=Sbf[:], in0=Sst[:],
                          scalar=m["glc"][:, 0:1], in1=po[:, 0:128], op0=ALU.mult, op1=ALU.add)
                        V("dve", "scalar_tensor_tensor", [S_b, mb["glc"], pob], [S_b], out=Sst[:], in0=Sst[:],
                          scalar=m["glc"][:, 0:1], in1=po[:, 0:128], op0=ALU.mult, op1=ALU.add)
                        ACT([pob2, der_b], [mb["o1"]], m["o1"], po2[:, 0:128], AF.Copy, scale=egc[:, c, h:h + 1])
                        V("dve", "tensor_tensor", [pob2, mb["o1"]], [OV_b[c]], out=OV[:, c, h * 128:(h + 1) * 128],
                          in0=po2[:, 128:256], in1=m["o1"], op=ALU.add)
                        rel(pob, pob2)

                    def st_R(item):
                        (st_Ra if item[0] == "a" else st_Rb)(item[1])

                    GS = 4
                    groups = [list(range(i, min(i + GS, NT))) for i in range(0, NT, GS)]
                    pending = []
                    for grp in groups:
                        stages = [("GE", grp[0:2]), ("GE", grp[2:4])] + [("L", lev) for lev in range(7)] + [("W", None)]
                        for kind, lev in stages:
                            if kind == "GE":
                                for c in lev:
                                    st_G(c)
                                for c in lev:
                                    st_E(c)
                            elif kind == "L":
                                for c in grp:
                                    st_Lmm(c, lev)
                                for c in grp:
                                    st_Lev(c, lev)
                            else:
                                for c in grp:
                                    st_W(c)
                            if pending:
                                st_R(pending.pop(0))
                        while pending:
                            st_R(pending.pop(0))
                        pending = [(ph, c) for c in grp for ph in ("a", "b")]
                    while pending:
                        st_R(pending.pop(0))
                S.barrier()
                if STOP == "A12":
                    raise _Stop()
                areset(0)
                gateW = carve([128, 8, 1024], BF16); outW = carve([128, 8, 1024], BF16); gw_b = S.buf(); ow_b = S.buf()
                sigL = [carve([128, 1024], F32) for _ in range(2)]; sigL_b = S.bufs(2)
                gsL = [carve([128, 1024], F32) for _ in range(2)]; gsL_b = S.bufs(2)
                osqL = [carve([128, 1024], F32) for _ in range(2)]; osqL_b = S.bufs(2)
                ossL = [carve([128, 8], F32) for _ in range(2)]; ossL_b = S.bufs(2)
                ogbL = [carve([128, 1024], BF16) for _ in range(2)]; ogbL_b = S.bufs(2)
                ogTL = [carve([128, 8, 128], BF16) for _ in range(2)]; ogTL_b = S.bufs(2)
                h1t = [carve([128, 1024], F32) for _ in range(2)]; h1t_b = S.bufs(2)
                DMA("sp", gateW, wview(w_in_b, 3072, 1024), [], [gw_b], "gw")
                DMA("sp", outW, wview(w_outa_b, 0, 1024), [], [ow_b], "ow")
                for t in range(NT):
                    sl = t % 2
                    sig, sig_b, gs, gs_b, osq, osq_b = sigL[sl], sigL_b[sl], gsL[sl], gsL_b[sl], osqL[sl], osqL_b[sl]
                    oss, oss_b, ogb, ogb_b, ogT, ogT_b = ossL[sl], ossL_b[sl], ogbL[sl], ogbL_b[sl], ogTL[sl], ogTL_b[sl]
                    load_x_tile(s, t, sl)
                    pgs = [bank(), bank()]
                    for hf in range(2):
                        pb, pbb = pgs[hf]
                        for kc in range(8):
                            MM(pb[:, :], hT[:, kc, t * 128:(t + 1) * 128], gateW[:, kc, hf * 512:(hf + 1) * 512], kc == 0, kc == 7,
                               [hT_b[t], gw_b], [pbb])
                        silu_from(pb[:, :], [pbb], sig[:, hf * 512:(hf + 1) * 512], sig_b, None)
                        V("dve", "tensor_tensor", [pbb, sig_b], [gs_b], out=gs[:, hf * 512:(hf + 1) * 512], in0=pb[:, :],
                          in1=sig[:, hf * 512:(hf + 1) * 512], op=ALU.mult)
                        rel(pbb)
                    ACT([OV_b[t]], [osq_b], osq, OV[:, t, :], AF.Square)
                    V("dve", "tensor_reduce", [osq_b], [oss_b], out=oss, in_=osq.rearrange("p (a b) -> p a b", a=8),
                      axis=AX.X, op=ALU.add)
                    ACT([oss_b], [oss_b], oss, oss, AF.Ln, scale=1.0 / 128, bias=EPS)
                    ACT([oss_b], [oss_b], oss, oss, AF.Exp, scale=-0.5)
                    V("dve", "tensor_tensor", [OV_b[t], oss_b], [osq_b], out=osq.rearrange("p (a b) -> p a b", a=8),
                      in0=OV[:, t, :].rearrange("p (a b) -> p a b", a=8), in1=oss.unsqueeze(2).to_broadcast([128, 8, 128]),
                      op=ALU.mult)
                    V("dve", "tensor_tensor", [osq_b, gs_b], [ogb_b], out=ogb, in0=osq, in1=gs, op=ALU.mult)
                    pb, pbb = bank()
                    pbv = pb[:].bitcast(BF16)
                    for kc in range(8):
                        TR(pbv[:, kc * 128:(kc + 1) * 128], ogb[:, kc * 128:(kc + 1) * 128], identB[:], [ogb_b, CONST], [pbb])
                    V("dve", "tensor_copy", [pbb], [ogT_b], out=ogT, in_=pbv.rearrange("p (k c) -> p k c", k=8))
                    rel(pbb)
                    for hf in range(2):
                        pb, pbb = bank()
                        for kc in range(8):
                            MM(pb[:, :], ogT[:, kc, :], outW[:, kc, hf * 512:(hf + 1) * 512], kc == 0, kc == 7, [ogT_b, ow_b], [pbb])
                        V("dve", "tensor_tensor", [pbb, xin_b[sl]], [h1t_b[sl]], out=h1t[sl][:, hf * 512:(hf + 1) * 512],
                          in0=pb[:, :], in1=xin[sl][:, hf * 512:(hf + 1) * 512], op=ALU.add)
                        rel(pbb)
                    DMA("act", h1_d[t * 128:(t + 1) * 128, :], h1t[sl], [h1t_b[sl]], [], f"h1s{sl}")
                    norm_to_hT(h1t[sl], h1t_b[sl], t, sl)
                S.barrier()
                if STOP == "A3":
                    raise _Stop()
                areset(0)
                KT = carve([128, 8, T], BF16); KT_b = S.buf()
                ksc = carve([128, NT, 8], F32); ksc_b = S.buf()
                kv_base = apos[0]
                bpool[0] = list(range(7))
                wdn = carve([128, 8, 320], BF16); wuk = carve([128, 2, 1024], BF16); wuv = carve([128, 2, 1024], BF16)
                wkv_b = S.buf()
                cTf = carve([128, 2, 512], F32); cT_b = S.buf()
                kpe = carve([64, 512], F32); kpe_b = S.buf()
                kpb = carve([64, 512], BF16); kpb_b = S.buf()
                csq = carve([128, 2, 512], BF16); csq_b = S.buf()
                rb = carve([128, 512], F32); rb_b = S.buf()
                ckv = carve([128, 2, 512], BF16); ckv_b = S.buf()
                sqK = [carve([128, 512], BF16) for _ in range(2)]; sqK_b = S.bufs(2)
                sqR = carve([64, 512], BF16); sqR_b = S.buf()
                t1 = carve([64, 512], F32); t1_b = S.buf()
                t2 = carve([64, 512], F32); t2_b = S.buf()
                DMA("sp", wdn, wview(w_down_b, 0, 320), [], [wkv_b], "wkv")
                DMA("sp", wuk, wview(w_uk_b, 0, 1024), [], [wkv_b], "wkv")
                DMA("sp", wuv, wview(w_uv_b, 0, 1024), [], [wkv_b], "wkv")
                pks, pksb = fbank(7)
                for (c0, cw) in CB:
                    rd = hT_reads(c0, cw)
                    pbs_ = [bank(), bank(), bank()]
                    for j, (lo, mw) in enumerate(((0, 128), (128, 128), (256, 64))):
                        pb, pbb = pbs_[j]
                        for kc in range(8):
                            MM(pb[0:mw, :cw], wdn[:, kc, lo:lo + mw], hT[:, kc, c0:c0 + cw], kc == 0, kc == 7, [wkv_b] + rd, [pbb])
                    for j in range(2):
                        pb, pbb = pbs_[j]
                        S.op("act", (lambda o, i: (lambda e: e.copy(out=o, in_=i)))(cTf[:, j, :cw], pb[:, :cw]), [pbb], [cT_b])
                        rel(pbb)
                    pb, pbb = pbs_[2]
                    V("dve", "tensor_scalar_mul", [pbb, CONST], [kpe_b], out=kpe[:, :cw], in0=pb[0:64, :cw], scalar1=kgr[:, 0:1])
                    ACT([pbb], [sqR_b], sqR[:, :cw], pb[0:64, :cw], AF.Square)
                    rel(pbb)
                    ACT([cT_b], [csq_b], csq[:, :, :cw], cTf[:, :, :cw], AF.Square)
                    pb, pbb = bank()
                    for j in range(2):
                        MM(pb[:, :cw], onesB[:], csq[:, j, :cw], j == 0, j == 1, [CONST, csq_b], [pbb])
                    ACT([pbb], [rb_b], rb[:, :cw], pb[:, :cw], AF.Ln, scale=1.0 / 256, bias=EPS)
                    rel(pbb)
                    ACT([rb_b], [rb_b], rb[:, :cw], rb[:, :cw], AF.Exp, scale=-0.5)
                    V("dve", "tensor_tensor", [cT_b, rb_b], [ckv_b], out=ckv[:, :, :cw], in0=cTf[:, :, :cw],
                      in1=rb[:, :cw].unsqueeze(1).to_broadcast([128, 2, cw]), op=ALU.mult)
                    V("pool", "tensor_copy", [kpe_b], [kpb_b], out=kpb[:, :cw], in_=kpe[:, :cw])
                    pb, pbb = bank()
                    MM(pb[0:64, :cw], rotB[:], kpb[:, :cw], True, True, [CONST, kpb_b], [pbb])
                    V("dve", "tensor_tensor", [pbb, CONST], [t1_b], out=t1[:, :cw], in0=pb[0:64, :cw], in1=sin2[:, c0:c0 + cw], op=ALU.mult)
                    rel(pbb)
                    V("dve", "tensor_tensor", [kpe_b, CONST], [t2_b], out=t2[:, :cw], in0=kpe[:, :cw], in1=cos2[:, c0:c0 + cw], op=ALU.mult)
                    V("dve", "tensor_tensor", [t1_b, t2_b], [RR_b], out=RR[:, c0:c0 + cw], in0=t1[:, :cw], in1=t2[:, :cw], op=ALU.add)
                    for h in range(H):
                        pb, pbb = bank()
                        for r in range(2):
                            MM(pb[:, :cw], wuk[:, r, h * 128:(h + 1) * 128], ckv[:, r, :cw], r == 0, r == 1, [wkv_b, ckv_b], [pbb])
                        V("dve", "tensor_scalar_mul", [pbb, CONST], [KT_b], out=KT[:, h, c0:c0 + cw], in0=pb[:, :cw], scalar1=kgn[:, 0:1])
                        q = h % 2
                        ACT([pbb], [sqK_b[q]], sqK[q][:, :cw], pb[:, :cw], AF.Square)
                        rel(pbb)
                        for ti in range(cw // 128):
                            t = c0 // 128 + ti
                            MM(pks[:, t * 8 + h:t * 8 + h + 1], sqK[q][:, ti * 128:(ti + 1) * 128], onesB[:, 0:1], True, False,
                               [sqK_b[q], CONST], [pksb])
                            MM(pks[:, t * 8 + h:t * 8 + h + 1], sqR[:, ti * 128:(ti + 1) * 128], onesB[0:64, 0:1], False, True,
                               [sqR_b, CONST], [pksb])
                    for ti in range(cw // 128):
                        t = c0 // 128 + ti
                        for hf in range(2):
                            pb, pbb = bank()
                            for r in range(2):
                                MM(pb[:, :], ckv[:, r, ti * 128:(ti + 1) * 128], wuv[:, r, hf * 512:(hf + 1) * 512], r == 0, r == 1,
                                   [ckv_b, wkv_b], [pbb])
                            S.op("act", (lambda o, i: (lambda e: e.copy(out=o, in_=i)))(OV[:, t, hf * 512:(hf + 1) * 512], pb[:, :]),
                                 [pbb], [OV_b[t]])
                            rel(pbb)
                ACT([pksb], [ksc_b], ksc, pks[:, 0:NT * 8].rearrange("p (t c) -> p t c", c=8), AF.Ln, scale=1.0 / 192, bias=EPS)
                ACT([ksc_b], [ksc_b], ksc, ksc, AF.Exp, scale=-0.5)
                V("dve", "tensor_scalar_mul", [ksc_b], [ksc_b], out=ksc, in0=ksc, scalar1=192.0 ** -0.5)
                S.barrier()
                if STOP == "KV":
                    raise _Stop()
                bpool[0] = list(range(4))
                b_base = kv_base
                for _once in range(1):
                    areset(b_base)
                    woutB = carve([128, 8, 1024], BF16); wo_b = S.buf()
                    wst = [carve([128, 8, 128], BF16) for _ in range(2)]; wst_b = S.bufs(2)
                    wuq = [carve([128, 3, 192], BF16) for _ in range(2)]; wuq_b = S.bufs(2)
                    cqT = carve([128, 3, 512], F32); cq_b = S.buf()
                    cqn = carve([128, 3, 512], BF16); cqn_b = S.buf()
                    cqs, cqs_b = cqn, cqn_b
                    rb = carve([128, 512], F32); rb_b = S.buf()
                    qn = carve([128, 512], F32); qn_b = S.buf()
                    qr = carve([64, 512], F32); qr_b = S.buf()
                    qrb = carve([64, 512], BF16); qrb_b = S.buf()
                    sqn = carve([128, 512], BF16); sqn_b = S.buf()
                    sqr = carve([64, 512], BF16); sqr_b = S.buf()
                    rq = carve([128, 512], F32); rq_b = S.buf()
                    QnT2 = [carve([128, 512], BF16) for _ in range(2)]; QrT2 = [carve([64, 512], BF16) for _ in range(2)]
                    Q2_b = S.bufs(2)
                    t1 = carve([64, 512], F32); t1_b = S.buf()
                    t2 = carve([64, 512], F32); t2_b = S.buf()
                    sig = carve([128, 512], F32); sig_b = S.buf()
                    gs2 = [carve([128, 512], F32) for _ in range(2)]; gs2_b = S.bufs(2)
                    PT = [carve([128, 512], BF16) for _ in range(3)]; PT_b = S.bufs(3)
                    rd_ = carve([128, 512], F32); rd_b = S.buf()
                    ot = carve([128, 512], F32); ot_b = S.buf()
                    OGT = carve([128, 8, 512], BF16); OGT_b = S.buf()
                for q0 in range(1, NT, 4):
                    q1 = min(q0 + 3, NT - 1)
                    bw = (q1 - q0 + 1) * 128
                    bc0 = q0 * 128
                    hrd = hT_reads(bc0, bw)
                    DMA("sp", woutB, wview(w_outb_b, 0, 1024), [], [wo_b], "wo")
                    wc = [0]
                    for fc in range(3):
                        wsl = wc[0] % 2; wc[0] += 1
                        DMA("sp", wst[wsl], wview(w_inb_b, fc * 128, 128), [], [wst_b[wsl]], f"wst{wsl}")
                        pb, pbb = bank()
                        for kc in range(8):
                            MM(pb[:, :bw], wst[wsl][:, kc, :], hT[:, kc, bc0:bc0 + bw], kc == 0, kc == 7, [wst_b[wsl]] + hrd, [pbb])
                        S.op("act", (lambda o, i: (lambda e: e.copy(out=o, in_=i)))(cqT[:, fc, :bw], pb[:, :bw]), [pbb], [cq_b])
                        rel(pbb)
                    ACT([cq_b], [cqs_b], cqs[:, :, :bw], cqT[:, :, :bw], AF.Square)
                    pb, pbb = bank()
                    for j in range(3):
                        MM(pb[:, :bw], onesB[:], cqs[:, j, :bw], j == 0, j == 2, [CONST, cqs_b], [pbb])
                    ACT([pbb], [rb_b], rb[:, :bw], pb[:, :bw], AF.Ln, scale=1.0 / 384, bias=EPS)
                    rel(pbb)
                    ACT([rb_b], [rb_b], rb[:, :bw], rb[:, :bw], AF.Exp, scale=-0.5)
                    V("dve", "tensor_tensor", [cq_b, rb_b], [cqn_b], out=cqn[:, :, :bw], in0=cqT[:, :, :bw],
                      in1=rb[:, :bw].unsqueeze(1).to_broadcast([128, 3, bw]), op=ALU.mult)
                    pctr = [0]

                    def prep(h):
                        us = h % 2
                        QnT, QrT, Q_b, gs, gs_b = QnT2[us], QrT2[us], Q2_b[us], gs2[us], gs2_b[us]
                        DMA("sp", wuq[us], wview(w_uq_b, h * 192, 192), [], [wuq_b[us]], f"wuq{us}")
                        pq, pqb = bank()
                        pr, prb = bank()
                        for j in range(3):
                            MM(pq[:, :bw], wuq[us][:, j, 0:128], cqn[:, j, :bw], j == 0, j == 2, [wuq_b[us], cqn_b], [pqb])
                        for j in range(3):
                            MM(pr[0:64, :bw], wuq[us][:, j, 128:192], cqn[:, j, :bw], j == 0, j == 2, [wuq_b[us], cqn_b], [prb])
                        yield
                        V("dve", "tensor_scalar_mul", [pqb, CONST], [qn_b], out=qn[:, :bw], in0=pq[:, :bw], scalar1=qgn[:, 0:1])
                        V("dve", "tensor_scalar_mul", [prb, CONST], [qr_b], out=qr[:, :bw], in0=pr[0:64, :bw], scalar1=qgr[:, 0:1])
                        ACT([pqb], [sqn_b], sqn[:, :bw], pq[:, :bw], AF.Square)
                        ACT([prb], [sqr_b], sqr[:, :bw], pr[0:64, :bw], AF.Square)
                        rel(pqb, prb)
                        yield
                        pb, pbb = bank()
                        MM(pb[:, :bw], onesB[:], sqn[:, :bw], True, False, [CONST, sqn_b], [pbb])
                        MM(pb[:, :bw], onesB[0:64, :], sqr[:, :bw], False, True, [CONST, sqr_b], [pbb])
                        yield
                        ACT([pbb], [rq_b], rq[:, :bw], pb[:, :bw], AF.Ln, scale=1.0 / 192, bias=EPS)
                        rel(pbb)
                        ACT([rq_b], [rq_b], rq[:, :bw], rq[:, :bw], AF.Exp, scale=-0.5)
                        V("pool", "tensor_copy", [qr_b], [qrb_b], out=qrb[:, :bw], in_=qr[:, :bw])
                        yield
                        V("dve", "tensor_tensor", [qn_b, rq_b], [Q_b], out=QnT[:, :bw], in0=qn[:, :bw], in1=rq[:, :bw], op=ALU.mult)
                        pb, pbb = bank()
                        MM(pb[0:64, :bw], rotB[:], qrb[:, :bw], True, True, [CONST, qrb_b], [pbb])
                        yield
                        V("dve", "tensor_tensor", [pbb, CONST], [t1_b], out=t1[:, :bw], in0=pb[0:64, :bw], in1=sin2[:, bc0:bc0 + bw], op=ALU.mult)
                        rel(pbb)
                        V("dve", "tensor_tensor", [qr_b, CONST], [t2_b], out=t2[:, :bw], in0=qr[:, :bw], in1=cos2[:, bc0:bc0 + bw], op=ALU.mult)
                        yield
                        V("dve", "tensor_tensor", [t1_b, t2_b], [t1_b], out=t1[:, :bw], in0=t1[:, :bw], in1=t2[:, :bw], op=ALU.add)
                        V("dve", "tensor_tensor", [t1_b, rq_b], [Q_b], out=QrT[:, :bw], in0=t1[:, :bw], in1=rq[0:64, :bw], op=ALU.mult)
                        wsl = wc[0] % 2; wc[0] += 1
                        DMA("sp", wst[wsl], wview(w_inb_b, 384 + h * 128, 128), [], [wst_b[wsl]], f"wst{wsl}")
                        pgt, pgtb = bank()
                        for kc in range(8):
                            MM(pgt[:, :bw], wst[wsl][:, kc, :], hT[:, kc, bc0:bc0 + bw], kc == 0, kc == 7, [wst_b[wsl]] + hrd, [pgtb])
                        yield
                        ACT([pgtb], [sig_b], sig[:, :bw], pgt[:, :bw], AF.Exp, scale=-1.0)
                        yield
                        ACT([sig_b], [sig_b], sig[:, :bw], sig[:, :bw], AF.Ln, bias=1.0)
                        ACT([sig_b], [sig_b], sig[:, :bw], sig[:, :bw], AF.Exp, scale=-1.0)
                        yield
                        V("dve", "tensor_tensor", [pgtb, sig_b], [gs_b], out=gs[:, :bw], in0=pgt[:, :bw], in1=sig[:, :bw], op=ALU.mult)
                        rel(pgtb)

                    def attn(h, nxt):
                        us = h % 2
                        QnT, QrT, Q_b, gs, gs_b = QnT2[us], QrT2[us], Q2_b[us], gs2[us], gs2_b[us]
                        pO, pOb = fbank(4 + 2 * (h % 2))
                        pD, pDb = fbank(5 + 2 * (h % 2))
                        def s_mm(kt):
                            o0 = (max(kt, q0) - q0) * 128
                            ks = slice(kt * 128, (kt + 1) * 128)
                            psc, pscb = bank()
                            diag = kt >= q0
                            MM(psc[:, o0:bw], KT[:, h, ks], QnT[:, o0:bw], True, False, [KT_b, Q_b], [pscb])
                            MM(psc[:, o0:bw], RR[:, ks], QrT[:, o0:bw], False, not diag, [RR_b, Q_b], [pscb])
                            if diag:
                                MM(psc[:, o0:o0 + 128], urowB[:], wrowB[:], False, True, [CONST], [pscb])
                            return psc, pscb

                        cur = s_mm(0)
                        for kt in range(0, q1 + 1):
                            o0 = (max(kt, q0) - q0) * 128
                            psc, pscb = cur
                            if kt + 1 <= q1:
                                cur = s_mm(kt + 1)
                            ps_ = pctr[0] % 3; pctr[0] += 1
                            ACT([pscb, ksc_b, CONST], [PT_b[ps_]], PT[ps_][:, o0:bw], psc[:, o0:bw], AF.Exp,
                                scale=ksc[:, kt, h:h + 1], bias=(padb[:, 0:1] if kt == 0 else zcol[:, 0:1]))
                            rel(pscb)
                            MM(pO[:, o0:bw], OV[:, kt, h * 128:(h + 1) * 128], PT[ps_][:, o0:bw], kt == 0, kt == q1, [OV_b[kt], PT_b[ps_]], [pOb])
                            MM(pD[:, o0:bw], onesB[:], PT[ps_][:, o0:bw], kt == 0, kt == q1, [CONST, PT_b[ps_]], [pDb])
                            if nxt is not None:
                                next(nxt, None)
                        if nxt is not None:
                            for _ in nxt:
                                pass
                        V("dve", "reciprocal", [pDb], [rd_b], out=rd_[:, :bw], in_=pD[:, :bw])
                        V("dve", "tensor_tensor", [pOb, rd_b], [ot_b], out=ot[:, :bw], in0=pO[:, :bw], in1=rd_[:, :bw], op=ALU.mult)
                        V("dve", "tensor_tensor", [ot_b, gs_b], [OGT_b], out=OGT[:, h, :bw], in0=ot[:, :bw], in1=gs[:, :bw], op=ALU.mult)

                    for _ in prep(0):
                        pass
                    for h in range(H):
                        attn(h, prep(h + 1) if h + 1 < H else None)
                    for qt in range(q0, q1 + 1):
                        sl = qt % 2
                        DMA("sp", xin[sl][:], h1_d[qt * 128:(qt + 1) * 128, :], [], [xin_b[sl]], f"xl{sl}")
                        lc = (qt - q0) * 128
                        for hf in range(2):
                            pb, pbb = bank()
                            for kc in range(8):
                                MM(pb[:, :], OGT[:, kc, lc:lc + 128], woutB[:, kc, hf * 512:(hf + 1) * 512], kc == 0, kc == 7,
                                   [OGT_b, wo_b], [pbb])
                            V("dve", "tensor_tensor", [pbb, xin_b[sl]], [xin_b[sl]], out=xin[sl][:, hf * 512:(hf + 1) * 512],
                              in0=pb[:, :], in1=xin[sl][:, hf * 512:(hf + 1) * 512], op=ALU.add)
                            rel(pbb)
                        r0 = s * SEQ + (qt - 1) * 128
                        DMA("act", out_d[r0:r0 + 128, :], xin[sl][:], [xin_b[sl]], [], f"os{sl}")
                S.barrier()
        except _Stop:
            S.barrier()
        S.emit(nc, st)
    nc._sched_info = S.info
    nc._sched = S
    return nc


_NC_CACHE = {}


def kernel(**inputs):
    x = np.ascontiguousarray(inputs["x"], dtype=np.float32)
    B, SEQ, _ = x.shape
    NSEQ = B // N_CORES
    key = (NSEQ, SEQ)
    if key not in _NC_CACHE:
        _NC_CACHE[key] = build(NSEQ, SEQ)
    nc = _NC_CACHE[key]
    consts = host_consts(SEQ + 128)
    shared = {}
    for k, v in inputs.items():
        if k == "x":
            continue
        a = np.ascontiguousarray(np.asarray(v, dtype=np.float32))
        if a.ndim >= 2 and a.shape[0] == 1:
            a = a[0]
        shared[k] = np.ascontiguousarray(a)
    shared.update(consts)
    in_maps = []
    for c in range(N_CORES):
        m = dict(shared)
        m["x"] = np.ascontiguousarray(x[c * NSEQ:(c + 1) * NSEQ].reshape(NSEQ * SEQ, D))
        in_maps.append(m)
    res = run_bass_kernel_spmd(nc, in_maps, core_ids=list(range(N_CORES)))
    out = np.concatenate([np.asarray(r["out"]).reshape(NSEQ, SEQ, D) for r in res.results], axis=0)
    return out.astype(np.float32)
```
